# Optimizing a Trainium2 kernel written in Bass

```python
import jax
import jax.numpy as jnp
from jax import lax
import numpy as np

D_MODEL = 2048
BATCH = 32
SEQ = 256
DEPTH = 2
DEC_BATCH = 4
DEC_SEQ = 4096
PAST_LEN = 256

GRID_W = 64
N_EVEN = (DEPTH + 1) // 2
N_ODD = DEPTH // 2
MIX_W = D_MODEL
GROUP_W = MIX_W // 2
D_FF = 4 * D_MODEL
N_MOD = 6
EPS = 1e-6
CONV_W = 4
GLA_H = 4
GLA_DK = GROUP_W // (2 * GLA_H)
GLA_DV = GROUP_W // GLA_H
GLA_KEY = GLA_H * GLA_DK
GLA_RANK = 16
GLA_TAU = 16.0
GLA_CHUNK = 32
LRU_W = GROUP_W
LRU_BLOCKS = 8
LRU_BS = LRU_W // LRU_BLOCKS
LRU_C = 8.0
GDN_H = 8
GDN_DH = GROUP_W // GDN_H
ML_H = 4
ML_DK = GROUP_W // (2 * ML_H)
ML_DV = GROUP_W // ML_H
ML_KEY = ML_H * ML_DK
CHUNK = 64
EVEN_SPLITS = (GLA_KEY, GLA_KEY, GROUP_W, GROUP_W, GLA_RANK, GLA_RANK, LRU_W, LRU_W)
ODD_SPLITS = (GROUP_W, GROUP_W, GROUP_W, GROUP_W, 2 * GDN_H, 2 * GDN_H,
              ML_KEY, ML_KEY, GROUP_W, GROUP_W, 2 * ML_H, 2 * ML_H)
IN_E = sum(EVEN_SPLITS)
IN_O = sum(ODD_SPLITS)

kernel_name = 'bidir_hybrid_gla_rglru_gdn_mlstm_diffusion_step'


def _split(x, sizes):
    idx = [int(i) for i in np.cumsum(sizes)[:-1]]
    return jnp.split(x, idx, axis=-1)


def _flip(t):
    return jnp.flip(t, axis=1)


def _chunks(t, c):
    B, L = t.shape[0], t.shape[1]
    t = t.reshape((B, L // c, c) + t.shape[2:])
    perm = (1, 0, 3, 2) + tuple(range(4, t.ndim))
    return t.transpose(perm)


def _unchunks(t):
    n, B, H, c, d = t.shape
    return t.transpose(1, 0, 3, 2, 4).reshape(B, n * c, H, d)


def rmsnorm(x, g):
    xf = x.astype(jnp.float32)
    y = xf * lax.rsqrt(jnp.mean(xf * xf, axis=-1, keepdims=True) + EPS)
    return (y * g.astype(jnp.float32)).astype(x.dtype)


def head_rmsnorm(x, n_heads, g):
    B, L, W = x.shape
    xh = x.reshape(B, L, n_heads, W // n_heads)
    xh = xh * lax.rsqrt(jnp.mean(xh * xh, axis=-1, keepdims=True) + EPS)
    return xh.reshape(B, L, W) * g.astype(jnp.float32)


def l2norm(x):
    return x * lax.rsqrt(jnp.sum(x * x, axis=-1, keepdims=True) + EPS)


def line_conv(x, w, line_len):
    B, L, C = x.shape
    xl = x.reshape(B, L // line_len, line_len, C)
    pad_l = CONV_W // 2
    xp = jnp.pad(xl, ((0, 0), (0, 0), (pad_l, CONV_W - 1 - pad_l), (0, 0)))
    w = w.astype(jnp.float32)
    y = xp[:, :, 0:line_len, :] * w[0]
    for j in range(1, CONV_W):
        y = y + xp[:, :, j:j + line_len, :] * w[j]
    return y.reshape(B, L, C)


def gla_scan(q, k, v, g, s0):
    qc, kc, vc, gc = (_chunks(t, GLA_CHUNK) for t in (q, k, v, g))
    bc = jnp.cumsum(gc, axis=3)
    tri = jnp.tril(jnp.ones((GLA_CHUNK, GLA_CHUNK), dtype=bool))

    def step(S, inp):
        qi, ki, vi, bi = inp
        o_inter = jnp.einsum('bhtk,bhkv->bhtv', qi * jnp.exp(bi), S)
        diff = bi[:, :, :, None, :] - bi[:, :, None, :, :]
        dec = jnp.exp(jnp.where(tri[:, :, None], diff, -jnp.inf))
        att = jnp.einsum('bhtk,bhsk,bhtsk->bhts', qi, ki, dec)
        o = o_inter + jnp.einsum('bhts,bhsv->bhtv', att, vi)
        b_last = bi[:, :, -1:, :]
        S_new = jnp.exp(b_last[:, :, 0, :, None]) * S + jnp.einsum(
            'bhsk,bhsv->bhkv', ki * jnp.exp(b_last - bi), vi)
        return S_new, o

    s_fin, oc = lax.scan(step, s0, (qc, kc, vc, bc))
    return _unchunks(oc), s_fin


def rglru(xc, gate_w, gate_b, lam, h0):
    B, L, W = xc.shape
    xb = xc.reshape(B, L, LRU_BLOCKS, LRU_BS)
    gates = jnp.einsum('blni,gnij->gblnj', xb, gate_w.astype(jnp.float32)).reshape(2, B, L, W)
    gates = jax.nn.sigmoid(gates + gate_b.astype(jnp.float32)[:, None, None, :])
    log_a = -LRU_C * gates[0] * jax.nn.softplus(-lam.astype(jnp.float32))
    a = jnp.exp(log_a)
    b = jnp.sqrt(-jnp.expm1(2.0 * log_a)) * (gates[1] * xc)
    b = b.at[:, 0].add(a[:, 0] * h0)

    def combine(left, right):
        return left[0] * right[0], right[0] * left[1] + right[1]

    _, h = lax.associative_scan(combine, (a, b), axis=1)
    return h, h[:, -1]


def gdn_scan(q, k, v, beta, g, s0):
    qc, kc, vc = (_chunks(t, CHUNK) for t in (q, k, v))
    betac, gc = _chunks(beta, CHUNK), _chunks(g, CHUNK)
    bc = jnp.cumsum(gc, axis=-1)
    tri = jnp.tril(jnp.ones((CHUNK, CHUNK), dtype=bool))
    strict = jnp.tril(jnp.ones((CHUNK, CHUNK), dtype=bool), k=-1)
    gam = jnp.exp(jnp.where(tri, bc[..., :, None] - bc[..., None, :], -jnp.inf))
    kb = kc * betac[..., None]
    M = jnp.where(strict, jnp.einsum('nbhtk,nbhsk->nbhts', kb, kc) * gam, 0.0)
    eye = jnp.eye(CHUNK, dtype=M.dtype)
    T = lax.linalg.triangular_solve(eye + M, jnp.broadcast_to(eye, M.shape), left_side=True, lower=True)
    u = jnp.einsum('nbhts,nbhsv->nbhtv', T, vc * betac[..., None])
    w = jnp.einsum('nbhts,nbhsk->nbhtk', T, kb * jnp.exp(bc)[..., None])
    qk = jnp.einsum('nbhtk,nbhsk->nbhts', qc, kc) * gam

    def step(S, inp):
        qi, ki, ui, wi, bi, qki = inp
        v_new = ui - jnp.einsum('bhtk,bhkv->bhtv', wi, S)
        o = jnp.einsum('bhtk,bhkv->bhtv', qi * jnp.exp(bi)[..., None], S) + jnp.einsum('bhts,bhsv->bhtv', qki, v_new)
        S_new = jnp.exp(bi[..., -1])[..., None, None] * S + jnp.einsum(
            'bhsk,bhsv->bhkv', ki * jnp.exp(bi[..., -1:] - bi)[..., None], v_new)
        return S_new, o

    s_fin, oc = lax.scan(step, s0, (qc, kc, u, w, bc, qk))
    return _unchunks(oc), s_fin


def mlstm_scan(q, k, v, log_i, log_f, c0, n0, m0):
    qc, kc, vc = (_chunks(t, CHUNK) for t in (q, k, v))
    ic, fc = _chunks(log_i, CHUNK), _chunks(log_f, CHUNK)
    tri = jnp.tril(jnp.ones((CHUNK, CHUNK), dtype=bool))

    def step(carry, inp):
        Cs, ns, ms = carry
        qi, ki, vi, ii, fi = inp
        F = jnp.cumsum(fi, axis=-1)
        logw = jnp.where(tri, F[..., :, None] - F[..., None, :] + ii[..., None, :], -jnp.inf)
        m_t = jnp.maximum(F + ms[..., None], jnp.max(logw, axis=-1))
        w_in = jnp.exp(F + ms[..., None] - m_t)
        qk = jnp.einsum('bhtk,bhsk->bhts', qi, ki) * jnp.exp(logw - m_t[..., None])
        num = w_in[..., None] * jnp.einsum('bhtk,bhkv->bhtv', qi, Cs) + jnp.einsum('bhts,bhsv->bhtv', qk, vi)
        den = w_in * jnp.einsum('bhtk,bhk->bht', qi, ns) + jnp.sum(qk, axis=-1)
        h = num / jnp.maximum(jnp.abs(den), jnp.exp(-m_t))[..., None]
        m_new = m_t[..., -1]
        w_last = jnp.exp(F[..., -1:] - F + ii - m_new[..., None])
        decay = jnp.exp(F[..., -1] + ms - m_new)
        C_new = decay[..., None, None] * Cs + jnp.einsum('bhs,bhsk,bhsv->bhkv', w_last, ki, vi)
        n_new = decay[..., None] * ns + jnp.einsum('bhs,bhsk->bhk', w_last, ki)
        return (C_new, n_new, m_new), h

    fin, hc = lax.scan(step, (c0, n0, m0), (qc, kc, vc, ic, fc))
    return _unchunks(hc), fin


def even_mixer(u, w_in, gate_w2, gate_b, gla_g, conv_w, conv_b, lgate_w, lgate_b, lam, w_out,
               st_gla, st_lru, line_len):
    B, L, _ = u.shape
    proj = jnp.einsum('bld,de->ble', u, w_in).astype(jnp.float32)
    q, k, v, r, lr_f, lr_b, lx, lg = _split(proj, EVEN_SPLITS)
    q = q.reshape(B, L, GLA_H, GLA_DK) * (GLA_DK ** -0.5)
    k = k.reshape(B, L, GLA_H, GLA_DK)
    v = v.reshape(B, L, GLA_H, GLA_DV)

    def log_decay(lr, d):
        logit = jnp.einsum('blr,rk->blk', lr, gate_w2[d].astype(jnp.float32)) + gate_b[d].astype(jnp.float32)
        return (jax.nn.log_sigmoid(logit) / GLA_TAU).reshape(B, L, GLA_H, GLA_DK)

    st_gla = st_gla.astype(jnp.float32)
    o_f, s_f = gla_scan(q, k, v, log_decay(lr_f, 0), st_gla[:, 0])
    o_b, s_b = gla_scan(_flip(q), _flip(k), _flip(v), log_decay(_flip(lr_b), 1), st_gla[:, 1])
    o_gla = head_rmsnorm((o_f + _flip(o_b)).reshape(B, L, GROUP_W), GLA_H, gla_g) * jax.nn.silu(r)
    xc = line_conv(lx, conv_w, line_len) + conv_b.astype(jnp.float32)
    st_lru = st_lru.astype(jnp.float32)
    h_f, hl_f = rglru(xc, lgate_w[0], lgate_b[0], lam[0], st_lru[:, 0])
    h_b, hl_b = rglru(_flip(xc), lgate_w[1], lgate_b[1], lam[1], st_lru[:, 1])
    o_lru = (h_f + _flip(h_b)) * jax.nn.gelu(lg)
    mixed = jnp.concatenate([o_gla, o_lru], axis=-1).astype(u.dtype)
    out = jnp.einsum('ble,ed->bld', mixed, w_out)
    return out, (jnp.stack([s_f, s_b], axis=1), jnp.stack([hl_f, hl_b], axis=1))


def odd_mixer(u, w_in, conv_w, a_log, dt_bias, gdn_g, ml_gate_b, ml_g, w_out,
              st_gdn, st_c, st_n, st_m, line_len):
    B, L, _ = u.shape
    proj = jnp.einsum('bld,de->ble', u, w_in).astype(jnp.float32)
    gq, gk, gv, gz, ga, gb, mq, mk, mv, mo, mi, mf = _split(proj, ODD_SPLITS)
    qkv = jax.nn.silu(line_conv(jnp.concatenate([gq, gk, gv], axis=-1), conv_w, line_len))
    gq, gk, gv = (t.reshape(B, L, GDN_H, GDN_DH) for t in jnp.split(qkv, 3, axis=-1))
    gq = l2norm(gq) * (GDN_DH ** -0.5)
    gk = l2norm(gk)
    g_log = -jnp.exp(a_log.astype(jnp.float32)) * jax.nn.softplus(
        ga.reshape(B, L, 2, GDN_H) + dt_bias.astype(jnp.float32))
    beta = jax.nn.sigmoid(gb.reshape(B, L, 2, GDN_H))
    st_gdn = st_gdn.astype(jnp.float32)
    o_f, s_f = gdn_scan(gq, gk, gv, beta[:, :, 0], g_log[:, :, 0], st_gdn[:, 0])
    o_b, s_b = gdn_scan(_flip(gq), _flip(gk), _flip(gv), _flip(beta[:, :, 1]), _flip(g_log[:, :, 1]), st_gdn[:, 1])
    o_gdn = head_rmsnorm((o_f + _flip(o_b)).reshape(B, L, GROUP_W), GDN_H, gdn_g) * jax.nn.silu(gz)
    mq = mq.reshape(B, L, ML_H, ML_DK)
    mk = mk.reshape(B, L, ML_H, ML_DK) * (ML_DK ** -0.5)
    mv = mv.reshape(B, L, ML_H, ML_DV)
    gbias = ml_gate_b.astype(jnp.float32)
    log_i = mi.reshape(B, L, 2, ML_H) + gbias[:, 0]
    log_f = jax.nn.log_sigmoid(mf.reshape(B, L, 2, ML_H) + gbias[:, 1])
    st_c, st_n, st_m = (s.astype(jnp.float32) for s in (st_c, st_n, st_m))
    h_f, (c_f, n_f, m_f) = mlstm_scan(mq, mk, mv, log_i[:, :, 0], log_f[:, :, 0],
                                      st_c[:, 0], st_n[:, 0], st_m[:, 0])
    h_b, (c_b, n_b, m_b) = mlstm_scan(_flip(mq), _flip(mk), _flip(mv), _flip(log_i[:, :, 1]), _flip(log_f[:, :, 1]),
                                      st_c[:, 1], st_n[:, 1], st_m[:, 1])
    o_ml = head_rmsnorm((h_f + _flip(h_b)).reshape(B, L, GROUP_W), ML_H, ml_g) * jax.nn.sigmoid(mo)
    mixed = jnp.concatenate([o_gdn, o_ml], axis=-1).astype(u.dtype)
    out = jnp.einsum('ble,ed->bld', mixed, w_out)
    states = (jnp.stack([s_f, s_b], axis=1), jnp.stack([c_f, c_b], axis=1),
              jnp.stack([n_f, n_b], axis=1), jnp.stack([m_f, m_b], axis=1))
    return out, states


def to_col_major(x, rows):
    B, L, D = x.shape
    return x.reshape(B, rows, GRID_W, D).transpose(0, 2, 1, 3).reshape(B, L, D)


def from_col_major(x, rows):
    B, L, D = x.shape
    return x.reshape(B, GRID_W, rows, D).transpose(0, 2, 1, 3).reshape(B, L, D)


def trunk_layer(x, mod, g4, w_up_l, w_down_l, mixer):
    shift_a, scale_a, gate_a, shift_m, scale_m, gate_m = jnp.split(mod, N_MOD, axis=-1)
    u = rmsnorm(x, g4[0]) * (1.0 + scale_a) + shift_a
    y, states = mixer(u)
    x = x + gate_a * rmsnorm(y, g4[1])
    u = rmsnorm(x, g4[2]) * (1.0 + scale_m) + shift_m
    hid = jnp.square(jax.nn.relu(jnp.einsum('bld,df->blf', u, w_up_l)))
    y = jnp.einsum('blf,fd->bld', hid, w_down_l)
    x = x + gate_m * rmsnorm(y, g4[3])
    return x, states


def setup_inputs(seed: int = 0) -> dict:
    key = jax.random.key(seed)
    ks = jax.random.split(key, 36)
    f32 = jnp.float32

    def nrm(i, shape, s):
        return s * jax.random.normal(ks[i], shape, f32)

    x_prompt = nrm(0, (BATCH, SEQ, D_MODEL), 1.0)
    x_sample = nrm(1, (DEC_BATCH, DEC_SEQ, D_MODEL), 1.0)
    c = nrm(2, (DEC_BATCH, D_MODEL), 1.0)
    state_gla = nrm(3, (DEC_BATCH, N_EVEN, 2, GLA_H, GLA_DK, GLA_DV), 0.3)
    state_lru = nrm(4, (DEC_BATCH, N_EVEN, 2, LRU_W), 0.5)
    state_gdn = nrm(5, (DEC_BATCH, N_ODD, 2, GDN_H, GDN_DH, GDN_DH), 0.1)
    state_mlstm_c = nrm(6, (DEC_BATCH, N_ODD, 2, ML_H, ML_DK, ML_DV), 0.3)
    state_mlstm_n = nrm(7, (DEC_BATCH, N_ODD, 2, ML_H, ML_DK), 0.3)
    state_mlstm_m = nrm(8, (DEC_BATCH, N_ODD, 2, ML_H), 1.0)
    c_ctx = nrm(9, (D_MODEL,), 1.0)
    w_mod = nrm(10, (DEPTH, D_MODEL, N_MOD * D_MODEL), 0.5 * D_MODEL ** -0.5)
    b_mod = nrm(11, (DEPTH, N_MOD * D_MODEL), 0.01)
    norm_g = 1.0 + nrm(12, (DEPTH, 4, D_MODEL), 0.05)
    w_up = nrm(13, (DEPTH, D_MODEL, D_FF), D_MODEL ** -0.5)
    w_down = nrm(14, (DEPTH, D_FF, D_MODEL), D_FF ** -0.5)
    w_in_e = nrm(15, (N_EVEN, D_MODEL, IN_E), D_MODEL ** -0.5)
    gla_gate_w2 = nrm(16, (N_EVEN, 2, GLA_RANK, GLA_KEY), GLA_RANK ** -0.5)
    gla_gate_b = nrm(17, (N_EVEN, 2, GLA_KEY), 0.1)
    gla_norm_g = 1.0 + nrm(18, (N_EVEN, GROUP_W), 0.05)
    lru_conv_w = nrm(19, (N_EVEN, CONV_W, LRU_W), CONV_W ** -0.5)
    lru_conv_b = nrm(20, (N_EVEN, LRU_W), 0.01)
    lru_gate_w = nrm(21, (N_EVEN, 2, 2, LRU_BLOCKS, LRU_BS, LRU_BS), LRU_BS ** -0.5)
    lru_gate_b = nrm(22, (N_EVEN, 2, 2, LRU_W), 0.01)
    a0 = jax.random.uniform(ks[23], (N_EVEN, 2, LRU_W), f32, 0.9, 0.999) ** (1.0 / LRU_C)
    lru_lambda = jnp.log(a0) - jnp.log1p(-a0)
    w_out_e = nrm(24, (N_EVEN, MIX_W, D_MODEL), MIX_W ** -0.5)
    w_in_o = nrm(25, (N_ODD, D_MODEL, IN_O), D_MODEL ** -0.5)
    gdn_conv_w = nrm(26, (N_ODD, CONV_W, 3 * GROUP_W), CONV_W ** -0.5)
    gdn_a_log = jnp.log(jax.random.uniform(ks[27], (N_ODD, 2, GDN_H), f32, 1.0, 16.0))
    dt = jnp.exp(jax.random.uniform(ks[28], (N_ODD, 2, GDN_H), f32, float(np.log(1e-3)), float(np.log(1e-1))))
    gdn_dt_bias = dt + jnp.log(-jnp.expm1(-dt))
    gdn_norm_g = 1.0 + nrm(29, (N_ODD, GROUP_W), 0.05)
    mlstm_gate_b = jnp.array([0.0, 3.0], f32)[None, None, :, None] + nrm(30, (N_ODD, 2, 2, ML_H), 0.5)
    mlstm_norm_g = 1.0 + nrm(31, (N_ODD, GROUP_W), 0.05)
    w_out_o = nrm(32, (N_ODD, MIX_W, D_MODEL), MIX_W ** -0.5)
    return {
        'x_prompt': x_prompt, 'x_sample': x_sample, 'c': c,
        'state_gla': state_gla, 'state_lru': state_lru, 'state_gdn': state_gdn,
        'state_mlstm_c': state_mlstm_c, 'state_mlstm_n': state_mlstm_n, 'state_mlstm_m': state_mlstm_m,
        'c_ctx': c_ctx, 'w_mod': w_mod, 'b_mod': b_mod, 'norm_g': norm_g, 'w_up': w_up, 'w_down': w_down,
        'w_in_e': w_in_e, 'gla_gate_w2': gla_gate_w2, 'gla_gate_b': gla_gate_b, 'gla_norm_g': gla_norm_g,
        'lru_conv_w': lru_conv_w, 'lru_conv_b': lru_conv_b, 'lru_gate_w': lru_gate_w, 'lru_gate_b': lru_gate_b,
        'lru_lambda': lru_lambda, 'w_out_e': w_out_e,
        'w_in_o': w_in_o, 'gdn_conv_w': gdn_conv_w, 'gdn_a_log': gdn_a_log, 'gdn_dt_bias': gdn_dt_bias,
        'gdn_norm_g': gdn_norm_g, 'mlstm_gate_b': mlstm_gate_b, 'mlstm_norm_g': mlstm_norm_g, 'w_out_o': w_out_o,
    }


def reference(x_prompt, x_sample, c, state_gla, state_lru, state_gdn, state_mlstm_c, state_mlstm_n, state_mlstm_m,
              c_ctx, w_mod, b_mod, norm_g, w_up, w_down,
              w_in_e, gla_gate_w2, gla_gate_b, gla_norm_g, lru_conv_w, lru_conv_b, lru_gate_w, lru_gate_b,
              lru_lambda, w_out_e,
              w_in_o, gdn_conv_w, gdn_a_log, gdn_dt_bias, gdn_norm_g, mlstm_gate_b, mlstm_norm_g, w_out_o):
    bp, ctx_len = x_prompt.shape[0], x_prompt.shape[1]
    rows = x_sample.shape[1] // GRID_W
    y_ctx, y_lat = x_prompt, x_sample
    s_ctx, s_lat = jax.nn.silu(c_ctx), jax.nn.silu(c)
    new_gla, new_lru, new_gdn, new_c, new_n, new_m = [], [], [], [], [], []
    for l in range(DEPTH):
        j = l // 2
        mod_ctx = (jnp.einsum('d,de->e', s_ctx, w_mod[l]) + b_mod[l])[None, None, :]
        mod_lat = (jnp.einsum('bd,de->be', s_lat, w_mod[l]) + b_mod[l])[:, None, :]
        if l % 2 == 0:
            p = (w_in_e[j], gla_gate_w2[j], gla_gate_b[j], gla_norm_g[j], lru_conv_w[j], lru_conv_b[j],
                 lru_gate_w[j], lru_gate_b[j], lru_lambda[j], w_out_e[j])
            zero = (jnp.zeros((bp, 2, GLA_H, GLA_DK, GLA_DV), jnp.float32),
                    jnp.zeros((bp, 2, LRU_W), jnp.float32))
            y_ctx, st = trunk_layer(y_ctx, mod_ctx, norm_g[l], w_up[l], w_down[l],
                                    lambda u: even_mixer(u, *p, *zero, ctx_len))
            new_gla.append(st[0])
            new_lru.append(st[1])
            y_lat, _ = trunk_layer(y_lat, mod_lat, norm_g[l], w_up[l], w_down[l],
                                   lambda u: even_mixer(u, *p, state_gla[:, j], state_lru[:, j], GRID_W))
        else:
            p = (w_in_o[j], gdn_conv_w[j], gdn_a_log[j], gdn_dt_bias[j], gdn_norm_g[j], mlstm_gate_b[j],
                 mlstm_norm_g[j], w_out_o[j])
            zero = (jnp.zeros((bp, 2, GDN_H, GDN_DH, GDN_DH), jnp.float32),
                    jnp.zeros((bp, 2, ML_H, ML_DK, ML_DV), jnp.float32),
                    jnp.zeros((bp, 2, ML_H, ML_DK), jnp.float32),
                    jnp.zeros((bp, 2, ML_H), jnp.float32))
            y_ctx, st = trunk_layer(y_ctx, mod_ctx, norm_g[l], w_up[l], w_down[l],
                                    lambda u: odd_mixer(u, *p, *zero, ctx_len))
            new_gdn.append(st[0])
            new_c.append(st[1])
            new_n.append(st[2])
            new_m.append(st[3])

            def lat_mixer(u):
                out, s = odd_mixer(to_col_major(u, rows), *p, state_gdn[:, j], state_mlstm_c[:, j],
                                   state_mlstm_n[:, j], state_mlstm_m[:, j], rows)
                return from_col_major(out, rows), s

            y_lat, _ = trunk_layer(y_lat, mod_lat, norm_g[l], w_up[l], w_down[l], lat_mixer)
    return (y_ctx, y_lat, jnp.stack(new_gla, axis=1), jnp.stack(new_lru, axis=1), jnp.stack(new_gdn, axis=1),
            jnp.stack(new_c, axis=1), jnp.stack(new_n, axis=1), jnp.stack(new_m, axis=1))
```

```python
import numpy as np
from contextlib import ExitStack
import concourse.bass as bass
import concourse.mybir as mybir
from concourse.alu_op_type import AluOpType as ALU
from concourse.bass_utils import run_bass_kernel_spmd

F32 = mybir.dt.float32
BF16 = mybir.dt.bfloat16
AF = mybir.ActivationFunctionType
AX = mybir.AxisListType

D = 2048
KC = 16
TT = 256
FF = 8192
EPS = 1e-6
NEG = -30000.0


class View:
    __slots__ = ("b", "ap")

    def __init__(self, b, ap):
        self.b = b
        self.ap = ap


class Buf:
    __slots__ = ("t", "name", "w", "r")

    def __init__(self, t, name=""):
        self.t = t
        self.name = name
        self.w = None
        self.r = {}

    def __getitem__(self, idx):
        return View(self, self.t[idx])

    def v(self, ap):
        return View(self, ap)


SEM_LIMIT = 60000
DBG_STAGE = 99
DBG_DUMP = ()
DBG_SUB2 = 99
TRACE = None
DBG_SUB = 99
DBG_GROUPS = (0, 1)


class EngQ:
    def __init__(self, S, name, eng, self_raw=True):
        self.S = S
        self.name = name
        self.eng = eng
        self.sem = S.nc.alloc_semaphore(name=f"prog_{name}")
        self.semids = {id(self.sem)}
        self.nroll = 0
        self.n = 0
        self.last_ev = None
        self.seen = {}
        self.self_raw = self_raw
        self.ninst = 0

    def roll(self):
        self.nroll += 1
        self.sem = self.S.nc.alloc_semaphore(name=f"prog_{self.name}_{self.nroll}")
        self.semids.add(id(self.sem))
        self.n = 0


class Sched:
    def __init__(self, nc, n_dma_sems=32):
        self.nc = nc
        self.pe = EngQ(self, "pe", nc.tensor, self_raw=False)
        self.act = EngQ(self, "act", nc.scalar)
        self.dve = EngQ(self, "dve", nc.vector)
        self.pool = EngQ(self, "pool", nc.gpsimd)
        self.sp = EngQ(self, "sp", nc.sync)
        self.engs = [self.pe, self.act, self.dve, self.pool, self.sp]
        self.dma_sems = [nc.alloc_semaphore(name=f"dma{i}") for i in range(n_dma_sems)]
        self.dma_val = [0] * n_dma_sems
        self.dma_rr = 0

    def _wait(self, q, ev):
        sem, val = ev
        k = id(sem)
        if q.seen.get(k, 0) < val:
            q.eng.wait_ge(sem, val)
            q.seen[k] = val
            q.ninst += 1
            if TRACE is not None:
                TRACE.setdefault(q.name, []).append(("w", k, val))

    def _deps(self, q, reads, writes):
        for b in reads:
            if b is None:
                continue
            if b.w is not None:
                if id(b.w[0]) in q.semids and not q.self_raw:
                    continue
                self._wait(q, b.w)
        for b in writes:
            if b is None:
                continue
            if b.w is not None and id(b.w[0]) not in q.semids:
                self._wait(q, b.w)
            for ev in b.r.values():
                if id(ev[0]) not in q.semids:
                    self._wait(q, ev)

    def _mark(self, ev, reads, writes):
        k = id(ev[0])
        for b in reads:
            if b is not None:
                b.r[k] = ev
        for b in writes:
            if b is not None:
                b.w = ev
                b.r = {}

    def op(self, q, fn, reads=(), writes=(), inc=True):
        if q.n >= SEM_LIMIT:
            q.roll()
        self._deps(q, reads, writes)
        inst = fn(q.eng)
        q.ninst += 1
        if inc:
            q.n += 1
            inst.then_inc(q.sem, 1)
            ev = (q.sem, q.n)
            q.last_ev = ev
            if TRACE is not None:
                TRACE.setdefault(q.name, []).append(("i", id(q.sem), 1))
        else:
            ev = (q.sem, q.n + 1)
        self._mark(ev, reads, writes)
        return inst

    def dma(self, q, out_ap, in_ap, reads=(), writes=(), **kw):
        self._deps(q, reads, writes)
        i = self.dma_rr
        self.dma_rr = (self.dma_rr + 1) % len(self.dma_sems)
        sem = self.dma_sems[i]
        if self.dma_val[i] > 0:
            self._wait(q, (sem, self.dma_val[i]))
        inst = q.eng.dma_start(out=out_ap, in_=in_ap, **kw)
        self.dma_val[i] += 16
        inst.then_inc(sem, 16)
        if TRACE is not None:
            TRACE.setdefault(q.name, []).append(("i", id(sem), 16))
        q.ninst += 1
        self._mark((sem, self.dma_val[i]), reads, writes)
        return inst

    def barrier(self):
        evs = [q.last_ev for q in self.engs if q.last_ev is not None]
        evs += [(s, v) for s, v in zip(self.dma_sems, self.dma_val) if v > 0]
        for q in self.engs:
            for ev in evs:
                if id(ev[0]) in q.semids:
                    continue
                self._wait(q, ev)


def make_consts():
    c = {}
    i128 = np.arange(128)
    r, cc = i128[:, None], i128[None, :]
    c["ident"] = np.eye(128, dtype=np.float32)
    c["ones"] = np.ones((128, 128), np.float32)
    c["tri_f"] = (r <= cc).astype(np.float32)
    c["tri_b"] = (r >= cc).astype(np.float32)
    c["str_f"] = (r > cc).astype(np.float32)
    c["str_b"] = (r < cc).astype(np.float32)
    same = (r // 64) == (cc // 64)
    for nm, cond in (("nlt", cc < r), ("ngt", cc > r), ("nle", cc <= r), ("nge", cc >= r)):
        m = np.where(same & cond, 0.0, NEG).astype(np.float32)
        c[nm] = np.tile(m, (1, 4))
    c["ident4"] = np.tile(np.eye(128, dtype=np.float32), (1, 4))
    c["mask4_f"] = np.tile(c["tri_f"], (1, 4))
    c["mask4_b"] = np.tile(c["tri_b"], (1, 4))
    rm = np.ones((128, 256), np.float32)
    rm[:, 0::64] = 0.0
    c["reset"] = rm
    names = list(c.keys())
    offs = {}
    o = 0
    for n in names:
        offs[n] = (o, c[n].shape[1])
        o += c[n].shape[1]
    arr = np.concatenate([c[n] for n in names], axis=1)
    return arr, offs


CONST_ARR, CONST_OFFS = make_consts()

EVEN_BLOCKS = [(0, 512), (512, 512), (1024, 512), (1536, 512), (2048, 512), (2560, 512), (3072, 32),
               (3104, 512), (3616, 512), (4128, 512), (4640, 512)]
ODD_BLOCKS = [(i * 512, 512) for i in range(8)] + [(4096, 32), (4128, 512), (4640, 512), (5152, 512), (5664, 512),
                                                   (6176, 512), (6688, 512)]


def build(NCT=4, NLT=16, depth=2):
    nc = bass.Bass("TRN2", target_bir_lowering=False)
    S = Sched(nc)
    TC, TL = NCT * TT, NLT * TT
    TMAX = max(TC, TL)

    def din(name, shape, dt=F32):
        return nc.dram_tensor(name, list(shape), dt, kind="ExternalInput").ap()

    def dout(name, shape, dt=F32):
        return nc.dram_tensor(name, list(shape), dt, kind="ExternalOutput").ap()

    def dscr(name, shape, dt=F32):
        return nc.dram_tensor(name, list(shape), dt, kind="Internal").ap()

    I = {}
    for name, shape in [("xc", [TC, D]), ("xl", [TL, D]), ("cc", [2, D]),
                        ("st_gla", [2, 4, 128, 256]), ("st_lru", [2, 1024]), ("st_gdn", [2, 8, 128, 128]),
                        ("st_mc", [2, 4, 128, 256]), ("st_mn", [2, 4, 128]), ("st_mm", [2, 4]),
                        ("w_mod", [2, D, 6 * D]), ("b_mod", [2, 6 * D]), ("norm_g", [2, 4, D]),
                        ("w_up", [2, D, FF]), ("w_down", [2, FF, D]),
                        ("w_in_e", [D, 5152]), ("gla_w2", [2, 16, 512]), ("gla_b", [2, 512]), ("gla_g", [1024]),
                        ("lru_cw", [4, 1024]), ("lru_cb", [1024]), ("lru_gw", [2, 2, 8, 128, 128]),
                        ("lru_gb", [2, 2, 1024]), ("lru_lam", [2, 1024]), ("w_out_e", [D, D]),
                        ("w_in_o", [D, 7216]), ("gdn_cw", [4, 3072]), ("gdn_alog", [2, 8]), ("gdn_dtb", [2, 8]),
                        ("gdn_g", [1024]), ("ml_gb", [2, 2, 4]), ("ml_g", [1024]), ("w_out_o", [D, D]),
                        ("consts", list(CONST_ARR.shape))]:
        I[name] = din(name, shape)
    O = {}
    for name, shape in [("yc", [TC, D]), ("yl", [TL, D]), ("o_gla", [NCT, 2, 4, 128, 256]), ("o_lru", [NCT, 2, 1024]),
                        ("o_gdn", [NCT, 2, 8, 128, 128]), ("o_mc", [NCT, 2, 4, 128, 256]), ("o_mn", [NCT, 2, 4, 128]),
                        ("o_mm", [NCT, 2, 4])]:
        O[name] = dout(name, shape)

    def mm(o, l, r, start=True, stop=True, inc=None):
        S.op(S.pe, lambda e: e.matmul(o.ap, lhsT=l.ap, rhs=r.ap, start=start, stop=stop), reads=[l.b, r.b], writes=[o.b],
             inc=bool(stop) if inc is None else inc)

    def act(o, i, func, bias=None, scale=None, accum=None):
        kw = {}
        rd = [i.b]
        wr = [o.b]
        if bias is not None:
            if isinstance(bias, View):
                kw["bias"] = bias.ap
                rd.append(bias.b)
            else:
                kw["bias"] = bias
        if scale is not None:
            if isinstance(scale, View):
                kw["scale"] = scale.ap
                rd.append(scale.b)
            else:
                kw["scale"] = scale
        if accum is not None:
            kw["accum_out"] = accum.ap
            wr.append(accum.b)
        S.op(S.act, lambda e: e.activation(out=o.ap, in_=i.ap, func=func, **kw), reads=rd, writes=wr)

    def tt(q, o, a, b, op):
        S.op(q, lambda e: e.tensor_tensor(out=o.ap, in0=a.ap, in1=b.ap, op=op), reads=[a.b, b.b], writes=[o.b])

    def _sc(s, rd):
        if isinstance(s, View):
            rd.append(s.b)
            return s.ap
        return s

    def ts(q, o, a, s1, op0, s2=None, op1=None):
        rd = [a.b]
        a1 = _sc(s1, rd)
        a2 = _sc(s2, rd)
        if op1 is None:
            S.op(q, lambda e: e.tensor_scalar(out=o.ap, in0=a.ap, scalar1=a1, scalar2=None, op0=op0), reads=rd, writes=[o.b])
        else:
            S.op(q, lambda e: e.tensor_scalar(out=o.ap, in0=a.ap, scalar1=a1, scalar2=a2, op0=op0, op1=op1), reads=rd, writes=[o.b])

    def stt(o, a, s, b, op0, op1):
        rd = [a.b, b.b]
        a1 = _sc(s, rd)
        S.op(S.dve, lambda e: e.scalar_tensor_tensor(out=o.ap, in0=a.ap, scalar=a1, in1=b.ap, op0=op0, op1=op1), reads=rd, writes=[o.b])

    def cp(q, o, i):
        if q is S.act:
            act(o, i, AF.Copy)
        else:
            S.op(q, lambda e: e.tensor_copy(out=o.ap, in_=i.ap), reads=[i.b], writes=[o.b])

    def memset(q, o, val):
        S.op(q, lambda e: e.memset(o.ap, val), writes=[o.b])

    def ld(dst, src_ap, q=None, **kw):
        S.dma(q or S.sp, dst.ap, src_ap, writes=[dst.b], **kw)

    def stq(dst_ap, src, q=None, **kw):
        S.dma(q or S.pool, dst_ap, src.ap, reads=[src.b], **kw)

    top = ExitStack()

    uid = [0]

    def alloc(es, name, shape, dt=F32):
        uid[0] += 1
        name = f"{name}_{uid[0]}"
        return Buf(es.enter_context(nc.sbuf_tensor(name, list(shape), dt)), name)

    CT = alloc(top, "consts_sb", list(CONST_ARR.shape))
    ld(CT[:, :], I["consts"])

    def C(name, rows=128):
        o, w = CONST_OFFS[name]
        return CT[0:rows, o:o + w]

    ident = C("ident")
    banks = [Buf(top.enter_context(nc.psum_tensor(f"psb{i}", [128, 512], F32)), f"psb{i}") for i in range(8)]
    bank_rr = [0]

    def PS():
        b = banks[bank_rr[0]]
        bank_rr[0] = (bank_rr[0] + 1) % 8
        return b

    def tr(o, i, n):
        S.op(S.pe, lambda e: e.transpose(o.ap, i.ap, ident.ap[0:n, 0:n]), reads=[i.b, CT], writes=[o.b])

    colstage = alloc(top, "colstage", [128, 128])

    def load_cols(dst, flat_ap, R):
        ld(colstage[0:R, :], flat_ap.rearrange("(r p) -> r p", p=128))
        ps = PS()
        r0 = 0
        while r0 < R:
            n = min(64 if R > 64 else R, R - r0)
            S.op(S.pe, lambda e, r0=r0, n=n: e.transpose(ps.t[:, r0:r0 + n], colstage.t[r0:r0 + n, :], ident.ap[r0:r0 + n, r0:r0 + n]),
                 reads=[colstage, CT], writes=[ps])
            r0 += n
        cp(S.dve, dst, ps[:, 0:R])

    def cast_blocks(src2d, nkc, blocks, prefix):
        outs = []
        for bi, (c0, w) in enumerate(blocks):
            dst = dscr(f"{prefix}{bi}", [128, nkc, w], BF16)
            for k0 in range(0, nkc, 16):
                S.dma(S.pool, dst[:, k0:k0 + 16, :],
                      src2d[k0 * 128:(k0 + 16) * 128, c0:c0 + w].rearrange("(kc p) f -> p kc f", p=128))
            outs.append(dst)
        return outs

    B512 = [(i * 512, 512) for i in range(4)]
    Wb = {}
    Wb["in0"] = cast_blocks(I["w_in_e"], KC, EVEN_BLOCKS, "wine")
    Wb["out0"] = cast_blocks(I["w_out_e"], KC, B512, "woute")
    if depth > 1:
        Wb["in1"] = cast_blocks(I["w_in_o"], KC, ODD_BLOCKS, "wino")
        Wb["out1"] = cast_blocks(I["w_out_o"], KC, B512, "wouto")
    for l in range(depth):
        Wb[f"up{l}"] = cast_blocks(I["w_up"][l], KC, [(i * 512, 512) for i in range(16)], f"wup{l}_")
        Wb[f"dn{l}"] = cast_blocks(I["w_down"][l], 64, B512, f"wdn{l}_")

    MODV = dscr("modv", [depth, 2, 6, D])
    with ExitStack() as es:
        cct = alloc(es, "cct", [2, D])
        sT = alloc(es, "sT", [128, KC, 2])
        wm = [alloc(es, f"wm{i}", [128, KC, 512]) for i in range(2)]
        modt = alloc(es, "modt", [2, 6 * D])
        bmt = [alloc(es, f"bmt{i}", [2, 512]) for i in range(2)]
        ngt = alloc(es, "ngt", [2, D])
        mvt = [alloc(es, f"mvt{i}", [2, D]) for i in range(2)]
        ld(cct[:, :], I["cc"])
        act(cct[:, :], cct[:, :], AF.Silu)
        ps = PS()
        for kc in range(KC):
            tr(ps[:, kc * 2:kc * 2 + 2], cct[:, kc * 128:(kc + 1) * 128], 2)
        cp(S.dve, sT.v(sT.t[:, :, :].rearrange("p a b -> p (a b)")), ps[:, 0:2 * KC])
        for l in range(depth):
            for cb in range(24):
                w = wm[cb % 2]
                bm = bmt[cb % 2]
                ld(w[:, :, :], I["w_mod"][l][:, cb * 512:(cb + 1) * 512].rearrange("(kc p) f -> p kc f", p=128))
                ld(bm[:, :], I["b_mod"][l:l + 1, cb * 512:(cb + 1) * 512].partition_broadcast(2).rearrange("p a b -> p (a b)"))
                ps = PS()
                for kc in range(KC):
                    mm(ps[0:2, :], sT[:, kc, :], w[:, kc, :], start=(kc == 0), stop=(kc == KC - 1))
                tt(S.dve, modt[:, cb * 512:(cb + 1) * 512], ps[0:2, :], bm[:, :], ALU.add)

            def sl(i):
                return modt[:, i * D:(i + 1) * D]
            for i, (kind, mi, gi_) in enumerate((("a", 1, 0), ("c", 0, None), ("m", 2, 1), ("a", 4, 2), ("c", 3, None), ("m", 5, 3))):
                mvb = mvt[i % 2]
                if gi_ is not None:
                    ld(ngt[:, :], I["norm_g"][l, gi_:gi_ + 1, :].partition_broadcast(2).rearrange("p a b -> p (a b)"))
                if kind == "a":
                    stt(mvb[:, :], sl(mi), 1.0, ngt[:, :], ALU.add, ALU.mult)
                elif kind == "m":
                    tt(S.dve, mvb[:, :], sl(mi), ngt[:, :], ALU.mult)
                else:
                    cp(S.dve, mvb[:, :], sl(mi))
                stq(MODV[l, :, i, :], mvb[:, :])
    S.barrier()

    X1 = {0: dscr("x1c", [TC, D]), 1: dscr("x1l", [TL, D])}
    MIXT = dscr("mixt", [128, 16, TMAX], BF16)
    sc = {}

    def scr(name, shape, dt=F32):
        if name not in sc:
            if name in DBG_DUMP:
                sc[name] = dout("s_" + name, shape, dt)
            else:
                sc[name] = dscr("s_" + name, shape, dt)
        return sc[name]

    class Grp:
        pass

    def make_groups(l):
        last = (l == depth - 1)
        gs = []
        for w in DBG_GROUPS:
            g = Grp()
            g.w = w
            g.ntiles = NCT if w == 0 else NLT
            g.T = g.ntiles * TT
            src = (I["xc"], I["xl"])[w] if l == 0 else X1[w]
            dst = (O["yc"], O["yl"])[w] if last else X1[w]
            g.colmajor = (w == 1 and l % 2 == 1)
            g.L = 256 if w == 0 else 64
            g.NL = TT // g.L
            if g.colmajor:
                sv = src.rearrange("(r c) d -> c r d", c=64)
                dv = dst.rearrange("(r c) d -> c r d", c=64)
                g.xsrc = lambda ti, st, sv=sv: [(0, 64, sv[ti * 4 + st * 2]), (64, 128, sv[ti * 4 + st * 2 + 1])]
                g.ydst = lambda ti, st, dv=dv: [(0, 64, dv[ti * 4 + st * 2]), (64, 128, dv[ti * 4 + st * 2 + 1])]
            else:
                g.xsrc = lambda ti, st, src=src: [(0, 128, src[ti * TT + st * 128: ti * TT + (st + 1) * 128, :])]
                g.ydst = lambda ti, st, dst=dst: [(0, 128, dst[ti * TT + st * 128: ti * TT + (st + 1) * 128, :])]
            g.seqs = [(i, 1, i) for i in range(NCT)] if w == 0 else [(0, NLT, None)]
            gs.append(g)
        return gs

    def bc_load(dst, l, w, i):
        ld(dst[:, :], MODV[l, w, i:i + 1, :].partition_broadcast(128).rearrange("p a b -> p (a b)"))

    def wstream(aps, bufs):
        n = len(aps)

        def issue(k):
            b = bufs[k % len(bufs)]
            shp = aps[k].shape
            ld(b[:, 0:shp[1], 0:shp[2]], aps[k])
            return b
        cur = issue(0)
        for k in range(n):
            nxt = issue(k + 1) if k + 1 < n else None
            yield cur
            cur = nxt

    def recip(o, i):
        S.op(S.dve, lambda e: e.reciprocal(out=o.ap, in_=i.ap), reads=[i.b], writes=[o.b])

    def rstd_from_ss(rstd, ss, n):
        act(rstd, ss, AF.Sqrt, scale=1.0 / n, bias=EPS)
        recip(rstd, rstd)

    evac_rr = [0]

    def evq():
        evac_rr[0] ^= 1
        return S.act if evac_rr[0] else S.dve

    def norm_mod_T(src, dst, bA, bB, junk, ss, rstd, uT, st, col):
        act(junk[:, :], src[:, :], AF.Square, accum=ss[:, col:col + 1])
        rstd_from_ss(rstd[:, col:col + 1], ss[:, col:col + 1], D)
        stt(dst[:, :], src[:, :], rstd[:, col:col + 1], bA[:, :], ALU.mult, ALU.mult)
        tt(S.pool, dst[:, :], dst[:, :], bB[:, :], ALU.add)
        transpose_to(uT, dst, st)

    def transpose_to(uT, src, st):
        for q4 in range(4):
            ps = PS()
            for j in range(4):
                kc = q4 * 4 + j
                tr(ps[:, j * 128:(j + 1) * 128], src[:, kc * 128:(kc + 1) * 128], 128)
            cp(evq(), uT[:, q4 * 4:(q4 + 1) * 4, st * 128:(st + 1) * 128],
               ps.v(ps.t[:, :].rearrange("p (a b) -> p a b", a=4)))

    def load_x(g, ti, st, xt):
        for (p0, p1, ap) in g.xsrc(ti, st):
            ld(xt[p0:p1, :], ap)

    def phase3(g, l, front_alloc, front):
        with ExitStack() as es:
            bX, bY = alloc(es, "bcX", [128, D]), alloc(es, "bcY", [128, D])
            bG1, bA2, bB2, bG3 = bX, bY, bX, bY
            xt = [alloc(es, f"xt{i}", [128, D]) for i in range(2)]
            yb = [alloc(es, f"yb{i}", [128, D]) for i in range(2)]
            junk = alloc(es, "junk", [128, D], BF16)
            ss = alloc(es, "ss", [128, 8])
            rstd = alloc(es, "rstd", [128, 8])
            mixT = alloc(es, "mixT", [128, KC, TT], BF16)
            u2T = alloc(es, "u2T", [128, KC, TT], BF16)
            hidT = alloc(es, "hidT", [128, 64, TT], BF16)
            wb = [alloc(es, f"wb{i}", [128, KC, 512], BF16) for i in range(2)]
            rtmp = [alloc(es, f"rtmp{i}", [128, 512]) for i in range(2)]
            fctx = front_alloc(es)
            aps = []
            for ti in range(g.ntiles):
                aps += list(Wb[f"out{l}"]) + list(Wb[f"up{l}"])
                for fb in range(4):
                    aps += [Wb[f"dn{l}"][fb][:, k0:k0 + 16, :] for k0 in range(0, 64, 16)]
            ws = wstream(aps, wb)
            for ti in range(g.ntiles):
                front(fctx, ti, mixT)
                bc_load(bG1, l, g.w, 2)
                bc_load(bA2, l, g.w, 3)
                for st in range(2):
                    load_x(g, ti, st, xt[st])
                for fb in range(4):
                    w = next(ws)
                    for st in range(2):
                        ps = PS()
                        for kc in range(KC):
                            mm(ps[:, :], mixT[:, kc, st * 128:(st + 1) * 128], w[:, kc, :], start=(kc == 0), stop=(kc == KC - 1))
                        cp(evq(), yb[st][:, fb * 512:(fb + 1) * 512], ps[:, :])
                for st in range(2):
                    act(junk[:, :], yb[st][:, :], AF.Square, accum=ss[:, st:st + 1])
                    rstd_from_ss(rstd[:, st:st + 1], ss[:, st:st + 1], D)
                    stt(yb[st][:, :], yb[st][:, :], rstd[:, st:st + 1], bG1[:, :], ALU.mult, ALU.mult)
                    tt(S.pool, xt[st][:, :], xt[st][:, :], yb[st][:, :], ALU.add)
                bc_load(bB2, l, g.w, 4)
                for st in range(2):
                    norm_mod_T(xt[st], yb[st], bA2, bB2, junk, ss, rstd, u2T, st, 2 + st)
                bc_load(bG3, l, g.w, 5)
                for ub in range(16):
                    w = next(ws)
                    for j in range(0, 4, 2):
                        ps = PS()
                        for jj in range(2):
                            for kc in range(KC):
                                mm(ps[:, jj * 256:(jj + 1) * 256], w[:, kc, (j + jj) * 128:(j + jj + 1) * 128], u2T[:, kc, :],
                                   start=(kc == 0), stop=(kc == KC - 1))
                        rt_ = rtmp[(j // 2) % 2]
                        act(rt_[:, :], ps[:, :], AF.Relu)
                        c0 = ub * 4 + j
                        tt(S.pool, hidT.v(hidT.t[:, c0:c0 + 2, :].rearrange("p a b -> p (a b)")), rt_[:, :], rt_[:, :], ALU.mult)
                for fb in range(4):
                    psd = [PS(), PS()]
                    for g4 in range(4):
                        w = next(ws)
                        for st in range(2):
                            for k in range(16):
                                fc = g4 * 16 + k
                                mm(psd[st][:, :], hidT[:, fc, st * 128:(st + 1) * 128], w[:, k, :], start=(fc == 0), stop=(fc == 63),
                                   inc=(k == 15))
                    for st in range(2):
                        cp(evq(), yb[st][:, fb * 512:(fb + 1) * 512], psd[st][:, :])
                for st in range(2):
                    act(junk[:, :], yb[st][:, :], AF.Square, accum=ss[:, 4 + st:5 + st])
                    rstd_from_ss(rstd[:, 4 + st:5 + st], ss[:, 4 + st:5 + st], D)
                    stt(yb[st][:, :], yb[st][:, :], rstd[:, 4 + st:5 + st], bG3[:, :], ALU.mult, ALU.mult)
                    tt(S.pool, yb[st][:, :], yb[st][:, :], xt[st][:, :], ALU.add)
                    for (p0, p1, ap) in g.ydst(ti, st):
                        stq(ap, yb[st][p0:p1, :])
        S.barrier()

    def even_layer(l, j):
        QT = scr("QT", [128, 4, TMAX])
        KT_ = scr("KT", [128, 4, TMAX])
        Kt = scr("Kt", [TMAX, 512])
        Vt = scr("Vt", [TMAX, 1024])
        Gd = [scr(f"G{d}", [TMAX, 512]) for d in range(2)]
        RT = scr("RT", [128, 8, TMAX])
        Ad = [scr(f"A{d}", [128, 8, TMAX]) for d in range(2)]
        Bd = [scr(f"B{d}", [128, 8, TMAX]) for d in range(2)]
        LGT = scr("LGT", [128, 8, TMAX])
        Od = [scr(f"O{d}", [128, 8, TMAX]) for d in range(2)]
        with ExitStack() as les:
            gcol = alloc(les, "gcol", [128, 8])
            load_cols(gcol[:, :], I["gla_g"], 8)
            cw = alloc(les, "cw", [128, 32])
            load_cols(cw[:, :], I["lru_cw"].rearrange("a b -> (a b)"), 32)
            cb = alloc(les, "cb", [128, 8])
            load_cols(cb[:, :], I["lru_cb"], 8)
            gb = alloc(les, "gb", [128, 32])
            load_cols(gb[:, :], I["lru_gb"].rearrange("a b c -> (a b c)"), 32)
            m8sp = alloc(les, "m8sp", [128, 16])
            load_cols(m8sp[:, :], I["lru_lam"].rearrange("a b -> (a b)"), 16)
            act(m8sp[:, :], m8sp[:, :], AF.Exp, scale=-1.0)
            act(m8sp[:, :], m8sp[:, :], AF.Ln, bias=1.0)
            ts(S.dve, m8sp[:, :], m8sp[:, :], -8.0, ALU.mult)
            ones = C("ones")

            for g in make_groups(l):
                L, NL = g.L, g.NL
                with ExitStack() as es:
                    bA, bB = alloc(es, "bA", [128, D]), alloc(es, "bB", [128, D])
                    bc_load(bA, l, g.w, 0)
                    bc_load(bB, l, g.w, 1)
                    w2 = alloc(es, "w2", [16, 2, 512])
                    ld(w2[:, :, :], I["gla_w2"].rearrange("d r k -> r d k"))
                    gbrow = alloc(es, "gbrow", [1, 2, 512])
                    ld(gbrow[:, :, :], I["gla_b"].rearrange("(o d) k -> o d k", o=1))
                    LW = alloc(es, "LW", [128, 32, 128])
                    ld(LW[:, :, :], I["lru_gw"].rearrange("d g n i j -> i (d g n) j"))
                    xt = [alloc(es, f"xt{i}", [128, D]) for i in range(2)]
                    uTs = [alloc(es, f"uT{i}", [128, KC, TT], BF16) for i in range(2)]
                    junk = alloc(es, "junk", [128, D], BF16)
                    ss = alloc(es, "ss", [128, 4])
                    rstd = alloc(es, "rstd", [128, 4])
                    wb = [alloc(es, f"wb{i}", [128, KC, 512], BF16) for i in range(2)]
                    qTs = alloc(es, "qTs", [128, 4, TT])
                    kTs = qTs
                    kts = alloc(es, "kts", [128, 2, 512])
                    vs = alloc(es, "vs", [128, 2, 1024])
                    rTs = alloc(es, "rTs", [128, 8, TT])
                    lrT = alloc(es, "lrT", [16, 2, TT])
                    e1 = alloc(es, "e1", [128, 512])
                    Gs = alloc(es, "Gs", [128, 2, 2, 512])
                    xp = alloc(es, "xp", [128, 8, NL, L + 3])
                    xcs = alloc(es, "xcs", [128, 8, TT])
                    gr = alloc(es, "gr", [128, TT])
                    gi = alloc(es, "gi", [128, TT])
                    as_ = alloc(es, "as_", [128, 8, TT])
                    bs_ = alloc(es, "bs_", [128, 8, TT])
                    lgs = rTs
                    memset(S.pool, xp[:, :, :, :], 0.0)
                    aps = []
                    for ti in range(g.ntiles):
                        aps += list(Wb[f"in{l}"])
                    ws = wstream(aps, wb)

                    def prep(ti):
                        for st in range(2):
                            load_x(g, ti, st, xt[st])
                        for st in range(2):
                            norm_mod_T(xt[st], xt[st], bA, bB, junk, ss, rstd, uTs[ti % 2], st, st)

                    def fm(ps, col, w, wc0, n, uT):
                        for kc in range(KC):
                            mm(ps[0:n, col:col + TT], w[:, kc, wc0:wc0 + n], uT[:, kc, :], start=(kc == 0), stop=(kc == KC - 1))

                    def tmj(ps, w, n, uT, st):
                        for kc in range(KC):
                            mm(ps[:, 0:n], uT[:, kc, st * 128:(st + 1) * 128], w[:, kc, 0:n], start=(kc == 0), stop=(kc == KC - 1))

                    prep(0)
                    for ti in range(g.ntiles):
                        t0 = ti * TT
                        uT = uTs[ti % 2]
                        for bi, (dst_s, dram, scl) in enumerate(((qTs, QT, 128.0 ** -0.5), (kTs, KT_, 1.0))):
                            w = next(ws)
                            for hp in range(2):
                                ps = PS()
                                for hh in range(2):
                                    fm(ps, hh * TT, w, (hp * 2 + hh) * 128, 128, uT)
                                act(dst_s.v(dst_s.t[:, hp * 2:hp * 2 + 2, :].rearrange("p a b -> p (a b)")), ps[:, :], AF.Copy, scale=scl)
                            stq(dram[:, :, t0:t0 + TT], dst_s[:, :, :])
                            if bi == 1:
                                for st in range(2):
                                    ps = PS()
                                    tmj(ps, w, 512, uT, st)
                                    cp(S.dve, kts[:, st, :], ps[:, :])
                                stq(Kt[t0:t0 + TT, :].rearrange("(s p) f -> p s f", p=128), kts[:, :, :])
                        for vb in range(2):
                            w = next(ws)
                            for st in range(2):
                                ps = PS()
                                tmj(ps, w, 512, uT, st)
                                cp(evq(), vs[:, st, vb * 512:(vb + 1) * 512], ps[:, :])
                        stq(Vt[t0:t0 + TT, :].rearrange("(s p) f -> p s f", p=128), vs[:, :, :])
                        for rb in range(2):
                            w = next(ws)
                            for cp_ in range(2):
                                ps = PS()
                                for hh in range(2):
                                    fm(ps, hh * TT, w, (cp_ * 2 + hh) * 128, 128, uT)
                                c0 = rb * 4 + cp_ * 2
                                act(rTs.v(rTs.t[:, c0:c0 + 2, :].rearrange("p a b -> p (a b)")), ps[:, :], AF.Silu)
                        stq(RT[:, :, t0:t0 + TT], rTs[:, :, :])
                        if ti + 1 < g.ntiles:
                            prep(ti + 1)
                        w = next(ws)
                        for d in range(2):
                            ps = PS()
                            fm(ps, 0, w, d * 16, 16, uT)
                            cp(S.dve, lrT[:, d, :], ps[0:16, 0:TT])
                        for d in range(2):
                            for st in range(2):
                                ps = PS()
                                mm(ps[:, :], lrT[0:16, d, st * 128:(st + 1) * 128], w2[0:16, d, :], start=True, stop=False)
                                mm(ps[:, :], ones.b.v(ones.ap[0:1, 0:128]), gbrow[0:1, d, :], start=False, stop=True)
                                act(e1[:, :], ps[:, :], AF.Exp, scale=-1.0)
                                act(e1[:, :], e1[:, :], AF.Ln, bias=1.0)
                                ts(S.pool, Gs[:, d, st, :], e1[:, :], -1.0 / 16.0, ALU.mult)
                            stq(Gd[d][t0:t0 + TT, :].rearrange("(s p) f -> p s f", p=128), Gs[:, d, :, :])
                        for xb in range(2):
                            w = next(ws)
                            for cp_ in range(2):
                                ps = PS()
                                for hh in range(2):
                                    fm(ps, hh * TT, w, (cp_ * 2 + hh) * 128, 128, uT)
                                for hh in range(2):
                                    n = xb * 4 + cp_ * 2 + hh
                                    act(xp[:, n, :, 2:2 + L], ps.v(ps.t[:, hh * TT:(hh + 1) * TT].rearrange("p (a b) -> p a b", a=NL)), AF.Copy)
                        for n in range(8):
                            xc = xcs.v(xcs.t[:, n, :].rearrange("p (a b) -> p a b", a=NL))
                            act(xc, xp[:, n, :, 2:2 + L], AF.Identity, scale=cw[:, 2 * 8 + n:2 * 8 + n + 1], bias=cb[:, n:n + 1])
                            for tap in (0, 1, 3):
                                stt(xc, xp[:, n, :, tap:tap + L], cw[:, tap * 8 + n:tap * 8 + n + 1], xc, ALU.mult, ALU.add)
                        for d in range(2):
                            for n in range(8):
                                ps = PS()
                                mm(ps[:, 0:TT], LW[:, (d * 2 + 0) * 8 + n, :], xcs[:, n, :])
                                mm(ps[:, TT:2 * TT], LW[:, (d * 2 + 1) * 8 + n, :], xcs[:, n, :])
                                i0 = (d * 2 + 0) * 8 + n
                                i1 = (d * 2 + 1) * 8 + n
                                act(gr[:, :], ps[:, 0:TT], AF.Sigmoid, bias=gb[:, i0:i0 + 1])
                                act(gi[:, :], ps[:, TT:2 * TT], AF.Sigmoid, bias=gb[:, i1:i1 + 1])
                                act(as_[:, n, :], gr[:, :], AF.Exp, scale=m8sp[:, d * 8 + n:d * 8 + n + 1])
                                tt(S.pool, gr[:, :], as_[:, n, :], as_[:, n, :], ALU.mult)
                                act(gr[:, :], gr[:, :], AF.Sqrt, scale=-1.0, bias=1.0)
                                tt(S.pool, gi[:, :], gi[:, :], gr[:, :], ALU.mult)
                                tt(S.dve, bs_[:, n, :], gi[:, :], xcs[:, n, :], ALU.mult)
                            stq(Ad[d][:, :, t0:t0 + TT], as_[:, :, :])
                            stq(Bd[d][:, :, t0:t0 + TT], bs_[:, :, :])
                        for gb_ in range(2):
                            w = next(ws)
                            for cp_ in range(2):
                                ps = PS()
                                for hh in range(2):
                                    fm(ps, hh * TT, w, (cp_ * 2 + hh) * 128, 128, uT)
                                c0 = gb_ * 4 + cp_ * 2
                                act(lgs.v(lgs.t[:, c0:c0 + 2, :].rearrange("p a b -> p (a b)")), ps[:, :], AF.Gelu_apprx_tanh)
                        stq(LGT[:, :, t0:t0 + TT], lgs[:, :, :])
                S.barrier()

                with ExitStack() as es:
                    Sst = alloc(es, "Sst", [128, 4, 256])
                    qTb = [alloc(es, f"qTb{i}", [128, 4, 128]) for i in range(2)]
                    kTb = [alloc(es, f"kTb{i}", [128, 4, 128]) for i in range(2)]
                    ktb = [alloc(es, f"ktb{i}", [128, 512]) for i in range(2)]
                    vb_ = [alloc(es, f"vb{i}", [128, 1024]) for i in range(2)]
                    ggb = [alloc(es, f"ggb{i}", [128, 512]) for i in range(2)]
                    E = alloc(es, "E", [128, 512])
                    Einv = alloc(es, "Einv", [128, 512])
                    qp = alloc(es, "qp", [128, 4, 128])
                    kp = alloc(es, "kp", [128, 4, 128])
                    e2 = alloc(es, "e2", [128, 512])
                    kpp = alloc(es, "kpp", [128, 512])
                    At = alloc(es, "At", [128, 512])
                    oTs = alloc(es, "oTs", [128, 8, 128])
                    for (tst, ntl, sidx) in g.seqs:
                        nch = ntl * 2
                        for d in range(2):
                            if sidx is None:
                                ld(Sst[:, :, :], I["st_gla"][d].rearrange("h d v -> d h v"))
                            else:
                                memset(S.pool, Sst[:, :, :], 0.0)
                            order = list(range(nch)) if d == 0 else list(range(nch - 1, -1, -1))
                            TRI = C("tri_f") if d == 0 else C("tri_b")
                            STR = C("str_f") if d == 0 else C("str_b")
                            MASK4 = C("mask4_f") if d == 0 else C("mask4_b")
                            last = 127 if d == 0 else 0

                            def loads(k):
                                c = order[k]
                                t0 = tst * TT + c * 128
                                i = k % 2
                                ld(qTb[i][:, :, :], QT[:, :, t0:t0 + 128])
                                ld(kTb[i][:, :, :], KT_[:, :, t0:t0 + 128])
                                ld(ktb[i][:, :], Kt[t0:t0 + 128, :])
                                ld(vb_[i][:, :], Vt[t0:t0 + 128, :])
                                ld(ggb[i][:, :], Gd[d][t0:t0 + 128, :])
                            loads(0)
                            for k in range(nch):
                                if k + 1 < nch:
                                    loads(k + 1)
                                c = order[k]
                                t0 = tst * TT + c * 128
                                i = k % 2
                                qT, kT, kt, v, gg = qTb[i], kTb[i], ktb[i], vb_[i], ggb[i]
                                ps1 = PS()
                                mm(ps1[:, :], STR, gg[:, :])
                                act(e2[:, :], ps1[:, :], AF.Exp)
                                tt(S.dve, kpp[:, :], kt[:, :], e2[:, :], ALU.mult)
                                ps2 = PS()
                                for h in range(4):
                                    mm(ps2[:, h * 128:(h + 1) * 128], gg[:, h * 128:(h + 1) * 128], TRI)
                                act(E[:, :], ps2[:, :], AF.Exp)
                                act(Einv[:, :], ps2[:, :], AF.Exp, scale=-1.0)
                                tt(S.dve, qp.v(qp.t[:, :, :].rearrange("p a b -> p (a b)")), qT.v(qT.t[:, :, :].rearrange("p a b -> p (a b)")), E[:, :], ALU.mult)
                                tt(S.pool, kp.v(kp.t[:, :, :].rearrange("p a b -> p (a b)")), kT.v(kT.t[:, :, :].rearrange("p a b -> p (a b)")), Einv[:, :], ALU.mult)
                                ps3 = PS()
                                for h in range(4):
                                    mm(ps3[:, h * 128:(h + 1) * 128], kp[:, h, :], qp[:, h, :])
                                tt(S.dve, At[:, :], ps3[:, :], MASK4, ALU.mult)
                                for half in range(2):
                                    ps4 = PS()
                                    for jq in range(4):
                                        idx = half * 4 + jq
                                        h, vc = idx // 2, idx % 2
                                        mm(ps4[:, jq * 128:(jq + 1) * 128], Sst[:, h, vc * 128:(vc + 1) * 128], qp[:, h, :], start=True, stop=False)
                                        mm(ps4[:, jq * 128:(jq + 1) * 128], v[:, h * 256 + vc * 128:h * 256 + (vc + 1) * 128], At[:, h * 128:(h + 1) * 128], start=False, stop=True)
                                    cp(S.act, oTs.v(oTs.t[:, half * 4:(half + 1) * 4, :].rearrange("p a b -> p (a b)")), ps4[:, :])
                                stq(Od[d][:, :, t0:t0 + 128], oTs[:, :, :])
                                for half in range(2):
                                    ps5 = PS()
                                    for jq in range(2):
                                        h = half * 2 + jq
                                        mm(ps5[:, jq * 256:(jq + 1) * 256], kpp[:, h * 128:(h + 1) * 128], v[:, h * 256:(h + 1) * 256])
                                    for jq in range(2):
                                        h = half * 2 + jq
                                        stt(Sst[:, h, :], Sst[:, h, :], E[:, h * 128 + last:h * 128 + last + 1], ps5[:, jq * 256:(jq + 1) * 256], ALU.mult, ALU.add)
                            if sidx is not None:
                                stq(O["o_gla"][sidx, d].rearrange("h d v -> d h v"), Sst[:, :, :])
                S.barrier()

                with ExitStack() as es:
                    TS = max(n_ for (_, n_, _) in g.seqs) * TT
                    a_ = alloc(es, "lru_a", [128, TS])
                    b_ = alloc(es, "lru_b", [128, TS])
                    hf = alloc(es, "lru_hf", [128, TS])
                    hb = alloc(es, "lru_hb", [128, TS])
                    lgt = alloc(es, "lru_lg", [128, TS])
                    mixo = alloc(es, "lru_mix", [128, TS], BF16)
                    h0c = alloc(es, "lru_h0", [128, 2])
                    hl = alloc(es, "lru_hl", [128, 2])
                    for (tst, ntl, sidx) in g.seqs:
                        T_ = ntl * TT
                        tsl = slice(tst * TT, tst * TT + T_)
                        for n in range(8):
                            ld(a_[:, 0:T_], Ad[0][:, n, tsl])
                            ld(b_[:, 0:T_], Bd[0][:, n, tsl])
                            if sidx is None:
                                ld(h0c[:, :], I["st_lru"][:, n * 128:(n + 1) * 128].rearrange("d p -> p d"), allow_slow_non_contiguous=True)
                                i0, i1 = h0c[:, 0:1], h0c[:, 1:2]
                            else:
                                i0, i1 = 0.0, 0.0

                            def scan(o, x0, x1, ini, rev):
                                sl_ = slice(None, None, -1) if rev else slice(None)
                                rd = [x0.b, x1.b]
                                iv = ini
                                if isinstance(ini, View):
                                    rd.append(ini.b)
                                    iv = ini.ap
                                S.op(S.dve, lambda e: e.tensor_tensor_scan(out=o.b.t[:, 0:T_][:, sl_], data0=x0.b.t[:, 0:T_][:, sl_], data1=x1.b.t[:, 0:T_][:, sl_],
                                                                          initial=iv, op0=ALU.mult, op1=ALU.add), reads=rd, writes=[o.b])
                            scan(hf[:, :], a_[:, :], b_[:, :], i0, False)
                            ld(a_[:, 0:T_], Ad[1][:, n, tsl])
                            ld(b_[:, 0:T_], Bd[1][:, n, tsl])
                            scan(hb[:, :], a_[:, :], b_[:, :], i1, True)
                            ld(lgt[:, 0:T_], LGT[:, n, tsl])
                            if sidx is not None:
                                cp(S.pool, hl[:, 0:1], hf[:, T_ - 1:T_])
                                cp(S.pool, hl[:, 1:2], hb[:, 0:1])
                                stq(O["o_lru"][sidx, :, n * 128:(n + 1) * 128].rearrange("d p -> p d"), hl[:, :], allow_slow_non_contiguous=True)
                            tt(S.pool, hf[:, 0:T_], hf[:, 0:T_], hb[:, 0:T_], ALU.add)
                            tt(S.dve, mixo[:, 0:T_], hf[:, 0:T_], lgt[:, 0:T_], ALU.mult)
                            stq(MIXT[:, 8 + n, tsl], mixo[:, 0:T_])
                S.barrier()

                def front_alloc(es):
                    f = Grp()
                    f.of = alloc(es, "f_of", [128, 8, TT])
                    f.ob = alloc(es, "f_ob", [128, 8, TT])
                    f.rs = alloc(es, "f_rs", [128, 4, TT])
                    return f

                def front(f, ti, mixT):
                    t0 = ti * TT
                    ld(f.of[:, :, :], Od[0][:, :, t0:t0 + TT])
                    ld(f.ob[:, :, :], Od[1][:, :, t0:t0 + TT])
                    ld(mixT[:, 8:16, :], MIXT[:, 8:16, t0:t0 + TT])
                    tt(S.pool, f.of[:, :, :], f.of[:, :, :], f.ob[:, :, :], ALU.add)
                    act(f.ob[:, :, :], f.of[:, :, :], AF.Square)
                    for half in range(2):
                        ps = PS()
                        for jq in range(2):
                            h = half * 2 + jq
                            for vc in range(2):
                                mm(ps[:, jq * TT:(jq + 1) * TT], ones, f.ob[:, h * 2 + vc, :], start=(vc == 0), stop=(vc == 1))
                        act(f.rs.v(f.rs.t[:, half * 2:half * 2 + 2, :].rearrange("p a b -> p (a b)")), ps[:, :], AF.Sqrt, scale=1.0 / 256.0, bias=EPS)
                    recip(f.rs[:, :, :], f.rs[:, :, :])
                    f.rt = f.ob
                    ld(f.rt[:, :, :], RT[:, :, t0:t0 + TT])
                    for c in range(8):
                        tt(S.pool, f.of[:, c, :], f.of[:, c, :], f.rs[:, c // 2, :], ALU.mult)
                        stt(mixT[:, c, :], f.of[:, c, :], gcol[:, c:c + 1], f.rt[:, c, :], ALU.mult, ALU.mult)

                phase3(g, l, front_alloc, front)

    def odd_layer(l, j):
        GQT = scr("GQT", [128, 8, TMAX])
        GKT = scr("GKT", [128, 8, TMAX])
        GKt = scr("GKt", [TMAX, 1024])
        GVt = scr("GVt", [TMAX, 1024])
        GZt = scr("GZt", [TMAX, 1024])
        GR = [scr(f"GR{d}", [5, 8, TMAX]) for d in range(2)]
        GC = [scr(f"GC{d}", [2, 8, TMAX]) for d in range(2)]
        DEC = [scr(f"DEC{d}", [TMAX // 64, 8]) for d in range(2)]
        OG = [scr(f"OG{d}", [TMAX, 1024]) for d in range(2)]
        MQT = scr("MQT", [128, 4, TMAX])
        MKT = scr("MKT", [128, 4, TMAX])
        MKt = scr("MKt", [TMAX, 512])
        MVt = scr("MVt", [TMAX, 1024])
        MOt = scr("MOt", [TMAX, 1024])
        LI = [scr(f"LI{d}", [4, TMAX]) for d in range(2)]
        LF = [scr(f"LF{d}", [4, TMAX]) for d in range(2)]
        MR = [scr(f"MR{d}", [6, 4, TMAX]) for d in range(2)]
        MDEC = [scr(f"MDEC{d}", [TMAX // 64, 4]) for d in range(2)]
        OM = [scr(f"OM{d}", [TMAX, 1024]) for d in range(2)]
        ones = C("ones")
        if DBG_STAGE <= -1:
            return
        with ExitStack() as les:
            cwg = alloc(les, "cwg", [128, 96])
            load_cols(cwg[:, :], I["gdn_cw"].rearrange("a b -> (a b)"), 96)
            dtb = alloc(les, "dtb", [8, 2])
            ld(dtb[:, :], I["gdn_dtb"].rearrange("d h -> h d"), allow_slow_non_contiguous=True)
            negA = alloc(les, "negA", [8, 2])
            ld(negA[:, :], I["gdn_alog"].rearrange("d h -> h d"), allow_slow_non_contiguous=True)
            act(negA[:, :], negA[:, :], AF.Exp)
            ts(S.dve, negA[:, :], negA[:, :], -1.0, ALU.mult)
            mlb = alloc(les, "mlb", [4, 4])
            ld(mlb[:, :], I["ml_gb"].rearrange("d g h -> h (d g)"), allow_slow_non_contiguous=True)
            mlbn = alloc(les, "mlbn", [4, 4])
            ts(S.dve, mlbn[:, :], mlb[:, :], -1.0, ALU.mult)
            wmif = alloc(les, "wmif", [128, KC, 16])
            ld(wmif[:, :, :], I["w_in_o"][:, 7200:7216].rearrange("(kc p) f -> p kc f", p=128))
            wmib = alloc(les, "wmib", [128, KC, 16], BF16)
            cp(S.dve, wmib[:, :, :], wmif[:, :, :])

            for g in make_groups(l):
                L, NL = g.L, g.NL
                if DBG_STAGE <= 0:
                    continue
                with ExitStack() as es:
                    bA, bB = alloc(es, "bA", [128, D]), alloc(es, "bB", [128, D])
                    bc_load(bA, l, g.w, 0)
                    bc_load(bB, l, g.w, 1)
                    xt = [alloc(es, f"xt{i}", [128, D]) for i in range(2)]
                    uTs = [alloc(es, f"uT{i}", [128, KC, TT], BF16) for i in range(2)]
                    junk = alloc(es, "junk", [128, D], BF16)
                    ss = alloc(es, "ss", [128, 4])
                    rstd = alloc(es, "rstd", [128, 4])
                    wb = [alloc(es, f"wb{i}", [128, KC, 512], BF16) for i in range(2)]
                    xp = alloc(es, "xp", [128, 2, NL, L + 3])
                    xc2 = alloc(es, "xc2", [128, 2, TT])
                    sqt = alloc(es, "sqt", [128, TT])
                    rs1 = alloc(es, "rs1", [128, TT])
                    fst = [alloc(es, f"fst{i}", [128, 8, TT]) for i in range(2)]
                    tst = [alloc(es, f"tst{i}", [128, 2, 1024]) for i in range(2)]
                    e8 = alloc(es, "e8", [8, TT])
                    glog = alloc(es, "glog", [8, TT])
                    lb = alloc(es, "lb", [8, TT])
                    rw = alloc(es, "rw", [8, 5, TT])
                    cs = alloc(es, "cs", [8, 2, TT])
                    dc = alloc(es, "dc", [8, 4])
                    g4 = alloc(es, "g4", [4, 2, TT])
                    e4 = alloc(es, "e4", [4, TT])
                    memset(S.pool, xp[:, :, :, :], 0.0)
                    memset(S.pool, rw[:, :, :], 1.0)
                    aps = []
                    for ti in range(g.ntiles):
                        aps += list(Wb[f"in{l}"])
                    ws = wstream(aps, wb)
                    fst_rr = [0]
                    tst_rr = [0]

                    def nfst():
                        fst_rr[0] ^= 1
                        return fst[fst_rr[0]]

                    def ntst():
                        tst_rr[0] ^= 1
                        return tst[tst_rr[0]]

                    def prep(ti):
                        for st in range(2):
                            load_x(g, ti, st, xt[st])
                        for st in range(2):
                            norm_mod_T(xt[st], xt[st], bA, bB, junk, ss, rstd, uTs[ti % 2], st, st)

                    def fm(ps, col, w, wc0, n, uT):
                        for kc in range(KC):
                            mm(ps[0:n, col:col + TT], w[:, kc, wc0:wc0 + n], uT[:, kc, :], start=(kc == 0), stop=(kc == KC - 1))

                    def tmj(ps, w, n, uT, st):
                        for kc in range(KC):
                            mm(ps[:, 0:n], uT[:, kc, st * 128:(st + 1) * 128], w[:, kc, 0:n], start=(kc == 0), stop=(kc == KC - 1))

                    def tm_block_pair(dram, func, scale=None):
                        t_ = ntst()
                        for zb in range(2):
                            w = next(ws)
                            for st in range(2):
                                ps = PS()
                                tmj(ps, w, 512, uT, st)
                                act(t_[:, st, zb * 512:(zb + 1) * 512], ps[:, :], func, scale=scale)
                        stq(dram[t0:t0 + TT, :].rearrange("(s p) f -> p s f", p=128), t_[:, :, :])

                    def to_tm(src, dram):
                        t_ = ntst()
                        for st in range(2):
                            for hq in range(2):
                                ps = PS()
                                for hh in range(4):
                                    h = hq * 4 + hh
                                    tr(ps[:, hh * 128:(hh + 1) * 128], src[:, h, st * 128:(st + 1) * 128], 128)
                                cp(evq(), t_[:, st, hq * 512:(hq + 1) * 512], ps[:, :])
                        stq(dram[t0:t0 + TT, :].rearrange("(s p) f -> p s f", p=128), t_[:, :, :])

                    prep(0)
                    for ti in range(g.ntiles):
                        t0 = ti * TT
                        uT = uTs[ti % 2]
                        for which in range(3):
                            f_ = nfst()
                            for half in range(2):
                                w = next(ws)
                                for cp_ in range(2):
                                    ps = PS()
                                    for hh in range(2):
                                        fm(ps, hh * TT, w, (cp_ * 2 + hh) * 128, 128, uT)
                                    for hh in range(2):
                                        h = half * 4 + cp_ * 2 + hh
                                        n = which * 8 + h
                                        act(xp[:, hh, :, 2:2 + L], ps.v(ps.t[:, hh * TT:(hh + 1) * TT].rearrange("p (a b) -> p a b", a=NL)), AF.Copy)
                                        xc = xc2.v(xc2.t[:, hh, :].rearrange("p (a b) -> p a b", a=NL))
                                        act(xc, xp[:, hh, :, 2:2 + L], AF.Identity, scale=cwg[:, 2 * 24 + n:2 * 24 + n + 1])
                                        for tap in (0, 1, 3):
                                            stt(xc, xp[:, hh, :, tap:tap + L], cwg[:, tap * 24 + n:tap * 24 + n + 1], xc, ALU.mult, ALU.add)
                                        if which == 2:
                                            act(f_[:, h, :], xc2[:, hh, :], AF.Silu)
                                        else:
                                            act(xc2[:, hh, :], xc2[:, hh, :], AF.Silu)
                                            act(sqt[:, :], xc2[:, hh, :], AF.Square)
                                            ps2 = PS()
                                            mm(ps2[:, 0:TT], ones, sqt[:, :])
                                            act(rs1[:, :], ps2[:, 0:TT], AF.Sqrt, bias=EPS)
                                            recip(rs1[:, :], rs1[:, :])
                                            stt(f_[:, h, :], xc2[:, hh, :], (128.0 ** -0.5) if which == 0 else 1.0, rs1[:, :], ALU.mult, ALU.mult)
                            if which == 0:
                                stq(GQT[:, :, t0:t0 + TT], f_[:, :, :])
                            elif which == 1:
                                stq(GKT[:, :, t0:t0 + TT], f_[:, :, :])
                                to_tm(f_, GKt)
                            else:
                                to_tm(f_, GVt)
                        if DBG_SUB <= 1:
                            continue
                        tm_block_pair(GZt, AF.Silu)
                        if ti + 1 < g.ntiles:
                            prep(ti + 1)
                        if DBG_SUB <= 2:
                            continue
                        w = next(ws)
                        for d in range(2):
                            lastoff = 63 if d == 0 else 0
                            ps = PS()
                            fm(ps, 0, w, d * 8, 8, uT)
                            act(e8[:, :], ps[0:8, 0:TT], AF.Exp, bias=dtb[:, d:d + 1])
                            act(e8[:, :], e8[:, :], AF.Ln, bias=1.0)
                            ts(S.dve, glog[:, :], e8[:, :], negA[:, d:d + 1], ALU.mult)
                            ps = PS()
                            fm(ps, 0, w, 16 + d * 8, 8, uT)
                            act(e8[:, :], ps[0:8, 0:TT], AF.Exp, scale=-1.0)
                            act(e8[:, :], e8[:, :], AF.Ln, bias=1.0)
                            ts(S.pool, lb[:, :], e8[:, :], -1.0, ALU.mult)
                            rst = C("reset")
                            if d == 0:
                                S.op(S.dve, lambda e: e.tensor_tensor_scan(out=rw.t[:, 0, :], data0=rst.ap[0:8, :], data1=glog.t[:, :], initial=0.0, op0=ALU.mult, op1=ALU.add),
                                     reads=[CT, glog], writes=[rw])
                            else:
                                S.op(S.dve, lambda e: e.tensor_tensor_scan(out=rw.t[:, 0, ::-1], data0=rst.ap[0:8, :], data1=glog.t[:, ::-1], initial=0.0, op0=ALU.mult, op1=ALU.add),
                                     reads=[CT, glog], writes=[rw])
                            ts(S.pool, rw[:, 4, :], rw[:, 0, :], -1.0, ALU.mult)
                            tt(S.pool, rw[:, 2, :], lb[:, :], rw[:, 0, :], ALU.subtract)
                            act(cs[:, 0, :], lb[:, :], AF.Exp)
                            for c in range(4):
                                li_ = c * 64 + lastoff
                                ts(S.dve, cs[:, 1, c * 64:(c + 1) * 64], rw[:, 0, c * 64:(c + 1) * 64], -1.0, ALU.mult, rw[:, 0, li_:li_ + 1], ALU.add)
                            act(cs[:, 1, :], cs[:, 1, :], AF.Exp)
                            act(dc[:, :], rw[:, 0, lastoff::64], AF.Exp)
                            stq(GR[d][:, :, t0:t0 + TT].rearrange("k h t -> h k t"), rw[:, :, :])
                            stq(GC[d][:, :, t0:t0 + TT].rearrange("k h t -> h k t"), cs[:, :, :])
                            stq(DEC[d][ti * 4:(ti + 1) * 4, :].rearrange("c h -> h c"), dc[:, :], allow_slow_non_contiguous=True)
                        if DBG_SUB <= 3:
                            continue
                        f_ = nfst()
                        for which in range(2):
                            w = next(ws)
                            for hp in range(2):
                                ps = PS()
                                for hh in range(2):
                                    fm(ps, hh * TT, w, (hp * 2 + hh) * 128, 128, uT)
                                c0 = which * 4 + hp * 2
                                act(f_.v(f_.t[:, c0:c0 + 2, :].rearrange("p a b -> p (a b)")), ps[:, :], AF.Copy, scale=1.0 if which == 0 else 128.0 ** -0.5)
                            stq((MQT, MKT)[which][:, :, t0:t0 + TT], f_[:, which * 4:which * 4 + 4, :])
                            if which == 1:
                                t_ = ntst()
                                for st in range(2):
                                    ps = PS()
                                    tmj(ps, w, 512, uT, st)
                                    act(t_[:, st, 0:512], ps[:, :], AF.Copy, scale=128.0 ** -0.5)
                                stq(MKt[t0:t0 + TT, :].rearrange("(s p) f -> p s f", p=128), t_[:, :, 0:512])
                        if DBG_SUB <= 4:
                            continue
                        tm_block_pair(MVt, AF.Copy)
                        tm_block_pair(MOt, AF.Sigmoid)
                        if DBG_SUB <= 5:
                            continue
                        w = wmib
                        for d in range(2):
                            ps = PS()
                            fm(ps, 0, w, d * 4, 4, uT)
                            act(g4[:, 0, :], ps[0:4, 0:TT], AF.Identity, bias=mlb[:, d * 2:d * 2 + 1])
                            ps = PS()
                            fm(ps, 0, w, 8 + d * 4, 4, uT)
                            act(e4[:, :], ps[0:4, 0:TT], AF.Exp, scale=-1.0, bias=mlbn[:, d * 2 + 1:d * 2 + 2])
                            act(e4[:, :], e4[:, :], AF.Ln, bias=1.0)
                            ts(S.pool, g4[:, 1, :], e4[:, :], -1.0, ALU.mult)
                            stq(LI[d][:, t0:t0 + TT], g4[:, 0, :])
                            stq(LF[d][:, t0:t0 + TT], g4[:, 1, :])
                S.barrier()

                if DBG_STAGE <= 1:
                    continue
                with ExitStack() as es:
                    TS = max(n_ for (_, n_, _) in g.seqs) * TT
                    lf = alloc(es, "m_lf", [4, TS])
                    li = alloc(es, "m_li", [4, TS])
                    mt = alloc(es, "m_m", [4, TS])
                    Ft = alloc(es, "m_F", [4, TS])
                    RWt = alloc(es, "m_RW", [4, TS])
                    WLt = alloc(es, "m_WL", [4, TS])
                    rstt = alloc(es, "m_rst", [4, TS])
                    one4 = alloc(es, "m_one", [4, TS])
                    m0c = alloc(es, "m_m0", [4, 2])
                    dcm = alloc(es, "m_dc", [4, TS // 64])
                    memset(S.pool, rstt[:, :], 1.0)
                    memset(S.pool, rstt[:, 0::64], 0.0)
                    memset(S.pool, one4[:, :], 1.0)
                    for (tst_, ntl, sidx) in g.seqs:
                        T_ = ntl * TT
                        nch = T_ // 64
                        tsl = slice(tst_ * TT, tst_ * TT + T_)
                        if sidx is None:
                            ld(m0c[:, :], I["st_mm"].rearrange("d h -> h d"), allow_slow_non_contiguous=True)
                        else:
                            memset(S.pool, m0c[:, :], 0.0)
                        for d in range(2):
                            ld(lf[:, 0:T_], LF[d][:, tsl])
                            ld(li[:, 0:T_], LI[d][:, tsl])
                            rv = slice(None, None, -1) if d == 1 else slice(None)

                            def sc_(o, a, b, ini, op0, op1, rev0=True):
                                rd = [a.b, b.b]
                                iv = ini
                                if isinstance(ini, View):
                                    rd.append(ini.b)
                                    iv = ini.ap
                                rv0 = rv if rev0 else slice(None)
                                S.op(S.dve, lambda e: e.tensor_tensor_scan(out=o.b.t[:, 0:T_][:, rv], data0=a.b.t[:, 0:T_][:, rv0], data1=b.b.t[:, 0:T_][:, rv],
                                                                          initial=iv, op0=op0, op1=op1), reads=rd, writes=[o.b])
                            sc_(mt[:, :], lf[:, :], li[:, :], m0c[:, d:d + 1], ALU.add, ALU.max)
                            sc_(Ft[:, :], rstt[:, :], lf[:, :], 0.0, ALU.mult, ALU.add, rev0=False)
                            tt(S.pool, li[:, 0:T_], li[:, 0:T_], Ft[:, 0:T_], ALU.subtract)
                            tt(S.pool, Ft[:, 0:T_], Ft[:, 0:T_], mt[:, 0:T_], ALU.subtract)
                            for c in range(nch):
                                lastc = c * 64 + (63 if d == 0 else 0)
                                if d == 0:
                                    mp = m0c[:, 0:1] if c == 0 else mt[:, c * 64 - 1:c * 64]
                                else:
                                    mp = m0c[:, 1:2] if c == nch - 1 else mt[:, (c + 1) * 64:(c + 1) * 64 + 1]
                                ts(S.dve, RWt[:, c * 64:(c + 1) * 64], Ft[:, c * 64:(c + 1) * 64], mp, ALU.add)
                                ts(S.dve, WLt[:, c * 64:(c + 1) * 64], li[:, c * 64:(c + 1) * 64], Ft[:, lastc:lastc + 1], ALU.add)
                            lo = 63 if d == 0 else 0
                            act(dcm[:, 0:nch], RWt[:, lo:T_:64], AF.Exp)
                            if sidx is not None:
                                le = T_ - 1 if d == 0 else 0
                                stq(O["o_mm"][sidx, d, :].rearrange("(h o) -> h o", o=1), mt[:, le:le + 1], allow_slow_non_contiguous=True)
                            stq(MR[d][0, :, tsl], Ft[:, 0:T_])
                            stq(MR[d][1, :, tsl], one4[:, 0:T_])
                            stq(MR[d][2, :, tsl], li[:, 0:T_])
                            stq(MR[d][3, :, tsl], RWt[:, 0:T_])
                            stq(MR[d][4, :, tsl], WLt[:, 0:T_])
                            ts(S.pool, mt[:, 0:T_], mt[:, 0:T_], -1.0, ALU.mult)
                            stq(MR[d][5, :, tsl], mt[:, 0:T_])
                            stq(MDEC[d][tst_ * 4:tst_ * 4 + nch, :].rearrange("c h -> h c"), dcm[:, 0:nch], allow_slow_non_contiguous=True)
                S.barrier()

                if DBG_STAGE <= 2:
                    continue
                with ExitStack() as es:
                    NCHM = max(n_ for (_, n_, _) in g.seqs) * 4
                    Sst = alloc(es, "gS", [128, 8, 128])
                    qTb = [alloc(es, f"gq{i}", [128, 8, 64]) for i in range(2)]
                    kTb = [alloc(es, f"gk{i}", [128, 8, 64]) for i in range(2)]
                    ktp = [alloc(es, f"gkt{i}", [128, 4, 128]) for i in range(2)]
                    vtp = [alloc(es, f"gvt{i}", [128, 4, 128]) for i in range(2)]
                    R01 = [alloc(es, f"gr01{i}", [2, 8, 64]) for i in range(2)]
                    R12 = [alloc(es, f"gr12{i}", [2, 8, 64]) for i in range(2)]
                    R34 = [alloc(es, f"gr34{i}", [2, 8, 64]) for i in range(2)]
                    COLS = [alloc(es, f"gcol{i}", [128, 4, 2]) for i in range(2)]
                    G1, G2, G3 = [alloc(es, f"gG{i}", [128, 512]) for i in range(3)]
                    Pb = [alloc(es, f"gP{i}", [128, 512]) for i in range(2)]
                    PTb = [alloc(es, f"gPT{i}", [128, 512]) for i in range(2)]
                    Ttb = [alloc(es, f"gTt{i}", [128, 512]) for i in range(2)]
                    QKT = alloc(es, "gQKT", [128, 512])
                    EB = alloc(es, "gEB", [128, 512])
                    qp = alloc(es, "gqp", [128, 8, 64])
                    kp = alloc(es, "gkp", [128, 8, 64])
                    rr = alloc(es, "grr", [128, 4, 128])
                    vn = alloc(es, "gvn", [128, 4, 128])
                    og = alloc(es, "gog", [128, 4, 128])
                    kpp = alloc(es, "gkpp", [128, 4, 128])
                    DECB = alloc(es, "gDECB", [128, NCHM * 8])
                    drow = alloc(es, "gdrow", [1, NCHM * 8])
                    id4 = C("ident4")

                    def fl(t):
                        return t.v(t.t[:, :, :].rearrange("p a b -> p (a b)"))

                    def pr(t, hp, rows=128):
                        return t.v(t.t[0:rows, 2 * hp:2 * hp + 2, :].rearrange("p a b -> p (a b)"))

                    def s4(t, hp):
                        return t[:, hp * 128:(hp + 1) * 128]

                    for (tst_, ntl, sidx) in g.seqs:
                        nch = ntl * 4
                        for d in range(2):
                            if sidx is None:
                                ld(Sst[:, :, :], I["st_gdn"][d].rearrange("h k v -> k h v"))
                            else:
                                memset(S.pool, Sst[:, :, :], 0.0)
                            c0g = tst_ * 4
                            ld(drow[0:1, 0:nch * 8], DEC[d][c0g:c0g + nch, :].rearrange("(o c) h -> o (c h)", o=1))
                            for q0 in range(0, nch * 8, 512):
                                q1 = min(q0 + 512, nch * 8)
                                ps = PS()
                                mm(ps[:, 0:q1 - q0], ones.b.v(ones.ap[0:1, 0:128]), drow[0:1, q0:q1])
                                cp(S.dve, DECB[:, q0:q1], ps[:, 0:q1 - q0])
                            order = list(range(nch)) if d == 0 else list(range(nch - 1, -1, -1))
                            NM_, NMT_, NQK_ = (C("nlt"), C("ngt"), C("nge")) if d == 0 else (C("ngt"), C("nlt"), C("nle"))

                            def loads(k):
                                c = order[k]
                                t0 = tst_ * TT + c * 64
                                i = k % 2
                                ld(qTb[i][:, :, :], GQT[:, :, t0:t0 + 64])
                                ld(kTb[i][:, :, :], GKT[:, :, t0:t0 + 64])
                                for h2 in range(2):
                                    ld(ktp[i][h2 * 64:(h2 + 1) * 64, :, :], GKt[t0:t0 + 64, :].rearrange("s (hp h2 d) -> h2 s hp d", hp=4, h2=2)[h2])
                                    ld(vtp[i][h2 * 64:(h2 + 1) * 64, :, :], GVt[t0:t0 + 64, :].rearrange("s (hp h2 d) -> h2 s hp d", hp=4, h2=2)[h2])
                                    for kk in range(2):
                                        ld(COLS[i][h2 * 64:(h2 + 1) * 64, :, kk], GC[d][kk, h2::2, t0:t0 + 64].rearrange("hp s -> s hp"), allow_slow_non_contiguous=True)
                                ld(R01[i][:, :, :], GR[d][0:2, :, t0:t0 + 64])
                                ld(R12[i][:, :, :], GR[d][1:3, :, t0:t0 + 64])
                                ld(R34[i][:, :, :], GR[d][3:5, :, t0:t0 + 64])
                            loads(0)
                            for k in range(nch):
                                if k + 1 < nch:
                                    loads(k + 1)
                                c = order[k]
                                t0 = tst_ * TT + c * 64
                                i = k % 2
                                qT, kT, kt, vt, r01, r12, r34, cols = qTb[i], kTb[i], ktp[i], vtp[i], R01[i], R12[i], R34[i], COLS[i]
                                if DBG_SUB2 <= 1:
                                    continue
                                pE = [PS(), PS(), PS()]
                                for e_, (msk, la, ra) in enumerate(((NM_, r01, r12), (NMT_, r12, r01), (NQK_, r34, r01))):
                                    mm(pE[e_][:, :], ident, msk, start=True, stop=False)
                                    for hp in range(4):
                                        mm(s4(pE[e_], hp), pr(la, hp, 2), pr(ra, hp, 2), start=False, stop=(hp == 3))
                                act(G1[:, :], pE[0][:, :], AF.Exp)
                                act(G2[:, :], pE[1][:, :], AF.Exp)
                                act(G3[:, :], pE[2][:, :], AF.Exp)
                                if DBG_SUB2 <= 2:
                                    continue
                                pKK, pKQ = PS(), PS()
                                for hp in range(4):
                                    mm(s4(pKK, hp), pr(kT, hp), pr(kT, hp))
                                for hp in range(4):
                                    mm(s4(pKQ, hp), pr(kT, hp), pr(qT, hp))
                                P, PT, Tt = Pb[0], PTb[0], Ttb[0]
                                tt(S.dve, P[:, :], pKK[:, :], G1[:, :], ALU.mult)
                                tt(S.dve, PT[:, :], pKK[:, :], G2[:, :], ALU.mult)
                                tt(S.dve, QKT[:, :], pKQ[:, :], G3[:, :], ALU.mult)
                                tt(S.pool, Tt[:, :], id4, PT[:, :], ALU.subtract)
                                if DBG_SUB2 <= 3:
                                    continue
                                cur = 0
                                for jl in range(1, 6):
                                    Pn, PTn, Ttn = Pb[1 - cur], PTb[1 - cur], Ttb[1 - cur]
                                    pP = PS()
                                    for hp in range(4):
                                        mm(s4(pP, hp), s4(PT, hp), s4(P, hp))
                                    if jl < 5:
                                        pPT = PS()
                                        for hp in range(4):
                                            mm(s4(pPT, hp), s4(P, hp), s4(PT, hp))
                                    cp(S.act, Pn[:, :], pP[:, :])
                                    if jl < 5:
                                        cp(S.dve, PTn[:, :], pPT[:, :])
                                    pT = PS()
                                    for hp in range(4):
                                        mm(s4(pT, hp), ident, s4(Tt, hp), start=True, stop=False)
                                        mm(s4(pT, hp), s4(Pn, hp), s4(Tt, hp), start=False, stop=True)
                                    cp(evq(), Ttn[:, :], pT[:, :])
                                    P, PT, Tt = Pn, PTn, Ttn
                                    cur = 1 - cur
                                if DBG_SUB2 <= 4:
                                    continue
                                pB = PS()
                                mm(pB[:, :], ones.b.v(ones.ap[0:1, 0:128]), r01.v(r01.t[0:1, :, :].rearrange("p a b -> p (a b)")))
                                act(EB[:, :], pB[:, :], AF.Exp)
                                tt(S.dve, fl(qp), fl(qT), EB[:, :], ALU.mult)
                                tt(S.pool, fl(kp), fl(kT), EB[:, :], ALU.mult)
                                if DBG_SUB2 <= 5:
                                    continue
                                pS_ = PS()
                                for hp in range(4):
                                    for h2 in range(2):
                                        h = hp * 2 + h2
                                        mm(pS_[h2 * 64:(h2 + 1) * 64, hp * 128:(hp + 1) * 128], kp[:, h, :], Sst[:, h, :])
                                tt(S.dve, fl(rr), fl(vt), pS_[:, :], ALU.subtract)
                                if DBG_SUB2 <= 6:
                                    continue
                                pV = PS()
                                for hp in range(4):
                                    mm(s4(pV, hp), s4(Tt, hp), rr[:, hp, :])
                                for hp in range(4):
                                    act(vn[:, hp, :], s4(pV, hp), AF.Copy, scale=cols[:, hp, 0:1])
                                if DBG_SUB2 <= 7:
                                    continue
                                pO = PS()
                                for hp in range(4):
                                    for h2 in range(2):
                                        h = hp * 2 + h2
                                        mm(pO[h2 * 64:(h2 + 1) * 64, hp * 128:(hp + 1) * 128], qp[:, h, :], Sst[:, h, :], start=True, stop=False)
                                    mm(s4(pO, hp), s4(QKT, hp), vn[:, hp, :], start=False, stop=True)
                                cp(evq(), fl(og), pO[:, :])
                                for h2 in range(2):
                                    stq(OG[d][t0:t0 + 64, :].rearrange("t (hp h2 v) -> h2 t hp v", hp=4, h2=2)[h2], og[h2 * 64:(h2 + 1) * 64, :, :])
                                if DBG_SUB2 <= 8:
                                    continue
                                tt(S.pool, kpp[:, :, :], kt[:, :, :], cols.v(cols.t[:, :, 1:2].to_broadcast([128, 4, 128])), ALU.mult)
                                pU = [PS(), PS()]
                                for h2 in range(2):
                                    for hp in range(4):
                                        mm(pU[h2][:, hp * 128:(hp + 1) * 128], kpp[h2 * 64:(h2 + 1) * 64, hp, :], vn[h2 * 64:(h2 + 1) * 64, hp, :])
                                tt(S.pool, Sst[:, :, :], Sst[:, :, :], DECB.v(DECB.t[:, c * 8:(c + 1) * 8].unsqueeze(2).to_broadcast([128, 8, 128])), ALU.mult)
                                for h2 in range(2):
                                    tt(S.dve, Sst.v(Sst.t[:, h2::2, :]), Sst.v(Sst.t[:, h2::2, :]),
                                       pU[h2].v(pU[h2].t[:, :].rearrange("p (a b) -> p a b", a=4)), ALU.add)
                            if sidx is not None:
                                stq(O["o_gdn"][sidx, d].rearrange("h k v -> k h v"), Sst[:, :, :])
                S.barrier()

                if DBG_STAGE <= 3:
                    continue
                with ExitStack() as es:
                    NCHM = max(n_ for (_, n_, _) in g.seqs) * 4
                    Cst = alloc(es, "mC", [128, 4, 257])
                    qTb = [alloc(es, f"mq{i}", [128, 4, 64]) for i in range(2)]
                    kTb = [alloc(es, f"mk{i}", [128, 4, 64]) for i in range(2)]
                    ktp = [alloc(es, f"mkt{i}", [128, 2, 128]) for i in range(2)]
                    vtp = [alloc(es, f"mvt{i}", [128, 2, 257]) for i in range(2)]
                    R01 = [alloc(es, f"mr01{i}", [2, 4, 64]) for i in range(2)]
                    R12 = [alloc(es, f"mr12{i}", [2, 4, 64]) for i in range(2)]
                    R3 = [alloc(es, f"mr3{i}", [1, 4, 64]) for i in range(2)]
                    MCL = [alloc(es, f"mcl{i}", [128, 2, 2]) for i in range(2)]
                    MC = alloc(es, "mMC", [128, 2, 2])
                    Gm = alloc(es, "mG", [128, 256])
                    QKT = alloc(es, "mQKT", [128, 256])
                    WB = alloc(es, "mWB", [128, 256])
                    qp = alloc(es, "mqp", [128, 4, 64])
                    hout = alloc(es, "mh", [128, 2, 256])
                    kpp = alloc(es, "mkpp", [128, 2, 128])
                    dcol = alloc(es, "mdcol", [128, 2])
                    DECB = alloc(es, "mDECB", [128, NCHM * 4])
                    drow = alloc(es, "mdrow", [1, NCHM * 4])
                    for i in range(2):
                        memset(S.pool, vtp[i][:, :, 256:257], 1.0)

                    def fl(t):
                        return t.v(t.t[:, :, :].rearrange("p a b -> p (a b)"))

                    def pr(t, hp, rows=128):
                        return t.v(t.t[0:rows, 2 * hp:2 * hp + 2, :].rearrange("p a b -> p (a b)"))

                    for (tst_, ntl, sidx) in g.seqs:
                        nch = ntl * 4
                        for d in range(2):
                            if sidx is None:
                                ld(Cst[:, :, 0:256], I["st_mc"][d].rearrange("h k v -> k h v"))
                                ld(Cst[:, :, 256:257], I["st_mn"][d].rearrange("h (k o) -> k h o", o=1), allow_slow_non_contiguous=True)
                            else:
                                memset(S.pool, Cst[:, :, :], 0.0)
                            c0g = tst_ * 4
                            ld(drow[0:1, 0:nch * 4], MDEC[d][c0g:c0g + nch, :].rearrange("(o c) h -> o (c h)", o=1))
                            ps = PS()
                            mm(ps[:, 0:nch * 4], ones.b.v(ones.ap[0:1, 0:128]), drow[0:1, 0:nch * 4])
                            cp(S.dve, DECB[:, 0:nch * 4], ps[:, 0:nch * 4])
                            order = list(range(nch)) if d == 0 else list(range(nch - 1, -1, -1))
                            NQK_ = C("nge") if d == 0 else C("nle")

                            def loads(k):
                                c = order[k]
                                t0 = tst_ * TT + c * 64
                                i = k % 2
                                ld(qTb[i][:, :, :], MQT[:, :, t0:t0 + 64])
                                ld(kTb[i][:, :, :], MKT[:, :, t0:t0 + 64])
                                for h2 in range(2):
                                    ld(ktp[i][h2 * 64:(h2 + 1) * 64, :, :], MKt[t0:t0 + 64, :].rearrange("s (hp h2 d) -> h2 s hp d", hp=2, h2=2)[h2])
                                    ld(vtp[i][h2 * 64:(h2 + 1) * 64, :, 0:256], MVt[t0:t0 + 64, :].rearrange("s (hp h2 v) -> h2 s hp v", hp=2, h2=2)[h2])
                                    for kk in range(2):
                                        ld(MCL[i][h2 * 64:(h2 + 1) * 64, :, kk], MR[d][4 + kk, h2::2, t0:t0 + 64].rearrange("hp s -> s hp"), allow_slow_non_contiguous=True)
                                ld(R01[i][:, :, :], MR[d][0:2, :, t0:t0 + 64])
                                ld(R12[i][:, :, :], MR[d][1:3, :, t0:t0 + 64])
                                ld(R3[i][:, :, :], MR[d][3:4, :, t0:t0 + 64])
                            loads(0)
                            for k in range(nch):
                                if k + 1 < nch:
                                    loads(k + 1)
                                c = order[k]
                                t0 = tst_ * TT + c * 64
                                i = k % 2
                                qT, kT, kt, vt, r01, r12, r3, mcl = qTb[i], kTb[i], ktp[i], vtp[i], R01[i], R12[i], R3[i], MCL[i]
                                pE = PS()
                                mm(pE[:, 0:256], ident, NQK_.b.v(NQK_.ap[:, 0:256]), start=True, stop=False)
                                for hp in range(2):
                                    mm(pE[:, hp * 128:(hp + 1) * 128], pr(r12, hp, 2), pr(r01, hp, 2), start=False, stop=(hp == 1))
                                act(Gm[:, :], pE[:, 0:256], AF.Exp)
                                pKQ = PS()
                                for hp in range(2):
                                    mm(pKQ[:, hp * 128:(hp + 1) * 128], pr(kT, hp), pr(qT, hp))
                                tt(S.dve, QKT[:, :], pKQ[:, 0:256], Gm[:, :], ALU.mult)
                                pB = PS()
                                mm(pB[:, 0:256], ones.b.v(ones.ap[0:1, 0:128]), r3.v(r3.t[0:1, :, :].rearrange("p a b -> p (a b)")))
                                act(WB[:, :], pB[:, 0:256], AF.Exp)
                                tt(S.dve, fl(qp), fl(qT), WB[:, :], ALU.mult)
                                act(MC[:, :, :], mcl[:, :, :], AF.Exp)
                                for hp in range(2):
                                    pN = PS()
                                    for h2 in range(2):
                                        h = hp * 2 + h2
                                        mm(pN[h2 * 64:(h2 + 1) * 64, 0:257], qp[:, h, :], Cst[:, h, :], start=True, stop=False)
                                    mm(pN[:, 0:257], QKT[:, hp * 128:(hp + 1) * 128], vt[:, hp, :], start=False, stop=True)
                                    act(dcol[:, hp:hp + 1], pN[:, 256:257], AF.Abs)
                                    ts(S.dve, dcol[:, hp:hp + 1], dcol[:, hp:hp + 1], MC[:, hp, 1:2], ALU.max)
                                    recip(dcol[:, hp:hp + 1], dcol[:, hp:hp + 1])
                                    act(hout[:, hp, :], pN[:, 0:256], AF.Copy, scale=dcol[:, hp:hp + 1])
                                for h2 in range(2):
                                    stq(OM[d][t0:t0 + 64, :].rearrange("t (hp h2 v) -> h2 t hp v", hp=2, h2=2)[h2], hout[h2 * 64:(h2 + 1) * 64, :, :])
                                tt(S.pool, kpp[:, :, :], kt[:, :, :], MC.v(MC.t[:, :, 0:1].to_broadcast([128, 2, 128])), ALU.mult)
                                for h in range(4):
                                    hp, h2 = h // 2, h % 2
                                    pU = PS()
                                    mm(pU[:, 0:257], kpp[h2 * 64:(h2 + 1) * 64, hp, :], vt[h2 * 64:(h2 + 1) * 64, hp, :])
                                    stt(Cst[:, h, :], Cst[:, h, :], DECB[:, c * 4 + h:c * 4 + h + 1], pU[:, 0:257], ALU.mult, ALU.add)
                            if sidx is not None:
                                stq(O["o_mc"][sidx, d].rearrange("h k v -> k h v"), Cst[:, :, 0:256])
                                stq(O["o_mn"][sidx, d].rearrange("h (k o) -> k h o", o=1), Cst[:, :, 256:257], allow_slow_non_contiguous=True)
                S.barrier()

                if DBG_STAGE <= 4:
                    continue
                def front_alloc(es):
                    f = Grp()
                    f.a = [alloc(es, f"fo_a{i}", [128, 1024]) for i in range(2)]
                    f.m = [alloc(es, f"fo_m{i}", [128, 1024]) for i in range(2)]
                    f.gz = alloc(es, "fo_gz", [128, 1024])
                    f.mo = alloc(es, "fo_mo", [128, 1024])
                    f.mix = alloc(es, "fo_mix", [128, D])
                    f.gg = alloc(es, "fo_gg", [128, 1024])
                    f.mg = alloc(es, "fo_mg", [128, 1024])
                    f.ssq = alloc(es, "fo_ss", [128, 16])
                    ld(f.gg[:, :], I["gdn_g"].rearrange("(o n) -> o n", o=1).partition_broadcast(128).rearrange("p a b -> p (a b)"))
                    ld(f.mg[:, :], I["ml_g"].rearrange("(o n) -> o n", o=1).partition_broadcast(128).rearrange("p a b -> p (a b)"))
                    return f

                def front(f, ti, mixT):
                    for st in range(2):
                        r0 = ti * TT + st * 128
                        ld(f.a[0][:, :], OG[0][r0:r0 + 128, :])
                        ld(f.a[1][:, :], OG[1][r0:r0 + 128, :])
                        ld(f.m[0][:, :], OM[0][r0:r0 + 128, :])
                        ld(f.m[1][:, :], OM[1][r0:r0 + 128, :])
                        ld(f.gz[:, :], GZt[r0:r0 + 128, :])
                        ld(f.mo[:, :], MOt[r0:r0 + 128, :])
                        for (bufs, nh, hd, gt, gate, off, sc0) in ((f.a, 8, 128, f.gg, f.gz, 0, 0), (f.m, 4, 256, f.mg, f.mo, 1024, 8)):
                            o_, sq_ = bufs
                            tt(S.pool, o_[:, :], o_[:, :], sq_[:, :], ALU.add)
                            act(sq_[:, :], o_[:, :], AF.Square)
                            S.op(S.dve, lambda e, sq_=sq_, nh=nh, sc0=sc0: e.tensor_reduce(out=f.ssq.t[:, sc0:sc0 + nh], in_=sq_.t[:, :].rearrange("p (a b) -> p a b", a=nh), axis=AX.X, op=ALU.add),
                                 reads=[sq_], writes=[f.ssq])
                            act(f.ssq[:, sc0:sc0 + nh], f.ssq[:, sc0:sc0 + nh], AF.Sqrt, scale=1.0 / hd, bias=EPS)
                            recip(f.ssq[:, sc0:sc0 + nh], f.ssq[:, sc0:sc0 + nh])
                            tt(S.dve, o_.v(o_.t[:, :].rearrange("p (a b) -> p a b", a=nh)), o_.v(o_.t[:, :].rearrange("p (a b) -> p a b", a=nh)),
                               f.ssq.v(f.ssq.t[:, sc0:sc0 + nh].unsqueeze(2).to_broadcast([128, nh, hd])), ALU.mult)
                            tt(S.pool, o_[:, :], o_[:, :], gt[:, :], ALU.mult)
                            tt(S.pool, f.mix[:, off:off + 1024], o_[:, :], gate[:, :], ALU.mult)
                        transpose_to(mixT, f.mix, st)

                phase3(g, l, front_alloc, front)

    for l in range(depth):
        if DBG_STAGE <= -2:
            break
        if l % 2 == 0:
            even_layer(l, l // 2)
        else:
            odd_layer(l, l // 2)
    S.barrier()
    top.close()
    build.stats = {q.name: q.ninst for q in S.engs}
    return nc


N_CORES = 8
_CACHE = {}


def make_in_maps(inp, NCT=4, NLT=16, n_cores=N_CORES):
    f = lambda a: np.ascontiguousarray(np.asarray(a, dtype=np.float32))
    shared = {
        "w_mod": f(inp["w_mod"]), "b_mod": f(inp["b_mod"]), "norm_g": f(inp["norm_g"]),
        "w_up": f(inp["w_up"]), "w_down": f(inp["w_down"]),
        "w_in_e": f(inp["w_in_e"][0]), "gla_w2": f(inp["gla_gate_w2"][0]), "gla_b": f(inp["gla_gate_b"][0]),
        "gla_g": f(inp["gla_norm_g"][0]), "lru_cw": f(inp["lru_conv_w"][0]), "lru_cb": f(inp["lru_conv_b"][0]),
        "lru_gw": f(inp["lru_gate_w"][0]), "lru_gb": f(inp["lru_gate_b"][0]), "lru_lam": f(inp["lru_lambda"][0]),
        "w_out_e": f(inp["w_out_e"][0]),
        "w_in_o": f(inp["w_in_o"][0]), "gdn_cw": f(inp["gdn_conv_w"][0]), "gdn_alog": f(inp["gdn_a_log"][0]),
        "gdn_dtb": f(inp["gdn_dt_bias"][0]), "gdn_g": f(inp["gdn_norm_g"][0]), "ml_gb": f(inp["mlstm_gate_b"][0]),
        "ml_g": f(inp["mlstm_norm_g"][0]), "w_out_o": f(inp["w_out_o"][0]),
        "consts": CONST_ARR,
    }
    maps = []
    xp_, xs_ = f(inp["x_prompt"]), f(inp["x_sample"])
    for c in range(n_cores):
        b = c % 4
        m = dict(shared)
        m["xc"] = xp_[c * NCT:(c + 1) * NCT].reshape(NCT * TT, D)
        m["xl"] = np.ascontiguousarray(xs_[b, :NLT * TT])
        m["cc"] = np.ascontiguousarray(np.stack([f(inp["c_ctx"]), f(inp["c"])[b]], 0))
        m["st_gla"] = f(inp["state_gla"])[b, 0]
        m["st_lru"] = f(inp["state_lru"])[b, 0]
        m["st_gdn"] = f(inp["state_gdn"])[b, 0]
        m["st_mc"] = f(inp["state_mlstm_c"])[b, 0]
        m["st_mn"] = f(inp["state_mlstm_n"])[b, 0]
        m["st_mm"] = f(inp["state_mlstm_m"])[b, 0]
        maps.append(m)
    return maps


def kernel(**inp):
    if "nc" not in _CACHE:
        _CACHE["nc"] = build()
    nc = _CACHE["nc"]
    maps = make_in_maps(inp)
    res = run_bass_kernel_spmd(nc, maps, core_ids=list(range(N_CORES)))
    R = res.results
    B = 32
    y_ctx = np.concatenate([R[c]["yc"].reshape(4, TT, D) for c in range(8)], 0)
    y_lat = np.stack([R[b]["yl"] for b in range(4)], 0)
    cat = lambda k: np.concatenate([R[c][k] for c in range(8)], 0)
    new_gla = cat("o_gla")[:, None]
    new_lru = cat("o_lru")[:, None]
    new_gdn = cat("o_gdn")[:, None]
    new_c = cat("o_mc")[:, None]
    new_n = cat("o_mn")[:, None]
    new_m = cat("o_mm")[:, None]
    return tuple(np.ascontiguousarray(a.astype(np.float32)) for a in (y_ctx, y_lat, new_gla, new_lru, new_gdn, new_c, new_n, new_m))


def simulate_trace(trace):
    sems = {}
    pos = {k: 0 for k in trace}
    progress = True
    while progress:
        progress = False
        for k, lst in trace.items():
            while pos[k] < len(lst):
                kind, sid, val = lst[pos[k]]
                if kind == "w":
                    if sems.get(sid, 0) >= val:
                        pos[k] += 1
                        progress = True
                    else:
                        break
                else:
                    sems[sid] = sems.get(sid, 0) + val
                    pos[k] += 1
                    progress = True
    stuck = {k: (pos[k], len(lst), lst[pos[k]] if pos[k] < len(lst) else None) for k, lst in trace.items()}
    if all(p == n for (p, n, _) in stuck.values()):
        return None
    return stuck, sems
```

```python
import numpy as np
from contextlib import ExitStack
import concourse.bass as bass
import concourse.mybir as mybir
from concourse.alu_op_type import AluOpType as ALU
from concourse.bass_utils import run_bass_kernel_spmd

F32 = mybir.dt.float32
BF16 = mybir.dt.bfloat16
AF = mybir.ActivationFunctionType
AX = mybir.AxisListType

D = 2048
KC = 16
TT = 256
FF = 8192
EPS = 1e-6
NEG = -30000.0


class View:
    __slots__ = ("b", "ap")

    def __init__(self, b, ap):
        self.b = b
        self.ap = ap


class Buf:
    __slots__ = ("t", "name", "w", "r")

    def __init__(self, t, name=""):
        self.t = t
        self.name = name
        self.w = None
        self.r = {}

    def __getitem__(self, idx):
        return View(self, self.t[idx])

    def v(self, ap):
        return View(self, ap)


SEM_LIMIT = 60000
DBG_STAGE = 99
DBG_DUMP = ()
DBG_SUB2 = 99
TRACE = None
DBG_SUB = 99
DBG_GROUPS = (0, 1)


class EngQ:
    def __init__(self, S, name, eng, self_raw=True):
        self.S = S
        self.name = name
        self.eng = eng
        self.sem = S.nc.alloc_semaphore(name=f"prog_{name}")
        self.semids = {id(self.sem)}
        self.nroll = 0
        self.n = 0
        self.last_ev = None
        self.seen = {}
        self.self_raw = self_raw
        self.ninst = 0

    def roll(self):
        self.nroll += 1
        self.sem = self.S.nc.alloc_semaphore(name=f"prog_{self.name}_{self.nroll}")
        self.semids.add(id(self.sem))
        self.n = 0


class Sched:
    def __init__(self, nc, n_dma_sems=32):
        self.nc = nc
        self.pe = EngQ(self, "pe", nc.tensor, self_raw=False)
        self.act = EngQ(self, "act", nc.scalar)
        self.dve = EngQ(self, "dve", nc.vector)
        self.pool = EngQ(self, "pool", nc.gpsimd)
        self.sp = EngQ(self, "sp", nc.sync)
        self.engs = [self.pe, self.act, self.dve, self.pool, self.sp]
        self.dma_sems = [nc.alloc_semaphore(name=f"dma{i}") for i in range(n_dma_sems)]
        self.dma_val = [0] * n_dma_sems
        self.dma_rr = 0

    def _wait(self, q, ev):
        sem, val = ev
        k = id(sem)
        if q.seen.get(k, 0) < val:
            q.eng.wait_ge(sem, val)
            q.seen[k] = val
            q.ninst += 1
            if TRACE is not None:
                TRACE.setdefault(q.name, []).append(("w", k, val))

    def _deps(self, q, reads, writes):
        for b in reads:
            if b is None:
                continue
            if b.w is not None:
                if id(b.w[0]) in q.semids and not q.self_raw:
                    continue
                self._wait(q, b.w)
        for b in writes:
            if b is None:
                continue
            if b.w is not None and id(b.w[0]) not in q.semids:
                self._wait(q, b.w)
            for ev in b.r.values():
                if id(ev[0]) not in q.semids:
                    self._wait(q, ev)

    def _mark(self, ev, reads, writes):
        k = id(ev[0])
        for b in reads:
            if b is not None:
                b.r[k] = ev
        for b in writes:
            if b is not None:
                b.w = ev
                b.r = {}

    def op(self, q, fn, reads=(), writes=(), inc=True):
        if q.n >= SEM_LIMIT:
            q.roll()
        self._deps(q, reads, writes)
        inst = fn(q.eng)
        q.ninst += 1
        if inc:
            q.n += 1
            inst.then_inc(q.sem, 1)
            ev = (q.sem, q.n)
            q.last_ev = ev
            if TRACE is not None:
                TRACE.setdefault(q.name, []).append(("i", id(q.sem), 1))
        else:
            ev = (q.sem, q.n + 1)
        self._mark(ev, reads, writes)
        return inst

    def dma(self, q, out_ap, in_ap, reads=(), writes=(), **kw):
        self._deps(q, reads, writes)
        i = self.dma_rr
        self.dma_rr = (self.dma_rr + 1) % len(self.dma_sems)
        sem = self.dma_sems[i]
        if self.dma_val[i] > 0:
            self._wait(q, (sem, self.dma_val[i]))
        inst = q.eng.dma_start(out=out_ap, in_=in_ap, **kw)
        self.dma_val[i] += 16
        inst.then_inc(sem, 16)
        if TRACE is not None:
            TRACE.setdefault(q.name, []).append(("i", id(sem), 16))
        q.ninst += 1
        self._mark((sem, self.dma_val[i]), reads, writes)
        return inst

    def barrier(self):
        evs = [q.last_ev for q in self.engs if q.last_ev is not None]
        evs += [(s, v) for s, v in zip(self.dma_sems, self.dma_val) if v > 0]
        for q in self.engs:
            for ev in evs:
                if id(ev[0]) in q.semids:
                    continue
                self._wait(q, ev)


def make_consts():
    c = {}
    i128 = np.arange(128)
    r, cc = i128[:, None], i128[None, :]
    c["ident"] = np.eye(128, dtype=np.float32)
    c["ones"] = np.ones((128, 128), np.float32)
    c["tri_f"] = (r <= cc).astype(np.float32)
    c["tri_b"] = (r >= cc).astype(np.float32)
    c["str_f"] = (r > cc).astype(np.float32)
    c["str_b"] = (r < cc).astype(np.float32)
    same = (r // 64) == (cc // 64)
    for nm, cond in (("nlt", cc < r), ("ngt", cc > r), ("nle", cc <= r), ("nge", cc >= r)):
        m = np.where(same & cond, 0.0, NEG).astype(np.float32)
        c[nm] = np.tile(m, (1, 4))
    c["ident4"] = np.tile(np.eye(128, dtype=np.float32), (1, 4))
    c["mask4_f"] = np.tile(c["tri_f"], (1, 4))
    c["mask4_b"] = np.tile(c["tri_b"], (1, 4))
    rm = np.ones((128, 256), np.float32)
    rm[:, 0::64] = 0.0
    c["reset"] = rm
    names = list(c.keys())
    offs = {}
    o = 0
    for n in names:
        offs[n] = (o, c[n].shape[1])
        o += c[n].shape[1]
    arr = np.concatenate([c[n] for n in names], axis=1)
    return arr, offs


CONST_ARR, CONST_OFFS = make_consts()

EVEN_BLOCKS = [(0, 512), (512, 512), (1024, 512), (1536, 512), (2048, 512), (2560, 512), (3072, 32),
               (3104, 512), (3616, 512), (4128, 512), (4640, 512)]
ODD_BLOCKS = [(i * 512, 512) for i in range(8)] + [(4096, 32), (4128, 512), (4640, 512), (5152, 512), (5664, 512),
                                                   (6176, 512), (6688, 512)]


def build(NCT=4, NLT=16, depth=2):
    nc = bass.Bass("TRN2", target_bir_lowering=False)
    S = Sched(nc)
    TC, TL = NCT * TT, NLT * TT
    TMAX = max(TC, TL)

    def din(name, shape, dt=F32):
        return nc.dram_tensor(name, list(shape), dt, kind="ExternalInput").ap()

    def dout(name, shape, dt=F32):
        return nc.dram_tensor(name, list(shape), dt, kind="ExternalOutput").ap()

    def dscr(name, shape, dt=F32):
        return nc.dram_tensor(name, list(shape), dt, kind="Internal").ap()

    I = {}
    for name, shape in [("xc", [TC, D]), ("xl", [TL, D]), ("cc", [2, D]),
                        ("st_gla", [2, 4, 128, 256]), ("st_lru", [2, 1024]), ("st_gdn", [2, 8, 128, 128]),
                        ("st_mc", [2, 4, 128, 256]), ("st_mn", [2, 4, 128]), ("st_mm", [2, 4]),
                        ("w_mod", [2, D, 6 * D]), ("b_mod", [2, 6 * D]), ("norm_g", [2, 4, D]),
                        ("w_up", [2, D, FF]), ("w_down", [2, FF, D]),
                        ("w_in_e", [D, 5152]), ("gla_w2", [2, 16, 512]), ("gla_b", [2, 512]), ("gla_g", [1024]),
                        ("lru_cw", [4, 1024]), ("lru_cb", [1024]), ("lru_gw", [2, 2, 8, 128, 128]),
                        ("lru_gb", [2, 2, 1024]), ("lru_lam", [2, 1024]), ("w_out_e", [D, D]),
                        ("w_in_o", [D, 7216]), ("gdn_cw", [4, 3072]), ("gdn_alog", [2, 8]), ("gdn_dtb", [2, 8]),
                        ("gdn_g", [1024]), ("ml_gb", [2, 2, 4]), ("ml_g", [1024]), ("w_out_o", [D, D]),
                        ("consts", list(CONST_ARR.shape))]:
        I[name] = din(name, shape)
    O = {}
    for name, shape in [("yc", [TC, D]), ("yl", [TL, D]), ("o_gla", [NCT, 2, 4, 128, 256]), ("o_lru", [NCT, 2, 1024]),
                        ("o_gdn", [NCT, 2, 8, 128, 128]), ("o_mc", [NCT, 2, 4, 128, 256]), ("o_mn", [NCT, 2, 4, 128]),
                        ("o_mm", [NCT, 2, 4])]:
        O[name] = dout(name, shape)

    def mm(o, l, r, start=True, stop=True, inc=None):
        S.op(S.pe, lambda e: e.matmul(o.ap, lhsT=l.ap, rhs=r.ap, start=start, stop=stop), reads=[l.b, r.b], writes=[o.b],
             inc=bool(stop) if inc is None else inc)

    def act(o, i, func, bias=None, scale=None, accum=None):
        kw = {}
        rd = [i.b]
        wr = [o.b]
        if bias is not None:
            if isinstance(bias, View):
                kw["bias"] = bias.ap
                rd.append(bias.b)
            else:
                kw["bias"] = bias
        if scale is not None:
            if isinstance(scale, View):
                kw["scale"] = scale.ap
                rd.append(scale.b)
            else:
                kw["scale"] = scale
        if accum is not None:
            kw["accum_out"] = accum.ap
            wr.append(accum.b)
        S.op(S.act, lambda e: e.activation(out=o.ap, in_=i.ap, func=func, **kw), reads=rd, writes=wr)

    def tt(q, o, a, b, op):
        S.op(q, lambda e: e.tensor_tensor(out=o.ap, in0=a.ap, in1=b.ap, op=op), reads=[a.b, b.b], writes=[o.b])

    def _sc(s, rd):
        if isinstance(s, View):
            rd.append(s.b)
            return s.ap
        return s

    def ts(q, o, a, s1, op0, s2=None, op1=None):
        rd = [a.b]
        a1 = _sc(s1, rd)
        a2 = _sc(s2, rd)
        if op1 is None:
            S.op(q, lambda e: e.tensor_scalar(out=o.ap, in0=a.ap, scalar1=a1, scalar2=None, op0=op0), reads=rd, writes=[o.b])
        else:
            S.op(q, lambda e: e.tensor_scalar(out=o.ap, in0=a.ap, scalar1=a1, scalar2=a2, op0=op0, op1=op1), reads=rd, writes=[o.b])

    def stt(o, a, s, b, op0, op1):
        rd = [a.b, b.b]
        a1 = _sc(s, rd)
        S.op(S.dve, lambda e: e.scalar_tensor_tensor(out=o.ap, in0=a.ap, scalar=a1, in1=b.ap, op0=op0, op1=op1), reads=rd, writes=[o.b])

    def cp(q, o, i):
        if q is S.act:
            act(o, i, AF.Copy)
        else:
            S.op(q, lambda e: e.tensor_copy(out=o.ap, in_=i.ap), reads=[i.b], writes=[o.b])

    def memset(q, o, val):
        S.op(q, lambda e: e.memset(o.ap, val), writes=[o.b])

    def ld(dst, src_ap, q=None, **kw):
        S.dma(q or S.sp, dst.ap, src_ap, writes=[dst.b], **kw)

    def stq(dst_ap, src, q=None, **kw):
        S.dma(q or S.pool, dst_ap, src.ap, reads=[src.b], **kw)

    top = ExitStack()

    uid = [0]

    def alloc(es, name, shape, dt=F32):
        uid[0] += 1
        name = f"{name}_{uid[0]}"
        return Buf(es.enter_context(nc.sbuf_tensor(name, list(shape), dt)), name)

    CT = alloc(top, "consts_sb", list(CONST_ARR.shape))
    ld(CT[:, :], I["consts"])

    def C(name, rows=128):
        o, w = CONST_OFFS[name]
        return CT[0:rows, o:o + w]

    ident = C("ident")
    banks = [Buf(top.enter_context(nc.psum_tensor(f"psb{i}", [128, 512], F32)), f"psb{i}") for i in range(8)]
    bank_rr = [0]

    def PS():
        b = banks[bank_rr[0]]
        bank_rr[0] = (bank_rr[0] + 1) % 8
        return b

    def tr(o, i, n):
        S.op(S.pe, lambda e: e.transpose(o.ap, i.ap, ident.ap[0:n, 0:n]), reads=[i.b, CT], writes=[o.b])

    colstage = alloc(top, "colstage", [128, 128])

    def load_cols(dst, flat_ap, R):
        ld(colstage[0:R, :], flat_ap.rearrange("(r p) -> r p", p=128))
        ps = PS()
        r0 = 0
        while r0 < R:
            n = min(64 if R > 64 else R, R - r0)
            S.op(S.pe, lambda e, r0=r0, n=n: e.transpose(ps.t[:, r0:r0 + n], colstage.t[r0:r0 + n, :], ident.ap[r0:r0 + n, r0:r0 + n]),
                 reads=[colstage, CT], writes=[ps])
            r0 += n
        cp(S.dve, dst, ps[:, 0:R])

    def cast_blocks(src2d, nkc, blocks, prefix):
        outs = []
        for bi, (c0, w) in enumerate(blocks):
            dst = dscr(f"{prefix}{bi}", [128, nkc, w], BF16)
            for k0 in range(0, nkc, 16):
                S.dma(S.pool, dst[:, k0:k0 + 16, :],
                      src2d[k0 * 128:(k0 + 16) * 128, c0:c0 + w].rearrange("(kc p) f -> p kc f", p=128))
            outs.append(dst)
        return outs

    B512 = [(i * 512, 512) for i in range(4)]
    Wb = {}
    Wb["in0"] = cast_blocks(I["w_in_e"], KC, EVEN_BLOCKS, "wine")
    Wb["out0"] = cast_blocks(I["w_out_e"], KC, B512, "woute")
    if depth > 1:
        Wb["in1"] = cast_blocks(I["w_in_o"], KC, ODD_BLOCKS, "wino")
        Wb["out1"] = cast_blocks(I["w_out_o"], KC, B512, "wouto")
    for l in range(depth):
        Wb[f"up{l}"] = cast_blocks(I["w_up"][l], KC, [(i * 512, 512) for i in range(16)], f"wup{l}_")
        Wb[f"dn{l}"] = cast_blocks(I["w_down"][l], 64, B512, f"wdn{l}_")

    MODV = dscr("modv", [depth, 2, 6, D])
    with ExitStack() as es:
        cct = alloc(es, "cct", [2, D])
        sT = alloc(es, "sT", [128, KC, 2])
        wm = [alloc(es, f"wm{i}", [128, KC, 512]) for i in range(2)]
        modt = alloc(es, "modt", [2, 6 * D])
        bmt = [alloc(es, f"bmt{i}", [2, 512]) for i in range(2)]
        ngt = alloc(es, "ngt", [2, D])
        mvt = [alloc(es, f"mvt{i}", [2, D]) for i in range(2)]
        ld(cct[:, :], I["cc"])
        act(cct[:, :], cct[:, :], AF.Silu)
        ps = PS()
        for kc in range(KC):
            tr(ps[:, kc * 2:kc * 2 + 2], cct[:, kc * 128:(kc + 1) * 128], 2)
        cp(S.dve, sT.v(sT.t[:, :, :].rearrange("p a b -> p (a b)")), ps[:, 0:2 * KC])
        for l in range(depth):
            for cb in range(24):
                w = wm[cb % 2]
                bm = bmt[cb % 2]
                ld(w[:, :, :], I["w_mod"][l][:, cb * 512:(cb + 1) * 512].rearrange("(kc p) f -> p kc f", p=128))
                ld(bm[:, :], I["b_mod"][l:l + 1, cb * 512:(cb + 1) * 512].partition_broadcast(2).rearrange("p a b -> p (a b)"))
                ps = PS()
                for kc in range(KC):
                    mm(ps[0:2, :], sT[:, kc, :], w[:, kc, :], start=(kc == 0), stop=(kc == KC - 1))
                tt(S.dve, modt[:, cb * 512:(cb + 1) * 512], ps[0:2, :], bm[:, :], ALU.add)

            def sl(i):
                return modt[:, i * D:(i + 1) * D]
            for i, (kind, mi, gi_) in enumerate((("a", 1, 0), ("c", 0, None), ("m", 2, 1), ("a", 4, 2), ("c", 3, None), ("m", 5, 3))):
                mvb = mvt[i % 2]
                if gi_ is not None:
                    ld(ngt[:, :], I["norm_g"][l, gi_:gi_ + 1, :].partition_broadcast(2).rearrange("p a b -> p (a b)"))
                if kind == "a":
                    stt(mvb[:, :], sl(mi), 1.0, ngt[:, :], ALU.add, ALU.mult)
                elif kind == "m":
                    tt(S.dve, mvb[:, :], sl(mi), ngt[:, :], ALU.mult)
                else:
                    cp(S.dve, mvb[:, :], sl(mi))
                stq(MODV[l, :, i, :], mvb[:, :])
    S.barrier()

    X1 = {0: dscr("x1c", [TC, D]), 1: dscr("x1l", [TL, D])}
    MIXT = dscr("mixt", [128, 16, TMAX], BF16)
    sc = {}

    def scr(name, shape, dt=F32):
        if name not in sc:
            if name in DBG_DUMP:
                sc[name] = dout("s_" + name, shape, dt)
            else:
                sc[name] = dscr("s_" + name, shape, dt)
        return sc[name]

    class Grp:
        pass

    def make_groups(l):
        last = (l == depth - 1)
        gs = []
        for w in DBG_GROUPS:
            g = Grp()
            g.w = w
            g.ntiles = NCT if w == 0 else NLT
            g.T = g.ntiles * TT
            src = (I["xc"], I["xl"])[w] if l == 0 else X1[w]
            dst = (O["yc"], O["yl"])[w] if last else X1[w]
            g.colmajor = (w == 1 and l % 2 == 1)
            g.L = 256 if w == 0 else 64
            g.NL = TT // g.L
            if g.colmajor:
                sv = src.rearrange("(r c) d -> c r d", c=64)
                dv = dst.rearrange("(r c) d -> c r d", c=64)
                g.xsrc = lambda ti, st, sv=sv: [(0, 64, sv[ti * 4 + st * 2]), (64, 128, sv[ti * 4 + st * 2 + 1])]
                g.ydst = lambda ti, st, dv=dv: [(0, 64, dv[ti * 4 + st * 2]), (64, 128, dv[ti * 4 + st * 2 + 1])]
            else:
                g.xsrc = lambda ti, st, src=src: [(0, 128, src[ti * TT + st * 128: ti * TT + (st + 1) * 128, :])]
                g.ydst = lambda ti, st, dst=dst: [(0, 128, dst[ti * TT + st * 128: ti * TT + (st + 1) * 128, :])]
            g.seqs = [(i, 1, i) for i in range(NCT)] if w == 0 else [(0, NLT, None)]
            gs.append(g)
        return gs

    def bc_load(dst, l, w, i):
        ld(dst[:, :], MODV[l, w, i:i + 1, :].partition_broadcast(128).rearrange("p a b -> p (a b)"))

    def wstream(aps, bufs):
        n = len(aps)

        def issue(k):
            b = bufs[k % len(bufs)]
            shp = aps[k].shape
            ld(b[:, 0:shp[1], 0:shp[2]], aps[k])
            return b
        cur = issue(0)
        for k in range(n):
            nxt = issue(k + 1) if k + 1 < n else None
            yield cur
            cur = nxt

    def recip(o, i):
        S.op(S.dve, lambda e: e.reciprocal(out=o.ap, in_=i.ap), reads=[i.b], writes=[o.b])

    def rstd_from_ss(rstd, ss, n):
        act(rstd, ss, AF.Sqrt, scale=1.0 / n, bias=EPS)
        recip(rstd, rstd)

    def run_il(gens):
        active = list(gens)
        while active:
            for g_ in list(active):
                try:
                    next(g_)
                except StopIteration:
                    active.remove(g_)

    evac_rr = [0]

    def evq():
        evac_rr[0] ^= 1
        return S.act if evac_rr[0] else S.dve

    def norm_mod_T(src, dst, bA, bB, junk, ss, rstd, uT, st, col):
        act(junk[:, :], src[:, :], AF.Square, accum=ss[:, col:col + 1])
        rstd_from_ss(rstd[:, col:col + 1], ss[:, col:col + 1], D)
        stt(dst[:, :], src[:, :], rstd[:, col:col + 1], bA[:, :], ALU.mult, ALU.mult)
        tt(S.pool, dst[:, :], dst[:, :], bB[:, :], ALU.add)
        transpose_to(uT, dst, st)

    def transpose_to(uT, src, st):
        for q4 in range(4):
            ps = PS()
            for j in range(4):
                kc = q4 * 4 + j
                tr(ps[:, j * 128:(j + 1) * 128], src[:, kc * 128:(kc + 1) * 128], 128)
            cp(evq(), uT[:, q4 * 4:(q4 + 1) * 4, st * 128:(st + 1) * 128],
               ps.v(ps.t[:, :].rearrange("p (a b) -> p a b", a=4)))

    def load_x(g, ti, st, xt):
        for (p0, p1, ap) in g.xsrc(ti, st):
            ld(xt[p0:p1, :], ap)

    def phase3(g, l, front_alloc, front):
        with ExitStack() as es:
            bX, bY = alloc(es, "bcX", [128, D]), alloc(es, "bcY", [128, D])
            bG1, bA2, bB2, bG3 = bX, bY, bX, bY
            xt = [alloc(es, f"xt{i}", [128, D]) for i in range(2)]
            yb = [alloc(es, f"yb{i}", [128, D]) for i in range(2)]
            junk = alloc(es, "junk", [128, D], BF16)
            ss = alloc(es, "ss", [128, 8])
            rstd = alloc(es, "rstd", [128, 8])
            mixT = alloc(es, "mixT", [128, KC, TT], BF16)
            u2T = alloc(es, "u2T", [128, KC, TT], BF16)
            hidT = alloc(es, "hidT", [128, 64, TT], BF16)
            wb = [alloc(es, f"wb{i}", [128, KC, 512], BF16) for i in range(2)]
            rtmp = [alloc(es, f"rtmp{i}", [128, 512]) for i in range(2)]
            fctx = front_alloc(es)
            aps = []
            for ti in range(g.ntiles):
                aps += list(Wb[f"out{l}"]) + list(Wb[f"up{l}"])
                for fb in range(4):
                    aps += [Wb[f"dn{l}"][fb][:, k0:k0 + 16, :] for k0 in range(0, 64, 16)]
            ws = wstream(aps, wb)
            for ti in range(g.ntiles):
                front(fctx, ti, mixT)
                bc_load(bG1, l, g.w, 2)
                bc_load(bA2, l, g.w, 3)
                for st in range(2):
                    load_x(g, ti, st, xt[st])
                for fb in range(4):
                    w = next(ws)
                    for st in range(2):
                        ps = PS()
                        for kc in range(KC):
                            mm(ps[:, :], mixT[:, kc, st * 128:(st + 1) * 128], w[:, kc, :], start=(kc == 0), stop=(kc == KC - 1))
                        cp(evq(), yb[st][:, fb * 512:(fb + 1) * 512], ps[:, :])
                for st in range(2):
                    act(junk[:, :], yb[st][:, :], AF.Square, accum=ss[:, st:st + 1])
                    rstd_from_ss(rstd[:, st:st + 1], ss[:, st:st + 1], D)
                    stt(yb[st][:, :], yb[st][:, :], rstd[:, st:st + 1], bG1[:, :], ALU.mult, ALU.mult)
                    tt(S.pool, xt[st][:, :], xt[st][:, :], yb[st][:, :], ALU.add)
                bc_load(bB2, l, g.w, 4)
                for st in range(2):
                    norm_mod_T(xt[st], yb[st], bA2, bB2, junk, ss, rstd, u2T, st, 2 + st)
                bc_load(bG3, l, g.w, 5)
                for ub in range(16):
                    w = next(ws)
                    for j in range(0, 4, 2):
                        ps = PS()
                        for jj in range(2):
                            for kc in range(KC):
                                mm(ps[:, jj * 256:(jj + 1) * 256], w[:, kc, (j + jj) * 128:(j + jj + 1) * 128], u2T[:, kc, :],
                                   start=(kc == 0), stop=(kc == KC - 1))
                        rt_ = rtmp[(j // 2) % 2]
                        act(rt_[:, :], ps[:, :], AF.Relu)
                        c0 = ub * 4 + j
                        tt(S.pool, hidT.v(hidT.t[:, c0:c0 + 2, :].rearrange("p a b -> p (a b)")), rt_[:, :], rt_[:, :], ALU.mult)
                for fb in range(4):
                    psd = [PS(), PS()]
                    for g4 in range(4):
                        w = next(ws)
                        for st in range(2):
                            for k in range(16):
                                fc = g4 * 16 + k
                                mm(psd[st][:, :], hidT[:, fc, st * 128:(st + 1) * 128], w[:, k, :], start=(fc == 0), stop=(fc == 63),
                                   inc=(k == 15))
                    for st in range(2):
                        cp(evq(), yb[st][:, fb * 512:(fb + 1) * 512], psd[st][:, :])
                for st in range(2):
                    act(junk[:, :], yb[st][:, :], AF.Square, accum=ss[:, 4 + st:5 + st])
                    rstd_from_ss(rstd[:, 4 + st:5 + st], ss[:, 4 + st:5 + st], D)
                    stt(yb[st][:, :], yb[st][:, :], rstd[:, 4 + st:5 + st], bG3[:, :], ALU.mult, ALU.mult)
                    tt(S.pool, yb[st][:, :], yb[st][:, :], xt[st][:, :], ALU.add)
                    for (p0, p1, ap) in g.ydst(ti, st):
                        stq(ap, yb[st][p0:p1, :])
        S.barrier()

    def even_layer(l, j):
        QT = scr("QT", [128, 4, TMAX])
        KT_ = scr("KT", [128, 4, TMAX])
        Kt = scr("Kt", [TMAX, 512])
        Vt = scr("Vt", [TMAX, 1024])
        Gd = [scr(f"G{d}", [TMAX, 512]) for d in range(2)]
        RT = scr("RT", [128, 8, TMAX])
        Ad = [scr(f"A{d}", [128, 8, TMAX]) for d in range(2)]
        Bd = [scr(f"B{d}", [128, 8, TMAX]) for d in range(2)]
        LGT = scr("LGT", [128, 8, TMAX])
        Od = [scr(f"O{d}", [128, 8, TMAX]) for d in range(2)]
        with ExitStack() as les:
            gcol = alloc(les, "gcol", [128, 8])
            load_cols(gcol[:, :], I["gla_g"], 8)
            cw = alloc(les, "cw", [128, 32])
            load_cols(cw[:, :], I["lru_cw"].rearrange("a b -> (a b)"), 32)
            cb = alloc(les, "cb", [128, 8])
            load_cols(cb[:, :], I["lru_cb"], 8)
            gb = alloc(les, "gb", [128, 32])
            load_cols(gb[:, :], I["lru_gb"].rearrange("a b c -> (a b c)"), 32)
            m8sp = alloc(les, "m8sp", [128, 16])
            load_cols(m8sp[:, :], I["lru_lam"].rearrange("a b -> (a b)"), 16)
            act(m8sp[:, :], m8sp[:, :], AF.Exp, scale=-1.0)
            act(m8sp[:, :], m8sp[:, :], AF.Ln, bias=1.0)
            ts(S.dve, m8sp[:, :], m8sp[:, :], -8.0, ALU.mult)
            ones = C("ones")

            for g in make_groups(l):
                L, NL = g.L, g.NL
                with ExitStack() as es:
                    bA, bB = alloc(es, "bA", [128, D]), alloc(es, "bB", [128, D])
                    bc_load(bA, l, g.w, 0)
                    bc_load(bB, l, g.w, 1)
                    w2 = alloc(es, "w2", [16, 2, 512])
                    ld(w2[:, :, :], I["gla_w2"].rearrange("d r k -> r d k"))
                    gbrow = alloc(es, "gbrow", [1, 2, 512])
                    ld(gbrow[:, :, :], I["gla_b"].rearrange("(o d) k -> o d k", o=1))
                    LW = alloc(es, "LW", [128, 32, 128])
                    ld(LW[:, :, :], I["lru_gw"].rearrange("d g n i j -> i (d g n) j"))
                    xt = [alloc(es, f"xt{i}", [128, D]) for i in range(2)]
                    uTs = [alloc(es, f"uT{i}", [128, KC, TT], BF16) for i in range(2)]
                    junk = alloc(es, "junk", [128, D], BF16)
                    ss = alloc(es, "ss", [128, 4])
                    rstd = alloc(es, "rstd", [128, 4])
                    wb = [alloc(es, f"wb{i}", [128, KC, 512], BF16) for i in range(2)]
                    qTs = alloc(es, "qTs", [128, 4, TT])
                    kTs = qTs
                    kts = alloc(es, "kts", [128, 2, 512])
                    vs = alloc(es, "vs", [128, 2, 1024])
                    rTs = alloc(es, "rTs", [128, 8, TT])
                    lrT = alloc(es, "lrT", [16, 2, TT])
                    e1 = alloc(es, "e1", [128, 512])
                    Gs = alloc(es, "Gs", [128, 2, 2, 512])
                    xp = alloc(es, "xp", [128, 8, NL, L + 3])
                    xcs = alloc(es, "xcs", [128, 8, TT])
                    gr = alloc(es, "gr", [128, TT])
                    gi = alloc(es, "gi", [128, TT])
                    as_ = alloc(es, "as_", [128, 8, TT])
                    bs_ = alloc(es, "bs_", [128, 8, TT])
                    lgs = rTs
                    memset(S.pool, xp[:, :, :, :], 0.0)
                    aps = []
                    for ti in range(g.ntiles):
                        aps += list(Wb[f"in{l}"])
                    ws = wstream(aps, wb)

                    def prep(ti):
                        for st in range(2):
                            load_x(g, ti, st, xt[st])
                        for st in range(2):
                            norm_mod_T(xt[st], xt[st], bA, bB, junk, ss, rstd, uTs[ti % 2], st, st)

                    def fm(ps, col, w, wc0, n, uT):
                        for kc in range(KC):
                            mm(ps[0:n, col:col + TT], w[:, kc, wc0:wc0 + n], uT[:, kc, :], start=(kc == 0), stop=(kc == KC - 1))

                    def tmj(ps, w, n, uT, st):
                        for kc in range(KC):
                            mm(ps[:, 0:n], uT[:, kc, st * 128:(st + 1) * 128], w[:, kc, 0:n], start=(kc == 0), stop=(kc == KC - 1))

                    prep(0)
                    for ti in range(g.ntiles):
                        t0 = ti * TT
                        uT = uTs[ti % 2]
                        for bi, (dst_s, dram, scl) in enumerate(((qTs, QT, 128.0 ** -0.5), (kTs, KT_, 1.0))):
                            w = next(ws)
                            for hp in range(2):
                                ps = PS()
                                for hh in range(2):
                                    fm(ps, hh * TT, w, (hp * 2 + hh) * 128, 128, uT)
                                act(dst_s.v(dst_s.t[:, hp * 2:hp * 2 + 2, :].rearrange("p a b -> p (a b)")), ps[:, :], AF.Copy, scale=scl)
                            stq(dram[:, :, t0:t0 + TT], dst_s[:, :, :])
                            if bi == 1:
                                for st in range(2):
                                    ps = PS()
                                    tmj(ps, w, 512, uT, st)
                                    cp(S.dve, kts[:, st, :], ps[:, :])
                                stq(Kt[t0:t0 + TT, :].rearrange("(s p) f -> p s f", p=128), kts[:, :, :])
                        for vb in range(2):
                            w = next(ws)
                            for st in range(2):
                                ps = PS()
                                tmj(ps, w, 512, uT, st)
                                cp(evq(), vs[:, st, vb * 512:(vb + 1) * 512], ps[:, :])
                        stq(Vt[t0:t0 + TT, :].rearrange("(s p) f -> p s f", p=128), vs[:, :, :])
                        for rb in range(2):
                            w = next(ws)
                            for cp_ in range(2):
                                ps = PS()
                                for hh in range(2):
                                    fm(ps, hh * TT, w, (cp_ * 2 + hh) * 128, 128, uT)
                                c0 = rb * 4 + cp_ * 2
                                act(rTs.v(rTs.t[:, c0:c0 + 2, :].rearrange("p a b -> p (a b)")), ps[:, :], AF.Silu)
                        stq(RT[:, :, t0:t0 + TT], rTs[:, :, :])
                        if ti + 1 < g.ntiles:
                            prep(ti + 1)
                        w = next(ws)
                        for d in range(2):
                            ps = PS()
                            fm(ps, 0, w, d * 16, 16, uT)
                            cp(S.dve, lrT[:, d, :], ps[0:16, 0:TT])
                        for d in range(2):
                            for st in range(2):
                                ps = PS()
                                mm(ps[:, :], lrT[0:16, d, st * 128:(st + 1) * 128], w2[0:16, d, :], start=True, stop=False)
                                mm(ps[:, :], ones.b.v(ones.ap[0:1, 0:128]), gbrow[0:1, d, :], start=False, stop=True)
                                act(e1[:, :], ps[:, :], AF.Exp, scale=-1.0)
                                act(e1[:, :], e1[:, :], AF.Ln, bias=1.0)
                                ts(S.pool, Gs[:, d, st, :], e1[:, :], -1.0 / 16.0, ALU.mult)
                            stq(Gd[d][t0:t0 + TT, :].rearrange("(s p) f -> p s f", p=128), Gs[:, d, :, :])
                        for xb in range(2):
                            w = next(ws)
                            for cp_ in range(2):
                                ps = PS()
                                for hh in range(2):
                                    fm(ps, hh * TT, w, (cp_ * 2 + hh) * 128, 128, uT)
                                for hh in range(2):
                                    n = xb * 4 + cp_ * 2 + hh
                                    act(xp[:, n, :, 2:2 + L], ps.v(ps.t[:, hh * TT:(hh + 1) * TT].rearrange("p (a b) -> p a b", a=NL)), AF.Copy)
                        for n in range(8):
                            xc = xcs.v(xcs.t[:, n, :].rearrange("p (a b) -> p a b", a=NL))
                            act(xc, xp[:, n, :, 2:2 + L], AF.Identity, scale=cw[:, 2 * 8 + n:2 * 8 + n + 1], bias=cb[:, n:n + 1])
                            for tap in (0, 1, 3):
                                stt(xc, xp[:, n, :, tap:tap + L], cw[:, tap * 8 + n:tap * 8 + n + 1], xc, ALU.mult, ALU.add)
                        for d in range(2):
                            for n in range(8):
                                ps = PS()
                                mm(ps[:, 0:TT], LW[:, (d * 2 + 0) * 8 + n, :], xcs[:, n, :])
                                mm(ps[:, TT:2 * TT], LW[:, (d * 2 + 1) * 8 + n, :], xcs[:, n, :])
                                i0 = (d * 2 + 0) * 8 + n
                                i1 = (d * 2 + 1) * 8 + n
                                act(gr[:, :], ps[:, 0:TT], AF.Sigmoid, bias=gb[:, i0:i0 + 1])
                                act(gi[:, :], ps[:, TT:2 * TT], AF.Sigmoid, bias=gb[:, i1:i1 + 1])
                                act(as_[:, n, :], gr[:, :], AF.Exp, scale=m8sp[:, d * 8 + n:d * 8 + n + 1])
                                tt(S.pool, gr[:, :], as_[:, n, :], as_[:, n, :], ALU.mult)
                                act(gr[:, :], gr[:, :], AF.Sqrt, scale=-1.0, bias=1.0)
                                tt(S.pool, gi[:, :], gi[:, :], gr[:, :], ALU.mult)
                                tt(S.dve, bs_[:, n, :], gi[:, :], xcs[:, n, :], ALU.mult)
                            stq(Ad[d][:, :, t0:t0 + TT], as_[:, :, :])
                            stq(Bd[d][:, :, t0:t0 + TT], bs_[:, :, :])
                        for gb_ in range(2):
                            w = next(ws)
                            for cp_ in range(2):
                                ps = PS()
                                for hh in range(2):
                                    fm(ps, hh * TT, w, (cp_ * 2 + hh) * 128, 128, uT)
                                c0 = gb_ * 4 + cp_ * 2
                                act(lgs.v(lgs.t[:, c0:c0 + 2, :].rearrange("p a b -> p (a b)")), ps[:, :], AF.Gelu_apprx_tanh)
                        stq(LGT[:, :, t0:t0 + TT], lgs[:, :, :])
                S.barrier()

                with ExitStack() as es:
                    def mkb():
                        Sst = alloc(es, "Sst", [128, 4, 256])
                        qTb = [alloc(es, f"qTb{i}", [128, 4, 128]) for i in range(2)]
                        kTb = [alloc(es, f"kTb{i}", [128, 4, 128]) for i in range(2)]
                        ktb = [alloc(es, f"ktb{i}", [128, 512]) for i in range(2)]
                        vb_ = [alloc(es, f"vb{i}", [128, 1024]) for i in range(2)]
                        ggb = [alloc(es, f"ggb{i}", [128, 512]) for i in range(2)]
                        E = alloc(es, "E", [128, 512])
                        Einv = alloc(es, "Einv", [128, 512])
                        qp = alloc(es, "qp", [128, 4, 128])
                        kp = alloc(es, "kp", [128, 4, 128])
                        e2 = alloc(es, "e2", [128, 512])
                        kpp = alloc(es, "kpp", [128, 512])
                        At = alloc(es, "At", [128, 512])
                        oTs = alloc(es, "oTs", [128, 8, 128])
                        return (Sst, qTb, kTb, ktb, vb_, ggb, E, Einv, qp, kp, e2, kpp, At, oTs)
                    BB = [mkb(), mkb()]
                    def chain(d):
                        (Sst, qTb, kTb, ktb, vb_, ggb, E, Einv, qp, kp, e2, kpp, At, oTs) = BB[d]
                        for (tst, ntl, sidx) in g.seqs:
                            nch = ntl * 2
                            if True:
                                if sidx is None:
                                    ld(Sst[:, :, :], I["st_gla"][d].rearrange("h d v -> d h v"))
                                else:
                                    memset(S.pool, Sst[:, :, :], 0.0)
                                order = list(range(nch)) if d == 0 else list(range(nch - 1, -1, -1))
                                TRI = C("tri_f") if d == 0 else C("tri_b")
                                STR = C("str_f") if d == 0 else C("str_b")
                                MASK4 = C("mask4_f") if d == 0 else C("mask4_b")
                                last = 127 if d == 0 else 0

                                def loads(k):
                                    c = order[k]
                                    t0 = tst * TT + c * 128
                                    i = k % 2
                                    ld(qTb[i][:, :, :], QT[:, :, t0:t0 + 128])
                                    ld(kTb[i][:, :, :], KT_[:, :, t0:t0 + 128])
                                    ld(ktb[i][:, :], Kt[t0:t0 + 128, :])
                                    ld(vb_[i][:, :], Vt[t0:t0 + 128, :])
                                    ld(ggb[i][:, :], Gd[d][t0:t0 + 128, :])
                                loads(0)
                                for k in range(nch):
                                    if k + 1 < nch:
                                        loads(k + 1)
                                    c = order[k]
                                    t0 = tst * TT + c * 128
                                    i = k % 2
                                    qT, kT, kt, v, gg = qTb[i], kTb[i], ktb[i], vb_[i], ggb[i]
                                    ps1 = PS()
                                    mm(ps1[:, :], STR, gg[:, :])
                                    act(e2[:, :], ps1[:, :], AF.Exp)
                                    tt(S.dve, kpp[:, :], kt[:, :], e2[:, :], ALU.mult)
                                    yield
                                    ps2 = PS()
                                    for h in range(4):
                                        mm(ps2[:, h * 128:(h + 1) * 128], gg[:, h * 128:(h + 1) * 128], TRI)
                                    act(E[:, :], ps2[:, :], AF.Exp)
                                    act(Einv[:, :], ps2[:, :], AF.Exp, scale=-1.0)
                                    tt(S.dve, qp.v(qp.t[:, :, :].rearrange("p a b -> p (a b)")), qT.v(qT.t[:, :, :].rearrange("p a b -> p (a b)")), E[:, :], ALU.mult)
                                    tt(S.pool, kp.v(kp.t[:, :, :].rearrange("p a b -> p (a b)")), kT.v(kT.t[:, :, :].rearrange("p a b -> p (a b)")), Einv[:, :], ALU.mult)
                                    yield
                                    ps3 = PS()
                                    for h in range(4):
                                        mm(ps3[:, h * 128:(h + 1) * 128], kp[:, h, :], qp[:, h, :])
                                    tt(S.dve, At[:, :], ps3[:, :], MASK4, ALU.mult)
                                    yield
                                    for half in range(2):
                                        ps4 = PS()
                                        for jq in range(4):
                                            idx = half * 4 + jq
                                            h, vc = idx // 2, idx % 2
                                            mm(ps4[:, jq * 128:(jq + 1) * 128], Sst[:, h, vc * 128:(vc + 1) * 128], qp[:, h, :], start=True, stop=False)
                                            mm(ps4[:, jq * 128:(jq + 1) * 128], v[:, h * 256 + vc * 128:h * 256 + (vc + 1) * 128], At[:, h * 128:(h + 1) * 128], start=False, stop=True)
                                        cp(S.act, oTs.v(oTs.t[:, half * 4:(half + 1) * 4, :].rearrange("p a b -> p (a b)")), ps4[:, :])
                                    stq(Od[d][:, :, t0:t0 + 128], oTs[:, :, :])
                                    yield
                                    for half in range(2):
                                        ps5 = PS()
                                        for jq in range(2):
                                            h = half * 2 + jq
                                            mm(ps5[:, jq * 256:(jq + 1) * 256], kpp[:, h * 128:(h + 1) * 128], v[:, h * 256:(h + 1) * 256])
                                        for jq in range(2):
                                            h = half * 2 + jq
                                            stt(Sst[:, h, :], Sst[:, h, :], E[:, h * 128 + last:h * 128 + last + 1], ps5[:, jq * 256:(jq + 1) * 256], ALU.mult, ALU.add)
                                            yield
                                if sidx is not None:
                                    stq(O["o_gla"][sidx, d].rearrange("h d v -> d h v"), Sst[:, :, :])
                    run_il([chain(0), chain(1)])
                S.barrier()

                with ExitStack() as es:
                    TS = max(n_ for (_, n_, _) in g.seqs) * TT
                    a_ = alloc(es, "lru_a", [128, TS])
                    b_ = alloc(es, "lru_b", [128, TS])
                    hf = alloc(es, "lru_hf", [128, TS])
                    hb = alloc(es, "lru_hb", [128, TS])
                    lgt = alloc(es, "lru_lg", [128, TS])
                    mixo = alloc(es, "lru_mix", [128, TS], BF16)
                    h0c = alloc(es, "lru_h0", [128, 2])
                    hl = alloc(es, "lru_hl", [128, 2])
                    for (tst, ntl, sidx) in g.seqs:
                        T_ = ntl * TT
                        tsl = slice(tst * TT, tst * TT + T_)
                        for n in range(8):
                            ld(a_[:, 0:T_], Ad[0][:, n, tsl])
                            ld(b_[:, 0:T_], Bd[0][:, n, tsl])
                            if sidx is None:
                                ld(h0c[:, :], I["st_lru"][:, n * 128:(n + 1) * 128].rearrange("d p -> p d"), allow_slow_non_contiguous=True)
                                i0, i1 = h0c[:, 0:1], h0c[:, 1:2]
                            else:
                                i0, i1 = 0.0, 0.0

                            def scan(o, x0, x1, ini, rev):
                                sl_ = slice(None, None, -1) if rev else slice(None)
                                rd = [x0.b, x1.b]
                                iv = ini
                                if isinstance(ini, View):
                                    rd.append(ini.b)
                                    iv = ini.ap
                                S.op(S.dve, lambda e: e.tensor_tensor_scan(out=o.b.t[:, 0:T_][:, sl_], data0=x0.b.t[:, 0:T_][:, sl_], data1=x1.b.t[:, 0:T_][:, sl_],
                                                                          initial=iv, op0=ALU.mult, op1=ALU.add), reads=rd, writes=[o.b])
                            scan(hf[:, :], a_[:, :], b_[:, :], i0, False)
                            ld(a_[:, 0:T_], Ad[1][:, n, tsl])
                            ld(b_[:, 0:T_], Bd[1][:, n, tsl])
                            scan(hb[:, :], a_[:, :], b_[:, :], i1, True)
                            ld(lgt[:, 0:T_], LGT[:, n, tsl])
                            if sidx is not None:
                                cp(S.pool, hl[:, 0:1], hf[:, T_ - 1:T_])
                                cp(S.pool, hl[:, 1:2], hb[:, 0:1])
                                stq(O["o_lru"][sidx, :, n * 128:(n + 1) * 128].rearrange("d p -> p d"), hl[:, :], allow_slow_non_contiguous=True)
                            tt(S.pool, hf[:, 0:T_], hf[:, 0:T_], hb[:, 0:T_], ALU.add)
                            tt(S.dve, mixo[:, 0:T_], hf[:, 0:T_], lgt[:, 0:T_], ALU.mult)
                            stq(MIXT[:, 8 + n, tsl], mixo[:, 0:T_])
                S.barrier()

                def front_alloc(es):
                    f = Grp()
                    f.of = alloc(es, "f_of", [128, 8, TT])
                    f.ob = alloc(es, "f_ob", [128, 8, TT])
                    f.rs = alloc(es, "f_rs", [128, 4, TT])
                    return f

                def front(f, ti, mixT):
                    t0 = ti * TT
                    ld(f.of[:, :, :], Od[0][:, :, t0:t0 + TT])
                    ld(f.ob[:, :, :], Od[1][:, :, t0:t0 + TT])
                    ld(mixT[:, 8:16, :], MIXT[:, 8:16, t0:t0 + TT])
                    tt(S.pool, f.of[:, :, :], f.of[:, :, :], f.ob[:, :, :], ALU.add)
                    act(f.ob[:, :, :], f.of[:, :, :], AF.Square)
                    for half in range(2):
                        ps = PS()
                        for jq in range(2):
                            h = half * 2 + jq
                            for vc in range(2):
                                mm(ps[:, jq * TT:(jq + 1) * TT], ones, f.ob[:, h * 2 + vc, :], start=(vc == 0), stop=(vc == 1))
                        act(f.rs.v(f.rs.t[:, half * 2:half * 2 + 2, :].rearrange("p a b -> p (a b)")), ps[:, :], AF.Sqrt, scale=1.0 / 256.0, bias=EPS)
                    recip(f.rs[:, :, :], f.rs[:, :, :])
                    f.rt = f.ob
                    ld(f.rt[:, :, :], RT[:, :, t0:t0 + TT])
                    for c in range(8):
                        tt(S.pool, f.of[:, c, :], f.of[:, c, :], f.rs[:, c // 2, :], ALU.mult)
                        stt(mixT[:, c, :], f.of[:, c, :], gcol[:, c:c + 1], f.rt[:, c, :], ALU.mult, ALU.mult)

                phase3(g, l, front_alloc, front)

    def odd_layer(l, j):
        GQT = scr("GQT", [128, 8, TMAX])
        GKT = scr("GKT", [128, 8, TMAX])
        GKt = scr("GKt", [TMAX, 1024])
        GVt = scr("GVt", [TMAX, 1024])
        GZt = scr("GZt", [TMAX, 1024])
        GR = [scr(f"GR{d}", [5, 8, TMAX]) for d in range(2)]
        GC = [scr(f"GC{d}", [2, 8, TMAX]) for d in range(2)]
        DEC = [scr(f"DEC{d}", [TMAX // 64, 8]) for d in range(2)]
        OG = [scr(f"OG{d}", [TMAX, 1024]) for d in range(2)]
        MQT = scr("MQT", [128, 4, TMAX])
        MKT = scr("MKT", [128, 4, TMAX])
        MKt = scr("MKt", [TMAX, 512])
        MVt = scr("MVt", [TMAX, 1024])
        MOt = scr("MOt", [TMAX, 1024])
        LI = [scr(f"LI{d}", [4, TMAX]) for d in range(2)]
        LF = [scr(f"LF{d}", [4, TMAX]) for d in range(2)]
        MR = [scr(f"MR{d}", [6, 4, TMAX]) for d in range(2)]
        MDEC = [scr(f"MDEC{d}", [TMAX // 64, 4]) for d in range(2)]
        OM = [scr(f"OM{d}", [TMAX, 1024]) for d in range(2)]
        ones = C("ones")
        if DBG_STAGE <= -1:
            return
        with ExitStack() as les:
            cwg = alloc(les, "cwg", [128, 96])
            load_cols(cwg[:, :], I["gdn_cw"].rearrange("a b -> (a b)"), 96)
            dtb = alloc(les, "dtb", [8, 2])
            ld(dtb[:, :], I["gdn_dtb"].rearrange("d h -> h d"), allow_slow_non_contiguous=True)
            negA = alloc(les, "negA", [8, 2])
            ld(negA[:, :], I["gdn_alog"].rearrange("d h -> h d"), allow_slow_non_contiguous=True)
            act(negA[:, :], negA[:, :], AF.Exp)
            ts(S.dve, negA[:, :], negA[:, :], -1.0, ALU.mult)
            mlb = alloc(les, "mlb", [4, 4])
            ld(mlb[:, :], I["ml_gb"].rearrange("d g h -> h (d g)"), allow_slow_non_contiguous=True)
            mlbn = alloc(les, "mlbn", [4, 4])
            ts(S.dve, mlbn[:, :], mlb[:, :], -1.0, ALU.mult)
            wmif = alloc(les, "wmif", [128, KC, 16])
            ld(wmif[:, :, :], I["w_in_o"][:, 7200:7216].rearrange("(kc p) f -> p kc f", p=128))
            wmib = alloc(les, "wmib", [128, KC, 16], BF16)
            cp(S.dve, wmib[:, :, :], wmif[:, :, :])

            for g in make_groups(l):
                L, NL = g.L, g.NL
                if DBG_STAGE <= 0:
                    continue
                with ExitStack() as es:
                    bA, bB = alloc(es, "bA", [128, D]), alloc(es, "bB", [128, D])
                    bc_load(bA, l, g.w, 0)
                    bc_load(bB, l, g.w, 1)
                    xt = [alloc(es, f"xt{i}", [128, D]) for i in range(2)]
                    uTs = [alloc(es, f"uT{i}", [128, KC, TT], BF16) for i in range(2)]
                    junk = alloc(es, "junk", [128, D], BF16)
                    ss = alloc(es, "ss", [128, 4])
                    rstd = alloc(es, "rstd", [128, 4])
                    wb = [alloc(es, f"wb{i}", [128, KC, 512], BF16) for i in range(2)]
                    xp = alloc(es, "xp", [128, 2, NL, L + 3])
                    xc2 = alloc(es, "xc2", [128, 2, TT])
                    sqt = alloc(es, "sqt", [128, TT])
                    rs1 = alloc(es, "rs1", [128, TT])
                    fst = [alloc(es, f"fst{i}", [128, 8, TT]) for i in range(2)]
                    tst = [alloc(es, f"tst{i}", [128, 2, 1024]) for i in range(2)]
                    e8 = alloc(es, "e8", [8, TT])
                    glog = alloc(es, "glog", [8, TT])
                    lb = alloc(es, "lb", [8, TT])
                    rw = alloc(es, "rw", [8, 5, TT])
                    cs = alloc(es, "cs", [8, 2, TT])
                    dc = alloc(es, "dc", [8, 4])
                    g4 = alloc(es, "g4", [4, 2, TT])
                    e4 = alloc(es, "e4", [4, TT])
                    memset(S.pool, xp[:, :, :, :], 0.0)
                    memset(S.pool, rw[:, :, :], 1.0)
                    aps = []
                    for ti in range(g.ntiles):
                        aps += list(Wb[f"in{l}"])
                    ws = wstream(aps, wb)
                    fst_rr = [0]
                    tst_rr = [0]

                    def nfst():
                        fst_rr[0] ^= 1
                        return fst[fst_rr[0]]

                    def ntst():
                        tst_rr[0] ^= 1
                        return tst[tst_rr[0]]

                    def prep(ti):
                        for st in range(2):
                            load_x(g, ti, st, xt[st])
                        for st in range(2):
                            norm_mod_T(xt[st], xt[st], bA, bB, junk, ss, rstd, uTs[ti % 2], st, st)

                    def fm(ps, col, w, wc0, n, uT):
                        for kc in range(KC):
                            mm(ps[0:n, col:col + TT], w[:, kc, wc0:wc0 + n], uT[:, kc, :], start=(kc == 0), stop=(kc == KC - 1))

                    def tmj(ps, w, n, uT, st):
                        for kc in range(KC):
                            mm(ps[:, 0:n], uT[:, kc, st * 128:(st + 1) * 128], w[:, kc, 0:n], start=(kc == 0), stop=(kc == KC - 1))

                    def tm_block_pair(dram, func, scale=None):
                        t_ = ntst()
                        for zb in range(2):
                            w = next(ws)
                            for st in range(2):
                                ps = PS()
                                tmj(ps, w, 512, uT, st)
                                act(t_[:, st, zb * 512:(zb + 1) * 512], ps[:, :], func, scale=scale)
                        stq(dram[t0:t0 + TT, :].rearrange("(s p) f -> p s f", p=128), t_[:, :, :])

                    def to_tm(src, dram):
                        t_ = ntst()
                        for st in range(2):
                            for hq in range(2):
                                ps = PS()
                                for hh in range(4):
                                    h = hq * 4 + hh
                                    tr(ps[:, hh * 128:(hh + 1) * 128], src[:, h, st * 128:(st + 1) * 128], 128)
                                cp(evq(), t_[:, st, hq * 512:(hq + 1) * 512], ps[:, :])
                        stq(dram[t0:t0 + TT, :].rearrange("(s p) f -> p s f", p=128), t_[:, :, :])

                    prep(0)
                    for ti in range(g.ntiles):
                        t0 = ti * TT
                        uT = uTs[ti % 2]
                        for which in range(3):
                            f_ = nfst()
                            for half in range(2):
                                w = next(ws)
                                for cp_ in range(2):
                                    ps = PS()
                                    for hh in range(2):
                                        fm(ps, hh * TT, w, (cp_ * 2 + hh) * 128, 128, uT)
                                    for hh in range(2):
                                        h = half * 4 + cp_ * 2 + hh
                                        n = which * 8 + h
                                        act(xp[:, hh, :, 2:2 + L], ps.v(ps.t[:, hh * TT:(hh + 1) * TT].rearrange("p (a b) -> p a b", a=NL)), AF.Copy)
                                        xc = xc2.v(xc2.t[:, hh, :].rearrange("p (a b) -> p a b", a=NL))
                                        act(xc, xp[:, hh, :, 2:2 + L], AF.Identity, scale=cwg[:, 2 * 24 + n:2 * 24 + n + 1])
                                        for tap in (0, 1, 3):
                                            stt(xc, xp[:, hh, :, tap:tap + L], cwg[:, tap * 24 + n:tap * 24 + n + 1], xc, ALU.mult, ALU.add)
                                        if which == 2:
                                            act(f_[:, h, :], xc2[:, hh, :], AF.Silu)
                                        else:
                                            act(xc2[:, hh, :], xc2[:, hh, :], AF.Silu)
                                            act(sqt[:, :], xc2[:, hh, :], AF.Square)
                                            ps2 = PS()
                                            mm(ps2[:, 0:TT], ones, sqt[:, :])
                                            act(rs1[:, :], ps2[:, 0:TT], AF.Sqrt, bias=EPS)
                                            recip(rs1[:, :], rs1[:, :])
                                            stt(f_[:, h, :], xc2[:, hh, :], (128.0 ** -0.5) if which == 0 else 1.0, rs1[:, :], ALU.mult, ALU.mult)
                            if which == 0:
                                stq(GQT[:, :, t0:t0 + TT], f_[:, :, :])
                            elif which == 1:
                                stq(GKT[:, :, t0:t0 + TT], f_[:, :, :])
                                to_tm(f_, GKt)
                            else:
                                to_tm(f_, GVt)
                        if DBG_SUB <= 1:
                            continue
                        tm_block_pair(GZt, AF.Silu)
                        if ti + 1 < g.ntiles:
                            prep(ti + 1)
                        if DBG_SUB <= 2:
                            continue
                        w = next(ws)
                        for d in range(2):
                            lastoff = 63 if d == 0 else 0
                            ps = PS()
                            fm(ps, 0, w, d * 8, 8, uT)
                            act(e8[:, :], ps[0:8, 0:TT], AF.Exp, bias=dtb[:, d:d + 1])
                            act(e8[:, :], e8[:, :], AF.Ln, bias=1.0)
                            ts(S.dve, glog[:, :], e8[:, :], negA[:, d:d + 1], ALU.mult)
                            ps = PS()
                            fm(ps, 0, w, 16 + d * 8, 8, uT)
                            act(e8[:, :], ps[0:8, 0:TT], AF.Exp, scale=-1.0)
                            act(e8[:, :], e8[:, :], AF.Ln, bias=1.0)
                            ts(S.pool, lb[:, :], e8[:, :], -1.0, ALU.mult)
                            rst = C("reset")
                            if d == 0:
                                S.op(S.dve, lambda e: e.tensor_tensor_scan(out=rw.t[:, 0, :], data0=rst.ap[0:8, :], data1=glog.t[:, :], initial=0.0, op0=ALU.mult, op1=ALU.add),
                                     reads=[CT, glog], writes=[rw])
                            else:
                                S.op(S.dve, lambda e: e.tensor_tensor_scan(out=rw.t[:, 0, ::-1], data0=rst.ap[0:8, :], data1=glog.t[:, ::-1], initial=0.0, op0=ALU.mult, op1=ALU.add),
                                     reads=[CT, glog], writes=[rw])
                            ts(S.pool, rw[:, 4, :], rw[:, 0, :], -1.0, ALU.mult)
                            tt(S.pool, rw[:, 2, :], lb[:, :], rw[:, 0, :], ALU.subtract)
                            act(cs[:, 0, :], lb[:, :], AF.Exp)
                            for c in range(4):
                                li_ = c * 64 + lastoff
                                ts(S.dve, cs[:, 1, c * 64:(c + 1) * 64], rw[:, 0, c * 64:(c + 1) * 64], -1.0, ALU.mult, rw[:, 0, li_:li_ + 1], ALU.add)
                            act(cs[:, 1, :], cs[:, 1, :], AF.Exp)
                            act(dc[:, :], rw[:, 0, lastoff::64], AF.Exp)
                            stq(GR[d][:, :, t0:t0 + TT].rearrange("k h t -> h k t"), rw[:, :, :])
                            stq(GC[d][:, :, t0:t0 + TT].rearrange("k h t -> h k t"), cs[:, :, :])
                            stq(DEC[d][ti * 4:(ti + 1) * 4, :].rearrange("c h -> h c"), dc[:, :], allow_slow_non_contiguous=True)
                        if DBG_SUB <= 3:
                            continue
                        f_ = nfst()
                        for which in range(2):
                            w = next(ws)
                            for hp in range(2):
                                ps = PS()
                                for hh in range(2):
                                    fm(ps, hh * TT, w, (hp * 2 + hh) * 128, 128, uT)
                                c0 = which * 4 + hp * 2
                                act(f_.v(f_.t[:, c0:c0 + 2, :].rearrange("p a b -> p (a b)")), ps[:, :], AF.Copy, scale=1.0 if which == 0 else 128.0 ** -0.5)
                            stq((MQT, MKT)[which][:, :, t0:t0 + TT], f_[:, which * 4:which * 4 + 4, :])
                            if which == 1:
                                t_ = ntst()
                                for st in range(2):
                                    ps = PS()
                                    tmj(ps, w, 512, uT, st)
                                    act(t_[:, st, 0:512], ps[:, :], AF.Copy, scale=128.0 ** -0.5)
                                stq(MKt[t0:t0 + TT, :].rearrange("(s p) f -> p s f", p=128), t_[:, :, 0:512])
                        if DBG_SUB <= 4:
                            continue
                        tm_block_pair(MVt, AF.Copy)
                        tm_block_pair(MOt, AF.Sigmoid)
                        if DBG_SUB <= 5:
                            continue
                        w = wmib
                        for d in range(2):
                            ps = PS()
                            fm(ps, 0, w, d * 4, 4, uT)
                            act(g4[:, 0, :], ps[0:4, 0:TT], AF.Identity, bias=mlb[:, d * 2:d * 2 + 1])
                            ps = PS()
                            fm(ps, 0, w, 8 + d * 4, 4, uT)
                            act(e4[:, :], ps[0:4, 0:TT], AF.Exp, scale=-1.0, bias=mlbn[:, d * 2 + 1:d * 2 + 2])
                            act(e4[:, :], e4[:, :], AF.Ln, bias=1.0)
                            ts(S.pool, g4[:, 1, :], e4[:, :], -1.0, ALU.mult)
                            stq(LI[d][:, t0:t0 + TT], g4[:, 0, :])
                            stq(LF[d][:, t0:t0 + TT], g4[:, 1, :])
                S.barrier()

                if DBG_STAGE <= 1:
                    continue
                with ExitStack() as es:
                    TS = max(n_ for (_, n_, _) in g.seqs) * TT
                    lf = alloc(es, "m_lf", [4, TS])
                    li = alloc(es, "m_li", [4, TS])
                    mt = alloc(es, "m_m", [4, TS])
                    Ft = alloc(es, "m_F", [4, TS])
                    RWt = alloc(es, "m_RW", [4, TS])
                    WLt = alloc(es, "m_WL", [4, TS])
                    rstt = alloc(es, "m_rst", [4, TS])
                    one4 = alloc(es, "m_one", [4, TS])
                    m0c = alloc(es, "m_m0", [4, 2])
                    dcm = alloc(es, "m_dc", [4, TS // 64])
                    memset(S.pool, rstt[:, :], 1.0)
                    memset(S.pool, rstt[:, 0::64], 0.0)
                    memset(S.pool, one4[:, :], 1.0)
                    for (tst_, ntl, sidx) in g.seqs:
                        T_ = ntl * TT
                        nch = T_ // 64
                        tsl = slice(tst_ * TT, tst_ * TT + T_)
                        if sidx is None:
                            ld(m0c[:, :], I["st_mm"].rearrange("d h -> h d"), allow_slow_non_contiguous=True)
                        else:
                            memset(S.pool, m0c[:, :], 0.0)
                        for d in range(2):
                            ld(lf[:, 0:T_], LF[d][:, tsl])
                            ld(li[:, 0:T_], LI[d][:, tsl])
                            rv = slice(None, None, -1) if d == 1 else slice(None)

                            def sc_(o, a, b, ini, op0, op1, rev0=True):
                                rd = [a.b, b.b]
                                iv = ini
                                if isinstance(ini, View):
                                    rd.append(ini.b)
                                    iv = ini.ap
                                rv0 = rv if rev0 else slice(None)
                                S.op(S.dve, lambda e: e.tensor_tensor_scan(out=o.b.t[:, 0:T_][:, rv], data0=a.b.t[:, 0:T_][:, rv0], data1=b.b.t[:, 0:T_][:, rv],
                                                                          initial=iv, op0=op0, op1=op1), reads=rd, writes=[o.b])
                            sc_(mt[:, :], lf[:, :], li[:, :], m0c[:, d:d + 1], ALU.add, ALU.max)
                            sc_(Ft[:, :], rstt[:, :], lf[:, :], 0.0, ALU.mult, ALU.add, rev0=False)
                            tt(S.pool, li[:, 0:T_], li[:, 0:T_], Ft[:, 0:T_], ALU.subtract)
                            tt(S.pool, Ft[:, 0:T_], Ft[:, 0:T_], mt[:, 0:T_], ALU.subtract)
                            for c in range(nch):
                                lastc = c * 64 + (63 if d == 0 else 0)
                                if d == 0:
                                    mp = m0c[:, 0:1] if c == 0 else mt[:, c * 64 - 1:c * 64]
                                else:
                                    mp = m0c[:, 1:2] if c == nch - 1 else mt[:, (c + 1) * 64:(c + 1) * 64 + 1]
                                ts(S.dve, RWt[:, c * 64:(c + 1) * 64], Ft[:, c * 64:(c + 1) * 64], mp, ALU.add)
                                ts(S.dve, WLt[:, c * 64:(c + 1) * 64], li[:, c * 64:(c + 1) * 64], Ft[:, lastc:lastc + 1], ALU.add)
                            lo = 63 if d == 0 else 0
                            act(dcm[:, 0:nch], RWt[:, lo:T_:64], AF.Exp)
                            if sidx is not None:
                                le = T_ - 1 if d == 0 else 0
                                stq(O["o_mm"][sidx, d, :].rearrange("(h o) -> h o", o=1), mt[:, le:le + 1], allow_slow_non_contiguous=True)
                            stq(MR[d][0, :, tsl], Ft[:, 0:T_])
                            stq(MR[d][1, :, tsl], one4[:, 0:T_])
                            stq(MR[d][2, :, tsl], li[:, 0:T_])
                            stq(MR[d][3, :, tsl], RWt[:, 0:T_])
                            stq(MR[d][4, :, tsl], WLt[:, 0:T_])
                            ts(S.pool, mt[:, 0:T_], mt[:, 0:T_], -1.0, ALU.mult)
                            stq(MR[d][5, :, tsl], mt[:, 0:T_])
                            stq(MDEC[d][tst_ * 4:tst_ * 4 + nch, :].rearrange("c h -> h c"), dcm[:, 0:nch], allow_slow_non_contiguous=True)
                S.barrier()

                if DBG_STAGE <= 2:
                    continue
                with ExitStack() as es:
                    NCHM = max(n_ for (_, n_, _) in g.seqs) * 4
                    def mkb():
                        Sst = alloc(es, "gS", [128, 8, 128])
                        qTb = [alloc(es, f"gq{i}", [128, 8, 64]) for i in range(2)]
                        kTb = [alloc(es, f"gk{i}", [128, 8, 64]) for i in range(2)]
                        ktp = [alloc(es, f"gkt{i}", [128, 4, 128]) for i in range(2)]
                        vtp = [alloc(es, f"gvt{i}", [128, 4, 128]) for i in range(2)]
                        R01 = [alloc(es, f"gr01{i}", [2, 8, 64]) for i in range(2)]
                        R12 = [alloc(es, f"gr12{i}", [2, 8, 64]) for i in range(2)]
                        R34 = [alloc(es, f"gr34{i}", [2, 8, 64]) for i in range(2)]
                        COLS = [alloc(es, f"gcol{i}", [128, 4, 2]) for i in range(2)]
                        G1, G2, G3 = [alloc(es, f"gG{i}", [128, 512]) for i in range(3)]
                        Pb = [alloc(es, f"gP{i}", [128, 512]) for i in range(2)]
                        PTb = [alloc(es, f"gPT{i}", [128, 512]) for i in range(2)]
                        Ttb = [alloc(es, f"gTt{i}", [128, 512]) for i in range(2)]
                        QKT = alloc(es, "gQKT", [128, 512])
                        EB = alloc(es, "gEB", [128, 512])
                        qp = alloc(es, "gqp", [128, 8, 64])
                        kp = alloc(es, "gkp", [128, 8, 64])
                        rr = alloc(es, "grr", [128, 4, 128])
                        vn = alloc(es, "gvn", [128, 4, 128])
                        og = alloc(es, "gog", [128, 4, 128])
                        kpp = alloc(es, "gkpp", [128, 4, 128])
                        DECB = alloc(es, "gDECB", [128, NCHM * 8])
                        drow = alloc(es, "gdrow", [1, NCHM * 8])
                        return (Sst, qTb, kTb, ktp, vtp, R01, R12, R34, COLS, G1, G2, G3, Pb, PTb, Ttb, QKT, EB, qp, kp, rr, vn, og, kpp, DECB, drow)
                    BB = [mkb(), mkb()]
                    id4 = C("ident4")

                    def fl(t):
                        return t.v(t.t[:, :, :].rearrange("p a b -> p (a b)"))

                    def pr(t, hp, rows=128):
                        return t.v(t.t[0:rows, 2 * hp:2 * hp + 2, :].rearrange("p a b -> p (a b)"))

                    def s4(t, hp):
                        return t[:, hp * 128:(hp + 1) * 128]

                    def chain(d):
                        (Sst, qTb, kTb, ktp, vtp, R01, R12, R34, COLS, G1, G2, G3, Pb, PTb, Ttb, QKT, EB, qp, kp, rr, vn, og, kpp, DECB, drow) = BB[d]
                        for (tst_, ntl, sidx) in g.seqs:
                            nch = ntl * 4
                            if True:
                                if sidx is None:
                                    ld(Sst[:, :, :], I["st_gdn"][d].rearrange("h k v -> k h v"))
                                else:
                                    memset(S.pool, Sst[:, :, :], 0.0)
                                c0g = tst_ * 4
                                ld(drow[0:1, 0:nch * 8], DEC[d][c0g:c0g + nch, :].rearrange("(o c) h -> o (c h)", o=1))
                                for q0 in range(0, nch * 8, 512):
                                    q1 = min(q0 + 512, nch * 8)
                                    ps = PS()
                                    mm(ps[:, 0:q1 - q0], ones.b.v(ones.ap[0:1, 0:128]), drow[0:1, q0:q1])
                                    cp(S.dve, DECB[:, q0:q1], ps[:, 0:q1 - q0])
                                order = list(range(nch)) if d == 0 else list(range(nch - 1, -1, -1))
                                NM_, NMT_, NQK_ = (C("nlt"), C("ngt"), C("nge")) if d == 0 else (C("ngt"), C("nlt"), C("nle"))

                                def loads(k):
                                    c = order[k]
                                    t0 = tst_ * TT + c * 64
                                    i = k % 2
                                    ld(qTb[i][:, :, :], GQT[:, :, t0:t0 + 64])
                                    ld(kTb[i][:, :, :], GKT[:, :, t0:t0 + 64])
                                    for h2 in range(2):
                                        ld(ktp[i][h2 * 64:(h2 + 1) * 64, :, :], GKt[t0:t0 + 64, :].rearrange("s (hp h2 d) -> h2 s hp d", hp=4, h2=2)[h2])
                                        ld(vtp[i][h2 * 64:(h2 + 1) * 64, :, :], GVt[t0:t0 + 64, :].rearrange("s (hp h2 d) -> h2 s hp d", hp=4, h2=2)[h2])
                                        for kk in range(2):
                                            ld(COLS[i][h2 * 64:(h2 + 1) * 64, :, kk], GC[d][kk, h2::2, t0:t0 + 64].rearrange("hp s -> s hp"), allow_slow_non_contiguous=True)
                                    ld(R01[i][:, :, :], GR[d][0:2, :, t0:t0 + 64])
                                    ld(R12[i][:, :, :], GR[d][1:3, :, t0:t0 + 64])
                                    ld(R34[i][:, :, :], GR[d][3:5, :, t0:t0 + 64])
                                loads(0)
                                for k in range(nch):
                                    if k + 1 < nch:
                                        loads(k + 1)
                                    c = order[k]
                                    t0 = tst_ * TT + c * 64
                                    i = k % 2
                                    qT, kT, kt, vt, r01, r12, r34, cols = qTb[i], kTb[i], ktp[i], vtp[i], R01[i], R12[i], R34[i], COLS[i]
                                    pE = [PS(), PS(), PS()]
                                    for e_, (msk, la, ra) in enumerate(((NM_, r01, r12), (NMT_, r12, r01), (NQK_, r34, r01))):
                                        mm(pE[e_][:, :], ident, msk, start=True, stop=False)
                                        for hp in range(4):
                                            mm(s4(pE[e_], hp), pr(la, hp, 2), pr(ra, hp, 2), start=False, stop=(hp == 3))
                                    act(G1[:, :], pE[0][:, :], AF.Exp)
                                    act(G2[:, :], pE[1][:, :], AF.Exp)
                                    act(G3[:, :], pE[2][:, :], AF.Exp)
                                    yield
                                    pKK, pKQ = PS(), PS()
                                    for hp in range(4):
                                        mm(s4(pKK, hp), pr(kT, hp), pr(kT, hp))
                                    for hp in range(4):
                                        mm(s4(pKQ, hp), pr(kT, hp), pr(qT, hp))
                                    P, PT, Tt = Pb[0], PTb[0], Ttb[0]
                                    tt(S.dve, P[:, :], pKK[:, :], G1[:, :], ALU.mult)
                                    tt(S.dve, PT[:, :], pKK[:, :], G2[:, :], ALU.mult)
                                    tt(S.dve, QKT[:, :], pKQ[:, :], G3[:, :], ALU.mult)
                                    tt(S.pool, Tt[:, :], id4, PT[:, :], ALU.subtract)
                                    yield
                                    cur = 0
                                    for jl in range(1, 6):
                                        Pn, PTn, Ttn = Pb[1 - cur], PTb[1 - cur], Ttb[1 - cur]
                                        pP = PS()
                                        for hp in range(4):
                                            mm(s4(pP, hp), s4(PT, hp), s4(P, hp))
                                        if jl < 5:
                                            pPT = PS()
                                            for hp in range(4):
                                                mm(s4(pPT, hp), s4(P, hp), s4(PT, hp))
                                        cp(S.act, Pn[:, :], pP[:, :])
                                        if jl < 5:
                                            cp(S.dve, PTn[:, :], pPT[:, :])
                                        yield
                                        pT = PS()
                                        for hp in range(4):
                                            mm(s4(pT, hp), ident, s4(Tt, hp), start=True, stop=False)
                                            mm(s4(pT, hp), s4(Pn, hp), s4(Tt, hp), start=False, stop=True)
                                        cp(evq(), Ttn[:, :], pT[:, :])
                                        P, PT, Tt = Pn, PTn, Ttn
                                        cur = 1 - cur
                                        yield
                                    yield
                                    pB = PS()
                                    mm(pB[:, :], ones.b.v(ones.ap[0:1, 0:128]), r01.v(r01.t[0:1, :, :].rearrange("p a b -> p (a b)")))
                                    act(EB[:, :], pB[:, :], AF.Exp)
                                    tt(S.dve, fl(qp), fl(qT), EB[:, :], ALU.mult)
                                    tt(S.pool, fl(kp), fl(kT), EB[:, :], ALU.mult)
                                    yield
                                    pS_ = PS()
                                    for hp in range(4):
                                        for h2 in range(2):
                                            h = hp * 2 + h2
                                            mm(pS_[h2 * 64:(h2 + 1) * 64, hp * 128:(hp + 1) * 128], kp[:, h, :], Sst[:, h, :])
                                    tt(S.dve, fl(rr), fl(vt), pS_[:, :], ALU.subtract)
                                    yield
                                    pV = PS()
                                    for hp in range(4):
                                        mm(s4(pV, hp), s4(Tt, hp), rr[:, hp, :])
                                    for hp in range(4):
                                        act(vn[:, hp, :], s4(pV, hp), AF.Copy, scale=cols[:, hp, 0:1])
                                    yield
                                    pO = PS()
                                    for hp in range(4):
                                        for h2 in range(2):
                                            h = hp * 2 + h2
                                            mm(pO[h2 * 64:(h2 + 1) * 64, hp * 128:(hp + 1) * 128], qp[:, h, :], Sst[:, h, :], start=True, stop=False)
                                        mm(s4(pO, hp), s4(QKT, hp), vn[:, hp, :], start=False, stop=True)
                                    cp(evq(), fl(og), pO[:, :])
                                    for h2 in range(2):
                                        stq(OG[d][t0:t0 + 64, :].rearrange("t (hp h2 v) -> h2 t hp v", hp=4, h2=2)[h2], og[h2 * 64:(h2 + 1) * 64, :, :])
                                    yield
                                    tt(S.pool, kpp[:, :, :], kt[:, :, :], cols.v(cols.t[:, :, 1:2].to_broadcast([128, 4, 128])), ALU.mult)
                                    pU = [PS(), PS()]
                                    for h2 in range(2):
                                        for hp in range(4):
                                            mm(pU[h2][:, hp * 128:(hp + 1) * 128], kpp[h2 * 64:(h2 + 1) * 64, hp, :], vn[h2 * 64:(h2 + 1) * 64, hp, :])
                                    tt(S.pool, Sst[:, :, :], Sst[:, :, :], DECB.v(DECB.t[:, c * 8:(c + 1) * 8].unsqueeze(2).to_broadcast([128, 8, 128])), ALU.mult)
                                    for h2 in range(2):
                                        tt(S.dve, Sst.v(Sst.t[:, h2::2, :]), Sst.v(Sst.t[:, h2::2, :]),
                                           pU[h2].v(pU[h2].t[:, :].rearrange("p (a b) -> p a b", a=4)), ALU.add)
                                    yield
                                if sidx is not None:
                                    stq(O["o_gdn"][sidx, d].rearrange("h k v -> k h v"), Sst[:, :, :])
                    run_il([chain(0), chain(1)])
                S.barrier()

                if DBG_STAGE <= 3:
                    continue
                with ExitStack() as es:
                    NCHM = max(n_ for (_, n_, _) in g.seqs) * 4
                    def mkb():
                        Cst = alloc(es, "mC", [128, 4, 257])
                        qTb = [alloc(es, f"mq{i}", [128, 4, 64]) for i in range(2)]
                        kTb = [alloc(es, f"mk{i}", [128, 4, 64]) for i in range(2)]
                        ktp = [alloc(es, f"mkt{i}", [128, 2, 128]) for i in range(2)]
                        vtp = [alloc(es, f"mvt{i}", [128, 2, 257]) for i in range(2)]
                        R01 = [alloc(es, f"mr01{i}", [2, 4, 64]) for i in range(2)]
                        R12 = [alloc(es, f"mr12{i}", [2, 4, 64]) for i in range(2)]
                        R3 = [alloc(es, f"mr3{i}", [1, 4, 64]) for i in range(2)]
                        MCL = [alloc(es, f"mcl{i}", [128, 2, 2]) for i in range(2)]
                        MC = alloc(es, "mMC", [128, 2, 2])
                        Gm = alloc(es, "mG", [128, 256])
                        QKT = alloc(es, "mQKT", [128, 256])
                        WB = alloc(es, "mWB", [128, 256])
                        qp = alloc(es, "mqp", [128, 4, 64])
                        hout = alloc(es, "mh", [128, 2, 256])
                        kpp = alloc(es, "mkpp", [128, 2, 128])
                        dcol = alloc(es, "mdcol", [128, 2])
                        DECB = alloc(es, "mDECB", [128, NCHM * 4])
                        drow = alloc(es, "mdrow", [1, NCHM * 4])
                        return (Cst, qTb, kTb, ktp, vtp, R01, R12, R3, MCL, MC, Gm, QKT, WB, qp, hout, kpp, dcol, DECB, drow)
                    BB = [mkb(), mkb()]
                    for B_ in BB:
                        for i in range(2):
                            memset(S.pool, B_[4][i][:, :, 256:257], 1.0)

                    def fl(t):
                        return t.v(t.t[:, :, :].rearrange("p a b -> p (a b)"))

                    def pr(t, hp, rows=128):
                        return t.v(t.t[0:rows, 2 * hp:2 * hp + 2, :].rearrange("p a b -> p (a b)"))

                    def chain(d):
                        (Cst, qTb, kTb, ktp, vtp, R01, R12, R3, MCL, MC, Gm, QKT, WB, qp, hout, kpp, dcol, DECB, drow) = BB[d]
                        for (tst_, ntl, sidx) in g.seqs:
                            nch = ntl * 4
                            if True:
                                if sidx is None:
                                    ld(Cst[:, :, 0:256], I["st_mc"][d].rearrange("h k v -> k h v"))
                                    ld(Cst[:, :, 256:257], I["st_mn"][d].rearrange("h (k o) -> k h o", o=1), allow_slow_non_contiguous=True)
                                else:
                                    memset(S.pool, Cst[:, :, :], 0.0)
                                c0g = tst_ * 4
                                ld(drow[0:1, 0:nch * 4], MDEC[d][c0g:c0g + nch, :].rearrange("(o c) h -> o (c h)", o=1))
                                ps = PS()
                                mm(ps[:, 0:nch * 4], ones.b.v(ones.ap[0:1, 0:128]), drow[0:1, 0:nch * 4])
                                cp(S.dve, DECB[:, 0:nch * 4], ps[:, 0:nch * 4])
                                order = list(range(nch)) if d == 0 else list(range(nch - 1, -1, -1))
                                NQK_ = C("nge") if d == 0 else C("nle")

                                def loads(k):
                                    c = order[k]
                                    t0 = tst_ * TT + c * 64
                                    i = k % 2
                                    ld(qTb[i][:, :, :], MQT[:, :, t0:t0 + 64])
                                    ld(kTb[i][:, :, :], MKT[:, :, t0:t0 + 64])
                                    for h2 in range(2):
                                        ld(ktp[i][h2 * 64:(h2 + 1) * 64, :, :], MKt[t0:t0 + 64, :].rearrange("s (hp h2 d) -> h2 s hp d", hp=2, h2=2)[h2])
                                        ld(vtp[i][h2 * 64:(h2 + 1) * 64, :, 0:256], MVt[t0:t0 + 64, :].rearrange("s (hp h2 v) -> h2 s hp v", hp=2, h2=2)[h2])
                                        for kk in range(2):
                                            ld(MCL[i][h2 * 64:(h2 + 1) * 64, :, kk], MR[d][4 + kk, h2::2, t0:t0 + 64].rearrange("hp s -> s hp"), allow_slow_non_contiguous=True)
                                    ld(R01[i][:, :, :], MR[d][0:2, :, t0:t0 + 64])
                                    ld(R12[i][:, :, :], MR[d][1:3, :, t0:t0 + 64])
                                    ld(R3[i][:, :, :], MR[d][3:4, :, t0:t0 + 64])
                                loads(0)
                                for k in range(nch):
                                    if k + 1 < nch:
                                        loads(k + 1)
                                    c = order[k]
                                    t0 = tst_ * TT + c * 64
                                    i = k % 2
                                    qT, kT, kt, vt, r01, r12, r3, mcl = qTb[i], kTb[i], ktp[i], vtp[i], R01[i], R12[i], R3[i], MCL[i]
                                    pE = PS()
                                    mm(pE[:, 0:256], ident, NQK_.b.v(NQK_.ap[:, 0:256]), start=True, stop=False)
                                    for hp in range(2):
                                        mm(pE[:, hp * 128:(hp + 1) * 128], pr(r12, hp, 2), pr(r01, hp, 2), start=False, stop=(hp == 1))
                                    act(Gm[:, :], pE[:, 0:256], AF.Exp)
                                    yield
                                    pKQ = PS()
                                    for hp in range(2):
                                        mm(pKQ[:, hp * 128:(hp + 1) * 128], pr(kT, hp), pr(qT, hp))
                                    tt(S.dve, QKT[:, :], pKQ[:, 0:256], Gm[:, :], ALU.mult)
                                    yield
                                    pB = PS()
                                    mm(pB[:, 0:256], ones.b.v(ones.ap[0:1, 0:128]), r3.v(r3.t[0:1, :, :].rearrange("p a b -> p (a b)")))
                                    act(WB[:, :], pB[:, 0:256], AF.Exp)
                                    tt(S.dve, fl(qp), fl(qT), WB[:, :], ALU.mult)
                                    yield
                                    act(MC[:, :, :], mcl[:, :, :], AF.Exp)
                                    for hp in range(2):
                                        pN = PS()
                                        for h2 in range(2):
                                            h = hp * 2 + h2
                                            mm(pN[h2 * 64:(h2 + 1) * 64, 0:257], qp[:, h, :], Cst[:, h, :], start=True, stop=False)
                                        mm(pN[:, 0:257], QKT[:, hp * 128:(hp + 1) * 128], vt[:, hp, :], start=False, stop=True)
                                        act(dcol[:, hp:hp + 1], pN[:, 256:257], AF.Abs)
                                        ts(S.dve, dcol[:, hp:hp + 1], dcol[:, hp:hp + 1], MC[:, hp, 1:2], ALU.max)
                                        recip(dcol[:, hp:hp + 1], dcol[:, hp:hp + 1])
                                        act(hout[:, hp, :], pN[:, 0:256], AF.Copy, scale=dcol[:, hp:hp + 1])
                                        yield
                                    for h2 in range(2):
                                        stq(OM[d][t0:t0 + 64, :].rearrange("t (hp h2 v) -> h2 t hp v", hp=2, h2=2)[h2], hout[h2 * 64:(h2 + 1) * 64, :, :])
                                    yield
                                    tt(S.pool, kpp[:, :, :], kt[:, :, :], MC.v(MC.t[:, :, 0:1].to_broadcast([128, 2, 128])), ALU.mult)
                                    for h in range(4):
                                        hp, h2 = h // 2, h % 2
                                        pU = PS()
                                        mm(pU[:, 0:257], kpp[h2 * 64:(h2 + 1) * 64, hp, :], vt[h2 * 64:(h2 + 1) * 64, hp, :])
                                        stt(Cst[:, h, :], Cst[:, h, :], DECB[:, c * 4 + h:c * 4 + h + 1], pU[:, 0:257], ALU.mult, ALU.add)
                                        yield
                                if sidx is not None:
                                    stq(O["o_mc"][sidx, d].rearrange("h k v -> k h v"), Cst[:, :, 0:256])
                                    stq(O["o_mn"][sidx, d].rearrange("h (k o) -> k h o", o=1), Cst[:, :, 256:257], allow_slow_non_contiguous=True)
                    run_il([chain(0), chain(1)])
                S.barrier()

                if DBG_STAGE <= 4:
                    continue
                def front_alloc(es):
                    f = Grp()
                    f.a = [alloc(es, f"fo_a{i}", [128, 1024]) for i in range(2)]
                    f.m = [alloc(es, f"fo_m{i}", [128, 1024]) for i in range(2)]
                    f.gz = alloc(es, "fo_gz", [128, 1024])
                    f.mo = alloc(es, "fo_mo", [128, 1024])
                    f.mix = alloc(es, "fo_mix", [128, D])
                    f.gg = alloc(es, "fo_gg", [128, 1024])
                    f.mg = alloc(es, "fo_mg", [128, 1024])
                    f.ssq = alloc(es, "fo_ss", [128, 16])
                    ld(f.gg[:, :], I["gdn_g"].rearrange("(o n) -> o n", o=1).partition_broadcast(128).rearrange("p a b -> p (a b)"))
                    ld(f.mg[:, :], I["ml_g"].rearrange("(o n) -> o n", o=1).partition_broadcast(128).rearrange("p a b -> p (a b)"))
                    return f

                def front(f, ti, mixT):
                    for st in range(2):
                        r0 = ti * TT + st * 128
                        ld(f.a[0][:, :], OG[0][r0:r0 + 128, :])
                        ld(f.a[1][:, :], OG[1][r0:r0 + 128, :])
                        ld(f.m[0][:, :], OM[0][r0:r0 + 128, :])
                        ld(f.m[1][:, :], OM[1][r0:r0 + 128, :])
                        ld(f.gz[:, :], GZt[r0:r0 + 128, :])
                        ld(f.mo[:, :], MOt[r0:r0 + 128, :])
                        for (bufs, nh, hd, gt, gate, off, sc0) in ((f.a, 8, 128, f.gg, f.gz, 0, 0), (f.m, 4, 256, f.mg, f.mo, 1024, 8)):
                            o_, sq_ = bufs
                            tt(S.pool, o_[:, :], o_[:, :], sq_[:, :], ALU.add)
                            act(sq_[:, :], o_[:, :], AF.Square)
                            S.op(S.dve, lambda e, sq_=sq_, nh=nh, sc0=sc0: e.tensor_reduce(out=f.ssq.t[:, sc0:sc0 + nh], in_=sq_.t[:, :].rearrange("p (a b) -> p a b", a=nh), axis=AX.X, op=ALU.add),
                                 reads=[sq_], writes=[f.ssq])
                            act(f.ssq[:, sc0:sc0 + nh], f.ssq[:, sc0:sc0 + nh], AF.Sqrt, scale=1.0 / hd, bias=EPS)
                            recip(f.ssq[:, sc0:sc0 + nh], f.ssq[:, sc0:sc0 + nh])
                            tt(S.dve, o_.v(o_.t[:, :].rearrange("p (a b) -> p a b", a=nh)), o_.v(o_.t[:, :].rearrange("p (a b) -> p a b", a=nh)),
                               f.ssq.v(f.ssq.t[:, sc0:sc0 + nh].unsqueeze(2).to_broadcast([128, nh, hd])), ALU.mult)
                            tt(S.pool, o_[:, :], o_[:, :], gt[:, :], ALU.mult)
                            tt(S.pool, f.mix[:, off:off + 1024], o_[:, :], gate[:, :], ALU.mult)
                        transpose_to(mixT, f.mix, st)

                phase3(g, l, front_alloc, front)

    for l in range(depth):
        if DBG_STAGE <= -2:
            break
        if l % 2 == 0:
            even_layer(l, l // 2)
        else:
            odd_layer(l, l // 2)
    S.barrier()
    top.close()
    build.stats = {q.name: q.ninst for q in S.engs}
    return nc


N_CORES = 8
_CACHE = {}


def make_in_maps(inp, NCT=4, NLT=16, n_cores=N_CORES):
    f = lambda a: np.ascontiguousarray(np.asarray(a, dtype=np.float32))
    shared = {
        "w_mod": f(inp["w_mod"]), "b_mod": f(inp["b_mod"]), "norm_g": f(inp["norm_g"]),
        "w_up": f(inp["w_up"]), "w_down": f(inp["w_down"]),
        "w_in_e": f(inp["w_in_e"][0]), "gla_w2": f(inp["gla_gate_w2"][0]), "gla_b": f(inp["gla_gate_b"][0]),
        "gla_g": f(inp["gla_norm_g"][0]), "lru_cw": f(inp["lru_conv_w"][0]), "lru_cb": f(inp["lru_conv_b"][0]),
        "lru_gw": f(inp["lru_gate_w"][0]), "lru_gb": f(inp["lru_gate_b"][0]), "lru_lam": f(inp["lru_lambda"][0]),
        "w_out_e": f(inp["w_out_e"][0]),
        "w_in_o": f(inp["w_in_o"][0]), "gdn_cw": f(inp["gdn_conv_w"][0]), "gdn_alog": f(inp["gdn_a_log"][0]),
        "gdn_dtb": f(inp["gdn_dt_bias"][0]), "gdn_g": f(inp["gdn_norm_g"][0]), "ml_gb": f(inp["mlstm_gate_b"][0]),
        "ml_g": f(inp["mlstm_norm_g"][0]), "w_out_o": f(inp["w_out_o"][0]),
        "consts": CONST_ARR,
    }
    maps = []
    xp_, xs_ = f(inp["x_prompt"]), f(inp["x_sample"])
    for c in range(n_cores):
        b = c % 4
        m = dict(shared)
        m["xc"] = xp_[c * NCT:(c + 1) * NCT].reshape(NCT * TT, D)
        m["xl"] = np.ascontiguousarray(xs_[b, :NLT * TT])
        m["cc"] = np.ascontiguousarray(np.stack([f(inp["c_ctx"]), f(inp["c"])[b]], 0))
        m["st_gla"] = f(inp["state_gla"])[b, 0]
        m["st_lru"] = f(inp["state_lru"])[b, 0]
        m["st_gdn"] = f(inp["state_gdn"])[b, 0]
        m["st_mc"] = f(inp["state_mlstm_c"])[b, 0]
        m["st_mn"] = f(inp["state_mlstm_n"])[b, 0]
        m["st_mm"] = f(inp["state_mlstm_m"])[b, 0]
        maps.append(m)
    return maps


def kernel(**inp):
    if "nc" not in _CACHE:
        _CACHE["nc"] = build()
    nc = _CACHE["nc"]
    maps = make_in_maps(inp)
    res = run_bass_kernel_spmd(nc, maps, core_ids=list(range(N_CORES)))
    R = res.results
    B = 32
    y_ctx = np.concatenate([R[c]["yc"].reshape(4, TT, D) for c in range(8)], 0)
    y_lat = np.stack([R[b]["yl"] for b in range(4)], 0)
    cat = lambda k: np.concatenate([R[c][k] for c in range(8)], 0)
    new_gla = cat("o_gla")[:, None]
    new_lru = cat("o_lru")[:, None]
    new_gdn = cat("o_gdn")[:, None]
    new_c = cat("o_mc")[:, None]
    new_n = cat("o_mn")[:, None]
    new_m = cat("o_mm")[:, None]
    return tuple(np.ascontiguousarray(a.astype(np.float32)) for a in (y_ctx, y_lat, new_gla, new_lru, new_gdn, new_c, new_n, new_m))


def simulate_trace(trace):
    sems = {}
    pos = {k: 0 for k in trace}
    progress = True
    while progress:
        progress = False
        for k, lst in trace.items():
            while pos[k] < len(lst):
                kind, sid, val = lst[pos[k]]
                if kind == "w":
                    if sems.get(sid, 0) >= val:
                        pos[k] += 1
                        progress = True
                    else:
                        break
                else:
                    sems[sid] = sems.get(sid, 0) + val
                    pos[k] += 1
                    progress = True
    stuck = {k: (pos[k], len(lst), lst[pos[k]] if pos[k] < len(lst) else None) for k, lst in trace.items()}
    if all(p == n for (p, n, _) in stuck.values()):
        return None
    return stuck, sems
```

```python
import numpy as np
from contextlib import ExitStack
import concourse.bass as bass
import concourse.mybir as mybir
from concourse.alu_op_type import AluOpType as ALU
from concourse.bass_utils import run_bass_kernel_spmd

F32 = mybir.dt.float32
BF16 = mybir.dt.bfloat16
AF = mybir.ActivationFunctionType
AX = mybir.AxisListType

D = 2048
KC = 16
TT = 256
FF = 8192
EPS = 1e-6
NEG = -30000.0


class View:
    __slots__ = ("b", "ap")

    def __init__(self, b, ap):
        self.b = b
        self.ap = ap


class Buf:
    __slots__ = ("t", "name", "w", "r")

    def __init__(self, t, name=""):
        self.t = t
        self.name = name
        self.w = None
        self.r = {}

    def __getitem__(self, idx):
        return View(self, self.t[idx])

    def v(self, ap):
        return View(self, ap)


SEM_LIMIT = 60000
DBG_STAGE = 99
DBG_DUMP = ()
DBG_SUB2 = 99
TRACE = None
DBG_SUB = 99
DBG_GROUPS = (0, 1)


class EngQ:
    def __init__(self, S, name, eng, self_raw=True):
        self.S = S
        self.name = name
        self.eng = eng
        self.sem = S.nc.alloc_semaphore(name=f"prog_{name}")
        self.semids = {id(self.sem)}
        self.nroll = 0
        self.n = 0
        self.last_ev = None
        self.seen = {}
        self.self_raw = self_raw
        self.ninst = 0

    def roll(self):
        self.nroll += 1
        self.sem = self.S.nc.alloc_semaphore(name=f"prog_{self.name}_{self.nroll}")
        self.semids.add(id(self.sem))
        self.n = 0


class Sched:
    def __init__(self, nc, n_dma_sems=32):
        self.nc = nc
        self.pe = EngQ(self, "pe", nc.tensor, self_raw=False)
        self.act = EngQ(self, "act", nc.scalar)
        self.dve = EngQ(self, "dve", nc.vector)
        self.pool = EngQ(self, "pool", nc.gpsimd)
        self.sp = EngQ(self, "sp", nc.sync)
        self.engs = [self.pe, self.act, self.dve, self.pool, self.sp]
        self.dma_sems = [nc.alloc_semaphore(name=f"dma{i}") for i in range(n_dma_sems)]
        self.dma_val = [0] * n_dma_sems
        self.dma_rr = 0

    def _wait(self, q, ev):
        sem, val = ev
        k = id(sem)
        if q.seen.get(k, 0) < val:
            q.eng.wait_ge(sem, val)
            q.seen[k] = val
            q.ninst += 1
            if TRACE is not None:
                TRACE.setdefault(q.name, []).append(("w", k, val))

    def _deps(self, q, reads, writes):
        for b in reads:
            if b is None:
                continue
            if b.w is not None:
                if id(b.w[0]) in q.semids and not q.self_raw:
                    continue
                self._wait(q, b.w)
        for b in writes:
            if b is None:
                continue
            if b.w is not None and id(b.w[0]) not in q.semids:
                self._wait(q, b.w)
            for ev in b.r.values():
                if id(ev[0]) not in q.semids:
                    self._wait(q, ev)

    def _mark(self, ev, reads, writes):
        k = id(ev[0])
        for b in reads:
            if b is not None:
                b.r[k] = ev
        for b in writes:
            if b is not None:
                b.w = ev
                b.r = {}

    def op(self, q, fn, reads=(), writes=(), inc=True):
        if q.n >= SEM_LIMIT:
            q.roll()
        self._deps(q, reads, writes)
        inst = fn(q.eng)
        q.ninst += 1
        if inc:
            q.n += 1
            inst.then_inc(q.sem, 1)
            ev = (q.sem, q.n)
            q.last_ev = ev
            if TRACE is not None:
                TRACE.setdefault(q.name, []).append(("i", id(q.sem), 1))
        else:
            ev = (q.sem, q.n + 1)
        self._mark(ev, reads, writes)
        return inst

    def dma(self, q, out_ap, in_ap, reads=(), writes=(), **kw):
        self._deps(q, reads, writes)
        i = self.dma_rr
        self.dma_rr = (self.dma_rr + 1) % len(self.dma_sems)
        sem = self.dma_sems[i]
        if self.dma_val[i] > 0:
            self._wait(q, (sem, self.dma_val[i]))
        inst = q.eng.dma_start(out=out_ap, in_=in_ap, **kw)
        self.dma_val[i] += 16
        inst.then_inc(sem, 16)
        if TRACE is not None:
            TRACE.setdefault(q.name, []).append(("i", id(sem), 16))
        q.ninst += 1
        self._mark((sem, self.dma_val[i]), reads, writes)
        return inst

    def barrier(self):
        evs = [q.last_ev for q in self.engs if q.last_ev is not None]
        evs += [(s, v) for s, v in zip(self.dma_sems, self.dma_val) if v > 0]
        for q in self.engs:
            for ev in evs:
                if id(ev[0]) in q.semids:
                    continue
                self._wait(q, ev)


def make_consts():
    c = {}
    i128 = np.arange(128)
    r, cc = i128[:, None], i128[None, :]
    c["ident"] = np.eye(128, dtype=np.float32)
    c["ones"] = np.ones((128, 128), np.float32)
    c["tri_f"] = (r <= cc).astype(np.float32)
    c["tri_b"] = (r >= cc).astype(np.float32)
    c["str_f"] = (r > cc).astype(np.float32)
    c["str_b"] = (r < cc).astype(np.float32)
    same = (r // 64) == (cc // 64)
    for nm, cond in (("nlt", cc < r), ("ngt", cc > r), ("nle", cc <= r), ("nge", cc >= r)):
        m = np.where(same & cond, 0.0, NEG).astype(np.float32)
        c[nm] = np.tile(m, (1, 4))
    c["ident4"] = np.tile(np.eye(128, dtype=np.float32), (1, 4))
    c["mask4_f"] = np.tile(c["tri_f"], (1, 4))
    c["mask4_b"] = np.tile(c["tri_b"], (1, 4))
    rm = np.ones((128, 256), np.float32)
    rm[:, 0::64] = 0.0
    c["reset"] = rm
    names = list(c.keys())
    offs = {}
    o = 0
    for n in names:
        offs[n] = (o, c[n].shape[1])
        o += c[n].shape[1]
    arr = np.concatenate([c[n] for n in names], axis=1)
    return arr, offs


CONST_ARR, CONST_OFFS = make_consts()

EVEN_BLOCKS = [(0, 512), (512, 512), (1024, 512), (1536, 512), (2048, 512), (2560, 512), (3072, 32),
               (3104, 512), (3616, 512), (4128, 512), (4640, 512)]
ODD_BLOCKS = [(i * 512, 512) for i in range(8)] + [(4096, 32), (4128, 512), (4640, 512), (5152, 512), (5664, 512),
                                                   (6176, 512), (6688, 512)]


def build(NCT=4, NLT=16, depth=2):
    nc = bass.Bass("TRN2", target_bir_lowering=False)
    S = Sched(nc)
    TC, TL = NCT * TT, NLT * TT
    TMAX = max(TC, TL)

    def din(name, shape, dt=F32):
        return nc.dram_tensor(name, list(shape), dt, kind="ExternalInput").ap()

    def dout(name, shape, dt=F32):
        return nc.dram_tensor(name, list(shape), dt, kind="ExternalOutput").ap()

    def dscr(name, shape, dt=F32):
        return nc.dram_tensor(name, list(shape), dt, kind="Internal").ap()

    I = {}
    for name, shape in [("xc", [TC, D]), ("xl", [TL, D]), ("cc", [2, D]),
                        ("st_gla", [2, 4, 128, 256]), ("st_lru", [2, 1024]), ("st_gdn", [2, 8, 128, 128]),
                        ("st_mc", [2, 4, 128, 256]), ("st_mn", [2, 4, 128]), ("st_mm", [2, 4]),
                        ("w_mod", [2, D, 6 * D]), ("b_mod", [2, 6 * D]), ("norm_g", [2, 4, D]),
                        ("w_up", [2, D, FF]), ("w_down", [2, FF, D]),
                        ("w_in_e", [D, 5152]), ("gla_w2", [2, 16, 512]), ("gla_b", [2, 512]), ("gla_g", [1024]),
                        ("lru_cw", [4, 1024]), ("lru_cb", [1024]), ("lru_gw", [2, 2, 8, 128, 128]),
                        ("lru_gb", [2, 2, 1024]), ("lru_lam", [2, 1024]), ("w_out_e", [D, D]),
                        ("w_in_o", [D, 7216]), ("gdn_cw", [4, 3072]), ("gdn_alog", [2, 8]), ("gdn_dtb", [2, 8]),
                        ("gdn_g", [1024]), ("ml_gb", [2, 2, 4]), ("ml_g", [1024]), ("w_out_o", [D, D]),
                        ("consts", list(CONST_ARR.shape))]:
        I[name] = din(name, shape)
    O = {}
    for name, shape in [("yc", [TC, D]), ("yl", [TL, D]), ("o_gla", [NCT, 2, 4, 128, 256]), ("o_lru", [NCT, 2, 1024]),
                        ("o_gdn", [NCT, 2, 8, 128, 128]), ("o_mc", [NCT, 2, 4, 128, 256]), ("o_mn", [NCT, 2, 4, 128]),
                        ("o_mm", [NCT, 2, 4])]:
        O[name] = dout(name, shape)

    def mm(o, l, r, start=True, stop=True, inc=None):
        S.op(S.pe, lambda e: e.matmul(o.ap, lhsT=l.ap, rhs=r.ap, start=start, stop=stop), reads=[l.b, r.b], writes=[o.b],
             inc=bool(stop) if inc is None else inc)

    def act(o, i, func, bias=None, scale=None, accum=None):
        kw = {}
        rd = [i.b]
        wr = [o.b]
        if bias is not None:
            if isinstance(bias, View):
                kw["bias"] = bias.ap
                rd.append(bias.b)
            else:
                kw["bias"] = bias
        if scale is not None:
            if isinstance(scale, View):
                kw["scale"] = scale.ap
                rd.append(scale.b)
            else:
                kw["scale"] = scale
        if accum is not None:
            kw["accum_out"] = accum.ap
            wr.append(accum.b)
        S.op(S.act, lambda e: e.activation(out=o.ap, in_=i.ap, func=func, **kw), reads=rd, writes=wr)

    def tt(q, o, a, b, op):
        S.op(q, lambda e: e.tensor_tensor(out=o.ap, in0=a.ap, in1=b.ap, op=op), reads=[a.b, b.b], writes=[o.b])

    def _sc(s, rd):
        if isinstance(s, View):
            rd.append(s.b)
            return s.ap
        return s

    def ts(q, o, a, s1, op0, s2=None, op1=None):
        rd = [a.b]
        a1 = _sc(s1, rd)
        a2 = _sc(s2, rd)
        if op1 is None:
            S.op(q, lambda e: e.tensor_scalar(out=o.ap, in0=a.ap, scalar1=a1, scalar2=None, op0=op0), reads=rd, writes=[o.b])
        else:
            S.op(q, lambda e: e.tensor_scalar(out=o.ap, in0=a.ap, scalar1=a1, scalar2=a2, op0=op0, op1=op1), reads=rd, writes=[o.b])

    def stt(o, a, s, b, op0, op1):
        rd = [a.b, b.b]
        a1 = _sc(s, rd)
        S.op(S.dve, lambda e: e.scalar_tensor_tensor(out=o.ap, in0=a.ap, scalar=a1, in1=b.ap, op0=op0, op1=op1), reads=rd, writes=[o.b])

    def cp(q, o, i):
        if q is S.act:
            act(o, i, AF.Copy)
        else:
            S.op(q, lambda e: e.tensor_copy(out=o.ap, in_=i.ap), reads=[i.b], writes=[o.b])

    def memset(q, o, val):
        S.op(q, lambda e: e.memset(o.ap, val), writes=[o.b])

    def ld(dst, src_ap, q=None, **kw):
        S.dma(q or S.sp, dst.ap, src_ap, writes=[dst.b], **kw)

    def stq(dst_ap, src, q=None, **kw):
        S.dma(q or S.pool, dst_ap, src.ap, reads=[src.b], **kw)

    top = ExitStack()

    uid = [0]

    def alloc(es, name, shape, dt=F32):
        uid[0] += 1
        name = f"{name}_{uid[0]}"
        return Buf(es.enter_context(nc.sbuf_tensor(name, list(shape), dt)), name)

    CT = alloc(top, "consts_sb", list(CONST_ARR.shape))
    ld(CT[:, :], I["consts"])

    def C(name, rows=128):
        o, w = CONST_OFFS[name]
        return CT[0:rows, o:o + w]

    ident = C("ident")
    banks = [Buf(top.enter_context(nc.psum_tensor(f"psb{i}", [128, 512], F32)), f"psb{i}") for i in range(8)]
    bank_rr = [0]

    def PS():
        b = banks[bank_rr[0]]
        bank_rr[0] = (bank_rr[0] + 1) % 8
        return b

    def tr(o, i, n):
        S.op(S.pe, lambda e: e.transpose(o.ap, i.ap, ident.ap[0:n, 0:n]), reads=[i.b, CT], writes=[o.b])

    colstage = alloc(top, "colstage", [128, 128])

    def load_cols(dst, flat_ap, R):
        ld(colstage[0:R, :], flat_ap.rearrange("(r p) -> r p", p=128))
        ps = PS()
        r0 = 0
        while r0 < R:
            n = min(64 if R > 64 else R, R - r0)
            S.op(S.pe, lambda e, r0=r0, n=n: e.transpose(ps.t[:, r0:r0 + n], colstage.t[r0:r0 + n, :], ident.ap[r0:r0 + n, r0:r0 + n]),
                 reads=[colstage, CT], writes=[ps])
            r0 += n
        cp(S.dve, dst, ps[:, 0:R])

    def cast_blocks(src2d, nkc, blocks, prefix):
        outs = []
        for bi, (c0, w) in enumerate(blocks):
            dst = dscr(f"{prefix}{bi}", [128, nkc, w], BF16)
            for k0 in range(0, nkc, 16):
                S.dma(S.pool, dst[:, k0:k0 + 16, :],
                      src2d[k0 * 128:(k0 + 16) * 128, c0:c0 + w].rearrange("(kc p) f -> p kc f", p=128))
            outs.append(dst)
        return outs

    B512 = [(i * 512, 512) for i in range(4)]
    Wb = {}
    Wb["in0"] = cast_blocks(I["w_in_e"], KC, EVEN_BLOCKS, "wine")
    Wb["out0"] = cast_blocks(I["w_out_e"], KC, B512, "woute")
    if depth > 1:
        Wb["in1"] = cast_blocks(I["w_in_o"], KC, ODD_BLOCKS, "wino")
        Wb["out1"] = cast_blocks(I["w_out_o"], KC, B512, "wouto")
    for l in range(depth):
        Wb[f"up{l}"] = cast_blocks(I["w_up"][l], KC, [(i * 512, 512) for i in range(16)], f"wup{l}_")
        Wb[f"dn{l}"] = cast_blocks(I["w_down"][l], 64, B512, f"wdn{l}_")

    MODV = dscr("modv", [depth, 2, 6, D])
    with ExitStack() as es:
        cct = alloc(es, "cct", [2, D])
        sT = alloc(es, "sT", [128, KC, 2])
        wm = [alloc(es, f"wm{i}", [128, KC, 512]) for i in range(2)]
        modt = alloc(es, "modt", [2, 6 * D])
        bmt = [alloc(es, f"bmt{i}", [2, 512]) for i in range(2)]
        ngt = alloc(es, "ngt", [2, D])
        mvt = [alloc(es, f"mvt{i}", [2, D]) for i in range(2)]
        ld(cct[:, :], I["cc"])
        act(cct[:, :], cct[:, :], AF.Silu)
        ps = PS()
        for kc in range(KC):
            tr(ps[:, kc * 2:kc * 2 + 2], cct[:, kc * 128:(kc + 1) * 128], 2)
        cp(S.dve, sT.v(sT.t[:, :, :].rearrange("p a b -> p (a b)")), ps[:, 0:2 * KC])
        for l in range(depth):
            for cb in range(24):
                w = wm[cb % 2]
                bm = bmt[cb % 2]
                ld(w[:, :, :], I["w_mod"][l][:, cb * 512:(cb + 1) * 512].rearrange("(kc p) f -> p kc f", p=128))
                ld(bm[:, :], I["b_mod"][l:l + 1, cb * 512:(cb + 1) * 512].partition_broadcast(2).rearrange("p a b -> p (a b)"))
                ps = PS()
                for kc in range(KC):
                    mm(ps[0:2, :], sT[:, kc, :], w[:, kc, :], start=(kc == 0), stop=(kc == KC - 1))
                tt(S.dve, modt[:, cb * 512:(cb + 1) * 512], ps[0:2, :], bm[:, :], ALU.add)

            def sl(i):
                return modt[:, i * D:(i + 1) * D]
            for i, (kind, mi, gi_) in enumerate((("a", 1, 0), ("c", 0, None), ("m", 2, 1), ("a", 4, 2), ("c", 3, None), ("m", 5, 3))):
                mvb = mvt[i % 2]
                if gi_ is not None:
                    ld(ngt[:, :], I["norm_g"][l, gi_:gi_ + 1, :].partition_broadcast(2).rearrange("p a b -> p (a b)"))
                if kind == "a":
                    stt(mvb[:, :], sl(mi), 1.0, ngt[:, :], ALU.add, ALU.mult)
                elif kind == "m":
                    tt(S.dve, mvb[:, :], sl(mi), ngt[:, :], ALU.mult)
                else:
                    cp(S.dve, mvb[:, :], sl(mi))
                stq(MODV[l, :, i, :], mvb[:, :])
    S.barrier()

    X1 = {0: dscr("x1c", [TC, D]), 1: dscr("x1l", [TL, D])}
    MIXT = dscr("mixt", [128, 16, TMAX], BF16)
    sc = {}

    def scr(name, shape, dt=F32):
        if name not in sc:
            if name in DBG_DUMP:
                sc[name] = dout("s_" + name, shape, dt)
            else:
                sc[name] = dscr("s_" + name, shape, dt)
        return sc[name]

    class Grp:
        pass

    def make_groups(l):
        last = (l == depth - 1)
        gs = []
        for w in DBG_GROUPS:
            g = Grp()
            g.w = w
            g.ntiles = NCT if w == 0 else NLT
            g.T = g.ntiles * TT
            src = (I["xc"], I["xl"])[w] if l == 0 else X1[w]
            dst = (O["yc"], O["yl"])[w] if last else X1[w]
            g.colmajor = (w == 1 and l % 2 == 1)
            g.L = 256 if w == 0 else 64
            g.NL = TT // g.L
            if g.colmajor:
                sv = src.rearrange("(r c) d -> c r d", c=64)
                dv = dst.rearrange("(r c) d -> c r d", c=64)
                g.xsrc = lambda ti, st, sv=sv: [(0, 64, sv[ti * 4 + st * 2]), (64, 128, sv[ti * 4 + st * 2 + 1])]
                g.ydst = lambda ti, st, dv=dv: [(0, 64, dv[ti * 4 + st * 2]), (64, 128, dv[ti * 4 + st * 2 + 1])]
            else:
                g.xsrc = lambda ti, st, src=src: [(0, 128, src[ti * TT + st * 128: ti * TT + (st + 1) * 128, :])]
                g.ydst = lambda ti, st, dst=dst: [(0, 128, dst[ti * TT + st * 128: ti * TT + (st + 1) * 128, :])]
            g.seqs = [(i, 1, i) for i in range(NCT)] if w == 0 else [(0, NLT, None)]
            gs.append(g)
        return gs

    def bc_load(dst, l, w, i):
        ld(dst[:, :], MODV[l, w, i:i + 1, :].partition_broadcast(128).rearrange("p a b -> p (a b)"))

    def wstream(aps, bufs):
        n = len(aps)

        def issue(k):
            b = bufs[k % len(bufs)]
            shp = aps[k].shape
            ld(b[:, 0:shp[1], 0:shp[2]], aps[k])
            return b
        cur = issue(0)
        for k in range(n):
            nxt = issue(k + 1) if k + 1 < n else None
            yield cur
            cur = nxt

    def recip(o, i):
        S.op(S.dve, lambda e: e.reciprocal(out=o.ap, in_=i.ap), reads=[i.b], writes=[o.b])

    def rstd_from_ss(rstd, ss, n):
        act(rstd, ss, AF.Sqrt, scale=1.0 / n, bias=EPS)
        recip(rstd, rstd)

    def run_il(gens):
        active = list(gens)
        while active:
            for g_ in list(active):
                try:
                    next(g_)
                except StopIteration:
                    active.remove(g_)

    evac_rr = [0]

    def evq():
        evac_rr[0] ^= 1
        return S.act if evac_rr[0] else S.dve

    def norm_mod_T(src, dst, bA, bB, junk, ss, rstd, uT, st, col):
        act(junk[:, :], src[:, :], AF.Square, accum=ss[:, col:col + 1])
        rstd_from_ss(rstd[:, col:col + 1], ss[:, col:col + 1], D)
        stt(dst[:, :], src[:, :], rstd[:, col:col + 1], bA[:, :], ALU.mult, ALU.mult)
        tt(S.pool, dst[:, :], dst[:, :], bB[:, :], ALU.add)
        transpose_to(uT, dst, st)

    def transpose_to(uT, src, st):
        for q4 in range(4):
            ps = PS()
            for j in range(4):
                kc = q4 * 4 + j
                tr(ps[:, j * 128:(j + 1) * 128], src[:, kc * 128:(kc + 1) * 128], 128)
            cp(evq(), uT[:, q4 * 4:(q4 + 1) * 4, st * 128:(st + 1) * 128],
               ps.v(ps.t[:, :].rearrange("p (a b) -> p a b", a=4)))

    def load_x(g, ti, st, xt):
        for (p0, p1, ap) in g.xsrc(ti, st):
            ld(xt[p0:p1, :], ap)

    def phase3(g, l, front_alloc, front):
        with ExitStack() as es:
            bX, bY = alloc(es, "bcX", [128, D]), alloc(es, "bcY", [128, D])
            bG1, bA2, bB2, bG3 = bX, bY, bX, bY
            xt = [alloc(es, f"xt{i}", [128, D]) for i in range(2)]
            yb = [alloc(es, f"yb{i}", [128, D]) for i in range(2)]
            junk = alloc(es, "junk", [128, D], BF16)
            ss = alloc(es, "ss", [128, 8])
            rstd = alloc(es, "rstd", [128, 8])
            mixT = alloc(es, "mixT", [128, KC, TT], BF16)
            u2T = alloc(es, "u2T", [128, KC, TT], BF16)
            hidT = alloc(es, "hidT", [128, 64, TT], BF16)
            wb = [alloc(es, f"wb{i}", [128, KC, 512], BF16) for i in range(2)]
            rtmp = [alloc(es, f"rtmp{i}", [128, 512]) for i in range(2)]
            fctx = front_alloc(es)
            aps = []
            for ti in range(g.ntiles):
                aps += list(Wb[f"out{l}"]) + list(Wb[f"up{l}"])
                for fb in range(4):
                    aps += [Wb[f"dn{l}"][fb][:, k0:k0 + 16, :] for k0 in range(0, 64, 16)]
            ws = wstream(aps, wb)
            front(fctx, 0, mixT)
            for ti in range(g.ntiles):
                bc_load(bG1, l, g.w, 2)
                bc_load(bA2, l, g.w, 3)
                for st in range(2):
                    load_x(g, ti, st, xt[st])
                for fb in range(4):
                    w = next(ws)
                    for st in range(2):
                        ps = PS()
                        for kc in range(KC):
                            mm(ps[:, :], mixT[:, kc, st * 128:(st + 1) * 128], w[:, kc, :], start=(kc == 0), stop=(kc == KC - 1))
                        cp(evq(), yb[st][:, fb * 512:(fb + 1) * 512], ps[:, :])
                for st in range(2):
                    act(junk[:, :], yb[st][:, :], AF.Square, accum=ss[:, st:st + 1])
                    rstd_from_ss(rstd[:, st:st + 1], ss[:, st:st + 1], D)
                    stt(yb[st][:, :], yb[st][:, :], rstd[:, st:st + 1], bG1[:, :], ALU.mult, ALU.mult)
                    tt(S.pool, xt[st][:, :], xt[st][:, :], yb[st][:, :], ALU.add)
                bc_load(bB2, l, g.w, 4)
                for st in range(2):
                    norm_mod_T(xt[st], yb[st], bA2, bB2, junk, ss, rstd, u2T, st, 2 + st)
                bc_load(bG3, l, g.w, 5)
                for ub in range(16):
                    w = next(ws)
                    for j in range(0, 4, 2):
                        ps = PS()
                        for jj in range(2):
                            for kc in range(KC):
                                mm(ps[:, jj * 256:(jj + 1) * 256], w[:, kc, (j + jj) * 128:(j + jj + 1) * 128], u2T[:, kc, :],
                                   start=(kc == 0), stop=(kc == KC - 1))
                        rt_ = rtmp[(j // 2) % 2]
                        act(rt_[:, :], ps[:, :], AF.Relu)
                        c0 = ub * 4 + j
                        tt(S.pool, hidT.v(hidT.t[:, c0:c0 + 2, :].rearrange("p a b -> p (a b)")), rt_[:, :], rt_[:, :], ALU.mult)
                if ti + 1 < g.ntiles:
                    front(fctx, ti + 1, mixT)
                for fb in range(4):
                    psd = [PS(), PS()]
                    for g4 in range(4):
                        w = next(ws)
                        for st in range(2):
                            for k in range(16):
                                fc = g4 * 16 + k
                                mm(psd[st][:, :], hidT[:, fc, st * 128:(st + 1) * 128], w[:, k, :], start=(fc == 0), stop=(fc == 63),
                                   inc=(k == 15))
                    for st in range(2):
                        cp(evq(), yb[st][:, fb * 512:(fb + 1) * 512], psd[st][:, :])
                for st in range(2):
                    act(junk[:, :], yb[st][:, :], AF.Square, accum=ss[:, 4 + st:5 + st])
                    rstd_from_ss(rstd[:, 4 + st:5 + st], ss[:, 4 + st:5 + st], D)
                    stt(yb[st][:, :], yb[st][:, :], rstd[:, 4 + st:5 + st], bG3[:, :], ALU.mult, ALU.mult)
                    tt(S.pool, yb[st][:, :], yb[st][:, :], xt[st][:, :], ALU.add)
                    for (p0, p1, ap) in g.ydst(ti, st):
                        stq(ap, yb[st][p0:p1, :])
        S.barrier()

    def even_layer(l, j):
        QT = scr("QT", [128, 4, TMAX])
        KT_ = scr("KT", [128, 4, TMAX])
        Kt = scr("Kt", [TMAX, 512])
        Vt = scr("Vt", [TMAX, 1024])
        Gd = [scr(f"G{d}", [TMAX, 512]) for d in range(2)]
        RT = scr("RT", [128, 8, TMAX])
        Ad = [scr(f"A{d}", [128, 8, TMAX]) for d in range(2)]
        Bd = [scr(f"B{d}", [128, 8, TMAX]) for d in range(2)]
        LGT = scr("LGT", [128, 8, TMAX])
        Od = [scr(f"O{d}", [128, 8, TMAX]) for d in range(2)]
        with ExitStack() as les:
            gcol = alloc(les, "gcol", [128, 8])
            load_cols(gcol[:, :], I["gla_g"], 8)
            cw = alloc(les, "cw", [128, 32])
            load_cols(cw[:, :], I["lru_cw"].rearrange("a b -> (a b)"), 32)
            cb = alloc(les, "cb", [128, 8])
            load_cols(cb[:, :], I["lru_cb"], 8)
            gb = alloc(les, "gb", [128, 32])
            load_cols(gb[:, :], I["lru_gb"].rearrange("a b c -> (a b c)"), 32)
            m8sp = alloc(les, "m8sp", [128, 16])
            load_cols(m8sp[:, :], I["lru_lam"].rearrange("a b -> (a b)"), 16)
            act(m8sp[:, :], m8sp[:, :], AF.Exp, scale=-1.0)
            act(m8sp[:, :], m8sp[:, :], AF.Ln, bias=1.0)
            ts(S.dve, m8sp[:, :], m8sp[:, :], -8.0, ALU.mult)
            ones = C("ones")

            for g in make_groups(l):
                L, NL = g.L, g.NL
                with ExitStack() as es:
                    bA, bB = alloc(es, "bA", [128, D]), alloc(es, "bB", [128, D])
                    bc_load(bA, l, g.w, 0)
                    bc_load(bB, l, g.w, 1)
                    w2 = alloc(es, "w2", [16, 2, 512])
                    ld(w2[:, :, :], I["gla_w2"].rearrange("d r k -> r d k"))
                    gbrow = alloc(es, "gbrow", [1, 2, 512])
                    ld(gbrow[:, :, :], I["gla_b"].rearrange("(o d) k -> o d k", o=1))
                    LW = alloc(es, "LW", [128, 32, 128])
                    ld(LW[:, :, :], I["lru_gw"].rearrange("d g n i j -> i (d g n) j"))
                    xt = [alloc(es, f"xt{i}", [128, D]) for i in range(2)]
                    uTs = [alloc(es, f"uT{i}", [128, KC, TT], BF16) for i in range(2)]
                    junk = alloc(es, "junk", [128, D], BF16)
                    ss = alloc(es, "ss", [128, 4])
                    rstd = alloc(es, "rstd", [128, 4])
                    wb = [alloc(es, f"wb{i}", [128, KC, 512], BF16) for i in range(2)]
                    qTs = alloc(es, "qTs", [128, 4, TT])
                    kTs = qTs
                    kts = alloc(es, "kts", [128, 2, 512])
                    vs = alloc(es, "vs", [128, 2, 1024])
                    rTs = alloc(es, "rTs", [128, 8, TT])
                    lrT = alloc(es, "lrT", [16, 2, TT])
                    e1 = alloc(es, "e1", [128, 512])
                    Gs = alloc(es, "Gs", [128, 2, 2, 512])
                    xp = alloc(es, "xp", [128, 8, NL, L + 3])
                    xcs = alloc(es, "xcs", [128, 8, TT])
                    grs = [alloc(es, f"gr{i}", [128, TT]) for i in range(4)]
                    gis = [alloc(es, f"gi{i}", [128, TT]) for i in range(4)]
                    as_ = alloc(es, "as_", [128, 8, TT])
                    bs_ = alloc(es, "bs_", [128, 8, TT])
                    lgs = rTs
                    memset(S.pool, xp[:, :, :, :], 0.0)
                    aps = []
                    for ti in range(g.ntiles):
                        aps += list(Wb[f"in{l}"])
                    ws = wstream(aps, wb)

                    def prep(ti):
                        for st in range(2):
                            load_x(g, ti, st, xt[st])
                        for st in range(2):
                            norm_mod_T(xt[st], xt[st], bA, bB, junk, ss, rstd, uTs[ti % 2], st, st)

                    def fm(ps, col, w, wc0, n, uT):
                        for kc in range(KC):
                            mm(ps[0:n, col:col + TT], w[:, kc, wc0:wc0 + n], uT[:, kc, :], start=(kc == 0), stop=(kc == KC - 1))

                    def tmj(ps, w, n, uT, st):
                        for kc in range(KC):
                            mm(ps[:, 0:n], uT[:, kc, st * 128:(st + 1) * 128], w[:, kc, 0:n], start=(kc == 0), stop=(kc == KC - 1))

                    prep(0)
                    for ti in range(g.ntiles):
                        t0 = ti * TT
                        uT = uTs[ti % 2]
                        for bi, (dst_s, dram, scl) in enumerate(((qTs, QT, 128.0 ** -0.5), (kTs, KT_, 1.0))):
                            w = next(ws)
                            for hp in range(2):
                                ps = PS()
                                for hh in range(2):
                                    fm(ps, hh * TT, w, (hp * 2 + hh) * 128, 128, uT)
                                act(dst_s.v(dst_s.t[:, hp * 2:hp * 2 + 2, :].rearrange("p a b -> p (a b)")), ps[:, :], AF.Copy, scale=scl)
                            stq(dram[:, :, t0:t0 + TT], dst_s[:, :, :])
                            if bi == 1:
                                for st in range(2):
                                    ps = PS()
                                    tmj(ps, w, 512, uT, st)
                                    cp(S.dve, kts[:, st, :], ps[:, :])
                                stq(Kt[t0:t0 + TT, :].rearrange("(s p) f -> p s f", p=128), kts[:, :, :])
                        for vb in range(2):
                            w = next(ws)
                            for st in range(2):
                                ps = PS()
                                tmj(ps, w, 512, uT, st)
                                cp(evq(), vs[:, st, vb * 512:(vb + 1) * 512], ps[:, :])
                        stq(Vt[t0:t0 + TT, :].rearrange("(s p) f -> p s f", p=128), vs[:, :, :])
                        for rb in range(2):
                            w = next(ws)
                            for cp_ in range(2):
                                ps = PS()
                                for hh in range(2):
                                    fm(ps, hh * TT, w, (cp_ * 2 + hh) * 128, 128, uT)
                                c0 = rb * 4 + cp_ * 2
                                act(rTs.v(rTs.t[:, c0:c0 + 2, :].rearrange("p a b -> p (a b)")), ps[:, :], AF.Silu)
                        stq(RT[:, :, t0:t0 + TT], rTs[:, :, :])
                        if ti + 1 < g.ntiles:
                            prep(ti + 1)
                        w = next(ws)
                        for d in range(2):
                            ps = PS()
                            fm(ps, 0, w, d * 16, 16, uT)
                            cp(S.dve, lrT[:, d, :], ps[0:16, 0:TT])
                        for d in range(2):
                            for st in range(2):
                                ps = PS()
                                mm(ps[:, :], lrT[0:16, d, st * 128:(st + 1) * 128], w2[0:16, d, :], start=True, stop=False)
                                mm(ps[:, :], ones.b.v(ones.ap[0:1, 0:128]), gbrow[0:1, d, :], start=False, stop=True)
                                act(e1[:, :], ps[:, :], AF.Exp, scale=-1.0)
                                act(e1[:, :], e1[:, :], AF.Ln, bias=1.0)
                                ts(S.pool, Gs[:, d, st, :], e1[:, :], -1.0 / 16.0, ALU.mult)
                            stq(Gd[d][t0:t0 + TT, :].rearrange("(s p) f -> p s f", p=128), Gs[:, d, :, :])
                        for xb in range(2):
                            w = next(ws)
                            for cp_ in range(2):
                                ps = PS()
                                for hh in range(2):
                                    fm(ps, hh * TT, w, (cp_ * 2 + hh) * 128, 128, uT)
                                for hh in range(2):
                                    n = xb * 4 + cp_ * 2 + hh
                                    act(xp[:, n, :, 2:2 + L], ps.v(ps.t[:, hh * TT:(hh + 1) * TT].rearrange("p (a b) -> p a b", a=NL)), AF.Copy)
                        for n in range(8):
                            xc = xcs.v(xcs.t[:, n, :].rearrange("p (a b) -> p a b", a=NL))
                            act(xc, xp[:, n, :, 2:2 + L], AF.Identity, scale=cw[:, 2 * 8 + n:2 * 8 + n + 1], bias=cb[:, n:n + 1])
                            for tap in (0, 1, 3):
                                stt(xc, xp[:, n, :, tap:tap + L], cw[:, tap * 8 + n:tap * 8 + n + 1], xc, ALU.mult, ALU.add)
                        for d in range(2):
                            for n in range(8):
                                gr, gi = grs[n % 4], gis[n % 4]
                                ps = PS()
                                mm(ps[:, 0:TT], LW[:, (d * 2 + 0) * 8 + n, :], xcs[:, n, :])
                                mm(ps[:, TT:2 * TT], LW[:, (d * 2 + 1) * 8 + n, :], xcs[:, n, :])
                                i0 = (d * 2 + 0) * 8 + n
                                i1 = (d * 2 + 1) * 8 + n
                                act(gr[:, :], ps[:, 0:TT], AF.Sigmoid, bias=gb[:, i0:i0 + 1])
                                act(gi[:, :], ps[:, TT:2 * TT], AF.Sigmoid, bias=gb[:, i1:i1 + 1])
                                act(as_[:, n, :], gr[:, :], AF.Exp, scale=m8sp[:, d * 8 + n:d * 8 + n + 1])
                                tt(S.pool, gr[:, :], as_[:, n, :], as_[:, n, :], ALU.mult)
                                act(gr[:, :], gr[:, :], AF.Sqrt, scale=-1.0, bias=1.0)
                                tt(S.pool, gi[:, :], gi[:, :], gr[:, :], ALU.mult)
                                tt(S.dve, bs_[:, n, :], gi[:, :], xcs[:, n, :], ALU.mult)
                            stq(Ad[d][:, :, t0:t0 + TT], as_[:, :, :])
                            stq(Bd[d][:, :, t0:t0 + TT], bs_[:, :, :])
                        for gb_ in range(2):
                            w = next(ws)
                            for cp_ in range(2):
                                ps = PS()
                                for hh in range(2):
                                    fm(ps, hh * TT, w, (cp_ * 2 + hh) * 128, 128, uT)
                                c0 = gb_ * 4 + cp_ * 2
                                act(lgs.v(lgs.t[:, c0:c0 + 2, :].rearrange("p a b -> p (a b)")), ps[:, :], AF.Gelu_apprx_tanh)
                        stq(LGT[:, :, t0:t0 + TT], lgs[:, :, :])
                S.barrier()

                with ExitStack() as es:
                    def mkb():
                        Sst = alloc(es, "Sst", [128, 4, 256])
                        qTb = [alloc(es, f"qTb{i}", [128, 4, 128]) for i in range(2)]
                        kTb = [alloc(es, f"kTb{i}", [128, 4, 128]) for i in range(2)]
                        ktb = [alloc(es, f"ktb{i}", [128, 512]) for i in range(2)]
                        vb_ = [alloc(es, f"vb{i}", [128, 1024]) for i in range(2)]
                        ggb = [alloc(es, f"ggb{i}", [128, 512]) for i in range(2)]
                        E = alloc(es, "E", [128, 512])
                        Einv = alloc(es, "Einv", [128, 512])
                        qp = alloc(es, "qp", [128, 4, 128])
                        kp = alloc(es, "kp", [128, 4, 128])
                        e2 = alloc(es, "e2", [128, 512])
                        kpp = alloc(es, "kpp", [128, 512])
                        At = alloc(es, "At", [128, 512])
                        oTs = alloc(es, "oTs", [128, 8, 128])
                        return (Sst, qTb, kTb, ktb, vb_, ggb, E, Einv, qp, kp, e2, kpp, At, oTs)
                    BB = [mkb(), mkb()]
                    def chain(d):
                        (Sst, qTb, kTb, ktb, vb_, ggb, E, Einv, qp, kp, e2, kpp, At, oTs) = BB[d]
                        for (tst, ntl, sidx) in g.seqs:
                            nch = ntl * 2
                            if True:
                                if sidx is None:
                                    ld(Sst[:, :, :], I["st_gla"][d].rearrange("h d v -> d h v"))
                                else:
                                    memset(S.pool, Sst[:, :, :], 0.0)
                                order = list(range(nch)) if d == 0 else list(range(nch - 1, -1, -1))
                                TRI = C("tri_f") if d == 0 else C("tri_b")
                                STR = C("str_f") if d == 0 else C("str_b")
                                MASK4 = C("mask4_f") if d == 0 else C("mask4_b")
                                last = 127 if d == 0 else 0

                                def loads(k):
                                    c = order[k]
                                    t0 = tst * TT + c * 128
                                    i = k % 2
                                    ld(qTb[i][:, :, :], QT[:, :, t0:t0 + 128])
                                    ld(kTb[i][:, :, :], KT_[:, :, t0:t0 + 128])
                                    ld(ktb[i][:, :], Kt[t0:t0 + 128, :])
                                    ld(vb_[i][:, :], Vt[t0:t0 + 128, :])
                                    ld(ggb[i][:, :], Gd[d][t0:t0 + 128, :])
                                loads(0)
                                for k in range(nch):
                                    if k + 1 < nch:
                                        loads(k + 1)
                                    c = order[k]
                                    t0 = tst * TT + c * 128
                                    i = k % 2
                                    qT, kT, kt, v, gg = qTb[i], kTb[i], ktb[i], vb_[i], ggb[i]
                                    ps1 = PS()
                                    mm(ps1[:, :], STR, gg[:, :])
                                    act(e2[:, :], ps1[:, :], AF.Exp)
                                    tt(S.dve, kpp[:, :], kt[:, :], e2[:, :], ALU.mult)
                                    yield
                                    ps2 = PS()
                                    for h in range(4):
                                        mm(ps2[:, h * 128:(h + 1) * 128], gg[:, h * 128:(h + 1) * 128], TRI)
                                    act(E[:, :], ps2[:, :], AF.Exp)
                                    act(Einv[:, :], ps2[:, :], AF.Exp, scale=-1.0)
                                    tt(S.dve, qp.v(qp.t[:, :, :].rearrange("p a b -> p (a b)")), qT.v(qT.t[:, :, :].rearrange("p a b -> p (a b)")), E[:, :], ALU.mult)
                                    tt(S.pool, kp.v(kp.t[:, :, :].rearrange("p a b -> p (a b)")), kT.v(kT.t[:, :, :].rearrange("p a b -> p (a b)")), Einv[:, :], ALU.mult)
                                    yield
                                    ps3 = PS()
                                    for h in range(4):
                                        mm(ps3[:, h * 128:(h + 1) * 128], kp[:, h, :], qp[:, h, :])
                                    tt(S.dve, At[:, :], ps3[:, :], MASK4, ALU.mult)
                                    yield
                                    for half in range(2):
                                        ps4 = PS()
                                        for jq in range(4):
                                            idx = half * 4 + jq
                                            h, vc = idx // 2, idx % 2
                                            mm(ps4[:, jq * 128:(jq + 1) * 128], Sst[:, h, vc * 128:(vc + 1) * 128], qp[:, h, :], start=True, stop=False)
                                            mm(ps4[:, jq * 128:(jq + 1) * 128], v[:, h * 256 + vc * 128:h * 256 + (vc + 1) * 128], At[:, h * 128:(h + 1) * 128], start=False, stop=True)
                                        cp(S.act, oTs.v(oTs.t[:, half * 4:(half + 1) * 4, :].rearrange("p a b -> p (a b)")), ps4[:, :])
                                    stq(Od[d][:, :, t0:t0 + 128], oTs[:, :, :])
                                    yield
                                    for half in range(2):
                                        ps5 = PS()
                                        for jq in range(2):
                                            h = half * 2 + jq
                                            mm(ps5[:, jq * 256:(jq + 1) * 256], kpp[:, h * 128:(h + 1) * 128], v[:, h * 256:(h + 1) * 256])
                                        for jq in range(2):
                                            h = half * 2 + jq
                                            stt(Sst[:, h, :], Sst[:, h, :], E[:, h * 128 + last:h * 128 + last + 1], ps5[:, jq * 256:(jq + 1) * 256], ALU.mult, ALU.add)
                                            yield
                                if sidx is not None:
                                    stq(O["o_gla"][sidx, d].rearrange("h d v -> d h v"), Sst[:, :, :])
                    run_il([chain(0), chain(1)])
                S.barrier()

                with ExitStack() as es:
                    TS = max(n_ for (_, n_, _) in g.seqs) * TT
                    a_ = alloc(es, "lru_a", [128, TS])
                    b_ = alloc(es, "lru_b", [128, TS])
                    hf = alloc(es, "lru_hf", [128, TS])
                    hb = alloc(es, "lru_hb", [128, TS])
                    lgt = alloc(es, "lru_lg", [128, TS])
                    mixo = alloc(es, "lru_mix", [128, TS], BF16)
                    h0c = alloc(es, "lru_h0", [128, 2])
                    hl = alloc(es, "lru_hl", [128, 2])
                    for (tst, ntl, sidx) in g.seqs:
                        T_ = ntl * TT
                        tsl = slice(tst * TT, tst * TT + T_)
                        for n in range(8):
                            ld(a_[:, 0:T_], Ad[0][:, n, tsl])
                            ld(b_[:, 0:T_], Bd[0][:, n, tsl])
                            if sidx is None:
                                ld(h0c[:, :], I["st_lru"][:, n * 128:(n + 1) * 128].rearrange("d p -> p d"), allow_slow_non_contiguous=True)
                                i0, i1 = h0c[:, 0:1], h0c[:, 1:2]
                            else:
                                i0, i1 = 0.0, 0.0

                            def scan(o, x0, x1, ini, rev):
                                sl_ = slice(None, None, -1) if rev else slice(None)
                                rd = [x0.b, x1.b]
                                iv = ini
                                if isinstance(ini, View):
                                    rd.append(ini.b)
                                    iv = ini.ap
                                S.op(S.dve, lambda e: e.tensor_tensor_scan(out=o.b.t[:, 0:T_][:, sl_], data0=x0.b.t[:, 0:T_][:, sl_], data1=x1.b.t[:, 0:T_][:, sl_],
                                                                          initial=iv, op0=ALU.mult, op1=ALU.add), reads=rd, writes=[o.b])
                            scan(hf[:, :], a_[:, :], b_[:, :], i0, False)
                            ld(a_[:, 0:T_], Ad[1][:, n, tsl])
                            ld(b_[:, 0:T_], Bd[1][:, n, tsl])
                            scan(hb[:, :], a_[:, :], b_[:, :], i1, True)
                            ld(lgt[:, 0:T_], LGT[:, n, tsl])
                            if sidx is not None:
                                cp(S.pool, hl[:, 0:1], hf[:, T_ - 1:T_])
                                cp(S.pool, hl[:, 1:2], hb[:, 0:1])
                                stq(O["o_lru"][sidx, :, n * 128:(n + 1) * 128].rearrange("d p -> p d"), hl[:, :], allow_slow_non_contiguous=True)
                            tt(S.pool, hf[:, 0:T_], hf[:, 0:T_], hb[:, 0:T_], ALU.add)
                            tt(S.dve, mixo[:, 0:T_], hf[:, 0:T_], lgt[:, 0:T_], ALU.mult)
                            stq(MIXT[:, 8 + n, tsl], mixo[:, 0:T_])
                S.barrier()

                def front_alloc(es):
                    f = Grp()
                    f.of = alloc(es, "f_of", [128, 8, TT])
                    f.ob = alloc(es, "f_ob", [128, 8, TT])
                    f.rs = alloc(es, "f_rs", [128, 4, TT])
                    f.rt = alloc(es, "f_rt", [128, 8, TT])
                    return f

                def front(f, ti, mixT):
                    t0 = ti * TT
                    ld(f.of[:, :, :], Od[0][:, :, t0:t0 + TT])
                    ld(f.ob[:, :, :], Od[1][:, :, t0:t0 + TT])
                    ld(f.rt[:, :, :], RT[:, :, t0:t0 + TT])
                    ld(mixT[:, 8:16, :], MIXT[:, 8:16, t0:t0 + TT])
                    tt(S.pool, f.of[:, :, :], f.of[:, :, :], f.ob[:, :, :], ALU.add)
                    act(f.ob[:, :, :], f.of[:, :, :], AF.Square)
                    for half in range(2):
                        ps = PS()
                        for jq in range(2):
                            h = half * 2 + jq
                            for vc in range(2):
                                mm(ps[:, jq * TT:(jq + 1) * TT], ones, f.ob[:, h * 2 + vc, :], start=(vc == 0), stop=(vc == 1))
                        act(f.rs.v(f.rs.t[:, half * 2:half * 2 + 2, :].rearrange("p a b -> p (a b)")), ps[:, :], AF.Sqrt, scale=1.0 / 256.0, bias=EPS)
                    recip(f.rs[:, :, :], f.rs[:, :, :])
                    for c in range(8):
                        tt(S.pool, f.of[:, c, :], f.of[:, c, :], f.rs[:, c // 2, :], ALU.mult)
                        stt(mixT[:, c, :], f.of[:, c, :], gcol[:, c:c + 1], f.rt[:, c, :], ALU.mult, ALU.mult)

                phase3(g, l, front_alloc, front)

    def odd_layer(l, j):
        GQT = scr("GQT", [128, 8, TMAX])
        GKT = scr("GKT", [128, 8, TMAX])
        GKt = scr("GKt", [TMAX, 1024])
        GVt = scr("GVt", [TMAX, 1024])
        GZt = scr("GZt", [TMAX, 1024])
        GR = [scr(f"GR{d}", [5, 8, TMAX]) for d in range(2)]
        GC = [scr(f"GC{d}", [2, 8, TMAX]) for d in range(2)]
        DEC = [scr(f"DEC{d}", [TMAX // 64, 8]) for d in range(2)]
        OG = [scr(f"OG{d}", [TMAX, 1024]) for d in range(2)]
        MQT = scr("MQT", [128, 4, TMAX])
        MKT = scr("MKT", [128, 4, TMAX])
        MKt = scr("MKt", [TMAX, 512])
        MVt = scr("MVt", [TMAX, 1024])
        MOt = scr("MOt", [TMAX, 1024])
        LI = [scr(f"LI{d}", [4, TMAX]) for d in range(2)]
        LF = [scr(f"LF{d}", [4, TMAX]) for d in range(2)]
        MR = [scr(f"MR{d}", [6, 4, TMAX]) for d in range(2)]
        MDEC = [scr(f"MDEC{d}", [TMAX // 64, 4]) for d in range(2)]
        OM = [scr(f"OM{d}", [TMAX, 1024]) for d in range(2)]
        ones = C("ones")
        if DBG_STAGE <= -1:
            return
        with ExitStack() as les:
            cwg = alloc(les, "cwg", [128, 96])
            load_cols(cwg[:, :], I["gdn_cw"].rearrange("a b -> (a b)"), 96)
            dtb = alloc(les, "dtb", [8, 2])
            ld(dtb[:, :], I["gdn_dtb"].rearrange("d h -> h d"), allow_slow_non_contiguous=True)
            negA = alloc(les, "negA", [8, 2])
            ld(negA[:, :], I["gdn_alog"].rearrange("d h -> h d"), allow_slow_non_contiguous=True)
            act(negA[:, :], negA[:, :], AF.Exp)
            ts(S.dve, negA[:, :], negA[:, :], -1.0, ALU.mult)
            mlb = alloc(les, "mlb", [4, 4])
            ld(mlb[:, :], I["ml_gb"].rearrange("d g h -> h (d g)"), allow_slow_non_contiguous=True)
            mlbn = alloc(les, "mlbn", [4, 4])
            ts(S.dve, mlbn[:, :], mlb[:, :], -1.0, ALU.mult)
            wmif = alloc(les, "wmif", [128, KC, 16])
            ld(wmif[:, :, :], I["w_in_o"][:, 7200:7216].rearrange("(kc p) f -> p kc f", p=128))
            wmib = alloc(les, "wmib", [128, KC, 16], BF16)
            cp(S.dve, wmib[:, :, :], wmif[:, :, :])

            for g in make_groups(l):
                L, NL = g.L, g.NL
                if DBG_STAGE <= 0:
                    continue
                with ExitStack() as es:
                    bA, bB = alloc(es, "bA", [128, D]), alloc(es, "bB", [128, D])
                    bc_load(bA, l, g.w, 0)
                    bc_load(bB, l, g.w, 1)
                    xt = [alloc(es, f"xt{i}", [128, D]) for i in range(2)]
                    uTs = [alloc(es, f"uT{i}", [128, KC, TT], BF16) for i in range(2)]
                    junk = alloc(es, "junk", [128, D], BF16)
                    ss = alloc(es, "ss", [128, 4])
                    rstd = alloc(es, "rstd", [128, 4])
                    wb = [alloc(es, f"wb{i}", [128, KC, 512], BF16) for i in range(2)]
                    xp = alloc(es, "xp", [128, 4, NL, L + 3])
                    xc2 = alloc(es, "xc2", [128, 4, TT])
                    sqts = [alloc(es, f"sqt{i}", [128, TT]) for i in range(4)]
                    rs1s = [alloc(es, f"rs1{i}", [128, TT]) for i in range(4)]
                    fst = [alloc(es, f"fst{i}", [128, 8, TT]) for i in range(2)]
                    tst = [alloc(es, f"tst{i}", [128, 2, 1024]) for i in range(2)]
                    e8 = alloc(es, "e8", [8, TT])
                    glog = alloc(es, "glog", [8, TT])
                    lb = alloc(es, "lb", [8, TT])
                    rw = alloc(es, "rw", [8, 5, TT])
                    cs = alloc(es, "cs", [8, 2, TT])
                    dc = alloc(es, "dc", [8, 4])
                    g4 = alloc(es, "g4", [4, 2, TT])
                    e4 = alloc(es, "e4", [4, TT])
                    memset(S.pool, xp[:, :, :, :], 0.0)
                    memset(S.pool, rw[:, :, :], 1.0)
                    aps = []
                    for ti in range(g.ntiles):
                        aps += list(Wb[f"in{l}"])
                    ws = wstream(aps, wb)
                    fst_rr = [0]
                    tst_rr = [0]

                    def nfst():
                        fst_rr[0] ^= 1
                        return fst[fst_rr[0]]

                    def ntst():
                        tst_rr[0] ^= 1
                        return tst[tst_rr[0]]

                    def prep(ti):
                        for st in range(2):
                            load_x(g, ti, st, xt[st])
                        for st in range(2):
                            norm_mod_T(xt[st], xt[st], bA, bB, junk, ss, rstd, uTs[ti % 2], st, st)

                    def fm(ps, col, w, wc0, n, uT):
                        for kc in range(KC):
                            mm(ps[0:n, col:col + TT], w[:, kc, wc0:wc0 + n], uT[:, kc, :], start=(kc == 0), stop=(kc == KC - 1))

                    def tmj(ps, w, n, uT, st):
                        for kc in range(KC):
                            mm(ps[:, 0:n], uT[:, kc, st * 128:(st + 1) * 128], w[:, kc, 0:n], start=(kc == 0), stop=(kc == KC - 1))

                    def tm_block_pair(dram, func, scale=None):
                        t_ = ntst()
                        for zb in range(2):
                            w = next(ws)
                            for st in range(2):
                                ps = PS()
                                tmj(ps, w, 512, uT, st)
                                act(t_[:, st, zb * 512:(zb + 1) * 512], ps[:, :], func, scale=scale)
                        stq(dram[t0:t0 + TT, :].rearrange("(s p) f -> p s f", p=128), t_[:, :, :])

                    def to_tm(src, dram):
                        t_ = ntst()
                        for st in range(2):
                            for hq in range(2):
                                ps = PS()
                                for hh in range(4):
                                    h = hq * 4 + hh
                                    tr(ps[:, hh * 128:(hh + 1) * 128], src[:, h, st * 128:(st + 1) * 128], 128)
                                cp(evq(), t_[:, st, hq * 512:(hq + 1) * 512], ps[:, :])
                        stq(dram[t0:t0 + TT, :].rearrange("(s p) f -> p s f", p=128), t_[:, :, :])

                    prep(0)
                    for ti in range(g.ntiles):
                        t0 = ti * TT
                        uT = uTs[ti % 2]
                        for which in range(3):
                            f_ = nfst()
                            for half in range(2):
                                w = next(ws)
                                for cp_ in range(2):
                                    ps = PS()
                                    for hh in range(2):
                                        fm(ps, hh * TT, w, (cp_ * 2 + hh) * 128, 128, uT)
                                    for hh in range(2):
                                        h = half * 4 + cp_ * 2 + hh
                                        n = which * 8 + h
                                        bi_ = (cp_ * 2 + hh) % 4
                                        sqt, rs1 = sqts[bi_], rs1s[bi_]
                                        act(xp[:, bi_, :, 2:2 + L], ps.v(ps.t[:, hh * TT:(hh + 1) * TT].rearrange("p (a b) -> p a b", a=NL)), AF.Copy)
                                        xc = xc2.v(xc2.t[:, bi_, :].rearrange("p (a b) -> p a b", a=NL))
                                        act(xc, xp[:, bi_, :, 2:2 + L], AF.Identity, scale=cwg[:, 2 * 24 + n:2 * 24 + n + 1])
                                        for tap in (0, 1, 3):
                                            stt(xc, xp[:, bi_, :, tap:tap + L], cwg[:, tap * 24 + n:tap * 24 + n + 1], xc, ALU.mult, ALU.add)
                                        if which == 2:
                                            act(f_[:, h, :], xc2[:, bi_, :], AF.Silu)
                                        else:
                                            act(xc2[:, bi_, :], xc2[:, bi_, :], AF.Silu)
                                            act(sqt[:, :], xc2[:, bi_, :], AF.Square)
                                            ps2 = PS()
                                            mm(ps2[:, 0:TT], ones, sqt[:, :])
                                            act(rs1[:, :], ps2[:, 0:TT], AF.Sqrt, bias=EPS)
                                            recip(rs1[:, :], rs1[:, :])
                                            stt(f_[:, h, :], xc2[:, bi_, :], (128.0 ** -0.5) if which == 0 else 1.0, rs1[:, :], ALU.mult, ALU.mult)
                            if which == 0:
                                stq(GQT[:, :, t0:t0 + TT], f_[:, :, :])
                            elif which == 1:
                                stq(GKT[:, :, t0:t0 + TT], f_[:, :, :])
                                to_tm(f_, GKt)
                            else:
                                to_tm(f_, GVt)
                        if DBG_SUB <= 1:
                            continue
                        tm_block_pair(GZt, AF.Silu)
                        if ti + 1 < g.ntiles:
                            prep(ti + 1)
                        if DBG_SUB <= 2:
                            continue
                        w = next(ws)
                        for d in range(2):
                            lastoff = 63 if d == 0 else 0
                            ps = PS()
                            fm(ps, 0, w, d * 8, 8, uT)
                            act(e8[:, :], ps[0:8, 0:TT], AF.Exp, bias=dtb[:, d:d + 1])
                            act(e8[:, :], e8[:, :], AF.Ln, bias=1.0)
                            ts(S.dve, glog[:, :], e8[:, :], negA[:, d:d + 1], ALU.mult)
                            ps = PS()
                            fm(ps, 0, w, 16 + d * 8, 8, uT)
                            act(e8[:, :], ps[0:8, 0:TT], AF.Exp, scale=-1.0)
                            act(e8[:, :], e8[:, :], AF.Ln, bias=1.0)
                            ts(S.pool, lb[:, :], e8[:, :], -1.0, ALU.mult)
                            rst = C("reset")
                            if d == 0:
                                S.op(S.dve, lambda e: e.tensor_tensor_scan(out=rw.t[:, 0, :], data0=rst.ap[0:8, :], data1=glog.t[:, :], initial=0.0, op0=ALU.mult, op1=ALU.add),
                                     reads=[CT, glog], writes=[rw])
                            else:
                                S.op(S.dve, lambda e: e.tensor_tensor_scan(out=rw.t[:, 0, ::-1], data0=rst.ap[0:8, :], data1=glog.t[:, ::-1], initial=0.0, op0=ALU.mult, op1=ALU.add),
                                     reads=[CT, glog], writes=[rw])
                            ts(S.pool, rw[:, 4, :], rw[:, 0, :], -1.0, ALU.mult)
                            tt(S.pool, rw[:, 2, :], lb[:, :], rw[:, 0, :], ALU.subtract)
                            act(cs[:, 0, :], lb[:, :], AF.Exp)
                            for c in range(4):
                                li_ = c * 64 + lastoff
                                ts(S.dve, cs[:, 1, c * 64:(c + 1) * 64], rw[:, 0, c * 64:(c + 1) * 64], -1.0, ALU.mult, rw[:, 0, li_:li_ + 1], ALU.add)
                            act(cs[:, 1, :], cs[:, 1, :], AF.Exp)
                            act(dc[:, :], rw[:, 0, lastoff::64], AF.Exp)
                            stq(GR[d][:, :, t0:t0 + TT].rearrange("k h t -> h k t"), rw[:, :, :])
                            stq(GC[d][:, :, t0:t0 + TT].rearrange("k h t -> h k t"), cs[:, :, :])
                            stq(DEC[d][ti * 4:(ti + 1) * 4, :].rearrange("c h -> h c"), dc[:, :], allow_slow_non_contiguous=True)
                        if DBG_SUB <= 3:
                            continue
                        f_ = nfst()
                        for which in range(2):
                            w = next(ws)
                            for hp in range(2):
                                ps = PS()
                                for hh in range(2):
                                    fm(ps, hh * TT, w, (hp * 2 + hh) * 128, 128, uT)
                                c0 = which * 4 + hp * 2
                                act(f_.v(f_.t[:, c0:c0 + 2, :].rearrange("p a b -> p (a b)")), ps[:, :], AF.Copy, scale=1.0 if which == 0 else 128.0 ** -0.5)
                            stq((MQT, MKT)[which][:, :, t0:t0 + TT], f_[:, which * 4:which * 4 + 4, :])
                            if which == 1:
                                t_ = ntst()
                                for st in range(2):
                                    ps = PS()
                                    tmj(ps, w, 512, uT, st)
                                    act(t_[:, st, 0:512], ps[:, :], AF.Copy, scale=128.0 ** -0.5)
                                stq(MKt[t0:t0 + TT, :].rearrange("(s p) f -> p s f", p=128), t_[:, :, 0:512])
                        if DBG_SUB <= 4:
                            continue
                        tm_block_pair(MVt, AF.Copy)
                        tm_block_pair(MOt, AF.Sigmoid)
                        if DBG_SUB <= 5:
                            continue
                        w = wmib
                        for d in range(2):
                            ps = PS()
                            fm(ps, 0, w, d * 4, 4, uT)
                            act(g4[:, 0, :], ps[0:4, 0:TT], AF.Identity, bias=mlb[:, d * 2:d * 2 + 1])
                            ps = PS()
                            fm(ps, 0, w, 8 + d * 4, 4, uT)
                            act(e4[:, :], ps[0:4, 0:TT], AF.Exp, scale=-1.0, bias=mlbn[:, d * 2 + 1:d * 2 + 2])
                            act(e4[:, :], e4[:, :], AF.Ln, bias=1.0)
                            ts(S.pool, g4[:, 1, :], e4[:, :], -1.0, ALU.mult)
                            stq(LI[d][:, t0:t0 + TT], g4[:, 0, :])
                            stq(LF[d][:, t0:t0 + TT], g4[:, 1, :])
                S.barrier()

                if DBG_STAGE <= 1:
                    continue
                with ExitStack() as es:
                    TS = max(n_ for (_, n_, _) in g.seqs) * TT
                    lf = alloc(es, "m_lf", [4, TS])
                    li = alloc(es, "m_li", [4, TS])
                    mt = alloc(es, "m_m", [4, TS])
                    Ft = alloc(es, "m_F", [4, TS])
                    RWt = alloc(es, "m_RW", [4, TS])
                    WLt = alloc(es, "m_WL", [4, TS])
                    rstt = alloc(es, "m_rst", [4, TS])
                    one4 = alloc(es, "m_one", [4, TS])
                    m0c = alloc(es, "m_m0", [4, 2])
                    dcm = alloc(es, "m_dc", [4, TS // 64])
                    memset(S.pool, rstt[:, :], 1.0)
                    memset(S.pool, rstt[:, 0::64], 0.0)
                    memset(S.pool, one4[:, :], 1.0)
                    for (tst_, ntl, sidx) in g.seqs:
                        T_ = ntl * TT
                        nch = T_ // 64
                        tsl = slice(tst_ * TT, tst_ * TT + T_)
                        if sidx is None:
                            ld(m0c[:, :], I["st_mm"].rearrange("d h -> h d"), allow_slow_non_contiguous=True)
                        else:
                            memset(S.pool, m0c[:, :], 0.0)
                        for d in range(2):
                            ld(lf[:, 0:T_], LF[d][:, tsl])
                            ld(li[:, 0:T_], LI[d][:, tsl])
                            rv = slice(None, None, -1) if d == 1 else slice(None)

                            def sc_(o, a, b, ini, op0, op1, rev0=True):
                                rd = [a.b, b.b]
                                iv = ini
                                if isinstance(ini, View):
                                    rd.append(ini.b)
                                    iv = ini.ap
                                rv0 = rv if rev0 else slice(None)
                                S.op(S.dve, lambda e: e.tensor_tensor_scan(out=o.b.t[:, 0:T_][:, rv], data0=a.b.t[:, 0:T_][:, rv0], data1=b.b.t[:, 0:T_][:, rv],
                                                                          initial=iv, op0=op0, op1=op1), reads=rd, writes=[o.b])
                            sc_(mt[:, :], lf[:, :], li[:, :], m0c[:, d:d + 1], ALU.add, ALU.max)
                            sc_(Ft[:, :], rstt[:, :], lf[:, :], 0.0, ALU.mult, ALU.add, rev0=False)
                            tt(S.pool, li[:, 0:T_], li[:, 0:T_], Ft[:, 0:T_], ALU.subtract)
                            tt(S.pool, Ft[:, 0:T_], Ft[:, 0:T_], mt[:, 0:T_], ALU.subtract)
                            for c in range(nch):
                                lastc = c * 64 + (63 if d == 0 else 0)
                                if d == 0:
                                    mp = m0c[:, 0:1] if c == 0 else mt[:, c * 64 - 1:c * 64]
                                else:
                                    mp = m0c[:, 1:2] if c == nch - 1 else mt[:, (c + 1) * 64:(c + 1) * 64 + 1]
                                ts(S.dve, RWt[:, c * 64:(c + 1) * 64], Ft[:, c * 64:(c + 1) * 64], mp, ALU.add)
                                ts(S.dve, WLt[:, c * 64:(c + 1) * 64], li[:, c * 64:(c + 1) * 64], Ft[:, lastc:lastc + 1], ALU.add)
                            lo = 63 if d == 0 else 0
                            act(dcm[:, 0:nch], RWt[:, lo:T_:64], AF.Exp)
                            if sidx is not None:
                                le = T_ - 1 if d == 0 else 0
                                stq(O["o_mm"][sidx, d, :].rearrange("(h o) -> h o", o=1), mt[:, le:le + 1], allow_slow_non_contiguous=True)
                            stq(MR[d][0, :, tsl], Ft[:, 0:T_])
                            stq(MR[d][1, :, tsl], one4[:, 0:T_])
                            stq(MR[d][2, :, tsl], li[:, 0:T_])
                            stq(MR[d][3, :, tsl], RWt[:, 0:T_])
                            stq(MR[d][4, :, tsl], WLt[:, 0:T_])
                            ts(S.pool, mt[:, 0:T_], mt[:, 0:T_], -1.0, ALU.mult)
                            stq(MR[d][5, :, tsl], mt[:, 0:T_])
                            stq(MDEC[d][tst_ * 4:tst_ * 4 + nch, :].rearrange("c h -> h c"), dcm[:, 0:nch], allow_slow_non_contiguous=True)
                S.barrier()

                if DBG_STAGE <= 2:
                    continue
                with ExitStack() as es:
                    NCHM = max(n_ for (_, n_, _) in g.seqs) * 4
                    def mkb():
                        Sst = alloc(es, "gS", [128, 8, 128])
                        qTb = [alloc(es, f"gq{i}", [128, 8, 64]) for i in range(2)]
                        kTb = [alloc(es, f"gk{i}", [128, 8, 64]) for i in range(2)]
                        ktp = [alloc(es, f"gkt{i}", [128, 4, 128]) for i in range(2)]
                        vtp = [alloc(es, f"gvt{i}", [128, 4, 128]) for i in range(2)]
                        R01 = [alloc(es, f"gr01{i}", [2, 8, 64]) for i in range(2)]
                        R12 = [alloc(es, f"gr12{i}", [2, 8, 64]) for i in range(2)]
                        R34 = [alloc(es, f"gr34{i}", [2, 8, 64]) for i in range(2)]
                        COLS = [alloc(es, f"gcol{i}", [128, 4, 2]) for i in range(2)]
                        G1, G2, G3 = [alloc(es, f"gG{i}", [128, 512]) for i in range(3)]
                        Pb = [alloc(es, f"gP{i}", [128, 512]) for i in range(2)]
                        PTb = [alloc(es, f"gPT{i}", [128, 512]) for i in range(2)]
                        Ttb = [alloc(es, f"gTt{i}", [128, 512]) for i in range(2)]
                        QKT = alloc(es, "gQKT", [128, 512])
                        EB = alloc(es, "gEB", [128, 512])
                        qp = alloc(es, "gqp", [128, 8, 64])
                        kp = alloc(es, "gkp", [128, 8, 64])
                        rr = alloc(es, "grr", [128, 4, 128])
                        vn = alloc(es, "gvn", [128, 4, 128])
                        og = alloc(es, "gog", [128, 4, 128])
                        kpp = alloc(es, "gkpp", [128, 4, 128])
                        DECB = alloc(es, "gDECB", [128, NCHM * 8])
                        drow = alloc(es, "gdrow", [1, NCHM * 8])
                        return (Sst, qTb, kTb, ktp, vtp, R01, R12, R34, COLS, G1, G2, G3, Pb, PTb, Ttb, QKT, EB, qp, kp, rr, vn, og, kpp, DECB, drow)
                    BB = [mkb(), mkb()]
                    id4 = C("ident4")

                    def fl(t):
                        return t.v(t.t[:, :, :].rearrange("p a b -> p (a b)"))

                    def pr(t, hp, rows=128):
                        return t.v(t.t[0:rows, 2 * hp:2 * hp + 2, :].rearrange("p a b -> p (a b)"))

                    def s4(t, hp):
                        return t[:, hp * 128:(hp + 1) * 128]

                    def chain(d):
                        (Sst, qTb, kTb, ktp, vtp, R01, R12, R34, COLS, G1, G2, G3, Pb, PTb, Ttb, QKT, EB, qp, kp, rr, vn, og, kpp, DECB, drow) = BB[d]
                        for (tst_, ntl, sidx) in g.seqs:
                            nch = ntl * 4
                            if True:
                                if sidx is None:
                                    ld(Sst[:, :, :], I["st_gdn"][d].rearrange("h k v -> k h v"))
                                else:
                                    memset(S.pool, Sst[:, :, :], 0.0)
                                c0g = tst_ * 4
                                ld(drow[0:1, 0:nch * 8], DEC[d][c0g:c0g + nch, :].rearrange("(o c) h -> o (c h)", o=1))
                                for q0 in range(0, nch * 8, 512):
                                    q1 = min(q0 + 512, nch * 8)
                                    ps = PS()
                                    mm(ps[:, 0:q1 - q0], ones.b.v(ones.ap[0:1, 0:128]), drow[0:1, q0:q1])
                                    cp(S.dve, DECB[:, q0:q1], ps[:, 0:q1 - q0])
                                order = list(range(nch)) if d == 0 else list(range(nch - 1, -1, -1))
                                NM_, NMT_, NQK_ = (C("nlt"), C("ngt"), C("nge")) if d == 0 else (C("ngt"), C("nlt"), C("nle"))

                                def loads(k):
                                    c = order[k]
                                    t0 = tst_ * TT + c * 64
                                    i = k % 2
                                    ld(qTb[i][:, :, :], GQT[:, :, t0:t0 + 64])
                                    ld(kTb[i][:, :, :], GKT[:, :, t0:t0 + 64])
                                    for h2 in range(2):
                                        ld(ktp[i][h2 * 64:(h2 + 1) * 64, :, :], GKt[t0:t0 + 64, :].rearrange("s (hp h2 d) -> h2 s hp d", hp=4, h2=2)[h2])
                                        ld(vtp[i][h2 * 64:(h2 + 1) * 64, :, :], GVt[t0:t0 + 64, :].rearrange("s (hp h2 d) -> h2 s hp d", hp=4, h2=2)[h2])
                                        for kk in range(2):
                                            ld(COLS[i][h2 * 64:(h2 + 1) * 64, :, kk], GC[d][kk, h2::2, t0:t0 + 64].rearrange("hp s -> s hp"), allow_slow_non_contiguous=True)
                                    ld(R01[i][:, :, :], GR[d][0:2, :, t0:t0 + 64])
                                    ld(R12[i][:, :, :], GR[d][1:3, :, t0:t0 + 64])
                                    ld(R34[i][:, :, :], GR[d][3:5, :, t0:t0 + 64])
                                loads(0)
                                for k in range(nch):
                                    if k + 1 < nch:
                                        loads(k + 1)
                                    c = order[k]
                                    t0 = tst_ * TT + c * 64
                                    i = k % 2
                                    qT, kT, kt, vt, r01, r12, r34, cols = qTb[i], kTb[i], ktp[i], vtp[i], R01[i], R12[i], R34[i], COLS[i]
                                    pE = [PS(), PS(), PS()]
                                    for e_, (msk, la, ra) in enumerate(((NM_, r01, r12), (NMT_, r12, r01), (NQK_, r34, r01))):
                                        mm(pE[e_][:, :], ident, msk, start=True, stop=False)
                                        for hp in range(4):
                                            mm(s4(pE[e_], hp), pr(la, hp, 2), pr(ra, hp, 2), start=False, stop=(hp == 3))
                                    act(G1[:, :], pE[0][:, :], AF.Exp)
                                    act(G2[:, :], pE[1][:, :], AF.Exp)
                                    act(G3[:, :], pE[2][:, :], AF.Exp)
                                    yield
                                    pKK, pKQ = PS(), PS()
                                    for hp in range(4):
                                        mm(s4(pKK, hp), pr(kT, hp), pr(kT, hp))
                                    for hp in range(4):
                                        mm(s4(pKQ, hp), pr(kT, hp), pr(qT, hp))
                                    P, PT, Tt = Pb[0], PTb[0], Ttb[0]
                                    tt(S.dve, P[:, :], pKK[:, :], G1[:, :], ALU.mult)
                                    tt(S.dve, PT[:, :], pKK[:, :], G2[:, :], ALU.mult)
                                    tt(S.dve, QKT[:, :], pKQ[:, :], G3[:, :], ALU.mult)
                                    tt(S.pool, Tt[:, :], id4, PT[:, :], ALU.subtract)
                                    yield
                                    cur = 0
                                    for jl in range(1, 6):
                                        Pn, PTn, Ttn = Pb[1 - cur], PTb[1 - cur], Ttb[1 - cur]
                                        pP = PS()
                                        for hp in range(4):
                                            mm(s4(pP, hp), s4(PT, hp), s4(P, hp))
                                        if jl < 5:
                                            pPT = PS()
                                            for hp in range(4):
                                                mm(s4(pPT, hp), s4(P, hp), s4(PT, hp))
                                        cp(S.act, Pn[:, :], pP[:, :])
                                        if jl < 5:
                                            cp(S.dve, PTn[:, :], pPT[:, :])
                                        yield
                                        pT = PS()
                                        for hp in range(4):
                                            mm(s4(pT, hp), s4(Pn, hp), s4(Tt, hp), start=True, stop=True)
                                        tt(S.dve, Ttn[:, :], pT[:, :], Tt[:, :], ALU.add)
                                        P, PT, Tt = Pn, PTn, Ttn
                                        cur = 1 - cur
                                        yield
                                    yield
                                    pB = PS()
                                    mm(pB[:, :], ones.b.v(ones.ap[0:1, 0:128]), r01.v(r01.t[0:1, :, :].rearrange("p a b -> p (a b)")))
                                    act(EB[:, :], pB[:, :], AF.Exp)
                                    tt(S.dve, fl(qp), fl(qT), EB[:, :], ALU.mult)
                                    tt(S.pool, fl(kp), fl(kT), EB[:, :], ALU.mult)
                                    yield
                                    pS_ = PS()
                                    for hp in range(4):
                                        for h2 in range(2):
                                            h = hp * 2 + h2
                                            mm(pS_[h2 * 64:(h2 + 1) * 64, hp * 128:(hp + 1) * 128], kp[:, h, :], Sst[:, h, :])
                                    tt(S.dve, fl(rr), fl(vt), pS_[:, :], ALU.subtract)
                                    yield
                                    pV = PS()
                                    for hp in range(4):
                                        mm(s4(pV, hp), s4(Tt, hp), rr[:, hp, :])
                                    for hp in range(4):
                                        act(vn[:, hp, :], s4(pV, hp), AF.Copy, scale=cols[:, hp, 0:1])
                                    yield
                                    pO = PS()
                                    for hp in range(4):
                                        for h2 in range(2):
                                            h = hp * 2 + h2
                                            mm(pO[h2 * 64:(h2 + 1) * 64, hp * 128:(hp + 1) * 128], qp[:, h, :], Sst[:, h, :], start=True, stop=False)
                                        mm(s4(pO, hp), s4(QKT, hp), vn[:, hp, :], start=False, stop=True)
                                    cp(evq(), fl(og), pO[:, :])
                                    for h2 in range(2):
                                        stq(OG[d][t0:t0 + 64, :].rearrange("t (hp h2 v) -> h2 t hp v", hp=4, h2=2)[h2], og[h2 * 64:(h2 + 1) * 64, :, :])
                                    yield
                                    tt(S.pool, kpp[:, :, :], kt[:, :, :], cols.v(cols.t[:, :, 1:2].to_broadcast([128, 4, 128])), ALU.mult)
                                    pU = [PS(), PS()]
                                    for h2 in range(2):
                                        for hp in range(4):
                                            mm(pU[h2][:, hp * 128:(hp + 1) * 128], kpp[h2 * 64:(h2 + 1) * 64, hp, :], vn[h2 * 64:(h2 + 1) * 64, hp, :])
                                    tt(S.pool, Sst[:, :, :], Sst[:, :, :], DECB.v(DECB.t[:, c * 8:(c + 1) * 8].unsqueeze(2).to_broadcast([128, 8, 128])), ALU.mult)
                                    for h2 in range(2):
                                        tt(S.dve, Sst.v(Sst.t[:, h2::2, :]), Sst.v(Sst.t[:, h2::2, :]),
                                           pU[h2].v(pU[h2].t[:, :].rearrange("p (a b) -> p a b", a=4)), ALU.add)
                                    yield
                                if sidx is not None:
                                    stq(O["o_gdn"][sidx, d].rearrange("h k v -> k h v"), Sst[:, :, :])
                    run_il([chain(0), chain(1)])
                S.barrier()

                if DBG_STAGE <= 3:
                    continue
                with ExitStack() as es:
                    NCHM = max(n_ for (_, n_, _) in g.seqs) * 4
                    def mkb():
                        Cst = alloc(es, "mC", [128, 4, 257])
                        qTb = [alloc(es, f"mq{i}", [128, 4, 64]) for i in range(2)]
                        kTb = [alloc(es, f"mk{i}", [128, 4, 64]) for i in range(2)]
                        ktp = [alloc(es, f"mkt{i}", [128, 2, 128]) for i in range(2)]
                        vtp = [alloc(es, f"mvt{i}", [128, 2, 257]) for i in range(2)]
                        R01 = [alloc(es, f"mr01{i}", [2, 4, 64]) for i in range(2)]
                        R12 = [alloc(es, f"mr12{i}", [2, 4, 64]) for i in range(2)]
                        R3 = [alloc(es, f"mr3{i}", [1, 4, 64]) for i in range(2)]
                        MCL = [alloc(es, f"mcl{i}", [128, 2, 2]) for i in range(2)]
                        MC = alloc(es, "mMC", [128, 2, 2])
                        Gm = alloc(es, "mG", [128, 256])
                        QKT = alloc(es, "mQKT", [128, 256])
                        WB = alloc(es, "mWB", [128, 256])
                        qp = alloc(es, "mqp", [128, 4, 64])
                        hout = alloc(es, "mh", [128, 2, 256])
                        kpp = alloc(es, "mkpp", [128, 2, 128])
                        dcol = alloc(es, "mdcol", [128, 2])
                        DECB = alloc(es, "mDECB", [128, NCHM * 4])
                        drow = alloc(es, "mdrow", [1, NCHM * 4])
                        return (Cst, qTb, kTb, ktp, vtp, R01, R12, R3, MCL, MC, Gm, QKT, WB, qp, hout, kpp, dcol, DECB, drow)
                    BB = [mkb(), mkb()]
                    for B_ in BB:
                        for i in range(2):
                            memset(S.pool, B_[4][i][:, :, 256:257], 1.0)

                    def fl(t):
                        return t.v(t.t[:, :, :].rearrange("p a b -> p (a b)"))

                    def pr(t, hp, rows=128):
                        return t.v(t.t[0:rows, 2 * hp:2 * hp + 2, :].rearrange("p a b -> p (a b)"))

                    def chain(d):
                        (Cst, qTb, kTb, ktp, vtp, R01, R12, R3, MCL, MC, Gm, QKT, WB, qp, hout, kpp, dcol, DECB, drow) = BB[d]
                        for (tst_, ntl, sidx) in g.seqs:
                            nch = ntl * 4
                            if True:
                                if sidx is None:
                                    ld(Cst[:, :, 0:256], I["st_mc"][d].rearrange("h k v -> k h v"))
                                    ld(Cst[:, :, 256:257], I["st_mn"][d].rearrange("h (k o) -> k h o", o=1), allow_slow_non_contiguous=True)
                                else:
                                    memset(S.pool, Cst[:, :, :], 0.0)
                                c0g = tst_ * 4
                                ld(drow[0:1, 0:nch * 4], MDEC[d][c0g:c0g + nch, :].rearrange("(o c) h -> o (c h)", o=1))
                                ps = PS()
                                mm(ps[:, 0:nch * 4], ones.b.v(ones.ap[0:1, 0:128]), drow[0:1, 0:nch * 4])
                                cp(S.dve, DECB[:, 0:nch * 4], ps[:, 0:nch * 4])
                                order = list(range(nch)) if d == 0 else list(range(nch - 1, -1, -1))
                                NQK_ = C("nge") if d == 0 else C("nle")

                                def loads(k):
                                    c = order[k]
                                    t0 = tst_ * TT + c * 64
                                    i = k % 2
                                    ld(qTb[i][:, :, :], MQT[:, :, t0:t0 + 64])
                                    ld(kTb[i][:, :, :], MKT[:, :, t0:t0 + 64])
                                    for h2 in range(2):
                                        ld(ktp[i][h2 * 64:(h2 + 1) * 64, :, :], MKt[t0:t0 + 64, :].rearrange("s (hp h2 d) -> h2 s hp d", hp=2, h2=2)[h2])
                                        ld(vtp[i][h2 * 64:(h2 + 1) * 64, :, 0:256], MVt[t0:t0 + 64, :].rearrange("s (hp h2 v) -> h2 s hp v", hp=2, h2=2)[h2])
                                        for kk in range(2):
                                            ld(MCL[i][h2 * 64:(h2 + 1) * 64, :, kk], MR[d][4 + kk, h2::2, t0:t0 + 64].rearrange("hp s -> s hp"), allow_slow_non_contiguous=True)
                                    ld(R01[i][:, :, :], MR[d][0:2, :, t0:t0 + 64])
                                    ld(R12[i][:, :, :], MR[d][1:3, :, t0:t0 + 64])
                                    ld(R3[i][:, :, :], MR[d][3:4, :, t0:t0 + 64])
                                loads(0)
                                for k in range(nch):
                                    if k + 1 < nch:
                                        loads(k + 1)
                                    c = order[k]
                                    t0 = tst_ * TT + c * 64
                                    i = k % 2
                                    qT, kT, kt, vt, r01, r12, r3, mcl = qTb[i], kTb[i], ktp[i], vtp[i], R01[i], R12[i], R3[i], MCL[i]
                                    pE = PS()
                                    mm(pE[:, 0:256], ident, NQK_.b.v(NQK_.ap[:, 0:256]), start=True, stop=False)
                                    for hp in range(2):
                                        mm(pE[:, hp * 128:(hp + 1) * 128], pr(r12, hp, 2), pr(r01, hp, 2), start=False, stop=(hp == 1))
                                    act(Gm[:, :], pE[:, 0:256], AF.Exp)
                                    yield
                                    pKQ = PS()
                                    for hp in range(2):
                                        mm(pKQ[:, hp * 128:(hp + 1) * 128], pr(kT, hp), pr(qT, hp))
                                    tt(S.dve, QKT[:, :], pKQ[:, 0:256], Gm[:, :], ALU.mult)
                                    yield
                                    pB = PS()
                                    mm(pB[:, 0:256], ones.b.v(ones.ap[0:1, 0:128]), r3.v(r3.t[0:1, :, :].rearrange("p a b -> p (a b)")))
                                    act(WB[:, :], pB[:, 0:256], AF.Exp)
                                    tt(S.dve, fl(qp), fl(qT), WB[:, :], ALU.mult)
                                    yield
                                    act(MC[:, :, :], mcl[:, :, :], AF.Exp)
                                    for hp in range(2):
                                        pN = PS()
                                        for h2 in range(2):
                                            h = hp * 2 + h2
                                            mm(pN[h2 * 64:(h2 + 1) * 64, 0:257], qp[:, h, :], Cst[:, h, :], start=True, stop=False)
                                        mm(pN[:, 0:257], QKT[:, hp * 128:(hp + 1) * 128], vt[:, hp, :], start=False, stop=True)
                                        act(dcol[:, hp:hp + 1], pN[:, 256:257], AF.Abs)
                                        ts(S.dve, dcol[:, hp:hp + 1], dcol[:, hp:hp + 1], MC[:, hp, 1:2], ALU.max)
                                        recip(dcol[:, hp:hp + 1], dcol[:, hp:hp + 1])
                                        act(hout[:, hp, :], pN[:, 0:256], AF.Copy, scale=dcol[:, hp:hp + 1])
                                        yield
                                    for h2 in range(2):
                                        stq(OM[d][t0:t0 + 64, :].rearrange("t (hp h2 v) -> h2 t hp v", hp=2, h2=2)[h2], hout[h2 * 64:(h2 + 1) * 64, :, :])
                                    yield
                                    tt(S.pool, kpp[:, :, :], kt[:, :, :], MC.v(MC.t[:, :, 0:1].to_broadcast([128, 2, 128])), ALU.mult)
                                    for h in range(4):
                                        hp, h2 = h // 2, h % 2
                                        pU = PS()
                                        mm(pU[:, 0:257], kpp[h2 * 64:(h2 + 1) * 64, hp, :], vt[h2 * 64:(h2 + 1) * 64, hp, :])
                                        stt(Cst[:, h, :], Cst[:, h, :], DECB[:, c * 4 + h:c * 4 + h + 1], pU[:, 0:257], ALU.mult, ALU.add)
                                        yield
                                if sidx is not None:
                                    stq(O["o_mc"][sidx, d].rearrange("h k v -> k h v"), Cst[:, :, 0:256])
                                    stq(O["o_mn"][sidx, d].rearrange("h (k o) -> k h o", o=1), Cst[:, :, 256:257], allow_slow_non_contiguous=True)
                    run_il([chain(0), chain(1)])
                S.barrier()

                if DBG_STAGE <= 4:
                    continue
                def front_alloc(es):
                    f = Grp()
                    f.a = [alloc(es, f"fo_a{i}", [128, 1024]) for i in range(2)]
                    f.m = [alloc(es, f"fo_m{i}", [128, 1024]) for i in range(2)]
                    f.gz = alloc(es, "fo_gz", [128, 1024])
                    f.mo = alloc(es, "fo_mo", [128, 1024])
                    f.mix = alloc(es, "fo_mix", [128, D])
                    f.gg = alloc(es, "fo_gg", [128, 1024])
                    f.mg = alloc(es, "fo_mg", [128, 1024])
                    f.ssq = alloc(es, "fo_ss", [128, 16])
                    ld(f.gg[:, :], I["gdn_g"].rearrange("(o n) -> o n", o=1).partition_broadcast(128).rearrange("p a b -> p (a b)"))
                    ld(f.mg[:, :], I["ml_g"].rearrange("(o n) -> o n", o=1).partition_broadcast(128).rearrange("p a b -> p (a b)"))
                    return f

                def front(f, ti, mixT):
                    for st in range(2):
                        r0 = ti * TT + st * 128
                        ld(f.a[0][:, :], OG[0][r0:r0 + 128, :])
                        ld(f.a[1][:, :], OG[1][r0:r0 + 128, :])
                        ld(f.m[0][:, :], OM[0][r0:r0 + 128, :])
                        ld(f.m[1][:, :], OM[1][r0:r0 + 128, :])
                        ld(f.gz[:, :], GZt[r0:r0 + 128, :])
                        ld(f.mo[:, :], MOt[r0:r0 + 128, :])
                        for (bufs, nh, hd, gt, gate, off, sc0) in ((f.a, 8, 128, f.gg, f.gz, 0, 0), (f.m, 4, 256, f.mg, f.mo, 1024, 8)):
                            o_, sq_ = bufs
                            tt(S.pool, o_[:, :], o_[:, :], sq_[:, :], ALU.add)
                            act(sq_[:, :], o_[:, :], AF.Square)
                            S.op(S.dve, lambda e, sq_=sq_, nh=nh, sc0=sc0: e.tensor_reduce(out=f.ssq.t[:, sc0:sc0 + nh], in_=sq_.t[:, :].rearrange("p (a b) -> p a b", a=nh), axis=AX.X, op=ALU.add),
                                 reads=[sq_], writes=[f.ssq])
                            act(f.ssq[:, sc0:sc0 + nh], f.ssq[:, sc0:sc0 + nh], AF.Sqrt, scale=1.0 / hd, bias=EPS)
                            recip(f.ssq[:, sc0:sc0 + nh], f.ssq[:, sc0:sc0 + nh])
                            tt(S.dve, o_.v(o_.t[:, :].rearrange("p (a b) -> p a b", a=nh)), o_.v(o_.t[:, :].rearrange("p (a b) -> p a b", a=nh)),
                               f.ssq.v(f.ssq.t[:, sc0:sc0 + nh].unsqueeze(2).to_broadcast([128, nh, hd])), ALU.mult)
                            tt(S.pool, o_[:, :], o_[:, :], gt[:, :], ALU.mult)
                            tt(S.pool, f.mix[:, off:off + 1024], o_[:, :], gate[:, :], ALU.mult)
                        transpose_to(mixT, f.mix, st)

                phase3(g, l, front_alloc, front)

    for l in range(depth):
        if DBG_STAGE <= -2:
            break
        if l % 2 == 0:
            even_layer(l, l // 2)
        else:
            odd_layer(l, l // 2)
    S.barrier()
    top.close()
    build.stats = {q.name: q.ninst for q in S.engs}
    return nc


N_CORES = 8
_CACHE = {}


def make_in_maps(inp, NCT=4, NLT=16, n_cores=N_CORES):
    f = lambda a: np.ascontiguousarray(np.asarray(a, dtype=np.float32))
    shared = {
        "w_mod": f(inp["w_mod"]), "b_mod": f(inp["b_mod"]), "norm_g": f(inp["norm_g"]),
        "w_up": f(inp["w_up"]), "w_down": f(inp["w_down"]),
        "w_in_e": f(inp["w_in_e"][0]), "gla_w2": f(inp["gla_gate_w2"][0]), "gla_b": f(inp["gla_gate_b"][0]),
        "gla_g": f(inp["gla_norm_g"][0]), "lru_cw": f(inp["lru_conv_w"][0]), "lru_cb": f(inp["lru_conv_b"][0]),
        "lru_gw": f(inp["lru_gate_w"][0]), "lru_gb": f(inp["lru_gate_b"][0]), "lru_lam": f(inp["lru_lambda"][0]),
        "w_out_e": f(inp["w_out_e"][0]),
        "w_in_o": f(inp["w_in_o"][0]), "gdn_cw": f(inp["gdn_conv_w"][0]), "gdn_alog": f(inp["gdn_a_log"][0]),
        "gdn_dtb": f(inp["gdn_dt_bias"][0]), "gdn_g": f(inp["gdn_norm_g"][0]), "ml_gb": f(inp["mlstm_gate_b"][0]),
        "ml_g": f(inp["mlstm_norm_g"][0]), "w_out_o": f(inp["w_out_o"][0]),
        "consts": CONST_ARR,
    }
    maps = []
    xp_, xs_ = f(inp["x_prompt"]), f(inp["x_sample"])
    for c in range(n_cores):
        b = c % 4
        m = dict(shared)
        m["xc"] = xp_[c * NCT:(c + 1) * NCT].reshape(NCT * TT, D)
        m["xl"] = np.ascontiguousarray(xs_[b, :NLT * TT])
        m["cc"] = np.ascontiguousarray(np.stack([f(inp["c_ctx"]), f(inp["c"])[b]], 0))
        m["st_gla"] = f(inp["state_gla"])[b, 0]
        m["st_lru"] = f(inp["state_lru"])[b, 0]
        m["st_gdn"] = f(inp["state_gdn"])[b, 0]
        m["st_mc"] = f(inp["state_mlstm_c"])[b, 0]
        m["st_mn"] = f(inp["state_mlstm_n"])[b, 0]
        m["st_mm"] = f(inp["state_mlstm_m"])[b, 0]
        maps.append(m)
    return maps


def kernel(**inp):
    if "nc" not in _CACHE:
        _CACHE["nc"] = build()
    nc = _CACHE["nc"]
    maps = make_in_maps(inp)
    res = run_bass_kernel_spmd(nc, maps, core_ids=list(range(N_CORES)))
    R = res.results
    B = 32
    y_ctx = np.concatenate([R[c]["yc"].reshape(4, TT, D) for c in range(8)], 0)
    y_lat = np.stack([R[b]["yl"] for b in range(4)], 0)
    cat = lambda k: np.concatenate([R[c][k] for c in range(8)], 0)
    new_gla = cat("o_gla")[:, None]
    new_lru = cat("o_lru")[:, None]
    new_gdn = cat("o_gdn")[:, None]
    new_c = cat("o_mc")[:, None]
    new_n = cat("o_mn")[:, None]
    new_m = cat("o_mm")[:, None]
    return tuple(np.ascontiguousarray(a.astype(np.float32)) for a in (y_ctx, y_lat, new_gla, new_lru, new_gdn, new_c, new_n, new_m))


def simulate_trace(trace):
    sems = {}
    pos = {k: 0 for k in trace}
    progress = True
    while progress:
        progress = False
        for k, lst in trace.items():
            while pos[k] < len(lst):
                kind, sid, val = lst[pos[k]]
                if kind == "w":
                    if sems.get(sid, 0) >= val:
                        pos[k] += 1
                        progress = True
                    else:
                        break
                else:
                    sems[sid] = sems.get(sid, 0) + val
                    pos[k] += 1
                    progress = True
    stuck = {k: (pos[k], len(lst), lst[pos[k]] if pos[k] < len(lst) else None) for k, lst in trace.items()}
    if all(p == n for (p, n, _) in stuck.values()):
        return None
    return stuck, sems
```

```python
import numpy as np
from contextlib import ExitStack
import concourse.bass as bass
import concourse.mybir as mybir
from concourse.alu_op_type import AluOpType as ALU
from concourse.bass_utils import run_bass_kernel_spmd

F32 = mybir.dt.float32
BF16 = mybir.dt.bfloat16
AF = mybir.ActivationFunctionType
AX = mybir.AxisListType

D = 2048
KC = 16
TT = 256
FF = 8192
EPS = 1e-6
NEG = -30000.0


class View:
    __slots__ = ("b", "ap")

    def __init__(self, b, ap):
        self.b = b
        self.ap = ap


class Buf:
    __slots__ = ("t", "name", "w", "r")

    def __init__(self, t, name=""):
        self.t = t
        self.name = name
        self.w = None
        self.r = {}

    def __getitem__(self, idx):
        return View(self, self.t[idx])

    def v(self, ap):
        return View(self, ap)


SEM_LIMIT = 60000
DBG_STAGE = 99
DBG_DUMP = ()
DBG_SUB2 = 99
TRACE = None
DBG_SUB = 99
DBG_GROUPS = (0, 1)


class EngQ:
    def __init__(self, S, name, eng, self_raw=True):
        self.S = S
        self.name = name
        self.eng = eng
        self.sem = S.nc.alloc_semaphore(name=f"prog_{name}")
        self.semids = {id(self.sem)}
        self.nroll = 0
        self.n = 0
        self.last_ev = None
        self.seen = {}
        self.self_raw = self_raw
        self.ninst = 0

    def roll(self):
        self.nroll += 1
        self.sem = self.S.nc.alloc_semaphore(name=f"prog_{self.name}_{self.nroll}")
        self.semids.add(id(self.sem))
        self.n = 0


class Sched:
    def __init__(self, nc, n_dma_sems=32):
        self.nc = nc
        self.pe = EngQ(self, "pe", nc.tensor, self_raw=False)
        self.act = EngQ(self, "act", nc.scalar)
        self.dve = EngQ(self, "dve", nc.vector)
        self.pool = EngQ(self, "pool", nc.gpsimd)
        self.sp = EngQ(self, "sp", nc.sync)
        self.engs = [self.pe, self.act, self.dve, self.pool, self.sp]
        self.dma_sems = [nc.alloc_semaphore(name=f"dma{i}") for i in range(n_dma_sems)]
        self.dma_val = [0] * n_dma_sems
        self.dma_rr = 0

    def _wait(self, q, ev):
        sem, val = ev
        k = id(sem)
        if q.seen.get(k, 0) < val:
            q.eng.wait_ge(sem, val)
            q.seen[k] = val
            q.ninst += 1
            if TRACE is not None:
                TRACE.setdefault(q.name, []).append(("w", k, val))

    def _deps(self, q, reads, writes):
        for b in reads:
            if b is None:
                continue
            if b.w is not None:
                if id(b.w[0]) in q.semids and not q.self_raw:
                    continue
                self._wait(q, b.w)
        for b in writes:
            if b is None:
                continue
            if b.w is not None and id(b.w[0]) not in q.semids:
                self._wait(q, b.w)
            for ev in b.r.values():
                if id(ev[0]) not in q.semids:
                    self._wait(q, ev)

    def _mark(self, ev, reads, writes):
        k = id(ev[0])
        for b in reads:
            if b is not None:
                b.r[k] = ev
        for b in writes:
            if b is not None:
                b.w = ev
                b.r = {}

    def op(self, q, fn, reads=(), writes=(), inc=True):
        if q.n >= SEM_LIMIT:
            q.roll()
        self._deps(q, reads, writes)
        inst = fn(q.eng)
        q.ninst += 1
        if inc:
            q.n += 1
            inst.then_inc(q.sem, 1)
            ev = (q.sem, q.n)
            q.last_ev = ev
            if TRACE is not None:
                TRACE.setdefault(q.name, []).append(("i", id(q.sem), 1))
        else:
            ev = (q.sem, q.n + 1)
        self._mark(ev, reads, writes)
        return inst

    def dma(self, q, out_ap, in_ap, reads=(), writes=(), **kw):
        self._deps(q, reads, writes)
        i = self.dma_rr
        self.dma_rr = (self.dma_rr + 1) % len(self.dma_sems)
        sem = self.dma_sems[i]
        if self.dma_val[i] > 0:
            self._wait(q, (sem, self.dma_val[i]))
        inst = q.eng.dma_start(out=out_ap, in_=in_ap, **kw)
        self.dma_val[i] += 16
        inst.then_inc(sem, 16)
        if TRACE is not None:
            TRACE.setdefault(q.name, []).append(("i", id(sem), 16))
        q.ninst += 1
        self._mark((sem, self.dma_val[i]), reads, writes)
        return inst

    def barrier(self):
        evs = [q.last_ev for q in self.engs if q.last_ev is not None]
        evs += [(s, v) for s, v in zip(self.dma_sems, self.dma_val) if v > 0]
        for q in self.engs:
            for ev in evs:
                if id(ev[0]) in q.semids:
                    continue
                self._wait(q, ev)


def make_consts():
    c = {}
    i128 = np.arange(128)
    r, cc = i128[:, None], i128[None, :]
    c["ident"] = np.eye(128, dtype=np.float32)
    c["ones"] = np.ones((128, 128), np.float32)
    c["tri_f"] = (r <= cc).astype(np.float32)
    c["tri_b"] = (r >= cc).astype(np.float32)
    c["str_f"] = (r > cc).astype(np.float32)
    c["str_b"] = (r < cc).astype(np.float32)
    same = (r // 64) == (cc // 64)
    for nm, cond in (("nlt", cc < r), ("ngt", cc > r), ("nle", cc <= r), ("nge", cc >= r)):
        m = np.where(same & cond, 0.0, NEG).astype(np.float32)
        c[nm] = np.tile(m, (1, 4))
    c["ident4"] = np.tile(np.eye(128, dtype=np.float32), (1, 4))
    c["mask4_f"] = np.tile(c["tri_f"], (1, 4))
    c["mask4_b"] = np.tile(c["tri_b"], (1, 4))
    rm = np.ones((128, 256), np.float32)
    rm[:, 0::64] = 0.0
    c["reset"] = rm
    names = list(c.keys())
    offs = {}
    o = 0
    for n in names:
        offs[n] = (o, c[n].shape[1])
        o += c[n].shape[1]
    arr = np.concatenate([c[n] for n in names], axis=1)
    return arr, offs


CONST_ARR, CONST_OFFS = make_consts()

EVEN_BLOCKS = [(0, 512), (512, 512), (1024, 512), (1536, 512), (2048, 512), (2560, 512), (3072, 32),
               (3104, 512), (3616, 512), (4128, 512), (4640, 512)]
ODD_BLOCKS = [(i * 512, 512) for i in range(8)] + [(4096, 32), (4128, 512), (4640, 512), (5152, 512), (5664, 512),
                                                   (6176, 512), (6688, 512)]


def build(NCT=4, NLT=16, depth=2):
    nc = bass.Bass("TRN2", target_bir_lowering=False)
    S = Sched(nc)
    TC, TL = NCT * TT, NLT * TT
    TMAX = max(TC, TL)

    def din(name, shape, dt=F32):
        return nc.dram_tensor(name, list(shape), dt, kind="ExternalInput").ap()

    def dout(name, shape, dt=F32):
        return nc.dram_tensor(name, list(shape), dt, kind="ExternalOutput").ap()

    def dscr(name, shape, dt=F32):
        return nc.dram_tensor(name, list(shape), dt, kind="Internal").ap()

    I = {}
    for name, shape in [("xc", [TC, D]), ("xl", [TL, D]), ("cc", [2, D]),
                        ("st_gla", [2, 4, 128, 256]), ("st_lru", [2, 1024]), ("st_gdn", [2, 8, 128, 128]),
                        ("st_mc", [2, 4, 128, 256]), ("st_mn", [2, 4, 128]), ("st_mm", [2, 4]),
                        ("w_mod", [2, D, 6 * D]), ("b_mod", [2, 6 * D]), ("norm_g", [2, 4, D]),
                        ("w_up", [2, D, FF]), ("w_down", [2, FF, D]),
                        ("w_in_e", [D, 5152]), ("gla_w2", [2, 16, 512]), ("gla_b", [2, 512]), ("gla_g", [1024]),
                        ("lru_cw", [4, 1024]), ("lru_cb", [1024]), ("lru_gw", [2, 2, 8, 128, 128]),
                        ("lru_gb", [2, 2, 1024]), ("lru_lam", [2, 1024]), ("w_out_e", [D, D]),
                        ("w_in_o", [D, 7216]), ("gdn_cw", [4, 3072]), ("gdn_alog", [2, 8]), ("gdn_dtb", [2, 8]),
                        ("gdn_g", [1024]), ("ml_gb", [2, 2, 4]), ("ml_g", [1024]), ("w_out_o", [D, D]),
                        ("consts", list(CONST_ARR.shape))]:
        I[name] = din(name, shape)
    O = {}
    for name, shape in [("yc", [TC, D]), ("yl", [TL, D]), ("o_gla", [NCT, 2, 4, 128, 256]), ("o_lru", [NCT, 2, 1024]),
                        ("o_gdn", [NCT, 2, 8, 128, 128]), ("o_mc", [NCT, 2, 4, 128, 256]), ("o_mn", [NCT, 2, 4, 128]),
                        ("o_mm", [NCT, 2, 4])]:
        O[name] = dout(name, shape)

    def mm(o, l, r, start=True, stop=True, inc=None):
        S.op(S.pe, lambda e: e.matmul(o.ap, lhsT=l.ap, rhs=r.ap, start=start, stop=stop), reads=[l.b, r.b], writes=[o.b],
             inc=bool(stop) if inc is None else inc)

    def act(o, i, func, bias=None, scale=None, accum=None):
        kw = {}
        rd = [i.b]
        wr = [o.b]
        if bias is not None:
            if isinstance(bias, View):
                kw["bias"] = bias.ap
                rd.append(bias.b)
            else:
                kw["bias"] = bias
        if scale is not None:
            if isinstance(scale, View):
                kw["scale"] = scale.ap
                rd.append(scale.b)
            else:
                kw["scale"] = scale
        if accum is not None:
            kw["accum_out"] = accum.ap
            wr.append(accum.b)
        S.op(S.act, lambda e: e.activation(out=o.ap, in_=i.ap, func=func, **kw), reads=rd, writes=wr)

    def tt(q, o, a, b, op):
        S.op(q, lambda e: e.tensor_tensor(out=o.ap, in0=a.ap, in1=b.ap, op=op), reads=[a.b, b.b], writes=[o.b])

    def _sc(s, rd):
        if isinstance(s, View):
            rd.append(s.b)
            return s.ap
        return s

    def ts(q, o, a, s1, op0, s2=None, op1=None):
        rd = [a.b]
        a1 = _sc(s1, rd)
        a2 = _sc(s2, rd)
        if op1 is None:
            S.op(q, lambda e: e.tensor_scalar(out=o.ap, in0=a.ap, scalar1=a1, scalar2=None, op0=op0), reads=rd, writes=[o.b])
        else:
            S.op(q, lambda e: e.tensor_scalar(out=o.ap, in0=a.ap, scalar1=a1, scalar2=a2, op0=op0, op1=op1), reads=rd, writes=[o.b])

    def stt(o, a, s, b, op0, op1):
        rd = [a.b, b.b]
        a1 = _sc(s, rd)
        S.op(S.dve, lambda e: e.scalar_tensor_tensor(out=o.ap, in0=a.ap, scalar=a1, in1=b.ap, op0=op0, op1=op1), reads=rd, writes=[o.b])

    def cp(q, o, i):
        if q is S.act:
            act(o, i, AF.Copy)
        else:
            S.op(q, lambda e: e.tensor_copy(out=o.ap, in_=i.ap), reads=[i.b], writes=[o.b])

    def memset(q, o, val):
        S.op(q, lambda e: e.memset(o.ap, val), writes=[o.b])

    def ld(dst, src_ap, q=None, **kw):
        S.dma(q or S.sp, dst.ap, src_ap, writes=[dst.b], **kw)

    def stq(dst_ap, src, q=None, **kw):
        S.dma(q or S.pool, dst_ap, src.ap, reads=[src.b], **kw)

    top = ExitStack()

    uid = [0]

    def alloc(es, name, shape, dt=F32):
        uid[0] += 1
        name = f"{name}_{uid[0]}"
        return Buf(es.enter_context(nc.sbuf_tensor(name, list(shape), dt)), name)

    CT = alloc(top, "consts_sb", list(CONST_ARR.shape))
    ld(CT[:, :], I["consts"])

    def C(name, rows=128):
        o, w = CONST_OFFS[name]
        return CT[0:rows, o:o + w]

    ident = C("ident")
    banks = [Buf(top.enter_context(nc.psum_tensor(f"psb{i}", [128, 512], F32)), f"psb{i}") for i in range(8)]
    bank_rr = [0]

    def PS():
        b = banks[bank_rr[0]]
        bank_rr[0] = (bank_rr[0] + 1) % 8
        return b

    def tr(o, i, n):
        S.op(S.pe, lambda e: e.transpose(o.ap, i.ap, ident.ap[0:n, 0:n]), reads=[i.b, CT], writes=[o.b])

    colstage = alloc(top, "colstage", [128, 128])

    def load_cols(dst, flat_ap, R):
        ld(colstage[0:R, :], flat_ap.rearrange("(r p) -> r p", p=128))
        ps = PS()
        r0 = 0
        while r0 < R:
            n = min(64 if R > 64 else R, R - r0)
            S.op(S.pe, lambda e, r0=r0, n=n: e.transpose(ps.t[:, r0:r0 + n], colstage.t[r0:r0 + n, :], ident.ap[r0:r0 + n, r0:r0 + n]),
                 reads=[colstage, CT], writes=[ps])
            r0 += n
        cp(S.dve, dst, ps[:, 0:R])

    def cast_blocks(src2d, nkc, blocks, prefix):
        outs = []
        for bi, (c0, w) in enumerate(blocks):
            dst = dscr(f"{prefix}{bi}", [128, nkc, w], BF16)
            for k0 in range(0, nkc, 16):
                S.dma(S.pool, dst[:, k0:k0 + 16, :],
                      src2d[k0 * 128:(k0 + 16) * 128, c0:c0 + w].rearrange("(kc p) f -> p kc f", p=128))
            outs.append(dst)
        return outs

    B512 = [(i * 512, 512) for i in range(4)]
    Wb = {}
    Wb["in0"] = cast_blocks(I["w_in_e"], KC, EVEN_BLOCKS, "wine")
    Wb["out0"] = cast_blocks(I["w_out_e"], KC, B512, "woute")
    if depth > 1:
        Wb["in1"] = cast_blocks(I["w_in_o"], KC, ODD_BLOCKS, "wino")
        Wb["out1"] = cast_blocks(I["w_out_o"], KC, B512, "wouto")
    for l in range(depth):
        Wb[f"up{l}"] = cast_blocks(I["w_up"][l], KC, [(i * 512, 512) for i in range(16)], f"wup{l}_")
        Wb[f"dn{l}"] = cast_blocks(I["w_down"][l], 64, B512, f"wdn{l}_")

    MODV = dscr("modv", [depth, 2, 6, D])
    with ExitStack() as es:
        cct = alloc(es, "cct", [2, D])
        sT = alloc(es, "sT", [128, KC, 2])
        wm = [alloc(es, f"wm{i}", [128, KC, 512]) for i in range(2)]
        modt = alloc(es, "modt", [2, 6 * D])
        bmt = [alloc(es, f"bmt{i}", [2, 512]) for i in range(2)]
        ngt = alloc(es, "ngt", [2, D])
        mvt = [alloc(es, f"mvt{i}", [2, D]) for i in range(2)]
        ld(cct[:, :], I["cc"])
        act(cct[:, :], cct[:, :], AF.Silu)
        ps = PS()
        for kc in range(KC):
            tr(ps[:, kc * 2:kc * 2 + 2], cct[:, kc * 128:(kc + 1) * 128], 2)
        cp(S.dve, sT.v(sT.t[:, :, :].rearrange("p a b -> p (a b)")), ps[:, 0:2 * KC])
        for l in range(depth):
            for cb in range(24):
                w = wm[cb % 2]
                bm = bmt[cb % 2]
                ld(w[:, :, :], I["w_mod"][l][:, cb * 512:(cb + 1) * 512].rearrange("(kc p) f -> p kc f", p=128))
                ld(bm[:, :], I["b_mod"][l:l + 1, cb * 512:(cb + 1) * 512].partition_broadcast(2).rearrange("p a b -> p (a b)"))
                ps = PS()
                for kc in range(KC):
                    mm(ps[0:2, :], sT[:, kc, :], w[:, kc, :], start=(kc == 0), stop=(kc == KC - 1))
                tt(S.dve, modt[:, cb * 512:(cb + 1) * 512], ps[0:2, :], bm[:, :], ALU.add)

            def sl(i):
                return modt[:, i * D:(i + 1) * D]
            for i, (kind, mi, gi_) in enumerate((("a", 1, 0), ("c", 0, None), ("m", 2, 1), ("a", 4, 2), ("c", 3, None), ("m", 5, 3))):
                mvb = mvt[i % 2]
                if gi_ is not None:
                    ld(ngt[:, :], I["norm_g"][l, gi_:gi_ + 1, :].partition_broadcast(2).rearrange("p a b -> p (a b)"))
                if kind == "a":
                    stt(mvb[:, :], sl(mi), 1.0, ngt[:, :], ALU.add, ALU.mult)
                elif kind == "m":
                    tt(S.dve, mvb[:, :], sl(mi), ngt[:, :], ALU.mult)
                else:
                    cp(S.dve, mvb[:, :], sl(mi))
                stq(MODV[l, :, i, :], mvb[:, :])
    S.barrier()

    X1 = {0: dscr("x1c", [TC, D]), 1: dscr("x1l", [TL, D])}
    MIXT = dscr("mixt", [128, 16, TMAX], BF16)
    sc = {}

    def scr(name, shape, dt=F32):
        if name not in sc:
            if name in DBG_DUMP:
                sc[name] = dout("s_" + name, shape, dt)
            else:
                sc[name] = dscr("s_" + name, shape, dt)
        return sc[name]

    class Grp:
        pass

    def make_groups(l):
        last = (l == depth - 1)
        gs = []
        for w in DBG_GROUPS:
            g = Grp()
            g.w = w
            g.ntiles = NCT if w == 0 else NLT
            g.T = g.ntiles * TT
            src = (I["xc"], I["xl"])[w] if l == 0 else X1[w]
            dst = (O["yc"], O["yl"])[w] if last else X1[w]
            g.colmajor = (w == 1 and l % 2 == 1)
            g.L = 256 if w == 0 else 64
            g.NL = TT // g.L
            if g.colmajor:
                sv = src.rearrange("(r c) d -> c r d", c=64)
                dv = dst.rearrange("(r c) d -> c r d", c=64)
                g.xsrc = lambda ti, st, sv=sv: [(0, 64, sv[ti * 4 + st * 2]), (64, 128, sv[ti * 4 + st * 2 + 1])]
                g.ydst = lambda ti, st, dv=dv: [(0, 64, dv[ti * 4 + st * 2]), (64, 128, dv[ti * 4 + st * 2 + 1])]
            else:
                g.xsrc = lambda ti, st, src=src: [(0, 128, src[ti * TT + st * 128: ti * TT + (st + 1) * 128, :])]
                g.ydst = lambda ti, st, dst=dst: [(0, 128, dst[ti * TT + st * 128: ti * TT + (st + 1) * 128, :])]
            g.seqs = [(i, 1, i) for i in range(NCT)] if w == 0 else [(0, NLT, None)]
            gs.append(g)
        return gs

    def bc_load(dst, l, w, i):
        ld(dst[:, :], MODV[l, w, i:i + 1, :].partition_broadcast(128).rearrange("p a b -> p (a b)"))

    def wstream(aps, bufs):
        n = len(aps)

        def issue(k):
            b = bufs[k % len(bufs)]
            shp = aps[k].shape
            ld(b[:, 0:shp[1], 0:shp[2]], aps[k])
            return b
        cur = issue(0)
        for k in range(n):
            nxt = issue(k + 1) if k + 1 < n else None
            yield cur
            cur = nxt

    def recip(o, i):
        S.op(S.dve, lambda e: e.reciprocal(out=o.ap, in_=i.ap), reads=[i.b], writes=[o.b])

    def rstd_from_ss(rstd, ss, n):
        act(rstd, ss, AF.Sqrt, scale=1.0 / n, bias=EPS)
        recip(rstd, rstd)

    def run_il(gens):
        active = list(gens)
        while active:
            for g_ in list(active):
                try:
                    next(g_)
                except StopIteration:
                    active.remove(g_)

    evac_rr = [0]

    def evq():
        evac_rr[0] ^= 1
        return S.act if evac_rr[0] else S.dve

    def norm_mod_T(src, dst, bA, bB, junk, ss, rstd, uT, st, col):
        act(junk[:, :], src[:, :], AF.Square, accum=ss[:, col:col + 1])
        rstd_from_ss(rstd[:, col:col + 1], ss[:, col:col + 1], D)
        stt(dst[:, :], src[:, :], rstd[:, col:col + 1], bA[:, :], ALU.mult, ALU.mult)
        tt(S.pool, dst[:, :], dst[:, :], bB[:, :], ALU.add)
        transpose_to(uT, dst, st)

    def transpose_to(uT, src, st):
        for q4 in range(4):
            ps = PS()
            for j in range(4):
                kc = q4 * 4 + j
                tr(ps[:, j * 128:(j + 1) * 128], src[:, kc * 128:(kc + 1) * 128], 128)
            cp(evq(), uT[:, q4 * 4:(q4 + 1) * 4, st * 128:(st + 1) * 128],
               ps.v(ps.t[:, :].rearrange("p (a b) -> p a b", a=4)))

    def load_x(g, ti, st, xt):
        for (p0, p1, ap) in g.xsrc(ti, st):
            ld(xt[p0:p1, :], ap)

    def phase3(g, l, front_alloc, front):
        with ExitStack() as es:
            bX, bY = alloc(es, "bcX", [128, D]), alloc(es, "bcY", [128, D])
            bG1, bA2, bB2, bG3 = bX, bY, bX, bY
            xt = [alloc(es, f"xt{i}", [128, D]) for i in range(2)]
            yb = [alloc(es, f"yb{i}", [128, D]) for i in range(2)]
            junk = alloc(es, "junk", [128, D], BF16)
            ss = alloc(es, "ss", [128, 8])
            rstd = alloc(es, "rstd", [128, 8])
            mixT = alloc(es, "mixT", [128, KC, TT], BF16)
            u2T = alloc(es, "u2T", [128, KC, TT], BF16)
            hidT = alloc(es, "hidT", [128, 64, TT], BF16)
            wb = [alloc(es, f"wb{i}", [128, KC, 512], BF16) for i in range(2)]
            rtmp = [alloc(es, f"rtmp{i}", [128, 512]) for i in range(2)]
            fctx = front_alloc(es)
            aps = []
            for ti in range(g.ntiles):
                aps += list(Wb[f"out{l}"]) + list(Wb[f"up{l}"])
                for fb in range(4):
                    aps += [Wb[f"dn{l}"][fb][:, k0:k0 + 16, :] for k0 in range(0, 64, 16)]
            ws = wstream(aps, wb)
            front(fctx, 0, mixT)
            for ti in range(g.ntiles):
                bc_load(bG1, l, g.w, 2)
                bc_load(bA2, l, g.w, 3)
                for st in range(2):
                    load_x(g, ti, st, xt[st])
                for fb in range(4):
                    w = next(ws)
                    for st in range(2):
                        ps = PS()
                        for kc in range(KC):
                            mm(ps[:, :], mixT[:, kc, st * 128:(st + 1) * 128], w[:, kc, :], start=(kc == 0), stop=(kc == KC - 1))
                        cp(evq(), yb[st][:, fb * 512:(fb + 1) * 512], ps[:, :])
                for st in range(2):
                    act(junk[:, :], yb[st][:, :], AF.Square, accum=ss[:, st:st + 1])
                    rstd_from_ss(rstd[:, st:st + 1], ss[:, st:st + 1], D)
                    stt(yb[st][:, :], yb[st][:, :], rstd[:, st:st + 1], bG1[:, :], ALU.mult, ALU.mult)
                    tt(S.pool, xt[st][:, :], xt[st][:, :], yb[st][:, :], ALU.add)
                bc_load(bB2, l, g.w, 4)
                for st in range(2):
                    norm_mod_T(xt[st], yb[st], bA2, bB2, junk, ss, rstd, u2T, st, 2 + st)
                bc_load(bG3, l, g.w, 5)
                for ub in range(16):
                    w = next(ws)
                    for j in range(0, 4, 2):
                        ps = PS()
                        for jj in range(2):
                            for kc in range(KC):
                                mm(ps[:, jj * 256:(jj + 1) * 256], w[:, kc, (j + jj) * 128:(j + jj + 1) * 128], u2T[:, kc, :],
                                   start=(kc == 0), stop=(kc == KC - 1))
                        rt_ = rtmp[(j // 2) % 2]
                        act(rt_[:, :], ps[:, :], AF.Relu)
                        c0 = ub * 4 + j
                        tt(S.pool, hidT.v(hidT.t[:, c0:c0 + 2, :].rearrange("p a b -> p (a b)")), rt_[:, :], rt_[:, :], ALU.mult)
                if ti + 1 < g.ntiles:
                    front(fctx, ti + 1, mixT)
                for fb in range(4):
                    psd = [PS(), PS()]
                    for g4 in range(4):
                        w = next(ws)
                        for st in range(2):
                            for k in range(16):
                                fc = g4 * 16 + k
                                mm(psd[st][:, :], hidT[:, fc, st * 128:(st + 1) * 128], w[:, k, :], start=(fc == 0), stop=(fc == 63),
                                   inc=(k == 15))
                    for st in range(2):
                        cp(evq(), yb[st][:, fb * 512:(fb + 1) * 512], psd[st][:, :])
                for st in range(2):
                    act(junk[:, :], yb[st][:, :], AF.Square, accum=ss[:, 4 + st:5 + st])
                    rstd_from_ss(rstd[:, 4 + st:5 + st], ss[:, 4 + st:5 + st], D)
                    stt(yb[st][:, :], yb[st][:, :], rstd[:, 4 + st:5 + st], bG3[:, :], ALU.mult, ALU.mult)
                    tt(S.pool, yb[st][:, :], yb[st][:, :], xt[st][:, :], ALU.add)
                    for (p0, p1, ap) in g.ydst(ti, st):
                        stq(ap, yb[st][p0:p1, :])
        S.barrier()

    def even_layer(l, j):
        QT = scr("QT", [128, 4, TMAX])
        KT_ = scr("KT", [128, 4, TMAX])
        Kt = scr("Kt", [TMAX, 512])
        Vt = scr("Vt", [TMAX, 1024])
        Gd = [scr(f"G{d}", [TMAX, 512]) for d in range(2)]
        RT = scr("RT", [128, 8, TMAX])
        Ad = [scr(f"A{d}", [128, 8, TMAX]) for d in range(2)]
        Bd = [scr(f"B{d}", [128, 8, TMAX]) for d in range(2)]
        LGT = scr("LGT", [128, 8, TMAX])
        Od = [scr(f"O{d}", [128, 8, TMAX]) for d in range(2)]
        with ExitStack() as les:
            gcol = alloc(les, "gcol", [128, 8])
            load_cols(gcol[:, :], I["gla_g"], 8)
            cw = alloc(les, "cw", [128, 32])
            load_cols(cw[:, :], I["lru_cw"].rearrange("a b -> (a b)"), 32)
            cb = alloc(les, "cb", [128, 8])
            load_cols(cb[:, :], I["lru_cb"], 8)
            gb = alloc(les, "gb", [128, 32])
            load_cols(gb[:, :], I["lru_gb"].rearrange("a b c -> (a b c)"), 32)
            m8sp = alloc(les, "m8sp", [128, 16])
            load_cols(m8sp[:, :], I["lru_lam"].rearrange("a b -> (a b)"), 16)
            act(m8sp[:, :], m8sp[:, :], AF.Exp, scale=-1.0)
            act(m8sp[:, :], m8sp[:, :], AF.Ln, bias=1.0)
            ts(S.dve, m8sp[:, :], m8sp[:, :], -8.0, ALU.mult)
            ones = C("ones")

            for g in make_groups(l):
                L, NL = g.L, g.NL
                with ExitStack() as es:
                    bA, bB = alloc(es, "bA", [128, D]), alloc(es, "bB", [128, D])
                    bc_load(bA, l, g.w, 0)
                    bc_load(bB, l, g.w, 1)
                    w2 = alloc(es, "w2", [16, 2, 512])
                    ld(w2[:, :, :], I["gla_w2"].rearrange("d r k -> r d k"))
                    gbrow = alloc(es, "gbrow", [1, 2, 512])
                    ld(gbrow[:, :, :], I["gla_b"].rearrange("(o d) k -> o d k", o=1))
                    LW = alloc(es, "LW", [128, 32, 128])
                    ld(LW[:, :, :], I["lru_gw"].rearrange("d g n i j -> i (d g n) j"))
                    xt = [alloc(es, f"xt{i}", [128, D]) for i in range(2)]
                    uTs = [alloc(es, f"uT{i}", [128, KC, TT], BF16) for i in range(2)]
                    junk = alloc(es, "junk", [128, D], BF16)
                    ss = alloc(es, "ss", [128, 4])
                    rstd = alloc(es, "rstd", [128, 4])
                    wb = [alloc(es, f"wb{i}", [128, KC, 512], BF16) for i in range(2)]
                    qTs = alloc(es, "qTs", [128, 4, TT])
                    kTs = qTs
                    kts = alloc(es, "kts", [128, 2, 512])
                    vs = alloc(es, "vs", [128, 2, 1024])
                    rTs = alloc(es, "rTs", [128, 8, TT])
                    lrT = alloc(es, "lrT", [16, 2, TT])
                    e1 = alloc(es, "e1", [128, 512])
                    Gs = alloc(es, "Gs", [128, 2, 2, 512])
                    xp = alloc(es, "xp", [128, 8, NL, L + 3])
                    xcs = alloc(es, "xcs", [128, 8, TT])
                    grs = [alloc(es, f"gr{i}", [128, TT]) for i in range(4)]
                    gis = [alloc(es, f"gi{i}", [128, TT]) for i in range(4)]
                    as_ = alloc(es, "as_", [128, 8, TT])
                    bs_ = alloc(es, "bs_", [128, 8, TT])
                    lgs = rTs
                    memset(S.pool, xp[:, :, :, :], 0.0)
                    aps = []
                    for ti in range(g.ntiles):
                        aps += list(Wb[f"in{l}"])
                    ws = wstream(aps, wb)

                    def prep(ti):
                        for st in range(2):
                            load_x(g, ti, st, xt[st])
                        for st in range(2):
                            norm_mod_T(xt[st], xt[st], bA, bB, junk, ss, rstd, uTs[ti % 2], st, st)

                    def fm(ps, col, w, wc0, n, uT):
                        for kc in range(KC):
                            mm(ps[0:n, col:col + TT], w[:, kc, wc0:wc0 + n], uT[:, kc, :], start=(kc == 0), stop=(kc == KC - 1))

                    def tmj(ps, w, n, uT, st):
                        for kc in range(KC):
                            mm(ps[:, 0:n], uT[:, kc, st * 128:(st + 1) * 128], w[:, kc, 0:n], start=(kc == 0), stop=(kc == KC - 1))

                    prep(0)
                    for ti in range(g.ntiles):
                        t0 = ti * TT
                        uT = uTs[ti % 2]
                        for bi, (dst_s, dram, scl) in enumerate(((qTs, QT, 128.0 ** -0.5), (kTs, KT_, 1.0))):
                            w = next(ws)
                            for hp in range(2):
                                ps = PS()
                                for hh in range(2):
                                    fm(ps, hh * TT, w, (hp * 2 + hh) * 128, 128, uT)
                                act(dst_s.v(dst_s.t[:, hp * 2:hp * 2 + 2, :].rearrange("p a b -> p (a b)")), ps[:, :], AF.Copy, scale=scl)
                            stq(dram[:, :, t0:t0 + TT], dst_s[:, :, :])
                            if bi == 1:
                                for st in range(2):
                                    ps = PS()
                                    tmj(ps, w, 512, uT, st)
                                    cp(S.dve, kts[:, st, :], ps[:, :])
                                stq(Kt[t0:t0 + TT, :].rearrange("(s p) f -> p s f", p=128), kts[:, :, :])
                        for vb in range(2):
                            w = next(ws)
                            for st in range(2):
                                ps = PS()
                                tmj(ps, w, 512, uT, st)
                                cp(evq(), vs[:, st, vb * 512:(vb + 1) * 512], ps[:, :])
                        stq(Vt[t0:t0 + TT, :].rearrange("(s p) f -> p s f", p=128), vs[:, :, :])
                        for rb in range(2):
                            w = next(ws)
                            for cp_ in range(2):
                                ps = PS()
                                for hh in range(2):
                                    fm(ps, hh * TT, w, (cp_ * 2 + hh) * 128, 128, uT)
                                c0 = rb * 4 + cp_ * 2
                                act(rTs.v(rTs.t[:, c0:c0 + 2, :].rearrange("p a b -> p (a b)")), ps[:, :], AF.Silu)
                        stq(RT[:, :, t0:t0 + TT], rTs[:, :, :])
                        if ti + 1 < g.ntiles:
                            prep(ti + 1)
                        w = next(ws)
                        for d in range(2):
                            ps = PS()
                            fm(ps, 0, w, d * 16, 16, uT)
                            cp(S.dve, lrT[:, d, :], ps[0:16, 0:TT])
                        for d in range(2):
                            for st in range(2):
                                ps = PS()
                                mm(ps[:, :], lrT[0:16, d, st * 128:(st + 1) * 128], w2[0:16, d, :], start=True, stop=False)
                                mm(ps[:, :], ones.b.v(ones.ap[0:1, 0:128]), gbrow[0:1, d, :], start=False, stop=True)
                                act(e1[:, :], ps[:, :], AF.Exp, scale=-1.0)
                                act(e1[:, :], e1[:, :], AF.Ln, bias=1.0)
                                ts(S.pool, Gs[:, d, st, :], e1[:, :], -1.0 / 16.0, ALU.mult)
                            stq(Gd[d][t0:t0 + TT, :].rearrange("(s p) f -> p s f", p=128), Gs[:, d, :, :])
                        for xb in range(2):
                            w = next(ws)
                            for cp_ in range(2):
                                ps = PS()
                                for hh in range(2):
                                    fm(ps, hh * TT, w, (cp_ * 2 + hh) * 128, 128, uT)
                                for hh in range(2):
                                    n = xb * 4 + cp_ * 2 + hh
                                    act(xp[:, n, :, 2:2 + L], ps.v(ps.t[:, hh * TT:(hh + 1) * TT].rearrange("p (a b) -> p a b", a=NL)), AF.Copy)
                        for n in range(8):
                            xc = xcs.v(xcs.t[:, n, :].rearrange("p (a b) -> p a b", a=NL))
                            act(xc, xp[:, n, :, 2:2 + L], AF.Identity, scale=cw[:, 2 * 8 + n:2 * 8 + n + 1], bias=cb[:, n:n + 1])
                            for tap in (0, 1, 3):
                                stt(xc, xp[:, n, :, tap:tap + L], cw[:, tap * 8 + n:tap * 8 + n + 1], xc, ALU.mult, ALU.add)
                        for d in range(2):
                            for n0 in range(0, 8, 4):
                                pss = []
                                for n in range(n0, n0 + 4):
                                    ps = PS()
                                    mm(ps[:, 0:TT], LW[:, (d * 2 + 0) * 8 + n, :], xcs[:, n, :])
                                    mm(ps[:, TT:2 * TT], LW[:, (d * 2 + 1) * 8 + n, :], xcs[:, n, :])
                                    pss.append(ps)
                                for n in range(n0, n0 + 4):
                                    i0 = (d * 2 + 0) * 8 + n
                                    i1 = (d * 2 + 1) * 8 + n
                                    act(grs[n % 4][:, :], pss[n - n0][:, 0:TT], AF.Sigmoid, bias=gb[:, i0:i0 + 1])
                                    act(gis[n % 4][:, :], pss[n - n0][:, TT:2 * TT], AF.Sigmoid, bias=gb[:, i1:i1 + 1])
                                for n in range(n0, n0 + 4):
                                    act(as_[:, n, :], grs[n % 4][:, :], AF.Exp, scale=m8sp[:, d * 8 + n:d * 8 + n + 1])
                                    tt(S.pool, grs[n % 4][:, :], as_[:, n, :], as_[:, n, :], ALU.mult)
                                for n in range(n0, n0 + 4):
                                    act(grs[n % 4][:, :], grs[n % 4][:, :], AF.Sqrt, scale=-1.0, bias=1.0)
                                    tt(S.pool, gis[n % 4][:, :], gis[n % 4][:, :], grs[n % 4][:, :], ALU.mult)
                                    tt(S.dve, bs_[:, n, :], gis[n % 4][:, :], xcs[:, n, :], ALU.mult)
                            stq(Ad[d][:, :, t0:t0 + TT], as_[:, :, :])
                            stq(Bd[d][:, :, t0:t0 + TT], bs_[:, :, :])
                        for gb_ in range(2):
                            w = next(ws)
                            for cp_ in range(2):
                                ps = PS()
                                for hh in range(2):
                                    fm(ps, hh * TT, w, (cp_ * 2 + hh) * 128, 128, uT)
                                c0 = gb_ * 4 + cp_ * 2
                                act(lgs.v(lgs.t[:, c0:c0 + 2, :].rearrange("p a b -> p (a b)")), ps[:, :], AF.Gelu_apprx_tanh)
                        stq(LGT[:, :, t0:t0 + TT], lgs[:, :, :])
                S.barrier()

                with ExitStack() as es:
                    def mkb():
                        Sst = alloc(es, "Sst", [128, 4, 256])
                        qTb = [alloc(es, f"qTb{i}", [128, 4, 128]) for i in range(2)]
                        kTb = [alloc(es, f"kTb{i}", [128, 4, 128]) for i in range(2)]
                        ktb = [alloc(es, f"ktb{i}", [128, 512]) for i in range(2)]
                        vb_ = [alloc(es, f"vb{i}", [128, 1024]) for i in range(2)]
                        ggb = [alloc(es, f"ggb{i}", [128, 512]) for i in range(2)]
                        E = alloc(es, "E", [128, 512])
                        Einv = alloc(es, "Einv", [128, 512])
                        qp = alloc(es, "qp", [128, 4, 128])
                        kp = alloc(es, "kp", [128, 4, 128])
                        e2 = alloc(es, "e2", [128, 512])
                        kpp = alloc(es, "kpp", [128, 512])
                        At = alloc(es, "At", [128, 512])
                        oTs = alloc(es, "oTs", [128, 8, 128])
                        return (Sst, qTb, kTb, ktb, vb_, ggb, E, Einv, qp, kp, e2, kpp, At, oTs)
                    BB = [mkb(), mkb()]
                    def chain(d):
                        (Sst, qTb, kTb, ktb, vb_, ggb, E, Einv, qp, kp, e2, kpp, At, oTs) = BB[d]
                        for (tst, ntl, sidx) in g.seqs:
                            nch = ntl * 2
                            if True:
                                if sidx is None:
                                    ld(Sst[:, :, :], I["st_gla"][d].rearrange("h d v -> d h v"))
                                else:
                                    memset(S.pool, Sst[:, :, :], 0.0)
                                order = list(range(nch)) if d == 0 else list(range(nch - 1, -1, -1))
                                TRI = C("tri_f") if d == 0 else C("tri_b")
                                STR = C("str_f") if d == 0 else C("str_b")
                                MASK4 = C("mask4_f") if d == 0 else C("mask4_b")
                                last = 127 if d == 0 else 0

                                def loads(k):
                                    c = order[k]
                                    t0 = tst * TT + c * 128
                                    i = k % 2
                                    ld(qTb[i][:, :, :], QT[:, :, t0:t0 + 128])
                                    ld(kTb[i][:, :, :], KT_[:, :, t0:t0 + 128])
                                    ld(ktb[i][:, :], Kt[t0:t0 + 128, :])
                                    ld(vb_[i][:, :], Vt[t0:t0 + 128, :])
                                    ld(ggb[i][:, :], Gd[d][t0:t0 + 128, :])
                                loads(0)
                                for k in range(nch):
                                    if k + 1 < nch:
                                        loads(k + 1)
                                    c = order[k]
                                    t0 = tst * TT + c * 128
                                    i = k % 2
                                    qT, kT, kt, v, gg = qTb[i], kTb[i], ktb[i], vb_[i], ggb[i]
                                    ps1 = PS()
                                    mm(ps1[:, :], STR, gg[:, :])
                                    act(e2[:, :], ps1[:, :], AF.Exp)
                                    tt(S.dve, kpp[:, :], kt[:, :], e2[:, :], ALU.mult)
                                    yield
                                    ps2 = PS()
                                    for h in range(4):
                                        mm(ps2[:, h * 128:(h + 1) * 128], gg[:, h * 128:(h + 1) * 128], TRI)
                                    act(E[:, :], ps2[:, :], AF.Exp)
                                    act(Einv[:, :], ps2[:, :], AF.Exp, scale=-1.0)
                                    tt(S.dve, qp.v(qp.t[:, :, :].rearrange("p a b -> p (a b)")), qT.v(qT.t[:, :, :].rearrange("p a b -> p (a b)")), E[:, :], ALU.mult)
                                    tt(S.pool, kp.v(kp.t[:, :, :].rearrange("p a b -> p (a b)")), kT.v(kT.t[:, :, :].rearrange("p a b -> p (a b)")), Einv[:, :], ALU.mult)
                                    yield
                                    ps3 = PS()
                                    for h in range(4):
                                        mm(ps3[:, h * 128:(h + 1) * 128], kp[:, h, :], qp[:, h, :])
                                    tt(S.dve, At[:, :], ps3[:, :], MASK4, ALU.mult)
                                    yield
                                    for half in range(2):
                                        ps4 = PS()
                                        for jq in range(4):
                                            idx = half * 4 + jq
                                            h, vc = idx // 2, idx % 2
                                            mm(ps4[:, jq * 128:(jq + 1) * 128], Sst[:, h, vc * 128:(vc + 1) * 128], qp[:, h, :], start=True, stop=False)
                                            mm(ps4[:, jq * 128:(jq + 1) * 128], v[:, h * 256 + vc * 128:h * 256 + (vc + 1) * 128], At[:, h * 128:(h + 1) * 128], start=False, stop=True)
                                        cp(S.act, oTs.v(oTs.t[:, half * 4:(half + 1) * 4, :].rearrange("p a b -> p (a b)")), ps4[:, :])
                                    stq(Od[d][:, :, t0:t0 + 128], oTs[:, :, :])
                                    yield
                                    for half in range(2):
                                        ps5 = PS()
                                        for jq in range(2):
                                            h = half * 2 + jq
                                            mm(ps5[:, jq * 256:(jq + 1) * 256], kpp[:, h * 128:(h + 1) * 128], v[:, h * 256:(h + 1) * 256])
                                        for jq in range(2):
                                            h = half * 2 + jq
                                            stt(Sst[:, h, :], Sst[:, h, :], E[:, h * 128 + last:h * 128 + last + 1], ps5[:, jq * 256:(jq + 1) * 256], ALU.mult, ALU.add)
                                            yield
                                if sidx is not None:
                                    stq(O["o_gla"][sidx, d].rearrange("h d v -> d h v"), Sst[:, :, :])
                    run_il([chain(0), chain(1)])
                S.barrier()

                with ExitStack() as es:
                    TS = max(n_ for (_, n_, _) in g.seqs) * TT
                    a_ = alloc(es, "lru_a", [128, TS])
                    b_ = alloc(es, "lru_b", [128, TS])
                    hf = alloc(es, "lru_hf", [128, TS])
                    hb = alloc(es, "lru_hb", [128, TS])
                    lgt = alloc(es, "lru_lg", [128, TS])
                    mixo = alloc(es, "lru_mix", [128, TS], BF16)
                    h0c = alloc(es, "lru_h0", [128, 2])
                    hl = alloc(es, "lru_hl", [128, 2])
                    for (tst, ntl, sidx) in g.seqs:
                        T_ = ntl * TT
                        tsl = slice(tst * TT, tst * TT + T_)
                        for n in range(8):
                            ld(a_[:, 0:T_], Ad[0][:, n, tsl])
                            ld(b_[:, 0:T_], Bd[0][:, n, tsl])
                            if sidx is None:
                                ld(h0c[:, :], I["st_lru"][:, n * 128:(n + 1) * 128].rearrange("d p -> p d"), allow_slow_non_contiguous=True)
                                i0, i1 = h0c[:, 0:1], h0c[:, 1:2]
                            else:
                                i0, i1 = 0.0, 0.0

                            def scan(o, x0, x1, ini, rev):
                                sl_ = slice(None, None, -1) if rev else slice(None)
                                rd = [x0.b, x1.b]
                                iv = ini
                                if isinstance(ini, View):
                                    rd.append(ini.b)
                                    iv = ini.ap
                                S.op(S.dve, lambda e: e.tensor_tensor_scan(out=o.b.t[:, 0:T_][:, sl_], data0=x0.b.t[:, 0:T_][:, sl_], data1=x1.b.t[:, 0:T_][:, sl_],
                                                                          initial=iv, op0=ALU.mult, op1=ALU.add), reads=rd, writes=[o.b])
                            scan(hf[:, :], a_[:, :], b_[:, :], i0, False)
                            ld(a_[:, 0:T_], Ad[1][:, n, tsl])
                            ld(b_[:, 0:T_], Bd[1][:, n, tsl])
                            scan(hb[:, :], a_[:, :], b_[:, :], i1, True)
                            ld(lgt[:, 0:T_], LGT[:, n, tsl])
                            if sidx is not None:
                                cp(S.pool, hl[:, 0:1], hf[:, T_ - 1:T_])
                                cp(S.pool, hl[:, 1:2], hb[:, 0:1])
                                stq(O["o_lru"][sidx, :, n * 128:(n + 1) * 128].rearrange("d p -> p d"), hl[:, :], allow_slow_non_contiguous=True)
                            tt(S.pool, hf[:, 0:T_], hf[:, 0:T_], hb[:, 0:T_], ALU.add)
                            tt(S.dve, mixo[:, 0:T_], hf[:, 0:T_], lgt[:, 0:T_], ALU.mult)
                            stq(MIXT[:, 8 + n, tsl], mixo[:, 0:T_])
                S.barrier()

                def front_alloc(es):
                    f = Grp()
                    f.of = alloc(es, "f_of", [128, 8, TT])
                    f.ob = alloc(es, "f_ob", [128, 8, TT])
                    f.rs = alloc(es, "f_rs", [128, 4, TT])
                    f.rt = alloc(es, "f_rt", [128, 8, TT])
                    return f

                def front(f, ti, mixT):
                    t0 = ti * TT
                    ld(f.of[:, :, :], Od[0][:, :, t0:t0 + TT])
                    ld(f.ob[:, :, :], Od[1][:, :, t0:t0 + TT])
                    ld(f.rt[:, :, :], RT[:, :, t0:t0 + TT])
                    ld(mixT[:, 8:16, :], MIXT[:, 8:16, t0:t0 + TT])
                    tt(S.pool, f.of[:, :, :], f.of[:, :, :], f.ob[:, :, :], ALU.add)
                    act(f.ob[:, :, :], f.of[:, :, :], AF.Square)
                    for half in range(2):
                        ps = PS()
                        for jq in range(2):
                            h = half * 2 + jq
                            for vc in range(2):
                                mm(ps[:, jq * TT:(jq + 1) * TT], ones, f.ob[:, h * 2 + vc, :], start=(vc == 0), stop=(vc == 1))
                        act(f.rs.v(f.rs.t[:, half * 2:half * 2 + 2, :].rearrange("p a b -> p (a b)")), ps[:, :], AF.Sqrt, scale=1.0 / 256.0, bias=EPS)
                    recip(f.rs[:, :, :], f.rs[:, :, :])
                    for c in range(8):
                        tt(S.pool, f.of[:, c, :], f.of[:, c, :], f.rs[:, c // 2, :], ALU.mult)
                        stt(mixT[:, c, :], f.of[:, c, :], gcol[:, c:c + 1], f.rt[:, c, :], ALU.mult, ALU.mult)

                phase3(g, l, front_alloc, front)

    def odd_layer(l, j):
        GQT = scr("GQT", [128, 8, TMAX])
        GKT = scr("GKT", [128, 8, TMAX])
        GKt = scr("GKt", [TMAX, 1024])
        GVt = scr("GVt", [TMAX, 1024])
        GZt = scr("GZt", [TMAX, 1024])
        GR = [scr(f"GR{d}", [5, 8, TMAX]) for d in range(2)]
        GC = [scr(f"GC{d}", [2, 8, TMAX]) for d in range(2)]
        DEC = [scr(f"DEC{d}", [TMAX // 64, 8]) for d in range(2)]
        OG = [scr(f"OG{d}", [TMAX, 1024]) for d in range(2)]
        MQT = scr("MQT", [128, 4, TMAX])
        MKT = scr("MKT", [128, 4, TMAX])
        MKt = scr("MKt", [TMAX, 512])
        MVt = scr("MVt", [TMAX, 1024])
        MOt = scr("MOt", [TMAX, 1024])
        LI = [scr(f"LI{d}", [4, TMAX]) for d in range(2)]
        LF = [scr(f"LF{d}", [4, TMAX]) for d in range(2)]
        MR = [scr(f"MR{d}", [6, 4, TMAX]) for d in range(2)]
        MDEC = [scr(f"MDEC{d}", [TMAX // 64, 4]) for d in range(2)]
        OM = [scr(f"OM{d}", [TMAX, 1024]) for d in range(2)]
        ones = C("ones")
        if DBG_STAGE <= -1:
            return
        with ExitStack() as les:
            cwg = alloc(les, "cwg", [128, 96])
            load_cols(cwg[:, :], I["gdn_cw"].rearrange("a b -> (a b)"), 96)
            dtb = alloc(les, "dtb", [8, 2])
            ld(dtb[:, :], I["gdn_dtb"].rearrange("d h -> h d"), allow_slow_non_contiguous=True)
            negA = alloc(les, "negA", [8, 2])
            ld(negA[:, :], I["gdn_alog"].rearrange("d h -> h d"), allow_slow_non_contiguous=True)
            act(negA[:, :], negA[:, :], AF.Exp)
            ts(S.dve, negA[:, :], negA[:, :], -1.0, ALU.mult)
            mlb = alloc(les, "mlb", [4, 4])
            ld(mlb[:, :], I["ml_gb"].rearrange("d g h -> h (d g)"), allow_slow_non_contiguous=True)
            mlbn = alloc(les, "mlbn", [4, 4])
            ts(S.dve, mlbn[:, :], mlb[:, :], -1.0, ALU.mult)
            wmif = alloc(les, "wmif", [128, KC, 16])
            ld(wmif[:, :, :], I["w_in_o"][:, 7200:7216].rearrange("(kc p) f -> p kc f", p=128))
            wmib = alloc(les, "wmib", [128, KC, 16], BF16)
            cp(S.dve, wmib[:, :, :], wmif[:, :, :])

            for g in make_groups(l):
                L, NL = g.L, g.NL
                if DBG_STAGE <= 0:
                    continue
                with ExitStack() as es:
                    bA, bB = alloc(es, "bA", [128, D]), alloc(es, "bB", [128, D])
                    bc_load(bA, l, g.w, 0)
                    bc_load(bB, l, g.w, 1)
                    xt = [alloc(es, f"xt{i}", [128, D]) for i in range(2)]
                    uTs = [alloc(es, f"uT{i}", [128, KC, TT], BF16) for i in range(2)]
                    junk = alloc(es, "junk", [128, D], BF16)
                    ss = alloc(es, "ss", [128, 4])
                    rstd = alloc(es, "rstd", [128, 4])
                    wb = [alloc(es, f"wb{i}", [128, KC, 512], BF16) for i in range(2)]
                    xp = alloc(es, "xp", [128, 4, NL, L + 3])
                    xc2 = alloc(es, "xc2", [128, 4, TT])
                    sqts = [alloc(es, f"sqt{i}", [128, TT]) for i in range(4)]
                    rs1s = [alloc(es, f"rs1{i}", [128, TT]) for i in range(4)]
                    fst = [alloc(es, f"fst{i}", [128, 8, TT]) for i in range(2)]
                    tst = [alloc(es, f"tst{i}", [128, 2, 1024]) for i in range(2)]
                    e8 = alloc(es, "e8", [8, TT])
                    glog = alloc(es, "glog", [8, TT])
                    lb = alloc(es, "lb", [8, TT])
                    rw = alloc(es, "rw", [8, 5, TT])
                    cs = alloc(es, "cs", [8, 2, TT])
                    dc = alloc(es, "dc", [8, 4])
                    g4 = alloc(es, "g4", [4, 2, TT])
                    e4 = alloc(es, "e4", [4, TT])
                    memset(S.pool, xp[:, :, :, :], 0.0)
                    memset(S.pool, rw[:, :, :], 1.0)
                    aps = []
                    for ti in range(g.ntiles):
                        aps += list(Wb[f"in{l}"])
                    ws = wstream(aps, wb)
                    fst_rr = [0]
                    tst_rr = [0]

                    def nfst():
                        fst_rr[0] ^= 1
                        return fst[fst_rr[0]]

                    def ntst():
                        tst_rr[0] ^= 1
                        return tst[tst_rr[0]]

                    def prep(ti):
                        for st in range(2):
                            load_x(g, ti, st, xt[st])
                        for st in range(2):
                            norm_mod_T(xt[st], xt[st], bA, bB, junk, ss, rstd, uTs[ti % 2], st, st)

                    def fm(ps, col, w, wc0, n, uT):
                        for kc in range(KC):
                            mm(ps[0:n, col:col + TT], w[:, kc, wc0:wc0 + n], uT[:, kc, :], start=(kc == 0), stop=(kc == KC - 1))

                    def tmj(ps, w, n, uT, st):
                        for kc in range(KC):
                            mm(ps[:, 0:n], uT[:, kc, st * 128:(st + 1) * 128], w[:, kc, 0:n], start=(kc == 0), stop=(kc == KC - 1))

                    def tm_block_pair(dram, func, scale=None):
                        t_ = ntst()
                        for zb in range(2):
                            w = next(ws)
                            for st in range(2):
                                ps = PS()
                                tmj(ps, w, 512, uT, st)
                                act(t_[:, st, zb * 512:(zb + 1) * 512], ps[:, :], func, scale=scale)
                        stq(dram[t0:t0 + TT, :].rearrange("(s p) f -> p s f", p=128), t_[:, :, :])

                    def to_tm(src, dram):
                        t_ = ntst()
                        for st in range(2):
                            for hq in range(2):
                                ps = PS()
                                for hh in range(4):
                                    h = hq * 4 + hh
                                    tr(ps[:, hh * 128:(hh + 1) * 128], src[:, h, st * 128:(st + 1) * 128], 128)
                                cp(evq(), t_[:, st, hq * 512:(hq + 1) * 512], ps[:, :])
                        stq(dram[t0:t0 + TT, :].rearrange("(s p) f -> p s f", p=128), t_[:, :, :])

                    prep(0)
                    for ti in range(g.ntiles):
                        t0 = ti * TT
                        uT = uTs[ti % 2]
                        for which in range(3):
                            f_ = nfst()
                            pend = []

                            def flush():
                                for (h_, bi2, sc_) in pend:
                                    ps2 = PS()
                                    mm(ps2[:, 0:TT], ones, sqts[bi2][:, :])
                                    act(rs1s[bi2][:, :], ps2[:, 0:TT], AF.Sqrt, bias=EPS)
                                    recip(rs1s[bi2][:, :], rs1s[bi2][:, :])
                                    stt(f_[:, h_, :], xc2[:, bi2, :], sc_, rs1s[bi2][:, :], ALU.mult, ALU.mult)
                                pend.clear()
                            for half in range(2):
                                w = next(ws)
                                for cp_ in range(2):
                                    ps = PS()
                                    for hh in range(2):
                                        fm(ps, hh * TT, w, (cp_ * 2 + hh) * 128, 128, uT)
                                    flush()
                                    for hh in range(2):
                                        h = half * 4 + cp_ * 2 + hh
                                        n = which * 8 + h
                                        bi_ = (cp_ * 2 + hh) % 4
                                        act(xp[:, bi_, :, 2:2 + L], ps.v(ps.t[:, hh * TT:(hh + 1) * TT].rearrange("p (a b) -> p a b", a=NL)), AF.Copy)
                                        xc = xc2.v(xc2.t[:, bi_, :].rearrange("p (a b) -> p a b", a=NL))
                                        act(xc, xp[:, bi_, :, 2:2 + L], AF.Identity, scale=cwg[:, 2 * 24 + n:2 * 24 + n + 1])
                                        for tap in (0, 1, 3):
                                            stt(xc, xp[:, bi_, :, tap:tap + L], cwg[:, tap * 24 + n:tap * 24 + n + 1], xc, ALU.mult, ALU.add)
                                        if which == 2:
                                            act(f_[:, h, :], xc2[:, bi_, :], AF.Silu)
                                        else:
                                            act(xc2[:, bi_, :], xc2[:, bi_, :], AF.Silu)
                                            act(sqts[bi_][:, :], xc2[:, bi_, :], AF.Square)
                                            pend.append((h, bi_, (128.0 ** -0.5) if which == 0 else 1.0))
                            flush()
                            if which == 0:
                                stq(GQT[:, :, t0:t0 + TT], f_[:, :, :])
                            elif which == 1:
                                stq(GKT[:, :, t0:t0 + TT], f_[:, :, :])
                                to_tm(f_, GKt)
                            else:
                                to_tm(f_, GVt)
                        if DBG_SUB <= 1:
                            continue
                        tm_block_pair(GZt, AF.Silu)
                        if ti + 1 < g.ntiles:
                            prep(ti + 1)
                        if DBG_SUB <= 2:
                            continue
                        w = next(ws)
                        for d in range(2):
                            lastoff = 63 if d == 0 else 0
                            ps = PS()
                            fm(ps, 0, w, d * 8, 8, uT)
                            act(e8[:, :], ps[0:8, 0:TT], AF.Exp, bias=dtb[:, d:d + 1])
                            act(e8[:, :], e8[:, :], AF.Ln, bias=1.0)
                            ts(S.dve, glog[:, :], e8[:, :], negA[:, d:d + 1], ALU.mult)
                            ps = PS()
                            fm(ps, 0, w, 16 + d * 8, 8, uT)
                            act(e8[:, :], ps[0:8, 0:TT], AF.Exp, scale=-1.0)
                            act(e8[:, :], e8[:, :], AF.Ln, bias=1.0)
                            ts(S.pool, lb[:, :], e8[:, :], -1.0, ALU.mult)
                            rst = C("reset")
                            if d == 0:
                                S.op(S.dve, lambda e: e.tensor_tensor_scan(out=rw.t[:, 0, :], data0=rst.ap[0:8, :], data1=glog.t[:, :], initial=0.0, op0=ALU.mult, op1=ALU.add),
                                     reads=[CT, glog], writes=[rw])
                            else:
                                S.op(S.dve, lambda e: e.tensor_tensor_scan(out=rw.t[:, 0, ::-1], data0=rst.ap[0:8, :], data1=glog.t[:, ::-1], initial=0.0, op0=ALU.mult, op1=ALU.add),
                                     reads=[CT, glog], writes=[rw])
                            ts(S.pool, rw[:, 4, :], rw[:, 0, :], -1.0, ALU.mult)
                            tt(S.pool, rw[:, 2, :], lb[:, :], rw[:, 0, :], ALU.subtract)
                            act(cs[:, 0, :], lb[:, :], AF.Exp)
                            for c in range(4):
                                li_ = c * 64 + lastoff
                                ts(S.dve, cs[:, 1, c * 64:(c + 1) * 64], rw[:, 0, c * 64:(c + 1) * 64], -1.0, ALU.mult, rw[:, 0, li_:li_ + 1], ALU.add)
                            act(cs[:, 1, :], cs[:, 1, :], AF.Exp)
                            act(dc[:, :], rw[:, 0, lastoff::64], AF.Exp)
                            stq(GR[d][:, :, t0:t0 + TT].rearrange("k h t -> h k t"), rw[:, :, :])
                            stq(GC[d][:, :, t0:t0 + TT].rearrange("k h t -> h k t"), cs[:, :, :])
                            stq(DEC[d][ti * 4:(ti + 1) * 4, :].rearrange("c h -> h c"), dc[:, :], allow_slow_non_contiguous=True)
                        if DBG_SUB <= 3:
                            continue
                        f_ = nfst()
                        for which in range(2):
                            w = next(ws)
                            for hp in range(2):
                                ps = PS()
                                for hh in range(2):
                                    fm(ps, hh * TT, w, (hp * 2 + hh) * 128, 128, uT)
                                c0 = which * 4 + hp * 2
                                act(f_.v(f_.t[:, c0:c0 + 2, :].rearrange("p a b -> p (a b)")), ps[:, :], AF.Copy, scale=1.0 if which == 0 else 128.0 ** -0.5)
                            stq((MQT, MKT)[which][:, :, t0:t0 + TT], f_[:, which * 4:which * 4 + 4, :])
                            if which == 1:
                                t_ = ntst()
                                for st in range(2):
                                    ps = PS()
                                    tmj(ps, w, 512, uT, st)
                                    act(t_[:, st, 0:512], ps[:, :], AF.Copy, scale=128.0 ** -0.5)
                                stq(MKt[t0:t0 + TT, :].rearrange("(s p) f -> p s f", p=128), t_[:, :, 0:512])
                        if DBG_SUB <= 4:
                            continue
                        tm_block_pair(MVt, AF.Copy)
                        tm_block_pair(MOt, AF.Sigmoid)
                        if DBG_SUB <= 5:
                            continue
                        w = wmib
                        for d in range(2):
                            ps = PS()
                            fm(ps, 0, w, d * 4, 4, uT)
                            act(g4[:, 0, :], ps[0:4, 0:TT], AF.Identity, bias=mlb[:, d * 2:d * 2 + 1])
                            ps = PS()
                            fm(ps, 0, w, 8 + d * 4, 4, uT)
                            act(e4[:, :], ps[0:4, 0:TT], AF.Exp, scale=-1.0, bias=mlbn[:, d * 2 + 1:d * 2 + 2])
                            act(e4[:, :], e4[:, :], AF.Ln, bias=1.0)
                            ts(S.pool, g4[:, 1, :], e4[:, :], -1.0, ALU.mult)
                            stq(LI[d][:, t0:t0 + TT], g4[:, 0, :])
                            stq(LF[d][:, t0:t0 + TT], g4[:, 1, :])
                S.barrier()

                if DBG_STAGE <= 1:
                    continue
                with ExitStack() as es:
                    TS = max(n_ for (_, n_, _) in g.seqs) * TT
                    lf = alloc(es, "m_lf", [4, TS])
                    li = alloc(es, "m_li", [4, TS])
                    mt = alloc(es, "m_m", [4, TS])
                    Ft = alloc(es, "m_F", [4, TS])
                    RWt = alloc(es, "m_RW", [4, TS])
                    WLt = alloc(es, "m_WL", [4, TS])
                    rstt = alloc(es, "m_rst", [4, TS])
                    one4 = alloc(es, "m_one", [4, TS])
                    m0c = alloc(es, "m_m0", [4, 2])
                    dcm = alloc(es, "m_dc", [4, TS // 64])
                    memset(S.pool, rstt[:, :], 1.0)
                    memset(S.pool, rstt[:, 0::64], 0.0)
                    memset(S.pool, one4[:, :], 1.0)
                    for (tst_, ntl, sidx) in g.seqs:
                        T_ = ntl * TT
                        nch = T_ // 64
                        tsl = slice(tst_ * TT, tst_ * TT + T_)
                        if sidx is None:
                            ld(m0c[:, :], I["st_mm"].rearrange("d h -> h d"), allow_slow_non_contiguous=True)
                        else:
                            memset(S.pool, m0c[:, :], 0.0)
                        for d in range(2):
                            ld(lf[:, 0:T_], LF[d][:, tsl])
                            ld(li[:, 0:T_], LI[d][:, tsl])
                            rv = slice(None, None, -1) if d == 1 else slice(None)

                            def sc_(o, a, b, ini, op0, op1, rev0=True):
                                rd = [a.b, b.b]
                                iv = ini
                                if isinstance(ini, View):
                                    rd.append(ini.b)
                                    iv = ini.ap
                                rv0 = rv if rev0 else slice(None)
                                S.op(S.dve, lambda e: e.tensor_tensor_scan(out=o.b.t[:, 0:T_][:, rv], data0=a.b.t[:, 0:T_][:, rv0], data1=b.b.t[:, 0:T_][:, rv],
                                                                          initial=iv, op0=op0, op1=op1), reads=rd, writes=[o.b])
                            sc_(mt[:, :], lf[:, :], li[:, :], m0c[:, d:d + 1], ALU.add, ALU.max)
                            sc_(Ft[:, :], rstt[:, :], lf[:, :], 0.0, ALU.mult, ALU.add, rev0=False)
                            tt(S.pool, li[:, 0:T_], li[:, 0:T_], Ft[:, 0:T_], ALU.subtract)
                            tt(S.pool, Ft[:, 0:T_], Ft[:, 0:T_], mt[:, 0:T_], ALU.subtract)
                            for c in range(nch):
                                lastc = c * 64 + (63 if d == 0 else 0)
                                if d == 0:
                                    mp = m0c[:, 0:1] if c == 0 else mt[:, c * 64 - 1:c * 64]
                                else:
                                    mp = m0c[:, 1:2] if c == nch - 1 else mt[:, (c + 1) * 64:(c + 1) * 64 + 1]
                                ts(S.dve, RWt[:, c * 64:(c + 1) * 64], Ft[:, c * 64:(c + 1) * 64], mp, ALU.add)
                                ts(S.dve, WLt[:, c * 64:(c + 1) * 64], li[:, c * 64:(c + 1) * 64], Ft[:, lastc:lastc + 1], ALU.add)
                            lo = 63 if d == 0 else 0
                            act(dcm[:, 0:nch], RWt[:, lo:T_:64], AF.Exp)
                            if sidx is not None:
                                le = T_ - 1 if d == 0 else 0
                                stq(O["o_mm"][sidx, d, :].rearrange("(h o) -> h o", o=1), mt[:, le:le + 1], allow_slow_non_contiguous=True)
                            stq(MR[d][0, :, tsl], Ft[:, 0:T_])
                            stq(MR[d][1, :, tsl], one4[:, 0:T_])
                            stq(MR[d][2, :, tsl], li[:, 0:T_])
                            stq(MR[d][3, :, tsl], RWt[:, 0:T_])
                            stq(MR[d][4, :, tsl], WLt[:, 0:T_])
                            ts(S.pool, mt[:, 0:T_], mt[:, 0:T_], -1.0, ALU.mult)
                            stq(MR[d][5, :, tsl], mt[:, 0:T_])
                            stq(MDEC[d][tst_ * 4:tst_ * 4 + nch, :].rearrange("c h -> h c"), dcm[:, 0:nch], allow_slow_non_contiguous=True)
                S.barrier()

                if DBG_STAGE <= 2:
                    continue
                with ExitStack() as es:
                    NCHM = max(n_ for (_, n_, _) in g.seqs) * 4
                    def mkb():
                        Sst = alloc(es, "gS", [128, 8, 128])
                        qTb = [alloc(es, f"gq{i}", [128, 8, 64]) for i in range(2)]
                        kTb = [alloc(es, f"gk{i}", [128, 8, 64]) for i in range(2)]
                        ktp = [alloc(es, f"gkt{i}", [128, 4, 128]) for i in range(2)]
                        vtp = [alloc(es, f"gvt{i}", [128, 4, 128]) for i in range(2)]
                        R01 = [alloc(es, f"gr01{i}", [2, 8, 64]) for i in range(2)]
                        R12 = [alloc(es, f"gr12{i}", [2, 8, 64]) for i in range(2)]
                        R34 = [alloc(es, f"gr34{i}", [2, 8, 64]) for i in range(2)]
                        COLS = [alloc(es, f"gcol{i}", [128, 4, 2]) for i in range(2)]
                        G1, G2, G3 = [alloc(es, f"gG{i}", [128, 512]) for i in range(3)]
                        Pb = [alloc(es, f"gP{i}", [128, 512]) for i in range(2)]
                        PTb = [alloc(es, f"gPT{i}", [128, 512]) for i in range(2)]
                        Ttb = [alloc(es, f"gTt{i}", [128, 512]) for i in range(2)]
                        QKT = alloc(es, "gQKT", [128, 512])
                        EB = alloc(es, "gEB", [128, 512])
                        qp = alloc(es, "gqp", [128, 8, 64])
                        kp = alloc(es, "gkp", [128, 8, 64])
                        rr = alloc(es, "grr", [128, 4, 128])
                        vn = alloc(es, "gvn", [128, 4, 128])
                        og = alloc(es, "gog", [128, 4, 128])
                        kpp = alloc(es, "gkpp", [128, 4, 128])
                        DECB = alloc(es, "gDECB", [128, NCHM * 8])
                        drow = alloc(es, "gdrow", [1, NCHM * 8])
                        return (Sst, qTb, kTb, ktp, vtp, R01, R12, R34, COLS, G1, G2, G3, Pb, PTb, Ttb, QKT, EB, qp, kp, rr, vn, og, kpp, DECB, drow)
                    BB = [mkb(), mkb()]
                    id4 = C("ident4")

                    def fl(t):
                        return t.v(t.t[:, :, :].rearrange("p a b -> p (a b)"))

                    def pr(t, hp, rows=128):
                        return t.v(t.t[0:rows, 2 * hp:2 * hp + 2, :].rearrange("p a b -> p (a b)"))

                    def s4(t, hp):
                        return t[:, hp * 128:(hp + 1) * 128]

                    def chain(d):
                        (Sst, qTb, kTb, ktp, vtp, R01, R12, R34, COLS, G1, G2, G3, Pb, PTb, Ttb, QKT, EB, qp, kp, rr, vn, og, kpp, DECB, drow) = BB[d]
                        for (tst_, ntl, sidx) in g.seqs:
                            nch = ntl * 4
                            if True:
                                if sidx is None:
                                    ld(Sst[:, :, :], I["st_gdn"][d].rearrange("h k v -> k h v"))
                                else:
                                    memset(S.pool, Sst[:, :, :], 0.0)
                                c0g = tst_ * 4
                                ld(drow[0:1, 0:nch * 8], DEC[d][c0g:c0g + nch, :].rearrange("(o c) h -> o (c h)", o=1))
                                for q0 in range(0, nch * 8, 512):
                                    q1 = min(q0 + 512, nch * 8)
                                    ps = PS()
                                    mm(ps[:, 0:q1 - q0], ones.b.v(ones.ap[0:1, 0:128]), drow[0:1, q0:q1])
                                    cp(S.dve, DECB[:, q0:q1], ps[:, 0:q1 - q0])
                                order = list(range(nch)) if d == 0 else list(range(nch - 1, -1, -1))
                                NM_, NMT_, NQK_ = (C("nlt"), C("ngt"), C("nge")) if d == 0 else (C("ngt"), C("nlt"), C("nle"))

                                def loads(k):
                                    c = order[k]
                                    t0 = tst_ * TT + c * 64
                                    i = k % 2
                                    ld(qTb[i][:, :, :], GQT[:, :, t0:t0 + 64])
                                    ld(kTb[i][:, :, :], GKT[:, :, t0:t0 + 64])
                                    for h2 in range(2):
                                        ld(ktp[i][h2 * 64:(h2 + 1) * 64, :, :], GKt[t0:t0 + 64, :].rearrange("s (hp h2 d) -> h2 s hp d", hp=4, h2=2)[h2])
                                        ld(vtp[i][h2 * 64:(h2 + 1) * 64, :, :], GVt[t0:t0 + 64, :].rearrange("s (hp h2 d) -> h2 s hp d", hp=4, h2=2)[h2])
                                        for kk in range(2):
                                            ld(COLS[i][h2 * 64:(h2 + 1) * 64, :, kk], GC[d][kk, h2::2, t0:t0 + 64].rearrange("hp s -> s hp"), allow_slow_non_contiguous=True)
                                    ld(R01[i][:, :, :], GR[d][0:2, :, t0:t0 + 64])
                                    ld(R12[i][:, :, :], GR[d][1:3, :, t0:t0 + 64])
                                    ld(R34[i][:, :, :], GR[d][3:5, :, t0:t0 + 64])
                                loads(0)
                                for k in range(nch):
                                    if k + 1 < nch:
                                        loads(k + 1)
                                    c = order[k]
                                    t0 = tst_ * TT + c * 64
                                    i = k % 2
                                    qT, kT, kt, vt, r01, r12, r34, cols = qTb[i], kTb[i], ktp[i], vtp[i], R01[i], R12[i], R34[i], COLS[i]
                                    pE = [PS(), PS(), PS()]
                                    for e_, (msk, la, ra) in enumerate(((NM_, r01, r12), (NMT_, r12, r01), (NQK_, r34, r01))):
                                        mm(pE[e_][:, :], ident, msk, start=True, stop=False)
                                        for hp in range(4):
                                            mm(s4(pE[e_], hp), pr(la, hp, 2), pr(ra, hp, 2), start=False, stop=(hp == 3))
                                    act(G1[:, :], pE[0][:, :], AF.Exp)
                                    act(G2[:, :], pE[1][:, :], AF.Exp)
                                    act(G3[:, :], pE[2][:, :], AF.Exp)
                                    yield
                                    pKK, pKQ = PS(), PS()
                                    for hp in range(4):
                                        mm(s4(pKK, hp), pr(kT, hp), pr(kT, hp))
                                    for hp in range(4):
                                        mm(s4(pKQ, hp), pr(kT, hp), pr(qT, hp))
                                    P, PT, Tt = Pb[0], PTb[0], Ttb[0]
                                    tt(S.dve, P[:, :], pKK[:, :], G1[:, :], ALU.mult)
                                    tt(S.dve, PT[:, :], pKK[:, :], G2[:, :], ALU.mult)
                                    tt(S.dve, QKT[:, :], pKQ[:, :], G3[:, :], ALU.mult)
                                    tt(S.pool, Tt[:, :], id4, PT[:, :], ALU.subtract)
                                    yield
                                    cur = 0
                                    for jl in range(1, 6):
                                        Pn, PTn, Ttn = Pb[1 - cur], PTb[1 - cur], Ttb[1 - cur]
                                        pP = PS()
                                        for hp in range(4):
                                            mm(s4(pP, hp), s4(PT, hp), s4(P, hp))
                                        if jl < 5:
                                            pPT = PS()
                                            for hp in range(4):
                                                mm(s4(pPT, hp), s4(P, hp), s4(PT, hp))
                                        cp(S.act, Pn[:, :], pP[:, :])
                                        if jl < 5:
                                            cp(S.dve, PTn[:, :], pPT[:, :])
                                        yield
                                        pT = PS()
                                        for hp in range(4):
                                            mm(s4(pT, hp), s4(Pn, hp), s4(Tt, hp), start=True, stop=True)
                                        tt(S.dve, Ttn[:, :], pT[:, :], Tt[:, :], ALU.add)
                                        P, PT, Tt = Pn, PTn, Ttn
                                        cur = 1 - cur
                                        yield
                                    yield
                                    pB = PS()
                                    mm(pB[:, :], ones.b.v(ones.ap[0:1, 0:128]), r01.v(r01.t[0:1, :, :].rearrange("p a b -> p (a b)")))
                                    act(EB[:, :], pB[:, :], AF.Exp)
                                    tt(S.dve, fl(qp), fl(qT), EB[:, :], ALU.mult)
                                    tt(S.pool, fl(kp), fl(kT), EB[:, :], ALU.mult)
                                    yield
                                    pS_ = PS()
                                    for hp in range(4):
                                        for h2 in range(2):
                                            h = hp * 2 + h2
                                            mm(pS_[h2 * 64:(h2 + 1) * 64, hp * 128:(hp + 1) * 128], kp[:, h, :], Sst[:, h, :])
                                    tt(S.dve, fl(rr), fl(vt), pS_[:, :], ALU.subtract)
                                    yield
                                    pV = PS()
                                    for hp in range(4):
                                        mm(s4(pV, hp), s4(Tt, hp), rr[:, hp, :])
                                    for hp in range(4):
                                        act(vn[:, hp, :], s4(pV, hp), AF.Copy, scale=cols[:, hp, 0:1])
                                    yield
                                    pO = PS()
                                    for hp in range(4):
                                        for h2 in range(2):
                                            h = hp * 2 + h2
                                            mm(pO[h2 * 64:(h2 + 1) * 64, hp * 128:(hp + 1) * 128], qp[:, h, :], Sst[:, h, :], start=True, stop=False)
                                        mm(s4(pO, hp), s4(QKT, hp), vn[:, hp, :], start=False, stop=True)
                                    cp(evq(), fl(og), pO[:, :])
                                    for h2 in range(2):
                                        stq(OG[d][t0:t0 + 64, :].rearrange("t (hp h2 v) -> h2 t hp v", hp=4, h2=2)[h2], og[h2 * 64:(h2 + 1) * 64, :, :])
                                    yield
                                    tt(S.pool, kpp[:, :, :], kt[:, :, :], cols.v(cols.t[:, :, 1:2].to_broadcast([128, 4, 128])), ALU.mult)
                                    pU = [PS(), PS()]
                                    for h2 in range(2):
                                        for hp in range(4):
                                            mm(pU[h2][:, hp * 128:(hp + 1) * 128], kpp[h2 * 64:(h2 + 1) * 64, hp, :], vn[h2 * 64:(h2 + 1) * 64, hp, :])
                                    tt(S.pool, Sst[:, :, :], Sst[:, :, :], DECB.v(DECB.t[:, c * 8:(c + 1) * 8].unsqueeze(2).to_broadcast([128, 8, 128])), ALU.mult)
                                    for h2 in range(2):
                                        tt(S.dve, Sst.v(Sst.t[:, h2::2, :]), Sst.v(Sst.t[:, h2::2, :]),
                                           pU[h2].v(pU[h2].t[:, :].rearrange("p (a b) -> p a b", a=4)), ALU.add)
                                    yield
                                if sidx is not None:
                                    stq(O["o_gdn"][sidx, d].rearrange("h k v -> k h v"), Sst[:, :, :])
                    run_il([chain(0), chain(1)])
                S.barrier()

                if DBG_STAGE <= 3:
                    continue
                with ExitStack() as es:
                    NCHM = max(n_ for (_, n_, _) in g.seqs) * 4
                    def mkb():
                        Cst = alloc(es, "mC", [128, 4, 257])
                        qTb = [alloc(es, f"mq{i}", [128, 4, 64]) for i in range(2)]
                        kTb = [alloc(es, f"mk{i}", [128, 4, 64]) for i in range(2)]
                        ktp = [alloc(es, f"mkt{i}", [128, 2, 128]) for i in range(2)]
                        vtp = [alloc(es, f"mvt{i}", [128, 2, 257]) for i in range(2)]
                        R01 = [alloc(es, f"mr01{i}", [2, 4, 64]) for i in range(2)]
                        R12 = [alloc(es, f"mr12{i}", [2, 4, 64]) for i in range(2)]
                        R3 = [alloc(es, f"mr3{i}", [1, 4, 64]) for i in range(2)]
                        MCL = [alloc(es, f"mcl{i}", [128, 2, 2]) for i in range(2)]
                        MC = alloc(es, "mMC", [128, 2, 2])
                        Gm = alloc(es, "mG", [128, 256])
                        QKT = alloc(es, "mQKT", [128, 256])
                        WB = alloc(es, "mWB", [128, 256])
                        qp = alloc(es, "mqp", [128, 4, 64])
                        hout = alloc(es, "mh", [128, 2, 256])
                        kpp = alloc(es, "mkpp", [128, 2, 128])
                        dcol = alloc(es, "mdcol", [128, 2])
                        DECB = alloc(es, "mDECB", [128, NCHM * 4])
                        drow = alloc(es, "mdrow", [1, NCHM * 4])
                        return (Cst, qTb, kTb, ktp, vtp, R01, R12, R3, MCL, MC, Gm, QKT, WB, qp, hout, kpp, dcol, DECB, drow)
                    BB = [mkb(), mkb()]
                    for B_ in BB:
                        for i in range(2):
                            memset(S.pool, B_[4][i][:, :, 256:257], 1.0)

                    def fl(t):
                        return t.v(t.t[:, :, :].rearrange("p a b -> p (a b)"))

                    def pr(t, hp, rows=128):
                        return t.v(t.t[0:rows, 2 * hp:2 * hp + 2, :].rearrange("p a b -> p (a b)"))

                    def chain(d):
                        (Cst, qTb, kTb, ktp, vtp, R01, R12, R3, MCL, MC, Gm, QKT, WB, qp, hout, kpp, dcol, DECB, drow) = BB[d]
                        for (tst_, ntl, sidx) in g.seqs:
                            nch = ntl * 4
                            if True:
                                if sidx is None:
                                    ld(Cst[:, :, 0:256], I["st_mc"][d].rearrange("h k v -> k h v"))
                                    ld(Cst[:, :, 256:257], I["st_mn"][d].rearrange("h (k o) -> k h o", o=1), allow_slow_non_contiguous=True)
                                else:
                                    memset(S.pool, Cst[:, :, :], 0.0)
                                c0g = tst_ * 4
                                ld(drow[0:1, 0:nch * 4], MDEC[d][c0g:c0g + nch, :].rearrange("(o c) h -> o (c h)", o=1))
                                ps = PS()
                                mm(ps[:, 0:nch * 4], ones.b.v(ones.ap[0:1, 0:128]), drow[0:1, 0:nch * 4])
                                cp(S.dve, DECB[:, 0:nch * 4], ps[:, 0:nch * 4])
                                order = list(range(nch)) if d == 0 else list(range(nch - 1, -1, -1))
                                NQK_ = C("nge") if d == 0 else C("nle")

                                def loads(k):
                                    c = order[k]
                                    t0 = tst_ * TT + c * 64
                                    i = k % 2
                                    ld(qTb[i][:, :, :], MQT[:, :, t0:t0 + 64])
                                    ld(kTb[i][:, :, :], MKT[:, :, t0:t0 + 64])
                                    for h2 in range(2):
                                        ld(ktp[i][h2 * 64:(h2 + 1) * 64, :, :], MKt[t0:t0 + 64, :].rearrange("s (hp h2 d) -> h2 s hp d", hp=2, h2=2)[h2])
                                        ld(vtp[i][h2 * 64:(h2 + 1) * 64, :, 0:256], MVt[t0:t0 + 64, :].rearrange("s (hp h2 v) -> h2 s hp v", hp=2, h2=2)[h2])
                                        for kk in range(2):
                                            ld(MCL[i][h2 * 64:(h2 + 1) * 64, :, kk], MR[d][4 + kk, h2::2, t0:t0 + 64].rearrange("hp s -> s hp"), allow_slow_non_contiguous=True)
                                    ld(R01[i][:, :, :], MR[d][0:2, :, t0:t0 + 64])
                                    ld(R12[i][:, :, :], MR[d][1:3, :, t0:t0 + 64])
                                    ld(R3[i][:, :, :], MR[d][3:4, :, t0:t0 + 64])
                                loads(0)
                                for k in range(nch):
                                    if k + 1 < nch:
                                        loads(k + 1)
                                    c = order[k]
                                    t0 = tst_ * TT + c * 64
                                    i = k % 2
                                    qT, kT, kt, vt, r01, r12, r3, mcl = qTb[i], kTb[i], ktp[i], vtp[i], R01[i], R12[i], R3[i], MCL[i]
                                    pE = PS()
                                    mm(pE[:, 0:256], ident, NQK_.b.v(NQK_.ap[:, 0:256]), start=True, stop=False)
                                    for hp in range(2):
                                        mm(pE[:, hp * 128:(hp + 1) * 128], pr(r12, hp, 2), pr(r01, hp, 2), start=False, stop=(hp == 1))
                                    act(Gm[:, :], pE[:, 0:256], AF.Exp)
                                    yield
                                    pKQ = PS()
                                    for hp in range(2):
                                        mm(pKQ[:, hp * 128:(hp + 1) * 128], pr(kT, hp), pr(qT, hp))
                                    tt(S.dve, QKT[:, :], pKQ[:, 0:256], Gm[:, :], ALU.mult)
                                    yield
                                    pB = PS()
                                    mm(pB[:, 0:256], ones.b.v(ones.ap[0:1, 0:128]), r3.v(r3.t[0:1, :, :].rearrange("p a b -> p (a b)")))
                                    act(WB[:, :], pB[:, 0:256], AF.Exp)
                                    tt(S.dve, fl(qp), fl(qT), WB[:, :], ALU.mult)
                                    yield
                                    act(MC[:, :, :], mcl[:, :, :], AF.Exp)
                                    for hp in range(2):
                                        pN = PS()
                                        for h2 in range(2):
                                            h = hp * 2 + h2
                                            mm(pN[h2 * 64:(h2 + 1) * 64, 0:257], qp[:, h, :], Cst[:, h, :], start=True, stop=False)
                                        mm(pN[:, 0:257], QKT[:, hp * 128:(hp + 1) * 128], vt[:, hp, :], start=False, stop=True)
                                        act(dcol[:, hp:hp + 1], pN[:, 256:257], AF.Abs)
                                        ts(S.dve, dcol[:, hp:hp + 1], dcol[:, hp:hp + 1], MC[:, hp, 1:2], ALU.max)
                                        recip(dcol[:, hp:hp + 1], dcol[:, hp:hp + 1])
                                        act(hout[:, hp, :], pN[:, 0:256], AF.Copy, scale=dcol[:, hp:hp + 1])
                                        yield
                                    for h2 in range(2):
                                        stq(OM[d][t0:t0 + 64, :].rearrange("t (hp h2 v) -> h2 t hp v", hp=2, h2=2)[h2], hout[h2 * 64:(h2 + 1) * 64, :, :])
                                    yield
                                    tt(S.pool, kpp[:, :, :], kt[:, :, :], MC.v(MC.t[:, :, 0:1].to_broadcast([128, 2, 128])), ALU.mult)
                                    for h in range(4):
                                        hp, h2 = h // 2, h % 2
                                        pU = PS()
                                        mm(pU[:, 0:257], kpp[h2 * 64:(h2 + 1) * 64, hp, :], vt[h2 * 64:(h2 + 1) * 64, hp, :])
                                        stt(Cst[:, h, :], Cst[:, h, :], DECB[:, c * 4 + h:c * 4 + h + 1], pU[:, 0:257], ALU.mult, ALU.add)
                                        yield
                                if sidx is not None:
                                    stq(O["o_mc"][sidx, d].rearrange("h k v -> k h v"), Cst[:, :, 0:256])
                                    stq(O["o_mn"][sidx, d].rearrange("h (k o) -> k h o", o=1), Cst[:, :, 256:257], allow_slow_non_contiguous=True)
                    run_il([chain(0), chain(1)])
                S.barrier()

                if DBG_STAGE <= 4:
                    continue
                def front_alloc(es):
                    f = Grp()
                    f.a = [alloc(es, f"fo_a{i}", [128, 1024]) for i in range(2)]
                    f.m = [alloc(es, f"fo_m{i}", [128, 1024]) for i in range(2)]
                    f.gz = alloc(es, "fo_gz", [128, 1024])
                    f.mo = alloc(es, "fo_mo", [128, 1024])
                    f.mix = alloc(es, "fo_mix", [128, D])
                    f.gg = alloc(es, "fo_gg", [128, 1024])
                    f.mg = alloc(es, "fo_mg", [128, 1024])
                    f.ssq = alloc(es, "fo_ss", [128, 16])
                    ld(f.gg[:, :], I["gdn_g"].rearrange("(o n) -> o n", o=1).partition_broadcast(128).rearrange("p a b -> p (a b)"))
                    ld(f.mg[:, :], I["ml_g"].rearrange("(o n) -> o n", o=1).partition_broadcast(128).rearrange("p a b -> p (a b)"))
                    return f

                def front(f, ti, mixT):
                    for st in range(2):
                        r0 = ti * TT + st * 128
                        ld(f.a[0][:, :], OG[0][r0:r0 + 128, :])
                        ld(f.a[1][:, :], OG[1][r0:r0 + 128, :])
                        ld(f.m[0][:, :], OM[0][r0:r0 + 128, :])
                        ld(f.m[1][:, :], OM[1][r0:r0 + 128, :])
                        ld(f.gz[:, :], GZt[r0:r0 + 128, :])
                        ld(f.mo[:, :], MOt[r0:r0 + 128, :])
                        for (bufs, nh, hd, gt, gate, off, sc0) in ((f.a, 8, 128, f.gg, f.gz, 0, 0), (f.m, 4, 256, f.mg, f.mo, 1024, 8)):
                            o_, sq_ = bufs
                            tt(S.pool, o_[:, :], o_[:, :], sq_[:, :], ALU.add)
                            act(sq_[:, :], o_[:, :], AF.Square)
                            S.op(S.dve, lambda e, sq_=sq_, nh=nh, sc0=sc0: e.tensor_reduce(out=f.ssq.t[:, sc0:sc0 + nh], in_=sq_.t[:, :].rearrange("p (a b) -> p a b", a=nh), axis=AX.X, op=ALU.add),
                                 reads=[sq_], writes=[f.ssq])
                            act(f.ssq[:, sc0:sc0 + nh], f.ssq[:, sc0:sc0 + nh], AF.Sqrt, scale=1.0 / hd, bias=EPS)
                            recip(f.ssq[:, sc0:sc0 + nh], f.ssq[:, sc0:sc0 + nh])
                            tt(S.dve, o_.v(o_.t[:, :].rearrange("p (a b) -> p a b", a=nh)), o_.v(o_.t[:, :].rearrange("p (a b) -> p a b", a=nh)),
                               f.ssq.v(f.ssq.t[:, sc0:sc0 + nh].unsqueeze(2).to_broadcast([128, nh, hd])), ALU.mult)
                            tt(S.pool, o_[:, :], o_[:, :], gt[:, :], ALU.mult)
                            tt(S.pool, f.mix[:, off:off + 1024], o_[:, :], gate[:, :], ALU.mult)
                        transpose_to(mixT, f.mix, st)

                phase3(g, l, front_alloc, front)

    for l in range(depth):
        if DBG_STAGE <= -2:
            break
        if l % 2 == 0:
            even_layer(l, l // 2)
        else:
            odd_layer(l, l // 2)
    S.barrier()
    top.close()
    build.stats = {q.name: q.ninst for q in S.engs}
    return nc


N_CORES = 8
_CACHE = {}


def make_in_maps(inp, NCT=4, NLT=16, n_cores=N_CORES):
    f = lambda a: np.ascontiguousarray(np.asarray(a, dtype=np.float32))
    shared = {
        "w_mod": f(inp["w_mod"]), "b_mod": f(inp["b_mod"]), "norm_g": f(inp["norm_g"]),
        "w_up": f(inp["w_up"]), "w_down": f(inp["w_down"]),
        "w_in_e": f(inp["w_in_e"][0]), "gla_w2": f(inp["gla_gate_w2"][0]), "gla_b": f(inp["gla_gate_b"][0]),
        "gla_g": f(inp["gla_norm_g"][0]), "lru_cw": f(inp["lru_conv_w"][0]), "lru_cb": f(inp["lru_conv_b"][0]),
        "lru_gw": f(inp["lru_gate_w"][0]), "lru_gb": f(inp["lru_gate_b"][0]), "lru_lam": f(inp["lru_lambda"][0]),
        "w_out_e": f(inp["w_out_e"][0]),
        "w_in_o": f(inp["w_in_o"][0]), "gdn_cw": f(inp["gdn_conv_w"][0]), "gdn_alog": f(inp["gdn_a_log"][0]),
        "gdn_dtb": f(inp["gdn_dt_bias"][0]), "gdn_g": f(inp["gdn_norm_g"][0]), "ml_gb": f(inp["mlstm_gate_b"][0]),
        "ml_g": f(inp["mlstm_norm_g"][0]), "w_out_o": f(inp["w_out_o"][0]),
        "consts": CONST_ARR,
    }
    maps = []
    xp_, xs_ = f(inp["x_prompt"]), f(inp["x_sample"])
    for c in range(n_cores):
        b = c % 4
        m = dict(shared)
        m["xc"] = xp_[c * NCT:(c + 1) * NCT].reshape(NCT * TT, D)
        m["xl"] = np.ascontiguousarray(xs_[b, :NLT * TT])
        m["cc"] = np.ascontiguousarray(np.stack([f(inp["c_ctx"]), f(inp["c"])[b]], 0))
        m["st_gla"] = f(inp["state_gla"])[b, 0]
        m["st_lru"] = f(inp["state_lru"])[b, 0]
        m["st_gdn"] = f(inp["state_gdn"])[b, 0]
        m["st_mc"] = f(inp["state_mlstm_c"])[b, 0]
        m["st_mn"] = f(inp["state_mlstm_n"])[b, 0]
        m["st_mm"] = f(inp["state_mlstm_m"])[b, 0]
        maps.append(m)
    return maps


def kernel(**inp):
    if "nc" not in _CACHE:
        _CACHE["nc"] = build()
    nc = _CACHE["nc"]
    maps = make_in_maps(inp)
    res = run_bass_kernel_spmd(nc, maps, core_ids=list(range(N_CORES)))
    R = res.results
    B = 32
    y_ctx = np.concatenate([R[c]["yc"].reshape(4, TT, D) for c in range(8)], 0)
    y_lat = np.stack([R[b]["yl"] for b in range(4)], 0)
    cat = lambda k: np.concatenate([R[c][k] for c in range(8)], 0)
    new_gla = cat("o_gla")[:, None]
    new_lru = cat("o_lru")[:, None]
    new_gdn = cat("o_gdn")[:, None]
    new_c = cat("o_mc")[:, None]
    new_n = cat("o_mn")[:, None]
    new_m = cat("o_mm")[:, None]
    return tuple(np.ascontiguousarray(a.astype(np.float32)) for a in (y_ctx, y_lat, new_gla, new_lru, new_gdn, new_c, new_n, new_m))


def simulate_trace(trace):
    sems = {}
    pos = {k: 0 for k in trace}
    progress = True
    while progress:
        progress = False
        for k, lst in trace.items():
            while pos[k] < len(lst):
                kind, sid, val = lst[pos[k]]
                if kind == "w":
                    if sems.get(sid, 0) >= val:
                        pos[k] += 1
                        progress = True
                    else:
                        break
                else:
                    sems[sid] = sems.get(sid, 0) + val
                    pos[k] += 1
                    progress = True
    stuck = {k: (pos[k], len(lst), lst[pos[k]] if pos[k] < len(lst) else None) for k, lst in trace.items()}
    if all(p == n for (p, n, _) in stuck.values()):
        return None
    return stuck, sems
```

```python
import numpy as np
from contextlib import ExitStack
import concourse.bass as bass
import concourse.mybir as mybir
from concourse.alu_op_type import AluOpType as ALU
from concourse.bass_utils import run_bass_kernel_spmd

F32 = mybir.dt.float32
BF16 = mybir.dt.bfloat16
AF = mybir.ActivationFunctionType
AX = mybir.AxisListType

D = 2048
KC = 16
TT = 256
FF = 8192
EPS = 1e-6
NEG = -30000.0


class View:
    __slots__ = ("b", "ap")

    def __init__(self, b, ap):
        self.b = b
        self.ap = ap


class Buf:
    __slots__ = ("t", "name", "w", "r")

    def __init__(self, t, name=""):
        self.t = t
        self.name = name
        self.w = None
        self.r = {}

    def __getitem__(self, idx):
        return View(self, self.t[idx])

    def v(self, ap):
        return View(self, ap)


SEM_LIMIT = 60000
DBG_STAGE = 99
DBG_DUMP = ()
DBG_SUB2 = 99
TRACE = None
DBG_SUB = 99
DBG_GROUPS = (0, 1)


class EngQ:
    def __init__(self, S, name, eng, self_raw=True):
        self.S = S
        self.name = name
        self.eng = eng
        self.sem = S.nc.alloc_semaphore(name=f"prog_{name}")
        self.semids = {id(self.sem)}
        self.nroll = 0
        self.n = 0
        self.last_ev = None
        self.seen = {}
        self.self_raw = self_raw
        self.ninst = 0

    def roll(self):
        self.nroll += 1
        self.sem = self.S.nc.alloc_semaphore(name=f"prog_{self.name}_{self.nroll}")
        self.semids.add(id(self.sem))
        self.n = 0


class Sched:
    def __init__(self, nc, n_dma_sems=32):
        self.nc = nc
        self.pe = EngQ(self, "pe", nc.tensor, self_raw=False)
        self.act = EngQ(self, "act", nc.scalar)
        self.dve = EngQ(self, "dve", nc.vector)
        self.pool = EngQ(self, "pool", nc.gpsimd)
        self.sp = EngQ(self, "sp", nc.sync)
        self.engs = [self.pe, self.act, self.dve, self.pool, self.sp]
        self.dma_sems = [nc.alloc_semaphore(name=f"dma{i}") for i in range(n_dma_sems)]
        self.dma_val = [0] * n_dma_sems
        self.dma_rr = 0

    def _wait(self, q, ev):
        sem, val = ev
        k = id(sem)
        if q.seen.get(k, 0) < val:
            q.eng.wait_ge(sem, val)
            q.seen[k] = val
            q.ninst += 1
            if TRACE is not None:
                TRACE.setdefault(q.name, []).append(("w", k, val))

    def _deps(self, q, reads, writes):
        for b in reads:
            if b is None:
                continue
            if b.w is not None:
                if id(b.w[0]) in q.semids and not q.self_raw:
                    continue
                self._wait(q, b.w)
        for b in writes:
            if b is None:
                continue
            if b.w is not None and id(b.w[0]) not in q.semids:
                self._wait(q, b.w)
            for ev in b.r.values():
                if id(ev[0]) not in q.semids:
                    self._wait(q, ev)

    def _mark(self, ev, reads, writes):
        k = id(ev[0])
        for b in reads:
            if b is not None:
                b.r[k] = ev
        for b in writes:
            if b is not None:
                b.w = ev
                b.r = {}

    def op(self, q, fn, reads=(), writes=(), inc=True):
        if q.n >= SEM_LIMIT:
            q.roll()
        self._deps(q, reads, writes)
        inst = fn(q.eng)
        q.ninst += 1
        if inc:
            q.n += 1
            inst.then_inc(q.sem, 1)
            ev = (q.sem, q.n)
            q.last_ev = ev
            if TRACE is not None:
                TRACE.setdefault(q.name, []).append(("i", id(q.sem), 1))
        else:
            ev = (q.sem, q.n + 1)
        self._mark(ev, reads, writes)
        return inst

    def dma(self, q, out_ap, in_ap, reads=(), writes=(), **kw):
        self._deps(q, reads, writes)
        i = self.dma_rr
        self.dma_rr = (self.dma_rr + 1) % len(self.dma_sems)
        sem = self.dma_sems[i]
        if self.dma_val[i] > 0:
            self._wait(q, (sem, self.dma_val[i]))
        inst = q.eng.dma_start(out=out_ap, in_=in_ap, **kw)
        self.dma_val[i] += 16
        inst.then_inc(sem, 16)
        if TRACE is not None:
            TRACE.setdefault(q.name, []).append(("i", id(sem), 16))
        q.ninst += 1
        self._mark((sem, self.dma_val[i]), reads, writes)
        return inst

    def barrier(self):
        evs = [q.last_ev for q in self.engs if q.last_ev is not None]
        evs += [(s, v) for s, v in zip(self.dma_sems, self.dma_val) if v > 0]
        for q in self.engs:
            for ev in evs:
                if id(ev[0]) in q.semids:
                    continue
                self._wait(q, ev)


def make_consts():
    c = {}
    i128 = np.arange(128)
    r, cc = i128[:, None], i128[None, :]
    c["ident"] = np.eye(128, dtype=np.float32)
    c["ones"] = np.ones((128, 128), np.float32)
    c["tri_f"] = (r <= cc).astype(np.float32)
    c["tri_b"] = (r >= cc).astype(np.float32)
    c["str_f"] = (r > cc).astype(np.float32)
    c["str_b"] = (r < cc).astype(np.float32)
    same = (r // 64) == (cc // 64)
    for nm, cond in (("nlt", cc < r), ("ngt", cc > r), ("nle", cc <= r), ("nge", cc >= r)):
        m = np.where(same & cond, 0.0, NEG).astype(np.float32)
        c[nm] = np.tile(m, (1, 4))
    c["ident4"] = np.tile(np.eye(128, dtype=np.float32), (1, 4))
    c["mask4_f"] = np.tile(c["tri_f"], (1, 4))
    c["mask4_b"] = np.tile(c["tri_b"], (1, 4))
    rm = np.ones((128, 256), np.float32)
    rm[:, 0::64] = 0.0
    c["reset"] = rm
    names = list(c.keys())
    offs = {}
    o = 0
    for n in names:
        offs[n] = (o, c[n].shape[1])
        o += c[n].shape[1]
    arr = np.concatenate([c[n] for n in names], axis=1)
    return arr, offs


CONST_ARR, CONST_OFFS = make_consts()

EVEN_BLOCKS = [(0, 512), (512, 512), (1024, 512), (1536, 512), (2048, 512), (2560, 512), (3072, 32),
               (3104, 512), (3616, 512), (4128, 512), (4640, 512)]
ODD_BLOCKS = [(i * 512, 512) for i in range(8)] + [(4096, 32), (4128, 512), (4640, 512), (5152, 512), (5664, 512),
                                                   (6176, 512), (6688, 512)]


def build(NCT=4, NLT=16, depth=2):
    nc = bass.Bass("TRN2", target_bir_lowering=False)
    S = Sched(nc)
    TC, TL = NCT * TT, NLT * TT
    TMAX = max(TC, TL)

    def din(name, shape, dt=F32):
        return nc.dram_tensor(name, list(shape), dt, kind="ExternalInput").ap()

    def dout(name, shape, dt=F32):
        return nc.dram_tensor(name, list(shape), dt, kind="ExternalOutput").ap()

    def dscr(name, shape, dt=F32):
        return nc.dram_tensor(name, list(shape), dt, kind="Internal").ap()

    I = {}
    for name, shape in [("xc", [TC, D]), ("xl", [TL, D]), ("cc", [2, D]),
                        ("st_gla", [2, 4, 128, 256]), ("st_lru", [2, 1024]), ("st_gdn", [2, 8, 128, 128]),
                        ("st_mc", [2, 4, 128, 256]), ("st_mn", [2, 4, 128]), ("st_mm", [2, 4]),
                        ("w_mod", [2, D, 6 * D]), ("b_mod", [2, 6 * D]), ("norm_g", [2, 4, D]),
                        ("w_up", [2, D, FF]), ("w_down", [2, FF, D]),
                        ("w_in_e", [D, 5152]), ("gla_w2", [2, 16, 512]), ("gla_b", [2, 512]), ("gla_g", [1024]),
                        ("lru_cw", [4, 1024]), ("lru_cb", [1024]), ("lru_gw", [2, 2, 8, 128, 128]),
                        ("lru_gb", [2, 2, 1024]), ("lru_lam", [2, 1024]), ("w_out_e", [D, D]),
                        ("w_in_o", [D, 7216]), ("gdn_cw", [4, 3072]), ("gdn_alog", [2, 8]), ("gdn_dtb", [2, 8]),
                        ("gdn_g", [1024]), ("ml_gb", [2, 2, 4]), ("ml_g", [1024]), ("w_out_o", [D, D]),
                        ("consts", list(CONST_ARR.shape))]:
        I[name] = din(name, shape)
    O = {}
    for name, shape in [("yc", [TC, D]), ("yl", [TL, D]), ("o_gla", [NCT, 2, 4, 128, 256]), ("o_lru", [NCT, 2, 1024]),
                        ("o_gdn", [NCT, 2, 8, 128, 128]), ("o_mc", [NCT, 2, 4, 128, 256]), ("o_mn", [NCT, 2, 4, 128]),
                        ("o_mm", [NCT, 2, 4])]:
        O[name] = dout(name, shape)

    def mm(o, l, r, start=True, stop=True, inc=None):
        S.op(S.pe, lambda e: e.matmul(o.ap, lhsT=l.ap, rhs=r.ap, start=start, stop=stop), reads=[l.b, r.b], writes=[o.b],
             inc=bool(stop) if inc is None else inc)

    def act(o, i, func, bias=None, scale=None, accum=None):
        kw = {}
        rd = [i.b]
        wr = [o.b]
        if bias is not None:
            if isinstance(bias, View):
                kw["bias"] = bias.ap
                rd.append(bias.b)
            else:
                kw["bias"] = bias
        if scale is not None:
            if isinstance(scale, View):
                kw["scale"] = scale.ap
                rd.append(scale.b)
            else:
                kw["scale"] = scale
        if accum is not None:
            kw["accum_out"] = accum.ap
            wr.append(accum.b)
        S.op(S.act, lambda e: e.activation(out=o.ap, in_=i.ap, func=func, **kw), reads=rd, writes=wr)

    def tt(q, o, a, b, op):
        S.op(q, lambda e: e.tensor_tensor(out=o.ap, in0=a.ap, in1=b.ap, op=op), reads=[a.b, b.b], writes=[o.b])

    def _sc(s, rd):
        if isinstance(s, View):
            rd.append(s.b)
            return s.ap
        return s

    def ts(q, o, a, s1, op0, s2=None, op1=None):
        rd = [a.b]
        a1 = _sc(s1, rd)
        a2 = _sc(s2, rd)
        if op1 is None:
            S.op(q, lambda e: e.tensor_scalar(out=o.ap, in0=a.ap, scalar1=a1, scalar2=None, op0=op0), reads=rd, writes=[o.b])
        else:
            S.op(q, lambda e: e.tensor_scalar(out=o.ap, in0=a.ap, scalar1=a1, scalar2=a2, op0=op0, op1=op1), reads=rd, writes=[o.b])

    def stt(o, a, s, b, op0, op1):
        rd = [a.b, b.b]
        a1 = _sc(s, rd)
        S.op(S.dve, lambda e: e.scalar_tensor_tensor(out=o.ap, in0=a.ap, scalar=a1, in1=b.ap, op0=op0, op1=op1), reads=rd, writes=[o.b])

    def cp(q, o, i):
        if q is S.act:
            act(o, i, AF.Copy)
        else:
            S.op(q, lambda e: e.tensor_copy(out=o.ap, in_=i.ap), reads=[i.b], writes=[o.b])

    def memset(q, o, val):
        S.op(q, lambda e: e.memset(o.ap, val), writes=[o.b])

    def ld(dst, src_ap, q=None, **kw):
        S.dma(q or S.sp, dst.ap, src_ap, writes=[dst.b], **kw)

    def stq(dst_ap, src, q=None, **kw):
        S.dma(q or S.pool, dst_ap, src.ap, reads=[src.b], **kw)

    top = ExitStack()

    uid = [0]

    def alloc(es, name, shape, dt=F32):
        uid[0] += 1
        name = f"{name}_{uid[0]}"
        return Buf(es.enter_context(nc.sbuf_tensor(name, list(shape), dt)), name)

    CT = alloc(top, "consts_sb", list(CONST_ARR.shape))
    ld(CT[:, :], I["consts"])

    def C(name, rows=128):
        o, w = CONST_OFFS[name]
        return CT[0:rows, o:o + w]

    ident = C("ident")
    banks = [Buf(top.enter_context(nc.psum_tensor(f"psb{i}", [128, 512], F32)), f"psb{i}") for i in range(8)]
    bank_rr = [0]

    def PS():
        b = banks[bank_rr[0]]
        bank_rr[0] = (bank_rr[0] + 1) % 8
        return b

    def tr(o, i, n):
        S.op(S.pe, lambda e: e.transpose(o.ap, i.ap, ident.ap[0:n, 0:n]), reads=[i.b, CT], writes=[o.b])

    colstage = alloc(top, "colstage", [128, 128])

    def load_cols(dst, flat_ap, R):
        ld(colstage[0:R, :], flat_ap.rearrange("(r p) -> r p", p=128))
        ps = PS()
        r0 = 0
        while r0 < R:
            n = min(64 if R > 64 else R, R - r0)
            S.op(S.pe, lambda e, r0=r0, n=n: e.transpose(ps.t[:, r0:r0 + n], colstage.t[r0:r0 + n, :], ident.ap[r0:r0 + n, r0:r0 + n]),
                 reads=[colstage, CT], writes=[ps])
            r0 += n
        cp(S.dve, dst, ps[:, 0:R])

    def cast_blocks(src2d, nkc, blocks, prefix):
        outs = []
        for bi, (c0, w) in enumerate(blocks):
            dst = dscr(f"{prefix}{bi}", [128, nkc, w], BF16)
            for k0 in range(0, nkc, 16):
                S.dma(S.pool, dst[:, k0:k0 + 16, :],
                      src2d[k0 * 128:(k0 + 16) * 128, c0:c0 + w].rearrange("(kc p) f -> p kc f", p=128))
            outs.append(dst)
        return outs

    B512 = [(i * 512, 512) for i in range(4)]
    Wb = {}
    Wb["in0"] = cast_blocks(I["w_in_e"], KC, EVEN_BLOCKS, "wine")
    Wb["out0"] = cast_blocks(I["w_out_e"], KC, B512, "woute")
    if depth > 1:
        Wb["in1"] = cast_blocks(I["w_in_o"], KC, ODD_BLOCKS, "wino")
        Wb["out1"] = cast_blocks(I["w_out_o"], KC, B512, "wouto")
    for l in range(depth):
        Wb[f"up{l}"] = cast_blocks(I["w_up"][l], KC, [(i * 512, 512) for i in range(16)], f"wup{l}_")
        Wb[f"dn{l}"] = cast_blocks(I["w_down"][l], 64, B512, f"wdn{l}_")

    MODV = dscr("modv", [depth, 2, 6, D])
    with ExitStack() as es:
        cct = alloc(es, "cct", [2, D])
        sT = alloc(es, "sT", [128, KC, 2])
        wm = [alloc(es, f"wm{i}", [128, KC, 512]) for i in range(2)]
        modt = alloc(es, "modt", [2, 6 * D])
        bmt = [alloc(es, f"bmt{i}", [2, 512]) for i in range(2)]
        ngt = alloc(es, "ngt", [2, D])
        mvt = [alloc(es, f"mvt{i}", [2, D]) for i in range(2)]
        ld(cct[:, :], I["cc"])
        act(cct[:, :], cct[:, :], AF.Silu)
        ps = PS()
        for kc in range(KC):
            tr(ps[:, kc * 2:kc * 2 + 2], cct[:, kc * 128:(kc + 1) * 128], 2)
        cp(S.dve, sT.v(sT.t[:, :, :].rearrange("p a b -> p (a b)")), ps[:, 0:2 * KC])
        for l in range(depth):
            for cb in range(24):
                w = wm[cb % 2]
                bm = bmt[cb % 2]
                ld(w[:, :, :], I["w_mod"][l][:, cb * 512:(cb + 1) * 512].rearrange("(kc p) f -> p kc f", p=128))
                ld(bm[:, :], I["b_mod"][l:l + 1, cb * 512:(cb + 1) * 512].partition_broadcast(2).rearrange("p a b -> p (a b)"))
                ps = PS()
                for kc in range(KC):
                    mm(ps[0:2, :], sT[:, kc, :], w[:, kc, :], start=(kc == 0), stop=(kc == KC - 1))
                tt(S.dve, modt[:, cb * 512:(cb + 1) * 512], ps[0:2, :], bm[:, :], ALU.add)

            def sl(i):
                return modt[:, i * D:(i + 1) * D]
            for i, (kind, mi, gi_) in enumerate((("a", 1, 0), ("c", 0, None), ("m", 2, 1), ("a", 4, 2), ("c", 3, None), ("m", 5, 3))):
                mvb = mvt[i % 2]
                if gi_ is not None:
                    ld(ngt[:, :], I["norm_g"][l, gi_:gi_ + 1, :].partition_broadcast(2).rearrange("p a b -> p (a b)"))
                if kind == "a":
                    stt(mvb[:, :], sl(mi), 1.0, ngt[:, :], ALU.add, ALU.mult)
                elif kind == "m":
                    tt(S.dve, mvb[:, :], sl(mi), ngt[:, :], ALU.mult)
                else:
                    cp(S.dve, mvb[:, :], sl(mi))
                stq(MODV[l, :, i, :], mvb[:, :])
    S.barrier()

    X1 = {0: dscr("x1c", [TC, D]), 1: dscr("x1l", [TL, D])}
    MIXT = dscr("mixt", [128, 16, TMAX], BF16)
    sc = {}

    def scr(name, shape, dt=F32):
        if name not in sc:
            if name in DBG_DUMP:
                sc[name] = dout("s_" + name, shape, dt)
            else:
                sc[name] = dscr("s_" + name, shape, dt)
        return sc[name]

    class Grp:
        pass

    def make_groups(l):
        last = (l == depth - 1)
        gs = []
        for w in DBG_GROUPS:
            g = Grp()
            g.w = w
            g.ntiles = NCT if w == 0 else NLT
            g.T = g.ntiles * TT
            src = (I["xc"], I["xl"])[w] if l == 0 else X1[w]
            dst = (O["yc"], O["yl"])[w] if last else X1[w]
            g.colmajor = (w == 1 and l % 2 == 1)
            g.L = 256 if w == 0 else 64
            g.NL = TT // g.L
            if g.colmajor:
                sv = src.rearrange("(r c) d -> c r d", c=64)
                dv = dst.rearrange("(r c) d -> c r d", c=64)
                g.xsrc = lambda ti, st, sv=sv: [(0, 64, sv[ti * 4 + st * 2]), (64, 128, sv[ti * 4 + st * 2 + 1])]
                g.ydst = lambda ti, st, dv=dv: [(0, 64, dv[ti * 4 + st * 2]), (64, 128, dv[ti * 4 + st * 2 + 1])]
            else:
                g.xsrc = lambda ti, st, src=src: [(0, 128, src[ti * TT + st * 128: ti * TT + (st + 1) * 128, :])]
                g.ydst = lambda ti, st, dst=dst: [(0, 128, dst[ti * TT + st * 128: ti * TT + (st + 1) * 128, :])]
            g.seqs = [(i, 1, i) for i in range(NCT)] if w == 0 else [(0, NLT, None)]
            gs.append(g)
        return gs

    def bc_load(dst, l, w, i):
        ld(dst[:, :], MODV[l, w, i:i + 1, :].partition_broadcast(128).rearrange("p a b -> p (a b)"))

    def wstream(aps, bufs):
        n = len(aps)

        def issue(k):
            b = bufs[k % len(bufs)]
            shp = aps[k].shape
            ld(b[:, 0:shp[1], 0:shp[2]], aps[k])
            return b
        cur = issue(0)
        for k in range(n):
            nxt = issue(k + 1) if k + 1 < n else None
            yield cur
            cur = nxt

    def recip(o, i):
        S.op(S.dve, lambda e: e.reciprocal(out=o.ap, in_=i.ap), reads=[i.b], writes=[o.b])

    def rstd_from_ss(rstd, ss, n):
        act(rstd, ss, AF.Sqrt, scale=1.0 / n, bias=EPS)
        recip(rstd, rstd)

    def run_il(gens):
        active = list(gens)
        while active:
            for g_ in list(active):
                try:
                    next(g_)
                except StopIteration:
                    active.remove(g_)

    evac_rr = [0]

    def evq():
        evac_rr[0] ^= 1
        return S.act if evac_rr[0] else S.dve

    def norm_mod_T(src, dst, bA, bB, junk, ss, rstd, uT, st, col, do_tr=True):
        act(junk[:, :], src[:, :], AF.Square, accum=ss[:, col:col + 1])
        rstd_from_ss(rstd[:, col:col + 1], ss[:, col:col + 1], D)
        stt(dst[:, :], src[:, :], rstd[:, col:col + 1], bA[:, :], ALU.mult, ALU.mult)
        tt(S.pool, dst[:, :], dst[:, :], bB[:, :], ALU.add)
        if do_tr:
            transpose_to(uT, dst, st)

    def transpose_to(uT, src, st):
        for q4 in range(4):
            ps = PS()
            for j in range(4):
                kc = q4 * 4 + j
                tr(ps[:, j * 128:(j + 1) * 128], src[:, kc * 128:(kc + 1) * 128], 128)
            cp(evq(), uT[:, q4 * 4:(q4 + 1) * 4, st * 128:(st + 1) * 128],
               ps.v(ps.t[:, :].rearrange("p (a b) -> p a b", a=4)))

    def load_x(g, ti, st, xt):
        for (p0, p1, ap) in g.xsrc(ti, st):
            ld(xt[p0:p1, :], ap)

    def phase3(g, l, front_alloc, front):
        with ExitStack() as es:
            bX, bY = alloc(es, "bcX", [128, D]), alloc(es, "bcY", [128, D])
            bG1, bA2, bB2, bG3 = bX, bY, bX, bY
            xt = [alloc(es, f"xt{i}", [128, D]) for i in range(2)]
            yb = [alloc(es, f"yb{i}", [128, D]) for i in range(2)]
            junk = alloc(es, "junk", [128, D], BF16)
            ss = alloc(es, "ss", [128, 8])
            rstd = alloc(es, "rstd", [128, 8])
            mixT = alloc(es, "mixT", [128, KC, TT], BF16)
            u2T = alloc(es, "u2T", [128, KC, TT], BF16)
            hidT = alloc(es, "hidT", [128, 64, TT], BF16)
            wb = [alloc(es, f"wb{i}", [128, KC, 512], BF16) for i in range(2)]
            rtmp = [alloc(es, f"rtmp{i}", [128, 512]) for i in range(2)]
            fctx = front_alloc(es)
            aps = []
            for ti in range(g.ntiles):
                aps += list(Wb[f"out{l}"]) + list(Wb[f"up{l}"])
                for fb in range(4):
                    aps += [Wb[f"dn{l}"][fb][:, k0:k0 + 16, :] for k0 in range(0, 64, 16)]
            ws = wstream(aps, wb)
            front(fctx, 0, mixT)
            for ti in range(g.ntiles):
                bc_load(bG1, l, g.w, 2)
                bc_load(bA2, l, g.w, 3)
                for st in range(2):
                    load_x(g, ti, st, xt[st])
                for fb in range(4):
                    w = next(ws)
                    for st in range(2):
                        ps = PS()
                        for kc in range(KC):
                            mm(ps[:, :], mixT[:, kc, st * 128:(st + 1) * 128], w[:, kc, :], start=(kc == 0), stop=(kc == KC - 1))
                        cp(evq(), yb[st][:, fb * 512:(fb + 1) * 512], ps[:, :])
                for st in range(2):
                    act(junk[:, :], yb[st][:, :], AF.Square, accum=ss[:, st:st + 1])
                    rstd_from_ss(rstd[:, st:st + 1], ss[:, st:st + 1], D)
                    stt(yb[st][:, :], yb[st][:, :], rstd[:, st:st + 1], bG1[:, :], ALU.mult, ALU.mult)
                    tt(S.pool, xt[st][:, :], xt[st][:, :], yb[st][:, :], ALU.add)
                bc_load(bB2, l, g.w, 4)
                for st in range(2):
                    norm_mod_T(xt[st], yb[st], bA2, bB2, junk, ss, rstd, u2T, st, 2 + st)
                bc_load(bG3, l, g.w, 5)
                for ub in range(16):
                    w = next(ws)
                    for j in range(0, 4, 2):
                        ps = PS()
                        for jj in range(2):
                            for kc in range(KC):
                                mm(ps[:, jj * 256:(jj + 1) * 256], w[:, kc, (j + jj) * 128:(j + jj + 1) * 128], u2T[:, kc, :],
                                   start=(kc == 0), stop=(kc == KC - 1))
                        rt_ = rtmp[(j // 2) % 2]
                        act(rt_[:, :], ps[:, :], AF.Relu)
                        c0 = ub * 4 + j
                        tt(S.pool, hidT.v(hidT.t[:, c0:c0 + 2, :].rearrange("p a b -> p (a b)")), rt_[:, :], rt_[:, :], ALU.mult)
                if ti + 1 < g.ntiles:
                    front(fctx, ti + 1, mixT)
                for fb in range(4):
                    psd = [PS(), PS()]
                    for g4 in range(4):
                        w = next(ws)
                        for st in range(2):
                            for k in range(16):
                                fc = g4 * 16 + k
                                mm(psd[st][:, :], hidT[:, fc, st * 128:(st + 1) * 128], w[:, k, :], start=(fc == 0), stop=(fc == 63),
                                   inc=(k == 15))
                    for st in range(2):
                        cp(evq(), yb[st][:, fb * 512:(fb + 1) * 512], psd[st][:, :])
                for st in range(2):
                    act(junk[:, :], yb[st][:, :], AF.Square, accum=ss[:, 4 + st:5 + st])
                    rstd_from_ss(rstd[:, 4 + st:5 + st], ss[:, 4 + st:5 + st], D)
                    stt(yb[st][:, :], yb[st][:, :], rstd[:, 4 + st:5 + st], bG3[:, :], ALU.mult, ALU.mult)
                    tt(S.pool, yb[st][:, :], yb[st][:, :], xt[st][:, :], ALU.add)
                    for (p0, p1, ap) in g.ydst(ti, st):
                        stq(ap, yb[st][p0:p1, :])
        S.barrier()

    def even_layer(l, j):
        QT = scr("QT", [128, 4, TMAX])
        KT_ = scr("KT", [128, 4, TMAX])
        Kt = scr("Kt", [TMAX, 512])
        Vt = scr("Vt", [TMAX, 1024])
        Gd = [scr(f"G{d}", [TMAX, 512]) for d in range(2)]
        RT = scr("RT", [128, 8, TMAX])
        Ad = [scr(f"A{d}", [128, 8, TMAX]) for d in range(2)]
        Bd = [scr(f"B{d}", [128, 8, TMAX]) for d in range(2)]
        LGT = scr("LGT", [128, 8, TMAX])
        Od = [scr(f"O{d}", [128, 8, TMAX]) for d in range(2)]
        with ExitStack() as les:
            gcol = alloc(les, "gcol", [128, 8])
            load_cols(gcol[:, :], I["gla_g"], 8)
            cw = alloc(les, "cw", [128, 32])
            load_cols(cw[:, :], I["lru_cw"].rearrange("a b -> (a b)"), 32)
            cb = alloc(les, "cb", [128, 8])
            load_cols(cb[:, :], I["lru_cb"], 8)
            gb = alloc(les, "gb", [128, 32])
            load_cols(gb[:, :], I["lru_gb"].rearrange("a b c -> (a b c)"), 32)
            m8sp = alloc(les, "m8sp", [128, 16])
            load_cols(m8sp[:, :], I["lru_lam"].rearrange("a b -> (a b)"), 16)
            act(m8sp[:, :], m8sp[:, :], AF.Exp, scale=-1.0)
            act(m8sp[:, :], m8sp[:, :], AF.Ln, bias=1.0)
            ts(S.dve, m8sp[:, :], m8sp[:, :], -8.0, ALU.mult)
            ones = C("ones")

            for g in make_groups(l):
                L, NL = g.L, g.NL
                with ExitStack() as es:
                    bA, bB = alloc(es, "bA", [128, D]), alloc(es, "bB", [128, D])
                    bc_load(bA, l, g.w, 0)
                    bc_load(bB, l, g.w, 1)
                    w2 = alloc(es, "w2", [16, 2, 512])
                    ld(w2[:, :, :], I["gla_w2"].rearrange("d r k -> r d k"))
                    gbrow = alloc(es, "gbrow", [1, 2, 512])
                    ld(gbrow[:, :, :], I["gla_b"].rearrange("(o d) k -> o d k", o=1))
                    LW = alloc(es, "LW", [128, 32, 128])
                    ld(LW[:, :, :], I["lru_gw"].rearrange("d g n i j -> i (d g n) j"))
                    xt = [alloc(es, f"xt{i}", [128, D]) for i in range(2)]
                    uTs = [alloc(es, f"uT{i}", [128, KC, TT], BF16) for i in range(2)]
                    junk = alloc(es, "junk", [128, D], BF16)
                    ss = alloc(es, "ss", [128, 4])
                    rstd = alloc(es, "rstd", [128, 4])
                    wb = [alloc(es, f"wb{i}", [128, KC, 512], BF16) for i in range(2)]
                    qTs = alloc(es, "qTs", [128, 4, TT])
                    kTs = qTs
                    kts = alloc(es, "kts", [128, 2, 512])
                    vs = alloc(es, "vs", [128, 2, 1024])
                    rTs = alloc(es, "rTs", [128, 8, TT])
                    lrT = alloc(es, "lrT", [16, 2, TT])
                    e1 = alloc(es, "e1", [128, 512])
                    Gs = alloc(es, "Gs", [128, 2, 2, 512])
                    xp = alloc(es, "xp", [128, 8, NL, L + 3])
                    xcs = alloc(es, "xcs", [128, 8, TT])
                    grs = [alloc(es, f"gr{i}", [128, TT]) for i in range(4)]
                    gis = [alloc(es, f"gi{i}", [128, TT]) for i in range(4)]
                    as_ = alloc(es, "as_", [128, 8, TT])
                    bs_ = alloc(es, "bs_", [128, 8, TT])
                    lgs = rTs
                    memset(S.pool, xp[:, :, :, :], 0.0)
                    aps = []
                    for ti in range(g.ntiles):
                        aps += list(Wb[f"in{l}"])
                    ws = wstream(aps, wb)

                    def prep(ti, do_tr=True):
                        for st in range(2):
                            load_x(g, ti, st, xt[st])
                        for st in range(2):
                            norm_mod_T(xt[st], xt[st], bA, bB, junk, ss, rstd, uTs[ti % 2], st, st, do_tr=do_tr)

                    def prep_b(ti):
                        for st in range(2):
                            transpose_to(uTs[ti % 2], xt[st], st)

                    def fm(ps, col, w, wc0, n, uT):
                        for kc in range(KC):
                            mm(ps[0:n, col:col + TT], w[:, kc, wc0:wc0 + n], uT[:, kc, :], start=(kc == 0), stop=(kc == KC - 1))

                    def tmj(ps, w, n, uT, st):
                        for kc in range(KC):
                            mm(ps[:, 0:n], uT[:, kc, st * 128:(st + 1) * 128], w[:, kc, 0:n], start=(kc == 0), stop=(kc == KC - 1))

                    prep(0)
                    for ti in range(g.ntiles):
                        t0 = ti * TT
                        uT = uTs[ti % 2]
                        for bi, (dst_s, dram, scl) in enumerate(((qTs, QT, 128.0 ** -0.5), (kTs, KT_, 1.0))):
                            w = next(ws)
                            for hp in range(2):
                                ps = PS()
                                for hh in range(2):
                                    fm(ps, hh * TT, w, (hp * 2 + hh) * 128, 128, uT)
                                act(dst_s.v(dst_s.t[:, hp * 2:hp * 2 + 2, :].rearrange("p a b -> p (a b)")), ps[:, :], AF.Copy, scale=scl)
                            stq(dram[:, :, t0:t0 + TT], dst_s[:, :, :])
                            if bi == 1:
                                for st in range(2):
                                    ps = PS()
                                    tmj(ps, w, 512, uT, st)
                                    cp(S.dve, kts[:, st, :], ps[:, :])
                                stq(Kt[t0:t0 + TT, :].rearrange("(s p) f -> p s f", p=128), kts[:, :, :])
                        for vb in range(2):
                            w = next(ws)
                            for st in range(2):
                                ps = PS()
                                tmj(ps, w, 512, uT, st)
                                cp(evq(), vs[:, st, vb * 512:(vb + 1) * 512], ps[:, :])
                        stq(Vt[t0:t0 + TT, :].rearrange("(s p) f -> p s f", p=128), vs[:, :, :])
                        for rb in range(2):
                            w = next(ws)
                            for cp_ in range(2):
                                ps = PS()
                                for hh in range(2):
                                    fm(ps, hh * TT, w, (cp_ * 2 + hh) * 128, 128, uT)
                                c0 = rb * 4 + cp_ * 2
                                act(rTs.v(rTs.t[:, c0:c0 + 2, :].rearrange("p a b -> p (a b)")), ps[:, :], AF.Silu)
                        stq(RT[:, :, t0:t0 + TT], rTs[:, :, :])
                        if ti + 1 < g.ntiles:
                            prep(ti + 1, do_tr=False)
                        w = next(ws)
                        for d in range(2):
                            ps = PS()
                            fm(ps, 0, w, d * 16, 16, uT)
                            cp(S.dve, lrT[:, d, :], ps[0:16, 0:TT])
                        for d in range(2):
                            for st in range(2):
                                ps = PS()
                                mm(ps[:, :], lrT[0:16, d, st * 128:(st + 1) * 128], w2[0:16, d, :], start=True, stop=False)
                                mm(ps[:, :], ones.b.v(ones.ap[0:1, 0:128]), gbrow[0:1, d, :], start=False, stop=True)
                                act(e1[:, :], ps[:, :], AF.Exp, scale=-1.0)
                                act(e1[:, :], e1[:, :], AF.Ln, bias=1.0)
                                ts(S.pool, Gs[:, d, st, :], e1[:, :], -1.0 / 16.0, ALU.mult)
                            stq(Gd[d][t0:t0 + TT, :].rearrange("(s p) f -> p s f", p=128), Gs[:, d, :, :])
                        if ti + 1 < g.ntiles:
                            prep_b(ti + 1)
                        for xb in range(2):
                            w = next(ws)
                            for cp_ in range(2):
                                ps = PS()
                                for hh in range(2):
                                    fm(ps, hh * TT, w, (cp_ * 2 + hh) * 128, 128, uT)
                                for hh in range(2):
                                    n = xb * 4 + cp_ * 2 + hh
                                    act(xp[:, n, :, 2:2 + L], ps.v(ps.t[:, hh * TT:(hh + 1) * TT].rearrange("p (a b) -> p a b", a=NL)), AF.Copy)
                        for n in range(8):
                            xc = xcs.v(xcs.t[:, n, :].rearrange("p (a b) -> p a b", a=NL))
                            act(xc, xp[:, n, :, 2:2 + L], AF.Identity, scale=cw[:, 2 * 8 + n:2 * 8 + n + 1], bias=cb[:, n:n + 1])
                            for tap in (0, 1, 3):
                                stt(xc, xp[:, n, :, tap:tap + L], cw[:, tap * 8 + n:tap * 8 + n + 1], xc, ALU.mult, ALU.add)
                        for d in range(2):
                            for n0 in range(0, 8, 4):
                                pss = []
                                for n in range(n0, n0 + 4):
                                    ps = PS()
                                    mm(ps[:, 0:TT], LW[:, (d * 2 + 0) * 8 + n, :], xcs[:, n, :])
                                    mm(ps[:, TT:2 * TT], LW[:, (d * 2 + 1) * 8 + n, :], xcs[:, n, :])
                                    pss.append(ps)
                                for n in range(n0, n0 + 4):
                                    i0 = (d * 2 + 0) * 8 + n
                                    i1 = (d * 2 + 1) * 8 + n
                                    act(grs[n % 4][:, :], pss[n - n0][:, 0:TT], AF.Sigmoid, bias=gb[:, i0:i0 + 1])
                                    act(gis[n % 4][:, :], pss[n - n0][:, TT:2 * TT], AF.Sigmoid, bias=gb[:, i1:i1 + 1])
                                for n in range(n0, n0 + 4):
                                    act(as_[:, n, :], grs[n % 4][:, :], AF.Exp, scale=m8sp[:, d * 8 + n:d * 8 + n + 1])
                                    tt(S.pool, grs[n % 4][:, :], as_[:, n, :], as_[:, n, :], ALU.mult)
                                for n in range(n0, n0 + 4):
                                    act(grs[n % 4][:, :], grs[n % 4][:, :], AF.Sqrt, scale=-1.0, bias=1.0)
                                    tt(S.pool, gis[n % 4][:, :], gis[n % 4][:, :], grs[n % 4][:, :], ALU.mult)
                                    tt(S.dve, bs_[:, n, :], gis[n % 4][:, :], xcs[:, n, :], ALU.mult)
                            stq(Ad[d][:, :, t0:t0 + TT], as_[:, :, :])
                            stq(Bd[d][:, :, t0:t0 + TT], bs_[:, :, :])
                        for gb_ in range(2):
                            w = next(ws)
                            for cp_ in range(2):
                                ps = PS()
                                for hh in range(2):
                                    fm(ps, hh * TT, w, (cp_ * 2 + hh) * 128, 128, uT)
                                c0 = gb_ * 4 + cp_ * 2
                                act(lgs.v(lgs.t[:, c0:c0 + 2, :].rearrange("p a b -> p (a b)")), ps[:, :], AF.Gelu_apprx_tanh)
                        stq(LGT[:, :, t0:t0 + TT], lgs[:, :, :])
                S.barrier()

                with ExitStack() as es:
                    def mkb():
                        Sst = alloc(es, "Sst", [128, 4, 256])
                        qTb = [alloc(es, f"qTb{i}", [128, 4, 128]) for i in range(2)]
                        kTb = [alloc(es, f"kTb{i}", [128, 4, 128]) for i in range(2)]
                        ktb = [alloc(es, f"ktb{i}", [128, 512]) for i in range(2)]
                        vb_ = [alloc(es, f"vb{i}", [128, 1024]) for i in range(2)]
                        ggb = [alloc(es, f"ggb{i}", [128, 512]) for i in range(2)]
                        E = alloc(es, "E", [128, 512])
                        Einv = alloc(es, "Einv", [128, 512])
                        qp = alloc(es, "qp", [128, 4, 128])
                        kp = alloc(es, "kp", [128, 4, 128])
                        e2 = alloc(es, "e2", [128, 512])
                        kpp = alloc(es, "kpp", [128, 512])
                        At = alloc(es, "At", [128, 512])
                        oTs = alloc(es, "oTs", [128, 8, 128])
                        return (Sst, qTb, kTb, ktb, vb_, ggb, E, Einv, qp, kp, e2, kpp, At, oTs)
                    BB = [mkb(), mkb()]
                    def chain(d):
                        (Sst, qTb, kTb, ktb, vb_, ggb, E, Einv, qp, kp, e2, kpp, At, oTs) = BB[d]
                        for (tst, ntl, sidx) in g.seqs:
                            nch = ntl * 2
                            if True:
                                if sidx is None:
                                    ld(Sst[:, :, :], I["st_gla"][d].rearrange("h d v -> d h v"))
                                else:
                                    memset(S.pool, Sst[:, :, :], 0.0)
                                order = list(range(nch)) if d == 0 else list(range(nch - 1, -1, -1))
                                TRI = C("tri_f") if d == 0 else C("tri_b")
                                STR = C("str_f") if d == 0 else C("str_b")
                                MASK4 = C("mask4_f") if d == 0 else C("mask4_b")
                                last = 127 if d == 0 else 0

                                def loads(k):
                                    c = order[k]
                                    t0 = tst * TT + c * 128
                                    i = k % 2
                                    ld(qTb[i][:, :, :], QT[:, :, t0:t0 + 128])
                                    ld(kTb[i][:, :, :], KT_[:, :, t0:t0 + 128])
                                    ld(ktb[i][:, :], Kt[t0:t0 + 128, :])
                                    ld(vb_[i][:, :], Vt[t0:t0 + 128, :])
                                    ld(ggb[i][:, :], Gd[d][t0:t0 + 128, :])
                                loads(0)
                                for k in range(nch):
                                    if k + 1 < nch:
                                        loads(k + 1)
                                    c = order[k]
                                    t0 = tst * TT + c * 128
                                    i = k % 2
                                    qT, kT, kt, v, gg = qTb[i], kTb[i], ktb[i], vb_[i], ggb[i]
                                    ps1 = PS()
                                    mm(ps1[:, :], STR, gg[:, :])
                                    act(e2[:, :], ps1[:, :], AF.Exp)
                                    tt(S.dve, kpp[:, :], kt[:, :], e2[:, :], ALU.mult)
                                    yield
                                    ps2 = PS()
                                    for h in range(4):
                                        mm(ps2[:, h * 128:(h + 1) * 128], gg[:, h * 128:(h + 1) * 128], TRI)
                                    act(E[:, :], ps2[:, :], AF.Exp)
                                    act(Einv[:, :], ps2[:, :], AF.Exp, scale=-1.0)
                                    tt(S.dve, qp.v(qp.t[:, :, :].rearrange("p a b -> p (a b)")), qT.v(qT.t[:, :, :].rearrange("p a b -> p (a b)")), E[:, :], ALU.mult)
                                    tt(S.pool, kp.v(kp.t[:, :, :].rearrange("p a b -> p (a b)")), kT.v(kT.t[:, :, :].rearrange("p a b -> p (a b)")), Einv[:, :], ALU.mult)
                                    yield
                                    ps3 = PS()
                                    for h in range(4):
                                        mm(ps3[:, h * 128:(h + 1) * 128], kp[:, h, :], qp[:, h, :])
                                    tt(S.dve, At[:, :], ps3[:, :], MASK4, ALU.mult)
                                    yield
                                    for half in range(2):
                                        ps4 = PS()
                                        for jq in range(4):
                                            idx = half * 4 + jq
                                            h, vc = idx // 2, idx % 2
                                            mm(ps4[:, jq * 128:(jq + 1) * 128], Sst[:, h, vc * 128:(vc + 1) * 128], qp[:, h, :], start=True, stop=False)
                                            mm(ps4[:, jq * 128:(jq + 1) * 128], v[:, h * 256 + vc * 128:h * 256 + (vc + 1) * 128], At[:, h * 128:(h + 1) * 128], start=False, stop=True)
                                        cp(S.act, oTs.v(oTs.t[:, half * 4:(half + 1) * 4, :].rearrange("p a b -> p (a b)")), ps4[:, :])
                                    stq(Od[d][:, :, t0:t0 + 128], oTs[:, :, :])
                                    yield
                                    for half in range(2):
                                        ps5 = PS()
                                        for jq in range(2):
                                            h = half * 2 + jq
                                            mm(ps5[:, jq * 256:(jq + 1) * 256], kpp[:, h * 128:(h + 1) * 128], v[:, h * 256:(h + 1) * 256])
                                        for jq in range(2):
                                            h = half * 2 + jq
                                            stt(Sst[:, h, :], Sst[:, h, :], E[:, h * 128 + last:h * 128 + last + 1], ps5[:, jq * 256:(jq + 1) * 256], ALU.mult, ALU.add)
                                            yield
                                if sidx is not None:
                                    stq(O["o_gla"][sidx, d].rearrange("h d v -> d h v"), Sst[:, :, :])
                    run_il([chain(0), chain(1)])
                S.barrier()

                with ExitStack() as es:
                    TS = max(n_ for (_, n_, _) in g.seqs) * TT
                    a_ = alloc(es, "lru_a", [128, TS])
                    b_ = alloc(es, "lru_b", [128, TS])
                    hf = alloc(es, "lru_hf", [128, TS])
                    hb = alloc(es, "lru_hb", [128, TS])
                    lgt = alloc(es, "lru_lg", [128, TS])
                    mixo = alloc(es, "lru_mix", [128, TS], BF16)
                    h0c = alloc(es, "lru_h0", [128, 2])
                    hl = alloc(es, "lru_hl", [128, 2])
                    for (tst, ntl, sidx) in g.seqs:
                        T_ = ntl * TT
                        tsl = slice(tst * TT, tst * TT + T_)
                        for n in range(8):
                            ld(a_[:, 0:T_], Ad[0][:, n, tsl])
                            ld(b_[:, 0:T_], Bd[0][:, n, tsl])
                            if sidx is None:
                                ld(h0c[:, :], I["st_lru"][:, n * 128:(n + 1) * 128].rearrange("d p -> p d"), allow_slow_non_contiguous=True)
                                i0, i1 = h0c[:, 0:1], h0c[:, 1:2]
                            else:
                                i0, i1 = 0.0, 0.0

                            def scan(o, x0, x1, ini, rev):
                                sl_ = slice(None, None, -1) if rev else slice(None)
                                rd = [x0.b, x1.b]
                                iv = ini
                                if isinstance(ini, View):
                                    rd.append(ini.b)
                                    iv = ini.ap
                                S.op(S.dve, lambda e: e.tensor_tensor_scan(out=o.b.t[:, 0:T_][:, sl_], data0=x0.b.t[:, 0:T_][:, sl_], data1=x1.b.t[:, 0:T_][:, sl_],
                                                                          initial=iv, op0=ALU.mult, op1=ALU.add), reads=rd, writes=[o.b])
                            scan(hf[:, :], a_[:, :], b_[:, :], i0, False)
                            ld(a_[:, 0:T_], Ad[1][:, n, tsl])
                            ld(b_[:, 0:T_], Bd[1][:, n, tsl])
                            scan(hb[:, :], a_[:, :], b_[:, :], i1, True)
                            ld(lgt[:, 0:T_], LGT[:, n, tsl])
                            if sidx is not None:
                                cp(S.pool, hl[:, 0:1], hf[:, T_ - 1:T_])
                                cp(S.pool, hl[:, 1:2], hb[:, 0:1])
                                stq(O["o_lru"][sidx, :, n * 128:(n + 1) * 128].rearrange("d p -> p d"), hl[:, :], allow_slow_non_contiguous=True)
                            tt(S.pool, hf[:, 0:T_], hf[:, 0:T_], hb[:, 0:T_], ALU.add)
                            tt(S.dve, mixo[:, 0:T_], hf[:, 0:T_], lgt[:, 0:T_], ALU.mult)
                            stq(MIXT[:, 8 + n, tsl], mixo[:, 0:T_])
                S.barrier()

                def front_alloc(es):
                    f = Grp()
                    f.of = alloc(es, "f_of", [128, 8, TT])
                    f.ob = alloc(es, "f_ob", [128, 8, TT])
                    f.rs = alloc(es, "f_rs", [128, 4, TT])
                    f.rt = alloc(es, "f_rt", [128, 8, TT])
                    return f

                def front(f, ti, mixT):
                    t0 = ti * TT
                    ld(f.of[:, :, :], Od[0][:, :, t0:t0 + TT])
                    ld(f.ob[:, :, :], Od[1][:, :, t0:t0 + TT])
                    ld(f.rt[:, :, :], RT[:, :, t0:t0 + TT])
                    ld(mixT[:, 8:16, :], MIXT[:, 8:16, t0:t0 + TT])
                    tt(S.pool, f.of[:, :, :], f.of[:, :, :], f.ob[:, :, :], ALU.add)
                    act(f.ob[:, :, :], f.of[:, :, :], AF.Square)
                    for half in range(2):
                        ps = PS()
                        for jq in range(2):
                            h = half * 2 + jq
                            for vc in range(2):
                                mm(ps[:, jq * TT:(jq + 1) * TT], ones, f.ob[:, h * 2 + vc, :], start=(vc == 0), stop=(vc == 1))
                        act(f.rs.v(f.rs.t[:, half * 2:half * 2 + 2, :].rearrange("p a b -> p (a b)")), ps[:, :], AF.Sqrt, scale=1.0 / 256.0, bias=EPS)
                    recip(f.rs[:, :, :], f.rs[:, :, :])
                    for c in range(8):
                        tt(S.pool, f.of[:, c, :], f.of[:, c, :], f.rs[:, c // 2, :], ALU.mult)
                        stt(mixT[:, c, :], f.of[:, c, :], gcol[:, c:c + 1], f.rt[:, c, :], ALU.mult, ALU.mult)

                phase3(g, l, front_alloc, front)

    def odd_layer(l, j):
        GQT = scr("GQT", [128, 8, TMAX])
        GKT = scr("GKT", [128, 8, TMAX])
        GKt = scr("GKt", [TMAX, 1024])
        GVt = scr("GVt", [TMAX, 1024])
        GZt = scr("GZt", [TMAX, 1024])
        GR = [scr(f"GR{d}", [5, 8, TMAX]) for d in range(2)]
        GC = [scr(f"GC{d}", [2, 8, TMAX]) for d in range(2)]
        DEC = [scr(f"DEC{d}", [TMAX // 64, 8]) for d in range(2)]
        OG = [scr(f"OG{d}", [TMAX, 1024]) for d in range(2)]
        MQT = scr("MQT", [128, 4, TMAX])
        MKT = scr("MKT", [128, 4, TMAX])
        MKt = scr("MKt", [TMAX, 512])
        MVt = scr("MVt", [TMAX, 1024])
        MOt = scr("MOt", [TMAX, 1024])
        LI = [scr(f"LI{d}", [4, TMAX]) for d in range(2)]
        LF = [scr(f"LF{d}", [4, TMAX]) for d in range(2)]
        MR = [scr(f"MR{d}", [6, 4, TMAX]) for d in range(2)]
        MDEC = [scr(f"MDEC{d}", [TMAX // 64, 4]) for d in range(2)]
        OM = [scr(f"OM{d}", [TMAX, 1024]) for d in range(2)]
        ones = C("ones")
        if DBG_STAGE <= -1:
            return
        with ExitStack() as les:
            cwg = alloc(les, "cwg", [128, 96])
            load_cols(cwg[:, :], I["gdn_cw"].rearrange("a b -> (a b)"), 96)
            dtb = alloc(les, "dtb", [8, 2])
            ld(dtb[:, :], I["gdn_dtb"].rearrange("d h -> h d"), allow_slow_non_contiguous=True)
            negA = alloc(les, "negA", [8, 2])
            ld(negA[:, :], I["gdn_alog"].rearrange("d h -> h d"), allow_slow_non_contiguous=True)
            act(negA[:, :], negA[:, :], AF.Exp)
            ts(S.dve, negA[:, :], negA[:, :], -1.0, ALU.mult)
            mlb = alloc(les, "mlb", [4, 4])
            ld(mlb[:, :], I["ml_gb"].rearrange("d g h -> h (d g)"), allow_slow_non_contiguous=True)
            mlbn = alloc(les, "mlbn", [4, 4])
            ts(S.dve, mlbn[:, :], mlb[:, :], -1.0, ALU.mult)
            wmif = alloc(les, "wmif", [128, KC, 16])
            ld(wmif[:, :, :], I["w_in_o"][:, 7200:7216].rearrange("(kc p) f -> p kc f", p=128))
            wmib = alloc(les, "wmib", [128, KC, 16], BF16)
            cp(S.dve, wmib[:, :, :], wmif[:, :, :])

            for g in make_groups(l):
                L, NL = g.L, g.NL
                if DBG_STAGE <= 0:
                    continue
                with ExitStack() as es:
                    bA, bB = alloc(es, "bA", [128, D]), alloc(es, "bB", [128, D])
                    bc_load(bA, l, g.w, 0)
                    bc_load(bB, l, g.w, 1)
                    xt = [alloc(es, f"xt{i}", [128, D]) for i in range(2)]
                    uTs = [alloc(es, f"uT{i}", [128, KC, TT], BF16) for i in range(2)]
                    junk = alloc(es, "junk", [128, D], BF16)
                    ss = alloc(es, "ss", [128, 4])
                    rstd = alloc(es, "rstd", [128, 4])
                    wb = [alloc(es, f"wb{i}", [128, KC, 512], BF16) for i in range(2)]
                    xp = alloc(es, "xp", [128, 4, NL, L + 3])
                    xc2 = alloc(es, "xc2", [128, 4, TT])
                    sqts = [alloc(es, f"sqt{i}", [128, TT]) for i in range(4)]
                    rs1s = [alloc(es, f"rs1{i}", [128, TT]) for i in range(4)]
                    fst = [alloc(es, f"fst{i}", [128, 8, TT]) for i in range(2)]
                    tst = [alloc(es, f"tst{i}", [128, 2, 1024]) for i in range(2)]
                    e8 = alloc(es, "e8", [8, TT])
                    glog = alloc(es, "glog", [8, TT])
                    lb = alloc(es, "lb", [8, TT])
                    rw = alloc(es, "rw", [8, 5, TT])
                    cs = alloc(es, "cs", [8, 2, TT])
                    dc = alloc(es, "dc", [8, 4])
                    g4 = alloc(es, "g4", [4, 2, TT])
                    e4 = alloc(es, "e4", [4, TT])
                    memset(S.pool, xp[:, :, :, :], 0.0)
                    memset(S.pool, rw[:, :, :], 1.0)
                    aps = []
                    for ti in range(g.ntiles):
                        aps += list(Wb[f"in{l}"])
                    ws = wstream(aps, wb)
                    fst_rr = [0]
                    tst_rr = [0]

                    def nfst():
                        fst_rr[0] ^= 1
                        return fst[fst_rr[0]]

                    def ntst():
                        tst_rr[0] ^= 1
                        return tst[tst_rr[0]]

                    def prep(ti, do_tr=True):
                        for st in range(2):
                            load_x(g, ti, st, xt[st])
                        for st in range(2):
                            norm_mod_T(xt[st], xt[st], bA, bB, junk, ss, rstd, uTs[ti % 2], st, st, do_tr=do_tr)

                    def prep_b(ti):
                        for st in range(2):
                            transpose_to(uTs[ti % 2], xt[st], st)

                    def fm(ps, col, w, wc0, n, uT):
                        for kc in range(KC):
                            mm(ps[0:n, col:col + TT], w[:, kc, wc0:wc0 + n], uT[:, kc, :], start=(kc == 0), stop=(kc == KC - 1))

                    def tmj(ps, w, n, uT, st):
                        for kc in range(KC):
                            mm(ps[:, 0:n], uT[:, kc, st * 128:(st + 1) * 128], w[:, kc, 0:n], start=(kc == 0), stop=(kc == KC - 1))

                    def tm_block_pair(dram, func, scale=None):
                        t_ = ntst()
                        for zb in range(2):
                            w = next(ws)
                            for st in range(2):
                                ps = PS()
                                tmj(ps, w, 512, uT, st)
                                act(t_[:, st, zb * 512:(zb + 1) * 512], ps[:, :], func, scale=scale)
                        stq(dram[t0:t0 + TT, :].rearrange("(s p) f -> p s f", p=128), t_[:, :, :])

                    def to_tm(src, dram):
                        t_ = ntst()
                        for st in range(2):
                            for hq in range(2):
                                ps = PS()
                                for hh in range(4):
                                    h = hq * 4 + hh
                                    tr(ps[:, hh * 128:(hh + 1) * 128], src[:, h, st * 128:(st + 1) * 128], 128)
                                cp(evq(), t_[:, st, hq * 512:(hq + 1) * 512], ps[:, :])
                        stq(dram[t0:t0 + TT, :].rearrange("(s p) f -> p s f", p=128), t_[:, :, :])

                    prep(0)
                    for ti in range(g.ntiles):
                        t0 = ti * TT
                        uT = uTs[ti % 2]
                        for which in range(3):
                            f_ = nfst()
                            pend = []

                            def flush():
                                for (h_, bi2, sc_) in pend:
                                    ps2 = PS()
                                    mm(ps2[:, 0:TT], ones, sqts[bi2][:, :])
                                    act(rs1s[bi2][:, :], ps2[:, 0:TT], AF.Sqrt, bias=EPS)
                                    recip(rs1s[bi2][:, :], rs1s[bi2][:, :])
                                    stt(f_[:, h_, :], xc2[:, bi2, :], sc_, rs1s[bi2][:, :], ALU.mult, ALU.mult)
                                pend.clear()
                            for half in range(2):
                                w = next(ws)
                                for cp_ in range(2):
                                    ps = PS()
                                    for hh in range(2):
                                        fm(ps, hh * TT, w, (cp_ * 2 + hh) * 128, 128, uT)
                                    flush()
                                    for hh in range(2):
                                        h = half * 4 + cp_ * 2 + hh
                                        n = which * 8 + h
                                        bi_ = (cp_ * 2 + hh) % 4
                                        act(xp[:, bi_, :, 2:2 + L], ps.v(ps.t[:, hh * TT:(hh + 1) * TT].rearrange("p (a b) -> p a b", a=NL)), AF.Copy)
                                        xc = xc2.v(xc2.t[:, bi_, :].rearrange("p (a b) -> p a b", a=NL))
                                        act(xc, xp[:, bi_, :, 2:2 + L], AF.Identity, scale=cwg[:, 2 * 24 + n:2 * 24 + n + 1])
                                        for tap in (0, 1, 3):
                                            stt(xc, xp[:, bi_, :, tap:tap + L], cwg[:, tap * 24 + n:tap * 24 + n + 1], xc, ALU.mult, ALU.add)
                                        if which == 2:
                                            act(f_[:, h, :], xc2[:, bi_, :], AF.Silu)
                                        else:
                                            act(xc2[:, bi_, :], xc2[:, bi_, :], AF.Silu)
                                            act(sqts[bi_][:, :], xc2[:, bi_, :], AF.Square)
                                            pend.append((h, bi_, (128.0 ** -0.5) if which == 0 else 1.0))
                            flush()
                            if which == 0:
                                stq(GQT[:, :, t0:t0 + TT], f_[:, :, :])
                            elif which == 1:
                                stq(GKT[:, :, t0:t0 + TT], f_[:, :, :])
                                to_tm(f_, GKt)
                            else:
                                to_tm(f_, GVt)
                        if DBG_SUB <= 1:
                            continue
                        if ti + 1 < g.ntiles:
                            prep(ti + 1, do_tr=False)
                        tm_block_pair(GZt, AF.Silu)
                        if ti + 1 < g.ntiles:
                            prep_b(ti + 1)
                        if DBG_SUB <= 2:
                            continue
                        w = next(ws)
                        for d in range(2):
                            lastoff = 63 if d == 0 else 0
                            ps = PS()
                            fm(ps, 0, w, d * 8, 8, uT)
                            act(e8[:, :], ps[0:8, 0:TT], AF.Exp, bias=dtb[:, d:d + 1])
                            act(e8[:, :], e8[:, :], AF.Ln, bias=1.0)
                            ts(S.dve, glog[:, :], e8[:, :], negA[:, d:d + 1], ALU.mult)
                            ps = PS()
                            fm(ps, 0, w, 16 + d * 8, 8, uT)
                            act(e8[:, :], ps[0:8, 0:TT], AF.Exp, scale=-1.0)
                            act(e8[:, :], e8[:, :], AF.Ln, bias=1.0)
                            ts(S.pool, lb[:, :], e8[:, :], -1.0, ALU.mult)
                            rst = C("reset")
                            if d == 0:
                                S.op(S.dve, lambda e: e.tensor_tensor_scan(out=rw.t[:, 0, :], data0=rst.ap[0:8, :], data1=glog.t[:, :], initial=0.0, op0=ALU.mult, op1=ALU.add),
                                     reads=[CT, glog], writes=[rw])
                            else:
                                S.op(S.dve, lambda e: e.tensor_tensor_scan(out=rw.t[:, 0, ::-1], data0=rst.ap[0:8, :], data1=glog.t[:, ::-1], initial=0.0, op0=ALU.mult, op1=ALU.add),
                                     reads=[CT, glog], writes=[rw])
                            ts(S.pool, rw[:, 4, :], rw[:, 0, :], -1.0, ALU.mult)
                            tt(S.pool, rw[:, 2, :], lb[:, :], rw[:, 0, :], ALU.subtract)
                            act(cs[:, 0, :], lb[:, :], AF.Exp)
                            for c in range(4):
                                li_ = c * 64 + lastoff
                                ts(S.dve, cs[:, 1, c * 64:(c + 1) * 64], rw[:, 0, c * 64:(c + 1) * 64], -1.0, ALU.mult, rw[:, 0, li_:li_ + 1], ALU.add)
                            act(cs[:, 1, :], cs[:, 1, :], AF.Exp)
                            act(dc[:, :], rw[:, 0, lastoff::64], AF.Exp)
                            stq(GR[d][:, :, t0:t0 + TT].rearrange("k h t -> h k t"), rw[:, :, :])
                            stq(GC[d][:, :, t0:t0 + TT].rearrange("k h t -> h k t"), cs[:, :, :])
                            stq(DEC[d][ti * 4:(ti + 1) * 4, :].rearrange("c h -> h c"), dc[:, :], allow_slow_non_contiguous=True)
                        if DBG_SUB <= 3:
                            continue
                        f_ = nfst()
                        for which in range(2):
                            w = next(ws)
                            for hp in range(2):
                                ps = PS()
                                for hh in range(2):
                                    fm(ps, hh * TT, w, (hp * 2 + hh) * 128, 128, uT)
                                c0 = which * 4 + hp * 2
                                act(f_.v(f_.t[:, c0:c0 + 2, :].rearrange("p a b -> p (a b)")), ps[:, :], AF.Copy, scale=1.0 if which == 0 else 128.0 ** -0.5)
                            stq((MQT, MKT)[which][:, :, t0:t0 + TT], f_[:, which * 4:which * 4 + 4, :])
                            if which == 1:
                                t_ = ntst()
                                for st in range(2):
                                    ps = PS()
                                    tmj(ps, w, 512, uT, st)
                                    act(t_[:, st, 0:512], ps[:, :], AF.Copy, scale=128.0 ** -0.5)
                                stq(MKt[t0:t0 + TT, :].rearrange("(s p) f -> p s f", p=128), t_[:, :, 0:512])
                        if DBG_SUB <= 4:
                            continue
                        tm_block_pair(MVt, AF.Copy)
                        tm_block_pair(MOt, AF.Sigmoid)
                        if DBG_SUB <= 5:
                            continue
                        w = wmib
                        for d in range(2):
                            ps = PS()
                            fm(ps, 0, w, d * 4, 4, uT)
                            act(g4[:, 0, :], ps[0:4, 0:TT], AF.Identity, bias=mlb[:, d * 2:d * 2 + 1])
                            ps = PS()
                            fm(ps, 0, w, 8 + d * 4, 4, uT)
                            act(e4[:, :], ps[0:4, 0:TT], AF.Exp, scale=-1.0, bias=mlbn[:, d * 2 + 1:d * 2 + 2])
                            act(e4[:, :], e4[:, :], AF.Ln, bias=1.0)
                            ts(S.pool, g4[:, 1, :], e4[:, :], -1.0, ALU.mult)
                            stq(LI[d][:, t0:t0 + TT], g4[:, 0, :])
                            stq(LF[d][:, t0:t0 + TT], g4[:, 1, :])
                S.barrier()

                if DBG_STAGE <= 1:
                    continue
                with ExitStack() as es:
                    TS = max(n_ for (_, n_, _) in g.seqs) * TT
                    lf = alloc(es, "m_lf", [4, TS])
                    li = alloc(es, "m_li", [4, TS])
                    mt = alloc(es, "m_m", [4, TS])
                    Ft = alloc(es, "m_F", [4, TS])
                    RWt = alloc(es, "m_RW", [4, TS])
                    WLt = alloc(es, "m_WL", [4, TS])
                    rstt = alloc(es, "m_rst", [4, TS])
                    one4 = alloc(es, "m_one", [4, TS])
                    m0c = alloc(es, "m_m0", [4, 2])
                    dcm = alloc(es, "m_dc", [4, TS // 64])
                    memset(S.pool, rstt[:, :], 1.0)
                    memset(S.pool, rstt[:, 0::64], 0.0)
                    memset(S.pool, one4[:, :], 1.0)
                    for (tst_, ntl, sidx) in g.seqs:
                        T_ = ntl * TT
                        nch = T_ // 64
                        tsl = slice(tst_ * TT, tst_ * TT + T_)
                        if sidx is None:
                            ld(m0c[:, :], I["st_mm"].rearrange("d h -> h d"), allow_slow_non_contiguous=True)
                        else:
                            memset(S.pool, m0c[:, :], 0.0)
                        for d in range(2):
                            ld(lf[:, 0:T_], LF[d][:, tsl])
                            ld(li[:, 0:T_], LI[d][:, tsl])
                            rv = slice(None, None, -1) if d == 1 else slice(None)

                            def sc_(o, a, b, ini, op0, op1, rev0=True):
                                rd = [a.b, b.b]
                                iv = ini
                                if isinstance(ini, View):
                                    rd.append(ini.b)
                                    iv = ini.ap
                                rv0 = rv if rev0 else slice(None)
                                S.op(S.dve, lambda e: e.tensor_tensor_scan(out=o.b.t[:, 0:T_][:, rv], data0=a.b.t[:, 0:T_][:, rv0], data1=b.b.t[:, 0:T_][:, rv],
                                                                          initial=iv, op0=op0, op1=op1), reads=rd, writes=[o.b])
                            sc_(mt[:, :], lf[:, :], li[:, :], m0c[:, d:d + 1], ALU.add, ALU.max)
                            sc_(Ft[:, :], rstt[:, :], lf[:, :], 0.0, ALU.mult, ALU.add, rev0=False)
                            tt(S.pool, li[:, 0:T_], li[:, 0:T_], Ft[:, 0:T_], ALU.subtract)
                            tt(S.pool, Ft[:, 0:T_], Ft[:, 0:T_], mt[:, 0:T_], ALU.subtract)
                            for c in range(nch):
                                lastc = c * 64 + (63 if d == 0 else 0)
                                if d == 0:
                                    mp = m0c[:, 0:1] if c == 0 else mt[:, c * 64 - 1:c * 64]
                                else:
                                    mp = m0c[:, 1:2] if c == nch - 1 else mt[:, (c + 1) * 64:(c + 1) * 64 + 1]
                                ts(S.dve, RWt[:, c * 64:(c + 1) * 64], Ft[:, c * 64:(c + 1) * 64], mp, ALU.add)
                                ts(S.dve, WLt[:, c * 64:(c + 1) * 64], li[:, c * 64:(c + 1) * 64], Ft[:, lastc:lastc + 1], ALU.add)
                            lo = 63 if d == 0 else 0
                            act(dcm[:, 0:nch], RWt[:, lo:T_:64], AF.Exp)
                            if sidx is not None:
                                le = T_ - 1 if d == 0 else 0
                                stq(O["o_mm"][sidx, d, :].rearrange("(h o) -> h o", o=1), mt[:, le:le + 1], allow_slow_non_contiguous=True)
                            stq(MR[d][0, :, tsl], Ft[:, 0:T_])
                            stq(MR[d][1, :, tsl], one4[:, 0:T_])
                            stq(MR[d][2, :, tsl], li[:, 0:T_])
                            stq(MR[d][3, :, tsl], RWt[:, 0:T_])
                            stq(MR[d][4, :, tsl], WLt[:, 0:T_])
                            ts(S.pool, mt[:, 0:T_], mt[:, 0:T_], -1.0, ALU.mult)
                            stq(MR[d][5, :, tsl], mt[:, 0:T_])
                            stq(MDEC[d][tst_ * 4:tst_ * 4 + nch, :].rearrange("c h -> h c"), dcm[:, 0:nch], allow_slow_non_contiguous=True)
                S.barrier()

                if DBG_STAGE <= 2:
                    continue
                with ExitStack() as es:
                    NCHM = max(n_ for (_, n_, _) in g.seqs) * 4
                    def mkb():
                        Sst = alloc(es, "gS", [128, 8, 128])
                        qTb = [alloc(es, f"gq{i}", [128, 8, 64]) for i in range(2)]
                        kTb = [alloc(es, f"gk{i}", [128, 8, 64]) for i in range(2)]
                        ktp = [alloc(es, f"gkt{i}", [128, 4, 128]) for i in range(2)]
                        vtp = [alloc(es, f"gvt{i}", [128, 4, 128]) for i in range(2)]
                        R01 = [alloc(es, f"gr01{i}", [2, 8, 64]) for i in range(2)]
                        R12 = [alloc(es, f"gr12{i}", [2, 8, 64]) for i in range(2)]
                        R34 = [alloc(es, f"gr34{i}", [2, 8, 64]) for i in range(2)]
                        COLS = [alloc(es, f"gcol{i}", [128, 4, 2]) for i in range(2)]
                        G1, G2, G3 = [alloc(es, f"gG{i}", [128, 512]) for i in range(3)]
                        Pb = [alloc(es, f"gP{i}", [128, 512]) for i in range(2)]
                        PTb = [alloc(es, f"gPT{i}", [128, 512]) for i in range(2)]
                        Ttb = [alloc(es, f"gTt{i}", [128, 512]) for i in range(2)]
                        QKT = alloc(es, "gQKT", [128, 512])
                        EB = alloc(es, "gEB", [128, 512])
                        qp = alloc(es, "gqp", [128, 8, 64])
                        kp = alloc(es, "gkp", [128, 8, 64])
                        rr = alloc(es, "grr", [128, 4, 128])
                        vn = alloc(es, "gvn", [128, 4, 128])
                        og = alloc(es, "gog", [128, 4, 128])
                        kpp = alloc(es, "gkpp", [128, 4, 128])
                        DECB = alloc(es, "gDECB", [128, NCHM * 8])
                        drow = alloc(es, "gdrow", [1, NCHM * 8])
                        return (Sst, qTb, kTb, ktp, vtp, R01, R12, R34, COLS, G1, G2, G3, Pb, PTb, Ttb, QKT, EB, qp, kp, rr, vn, og, kpp, DECB, drow)
                    BB = [mkb(), mkb()]
                    id4 = C("ident4")

                    def fl(t):
                        return t.v(t.t[:, :, :].rearrange("p a b -> p (a b)"))

                    def pr(t, hp, rows=128):
                        return t.v(t.t[0:rows, 2 * hp:2 * hp + 2, :].rearrange("p a b -> p (a b)"))

                    def s4(t, hp):
                        return t[:, hp * 128:(hp + 1) * 128]

                    def chain(d):
                        (Sst, qTb, kTb, ktp, vtp, R01, R12, R34, COLS, G1, G2, G3, Pb, PTb, Ttb, QKT, EB, qp, kp, rr, vn, og, kpp, DECB, drow) = BB[d]
                        for (tst_, ntl, sidx) in g.seqs:
                            nch = ntl * 4
                            if True:
                                if sidx is None:
                                    ld(Sst[:, :, :], I["st_gdn"][d].rearrange("h k v -> k h v"))
                                else:
                                    memset(S.pool, Sst[:, :, :], 0.0)
                                c0g = tst_ * 4
                                ld(drow[0:1, 0:nch * 8], DEC[d][c0g:c0g + nch, :].rearrange("(o c) h -> o (c h)", o=1))
                                for q0 in range(0, nch * 8, 512):
                                    q1 = min(q0 + 512, nch * 8)
                                    ps = PS()
                                    mm(ps[:, 0:q1 - q0], ones.b.v(ones.ap[0:1, 0:128]), drow[0:1, q0:q1])
                                    cp(S.dve, DECB[:, q0:q1], ps[:, 0:q1 - q0])
                                order = list(range(nch)) if d == 0 else list(range(nch - 1, -1, -1))
                                NM_, NMT_, NQK_ = (C("nlt"), C("ngt"), C("nge")) if d == 0 else (C("ngt"), C("nlt"), C("nle"))

                                def loads(k):
                                    c = order[k]
                                    t0 = tst_ * TT + c * 64
                                    i = k % 2
                                    ld(qTb[i][:, :, :], GQT[:, :, t0:t0 + 64])
                                    ld(kTb[i][:, :, :], GKT[:, :, t0:t0 + 64])
                                    for h2 in range(2):
                                        ld(ktp[i][h2 * 64:(h2 + 1) * 64, :, :], GKt[t0:t0 + 64, :].rearrange("s (hp h2 d) -> h2 s hp d", hp=4, h2=2)[h2])
                                        ld(vtp[i][h2 * 64:(h2 + 1) * 64, :, :], GVt[t0:t0 + 64, :].rearrange("s (hp h2 d) -> h2 s hp d", hp=4, h2=2)[h2])
                                        for kk in range(2):
                                            ld(COLS[i][h2 * 64:(h2 + 1) * 64, :, kk], GC[d][kk, h2::2, t0:t0 + 64].rearrange("hp s -> s hp"), allow_slow_non_contiguous=True)
                                    ld(R01[i][:, :, :], GR[d][0:2, :, t0:t0 + 64])
                                    ld(R12[i][:, :, :], GR[d][1:3, :, t0:t0 + 64])
                                    ld(R34[i][:, :, :], GR[d][3:5, :, t0:t0 + 64])
                                loads(0)
                                for k in range(nch):
                                    if k + 1 < nch:
                                        loads(k + 1)
                                    c = order[k]
                                    t0 = tst_ * TT + c * 64
                                    i = k % 2
                                    qT, kT, kt, vt, r01, r12, r34, cols = qTb[i], kTb[i], ktp[i], vtp[i], R01[i], R12[i], R34[i], COLS[i]
                                    pE = [PS(), PS(), PS()]
                                    for e_, (msk, la, ra) in enumerate(((NM_, r01, r12), (NMT_, r12, r01), (NQK_, r34, r01))):
                                        mm(pE[e_][:, :], ident, msk, start=True, stop=False)
                                        for hp in range(4):
                                            mm(s4(pE[e_], hp), pr(la, hp, 2), pr(ra, hp, 2), start=False, stop=(hp == 3))
                                    act(G1[:, :], pE[0][:, :], AF.Exp)
                                    act(G2[:, :], pE[1][:, :], AF.Exp)
                                    act(G3[:, :], pE[2][:, :], AF.Exp)
                                    yield
                                    pKK, pKQ = PS(), PS()
                                    for hp in range(4):
                                        mm(s4(pKK, hp), pr(kT, hp), pr(kT, hp))
                                    for hp in range(4):
                                        mm(s4(pKQ, hp), pr(kT, hp), pr(qT, hp))
                                    P, PT, Tt = Pb[0], PTb[0], Ttb[0]
                                    tt(S.dve, P[:, :], pKK[:, :], G1[:, :], ALU.mult)
                                    tt(S.dve, PT[:, :], pKK[:, :], G2[:, :], ALU.mult)
                                    tt(S.dve, QKT[:, :], pKQ[:, :], G3[:, :], ALU.mult)
                                    tt(S.pool, Tt[:, :], id4, PT[:, :], ALU.subtract)
                                    yield
                                    cur = 0
                                    for jl in range(1, 6):
                                        Pn, PTn, Ttn = Pb[1 - cur], PTb[1 - cur], Ttb[1 - cur]
                                        pP = PS()
                                        for hp in range(4):
                                            mm(s4(pP, hp), s4(PT, hp), s4(P, hp))
                                        if jl < 5:
                                            pPT = PS()
                                            for hp in range(4):
                                                mm(s4(pPT, hp), s4(P, hp), s4(PT, hp))
                                        cp(S.act, Pn[:, :], pP[:, :])
                                        if jl < 5:
                                            cp(S.dve, PTn[:, :], pPT[:, :])
                                        yield
                                        pT = PS()
                                        for hp in range(4):
                                            mm(s4(pT, hp), s4(Pn, hp), s4(Tt, hp), start=True, stop=True)
                                        tt(S.dve, Ttn[:, :], pT[:, :], Tt[:, :], ALU.add)
                                        P, PT, Tt = Pn, PTn, Ttn
                                        cur = 1 - cur
                                        yield
                                    yield
                                    pB = PS()
                                    mm(pB[:, :], ones.b.v(ones.ap[0:1, 0:128]), r01.v(r01.t[0:1, :, :].rearrange("p a b -> p (a b)")))
                                    act(EB[:, :], pB[:, :], AF.Exp)
                                    tt(S.dve, fl(qp), fl(qT), EB[:, :], ALU.mult)
                                    tt(S.pool, fl(kp), fl(kT), EB[:, :], ALU.mult)
                                    yield
                                    pS_ = PS()
                                    for hp in range(4):
                                        for h2 in range(2):
                                            h = hp * 2 + h2
                                            mm(pS_[h2 * 64:(h2 + 1) * 64, hp * 128:(hp + 1) * 128], kp[:, h, :], Sst[:, h, :])
                                    tt(S.dve, fl(rr), fl(vt), pS_[:, :], ALU.subtract)
                                    yield
                                    pV = PS()
                                    for hp in range(4):
                                        mm(s4(pV, hp), s4(Tt, hp), rr[:, hp, :])
                                    for hp in range(4):
                                        act(vn[:, hp, :], s4(pV, hp), AF.Copy, scale=cols[:, hp, 0:1])
                                    yield
                                    pO = PS()
                                    for hp in range(4):
                                        for h2 in range(2):
                                            h = hp * 2 + h2
                                            mm(pO[h2 * 64:(h2 + 1) * 64, hp * 128:(hp + 1) * 128], qp[:, h, :], Sst[:, h, :], start=True, stop=False)
                                        mm(s4(pO, hp), s4(QKT, hp), vn[:, hp, :], start=False, stop=True)
                                    cp(evq(), fl(og), pO[:, :])
                                    for h2 in range(2):
                                        stq(OG[d][t0:t0 + 64, :].rearrange("t (hp h2 v) -> h2 t hp v", hp=4, h2=2)[h2], og[h2 * 64:(h2 + 1) * 64, :, :])
                                    yield
                                    tt(S.pool, kpp[:, :, :], kt[:, :, :], cols.v(cols.t[:, :, 1:2].to_broadcast([128, 4, 128])), ALU.mult)
                                    pU = [PS(), PS()]
                                    for h2 in range(2):
                                        for hp in range(4):
                                            mm(pU[h2][:, hp * 128:(hp + 1) * 128], kpp[h2 * 64:(h2 + 1) * 64, hp, :], vn[h2 * 64:(h2 + 1) * 64, hp, :])
                                    tt(S.pool, Sst[:, :, :], Sst[:, :, :], DECB.v(DECB.t[:, c * 8:(c + 1) * 8].unsqueeze(2).to_broadcast([128, 8, 128])), ALU.mult)
                                    for h2 in range(2):
                                        tt(S.dve, Sst.v(Sst.t[:, h2::2, :]), Sst.v(Sst.t[:, h2::2, :]),
                                           pU[h2].v(pU[h2].t[:, :].rearrange("p (a b) -> p a b", a=4)), ALU.add)
                                    yield
                                if sidx is not None:
                                    stq(O["o_gdn"][sidx, d].rearrange("h k v -> k h v"), Sst[:, :, :])
                    run_il([chain(0), chain(1)])
                S.barrier()

                if DBG_STAGE <= 3:
                    continue
                with ExitStack() as es:
                    NCHM = max(n_ for (_, n_, _) in g.seqs) * 4
                    def mkb():
                        Cst = alloc(es, "mC", [128, 4, 257])
                        qTb = [alloc(es, f"mq{i}", [128, 4, 64]) for i in range(2)]
                        kTb = [alloc(es, f"mk{i}", [128, 4, 64]) for i in range(2)]
                        ktp = [alloc(es, f"mkt{i}", [128, 2, 128]) for i in range(2)]
                        vtp = [alloc(es, f"mvt{i}", [128, 2, 257]) for i in range(2)]
                        R01 = [alloc(es, f"mr01{i}", [2, 4, 64]) for i in range(2)]
                        R12 = [alloc(es, f"mr12{i}", [2, 4, 64]) for i in range(2)]
                        R3 = [alloc(es, f"mr3{i}", [1, 4, 64]) for i in range(2)]
                        MCL = [alloc(es, f"mcl{i}", [128, 2, 2]) for i in range(2)]
                        MC = alloc(es, "mMC", [128, 2, 2])
                        Gm = alloc(es, "mG", [128, 256])
                        QKT = alloc(es, "mQKT", [128, 256])
                        WB = alloc(es, "mWB", [128, 256])
                        qp = alloc(es, "mqp", [128, 4, 64])
                        hout = alloc(es, "mh", [128, 2, 256])
                        kpp = alloc(es, "mkpp", [128, 2, 128])
                        dcol = alloc(es, "mdcol", [128, 2])
                        DECB = alloc(es, "mDECB", [128, NCHM * 4])
                        drow = alloc(es, "mdrow", [1, NCHM * 4])
                        return (Cst, qTb, kTb, ktp, vtp, R01, R12, R3, MCL, MC, Gm, QKT, WB, qp, hout, kpp, dcol, DECB, drow)
                    BB = [mkb(), mkb()]
                    for B_ in BB:
                        for i in range(2):
                            memset(S.pool, B_[4][i][:, :, 256:257], 1.0)

                    def fl(t):
                        return t.v(t.t[:, :, :].rearrange("p a b -> p (a b)"))

                    def pr(t, hp, rows=128):
                        return t.v(t.t[0:rows, 2 * hp:2 * hp + 2, :].rearrange("p a b -> p (a b)"))

                    def chain(d):
                        (Cst, qTb, kTb, ktp, vtp, R01, R12, R3, MCL, MC, Gm, QKT, WB, qp, hout, kpp, dcol, DECB, drow) = BB[d]
                        for (tst_, ntl, sidx) in g.seqs:
                            nch = ntl * 4
                            if True:
                                if sidx is None:
                                    ld(Cst[:, :, 0:256], I["st_mc"][d].rearrange("h k v -> k h v"))
                                    ld(Cst[:, :, 256:257], I["st_mn"][d].rearrange("h (k o) -> k h o", o=1), allow_slow_non_contiguous=True)
                                else:
                                    memset(S.pool, Cst[:, :, :], 0.0)
                                c0g = tst_ * 4
                                ld(drow[0:1, 0:nch * 4], MDEC[d][c0g:c0g + nch, :].rearrange("(o c) h -> o (c h)", o=1))
                                ps = PS()
                                mm(ps[:, 0:nch * 4], ones.b.v(ones.ap[0:1, 0:128]), drow[0:1, 0:nch * 4])
                                cp(S.dve, DECB[:, 0:nch * 4], ps[:, 0:nch * 4])
                                order = list(range(nch)) if d == 0 else list(range(nch - 1, -1, -1))
                                NQK_ = C("nge") if d == 0 else C("nle")

                                def loads(k):
                                    c = order[k]
                                    t0 = tst_ * TT + c * 64
                                    i = k % 2
                                    ld(qTb[i][:, :, :], MQT[:, :, t0:t0 + 64])
                                    ld(kTb[i][:, :, :], MKT[:, :, t0:t0 + 64])
                                    for h2 in range(2):
                                        ld(ktp[i][h2 * 64:(h2 + 1) * 64, :, :], MKt[t0:t0 + 64, :].rearrange("s (hp h2 d) -> h2 s hp d", hp=2, h2=2)[h2])
                                        ld(vtp[i][h2 * 64:(h2 + 1) * 64, :, 0:256], MVt[t0:t0 + 64, :].rearrange("s (hp h2 v) -> h2 s hp v", hp=2, h2=2)[h2])
                                        for kk in range(2):
                                            ld(MCL[i][h2 * 64:(h2 + 1) * 64, :, kk], MR[d][4 + kk, h2::2, t0:t0 + 64].rearrange("hp s -> s hp"), allow_slow_non_contiguous=True)
                                    ld(R01[i][:, :, :], MR[d][0:2, :, t0:t0 + 64])
                                    ld(R12[i][:, :, :], MR[d][1:3, :, t0:t0 + 64])
                                    ld(R3[i][:, :, :], MR[d][3:4, :, t0:t0 + 64])
                                loads(0)
                                for k in range(nch):
                                    if k + 1 < nch:
                                        loads(k + 1)
                                    c = order[k]
                                    t0 = tst_ * TT + c * 64
                                    i = k % 2
                                    qT, kT, kt, vt, r01, r12, r3, mcl = qTb[i], kTb[i], ktp[i], vtp[i], R01[i], R12[i], R3[i], MCL[i]
                                    pE = PS()
                                    mm(pE[:, 0:256], ident, NQK_.b.v(NQK_.ap[:, 0:256]), start=True, stop=False)
                                    for hp in range(2):
                                        mm(pE[:, hp * 128:(hp + 1) * 128], pr(r12, hp, 2), pr(r01, hp, 2), start=False, stop=(hp == 1))
                                    act(Gm[:, :], pE[:, 0:256], AF.Exp)
                                    yield
                                    pKQ = PS()
                                    for hp in range(2):
                                        mm(pKQ[:, hp * 128:(hp + 1) * 128], pr(kT, hp), pr(qT, hp))
                                    tt(S.dve, QKT[:, :], pKQ[:, 0:256], Gm[:, :], ALU.mult)
                                    yield
                                    pB = PS()
                                    mm(pB[:, 0:256], ones.b.v(ones.ap[0:1, 0:128]), r3.v(r3.t[0:1, :, :].rearrange("p a b -> p (a b)")))
                                    act(WB[:, :], pB[:, 0:256], AF.Exp)
                                    tt(S.dve, fl(qp), fl(qT), WB[:, :], ALU.mult)
                                    yield
                                    act(MC[:, :, :], mcl[:, :, :], AF.Exp)
                                    for hp in range(2):
                                        pN = PS()
                                        for h2 in range(2):
                                            h = hp * 2 + h2
                                            mm(pN[h2 * 64:(h2 + 1) * 64, 0:257], qp[:, h, :], Cst[:, h, :], start=True, stop=False)
                                        mm(pN[:, 0:257], QKT[:, hp * 128:(hp + 1) * 128], vt[:, hp, :], start=False, stop=True)
                                        act(dcol[:, hp:hp + 1], pN[:, 256:257], AF.Abs)
                                        ts(S.dve, dcol[:, hp:hp + 1], dcol[:, hp:hp + 1], MC[:, hp, 1:2], ALU.max)
                                        recip(dcol[:, hp:hp + 1], dcol[:, hp:hp + 1])
                                        act(hout[:, hp, :], pN[:, 0:256], AF.Copy, scale=dcol[:, hp:hp + 1])
                                        yield
                                    for h2 in range(2):
                                        stq(OM[d][t0:t0 + 64, :].rearrange("t (hp h2 v) -> h2 t hp v", hp=2, h2=2)[h2], hout[h2 * 64:(h2 + 1) * 64, :, :])
                                    yield
                                    tt(S.pool, kpp[:, :, :], kt[:, :, :], MC.v(MC.t[:, :, 0:1].to_broadcast([128, 2, 128])), ALU.mult)
                                    for h in range(4):
                                        hp, h2 = h // 2, h % 2
                                        pU = PS()
                                        mm(pU[:, 0:257], kpp[h2 * 64:(h2 + 1) * 64, hp, :], vt[h2 * 64:(h2 + 1) * 64, hp, :])
                                        stt(Cst[:, h, :], Cst[:, h, :], DECB[:, c * 4 + h:c * 4 + h + 1], pU[:, 0:257], ALU.mult, ALU.add)
                                        yield
                                if sidx is not None:
                                    stq(O["o_mc"][sidx, d].rearrange("h k v -> k h v"), Cst[:, :, 0:256])
                                    stq(O["o_mn"][sidx, d].rearrange("h (k o) -> k h o", o=1), Cst[:, :, 256:257], allow_slow_non_contiguous=True)
                    run_il([chain(0), chain(1)])
                S.barrier()

                if DBG_STAGE <= 4:
                    continue
                def front_alloc(es):
                    f = Grp()
                    f.a = [alloc(es, f"fo_a{i}", [128, 1024]) for i in range(2)]
                    f.m = [alloc(es, f"fo_m{i}", [128, 1024]) for i in range(2)]
                    f.gz = alloc(es, "fo_gz", [128, 1024])
                    f.mo = alloc(es, "fo_mo", [128, 1024])
                    f.mix = alloc(es, "fo_mix", [128, D])
                    f.gg = alloc(es, "fo_gg", [128, 1024])
                    f.mg = alloc(es, "fo_mg", [128, 1024])
                    f.ssq = alloc(es, "fo_ss", [128, 16])
                    ld(f.gg[:, :], I["gdn_g"].rearrange("(o n) -> o n", o=1).partition_broadcast(128).rearrange("p a b -> p (a b)"))
                    ld(f.mg[:, :], I["ml_g"].rearrange("(o n) -> o n", o=1).partition_broadcast(128).rearrange("p a b -> p (a b)"))
                    return f

                def front(f, ti, mixT):
                    for st in range(2):
                        r0 = ti * TT + st * 128
                        ld(f.a[0][:, :], OG[0][r0:r0 + 128, :])
                        ld(f.a[1][:, :], OG[1][r0:r0 + 128, :])
                        ld(f.m[0][:, :], OM[0][r0:r0 + 128, :])
                        ld(f.m[1][:, :], OM[1][r0:r0 + 128, :])
                        ld(f.gz[:, :], GZt[r0:r0 + 128, :])
                        ld(f.mo[:, :], MOt[r0:r0 + 128, :])
                        for (bufs, nh, hd, gt, gate, off, sc0) in ((f.a, 8, 128, f.gg, f.gz, 0, 0), (f.m, 4, 256, f.mg, f.mo, 1024, 8)):
                            o_, sq_ = bufs
                            tt(S.pool, o_[:, :], o_[:, :], sq_[:, :], ALU.add)
                            act(sq_[:, :], o_[:, :], AF.Square)
                            S.op(S.dve, lambda e, sq_=sq_, nh=nh, sc0=sc0: e.tensor_reduce(out=f.ssq.t[:, sc0:sc0 + nh], in_=sq_.t[:, :].rearrange("p (a b) -> p a b", a=nh), axis=AX.X, op=ALU.add),
                                 reads=[sq_], writes=[f.ssq])
                            act(f.ssq[:, sc0:sc0 + nh], f.ssq[:, sc0:sc0 + nh], AF.Sqrt, scale=1.0 / hd, bias=EPS)
                            recip(f.ssq[:, sc0:sc0 + nh], f.ssq[:, sc0:sc0 + nh])
                            tt(S.dve, o_.v(o_.t[:, :].rearrange("p (a b) -> p a b", a=nh)), o_.v(o_.t[:, :].rearrange("p (a b) -> p a b", a=nh)),
                               f.ssq.v(f.ssq.t[:, sc0:sc0 + nh].unsqueeze(2).to_broadcast([128, nh, hd])), ALU.mult)
                            tt(S.pool, o_[:, :], o_[:, :], gt[:, :], ALU.mult)
                            tt(S.pool, f.mix[:, off:off + 1024], o_[:, :], gate[:, :], ALU.mult)
                        transpose_to(mixT, f.mix, st)

                phase3(g, l, front_alloc, front)

    for l in range(depth):
        if DBG_STAGE <= -2:
            break
        if l % 2 == 0:
            even_layer(l, l // 2)
        else:
            odd_layer(l, l // 2)
    S.barrier()
    top.close()
    build.stats = {q.name: q.ninst for q in S.engs}
    return nc


N_CORES = 8
_CACHE = {}


def make_in_maps(inp, NCT=4, NLT=16, n_cores=N_CORES):
    f = lambda a: np.ascontiguousarray(np.asarray(a, dtype=np.float32))
    shared = {
        "w_mod": f(inp["w_mod"]), "b_mod": f(inp["b_mod"]), "norm_g": f(inp["norm_g"]),
        "w_up": f(inp["w_up"]), "w_down": f(inp["w_down"]),
        "w_in_e": f(inp["w_in_e"][0]), "gla_w2": f(inp["gla_gate_w2"][0]), "gla_b": f(inp["gla_gate_b"][0]),
        "gla_g": f(inp["gla_norm_g"][0]), "lru_cw": f(inp["lru_conv_w"][0]), "lru_cb": f(inp["lru_conv_b"][0]),
        "lru_gw": f(inp["lru_gate_w"][0]), "lru_gb": f(inp["lru_gate_b"][0]), "lru_lam": f(inp["lru_lambda"][0]),
        "w_out_e": f(inp["w_out_e"][0]),
        "w_in_o": f(inp["w_in_o"][0]), "gdn_cw": f(inp["gdn_conv_w"][0]), "gdn_alog": f(inp["gdn_a_log"][0]),
        "gdn_dtb": f(inp["gdn_dt_bias"][0]), "gdn_g": f(inp["gdn_norm_g"][0]), "ml_gb": f(inp["mlstm_gate_b"][0]),
        "ml_g": f(inp["mlstm_norm_g"][0]), "w_out_o": f(inp["w_out_o"][0]),
        "consts": CONST_ARR,
    }
    maps = []
    xp_, xs_ = f(inp["x_prompt"]), f(inp["x_sample"])
    for c in range(n_cores):
        b = c % 4
        m = dict(shared)
        m["xc"] = xp_[c * NCT:(c + 1) * NCT].reshape(NCT * TT, D)
        m["xl"] = np.ascontiguousarray(xs_[b, :NLT * TT])
        m["cc"] = np.ascontiguousarray(np.stack([f(inp["c_ctx"]), f(inp["c"])[b]], 0))
        m["st_gla"] = f(inp["state_gla"])[b, 0]
        m["st_lru"] = f(inp["state_lru"])[b, 0]
        m["st_gdn"] = f(inp["state_gdn"])[b, 0]
        m["st_mc"] = f(inp["state_mlstm_c"])[b, 0]
        m["st_mn"] = f(inp["state_mlstm_n"])[b, 0]
        m["st_mm"] = f(inp["state_mlstm_m"])[b, 0]
        maps.append(m)
    return maps


def kernel(**inp):
    if "nc" not in _CACHE:
        _CACHE["nc"] = build()
    nc = _CACHE["nc"]
    maps = make_in_maps(inp)
    res = run_bass_kernel_spmd(nc, maps, core_ids=list(range(N_CORES)))
    R = res.results
    B = 32
    y_ctx = np.concatenate([R[c]["yc"].reshape(4, TT, D) for c in range(8)], 0)
    y_lat = np.stack([R[b]["yl"] for b in range(4)], 0)
    cat = lambda k: np.concatenate([R[c][k] for c in range(8)], 0)
    new_gla = cat("o_gla")[:, None]
    new_lru = cat("o_lru")[:, None]
    new_gdn = cat("o_gdn")[:, None]
    new_c = cat("o_mc")[:, None]
    new_n = cat("o_mn")[:, None]
    new_m = cat("o_mm")[:, None]
    return tuple(np.ascontiguousarray(a.astype(np.float32)) for a in (y_ctx, y_lat, new_gla, new_lru, new_gdn, new_c, new_n, new_m))


def simulate_trace(trace):
    sems = {}
    pos = {k: 0 for k in trace}
    progress = True
    while progress:
        progress = False
        for k, lst in trace.items():
            while pos[k] < len(lst):
                kind, sid, val = lst[pos[k]]
                if kind == "w":
                    if sems.get(sid, 0) >= val:
                        pos[k] += 1
                        progress = True
                    else:
                        break
                else:
                    sems[sid] = sems.get(sid, 0) + val
                    pos[k] += 1
                    progress = True
    stuck = {k: (pos[k], len(lst), lst[pos[k]] if pos[k] < len(lst) else None) for k, lst in trace.items()}
    if all(p == n for (p, n, _) in stuck.values()):
        return None
    return stuck, sems
```
